# Optimizing a Trainium2 kernel written in Bass

```python
import math
import jax, jax.numpy as jnp
from jax import lax
import numpy as np

D_MODEL = 1024
BATCH = 2
SEQ = 16384
DEPTH = 2

MEM_LEN = 256
EPS = 1e-6
LRU_WIDTH = 768
LRU_BLOCKS = 12
LRU_BLOCK = LRU_WIDTH // LRU_BLOCKS
LRU_CONV = 4
LRU_C = 8.0
ATT_HEADS = 12
HEAD_DIM = 64
ATT_WIDTH = ATT_HEADS * HEAD_DIM
DIL_GROUPS = ((128, 1), (512, 4), (2048, 16))
HEADS_PER_GROUP = ATT_HEADS // len(DIL_GROUPS)
ATT_OUT = HEADS_PER_GROUP * HEAD_DIM
Q_BLOCK = 128
ALIBI_SLOPES = tuple(2.0 ** (-8.0 * (h + 1) / ATT_HEADS) for h in range(ATT_HEADS))
SSM_WIDTH = 768
SSM_GROUP = 16
SSM_GROUPS = SSM_WIDTH // SSM_GROUP
SSM_STATE = 64
DT_MIN = 1e-3
DT_MAX = 1e-1
X_HEADS = 4
X_HEAD_DIM = 192
X_WIDTH = X_HEADS * X_HEAD_DIM
N_BRANCH = 4
D_FF = 3 * D_MODEL
FFN_CONV = 3
IN_SPLITS = (LRU_WIDTH, LRU_WIDTH, ATT_WIDTH, ATT_WIDTH, ATT_WIDTH, SSM_WIDTH, X_WIDTH, N_BRANCH * D_MODEL)
IN_WIDTH = sum(IN_SPLITS)
SPLIT_IDX = tuple(int(c) for c in np.cumsum(IN_SPLITS)[:-1])

kernel_name = 'hybrid_griffin_dilated_s5_block'


def rms_norm(x, g):
    xf = x.astype(jnp.float32)
    y = xf * lax.rsqrt(jnp.mean(xf * xf, axis=-1, keepdims=True) + EPS)
    return (y * g.astype(jnp.float32)).astype(x.dtype)


def causal_dwconv(x, w, b):
    K = w.shape[0]
    L = x.shape[1]
    xp = jnp.pad(x, ((0, 0), (K - 1, 0), (0, 0)))
    return b + sum(w[j] * xp[:, j:j + L] for j in range(K))


def _lin_combine(e1, e2):
    a1, b1 = e1
    a2, b2 = e2
    return a1 * a2, a2 * b1 + b2


def _cplx_combine(e1, e2):
    a1r, a1i, b1r, b1i = e1
    a2r, a2i, b2r, b2i = e2
    return (a2r * a1r - a2i * a1i, a2r * a1i + a2i * a1r,
            a2r * b1r - a2i * b1i + b2r, a2r * b1i + a2i * b1r + b2i)


def rg_lru_branch(xa, gate, conv_w, conv_b, wa, ba, wx, bx, lam):
    f32 = jnp.float32
    xc = causal_dwconv(xa, conv_w, conv_b)
    B_, L, W = xc.shape
    xb = xc.reshape(B_, L, LRU_BLOCKS, LRU_BLOCK)
    r = jax.nn.sigmoid((jnp.einsum('blni,nij->blnj', xb, wa).reshape(B_, L, W) + ba).astype(f32))
    i = jax.nn.sigmoid((jnp.einsum('blni,nij->blnj', xb, wx).reshape(B_, L, W) + bx).astype(f32))
    log_a = -LRU_C * r * jax.nn.softplus(-lam.astype(f32))
    a = jnp.exp(log_a)
    u = jnp.sqrt(-jnp.expm1(2.0 * log_a)) * (i * xc.astype(f32))
    _, h = lax.associative_scan(_lin_combine, (a, u), axis=1)
    return h.astype(xa.dtype) * jax.nn.gelu(gate)


def _to_strided(t, dil, n_pad):
    B_, L, H, E = t.shape
    n = L // dil
    t = t.reshape(B_, n, dil, H, E).transpose(0, 2, 1, 3, 4).reshape(B_ * dil, n, H, E)
    return jnp.pad(t, ((0, 0), (0, n_pad - n), (0, 0), (0, 0)))


def _band(t):
    prev = jnp.pad(t, ((0, 0), (1, 0), (0, 0), (0, 0), (0, 0)))[:, :-1]
    return jnp.concatenate([prev, t], axis=2)


def dilated_window_group(q, k, v, window, dil, slopes):
    f32 = jnp.float32
    B_, L, H, E = q.shape
    n = L // dil
    span = window // dil
    nb = -(-n // Q_BLOCK)
    n_pad = nb * Q_BLOCK
    qs = _to_strided(q, dil, n_pad).reshape(B_ * dil, nb, Q_BLOCK, H, E)
    kb = _band(_to_strided(k, dil, n_pad).reshape(B_ * dil, nb, Q_BLOCK, H, E))
    vb = _band(_to_strided(v, dil, n_pad).reshape(B_ * dil, nb, Q_BLOCK, H, E))
    s = jnp.einsum('znqhe,znkhe->znhqk', qs, kb).astype(f32) * (E ** -0.5)
    qi = jnp.arange(Q_BLOCK)[:, None]
    ki = jnp.arange(2 * Q_BLOCK)[None, :]
    dist = qi + Q_BLOCK - ki
    blk = jnp.arange(nb)[:, None, None]
    valid = (dist >= 0) & (dist <= span) & (blk * Q_BLOCK + ki - Q_BLOCK >= 0)
    bias = -(slopes * dil)[:, None, None] * dist.astype(f32)
    s = jnp.where(valid[None, :, None], s + bias[None, None], -jnp.inf)
    m = jnp.max(s, axis=-1, keepdims=True)
    p = jnp.exp(s - m)
    l = jnp.sum(p, axis=-1, keepdims=True)
    o = jnp.einsum('znhqk,znkhe->znqhe', p, vb.astype(f32)) / l.transpose(0, 1, 3, 2, 4)
    lse = (m + jnp.log(l))[..., 0].transpose(0, 1, 3, 2)
    o = o.reshape(B_, dil, n_pad, H, E)[:, :, :n].transpose(0, 2, 1, 3, 4).reshape(B_, L, H, E)
    lse = lse.reshape(B_, dil, n_pad, H)[:, :, :n].transpose(0, 2, 1, 3).reshape(B_, L, H)
    return o, lse


def dilated_attention(q, k, v):
    B_, L, _ = q.shape
    qh = q.reshape(B_, L, ATT_HEADS, HEAD_DIM)
    kh = k.reshape(B_, L, ATT_HEADS, HEAD_DIM)
    vh = v.reshape(B_, L, ATT_HEADS, HEAD_DIM)
    outs, lses = [], []
    for g, (window, dil) in enumerate(DIL_GROUPS):
        hs = slice(g * HEADS_PER_GROUP, (g + 1) * HEADS_PER_GROUP)
        slopes = jnp.asarray(ALIBI_SLOPES[hs], jnp.float32)
        o, lse = dilated_window_group(qh[:, :, hs], kh[:, :, hs], vh[:, :, hs], window, dil, slopes)
        outs.append(o)
        lses.append(lse)
    wts = jax.nn.softmax(jnp.stack(lses), axis=0)
    o = jnp.einsum('gblh,gblhe->blhe', wts, jnp.stack(outs))
    return o.reshape(B_, L, ATT_OUT).astype(q.dtype)


def s5_branch(u, a_re, a_im, log_dt, b_re, b_im, c_re, c_im, d_skip, w_glu):
    f32 = jnp.float32
    B_, L, W = u.shape
    dt = jnp.exp(log_dt.astype(f32))[:, None]
    lr, li = a_re.astype(f32), a_im.astype(f32)
    mag = jnp.exp(lr * dt)
    ab_re, ab_im = mag * jnp.cos(li * dt), mag * jnp.sin(li * dt)
    den = lr * lr + li * li
    z_re = ((ab_re - 1.0) * lr + ab_im * li) / den
    z_im = (ab_im * lr - (ab_re - 1.0) * li) / den
    br, bi = b_re.astype(f32), b_im.astype(f32)
    bb_re = z_re[..., None] * br - z_im[..., None] * bi
    bb_im = z_re[..., None] * bi + z_im[..., None] * br
    ut = u.astype(f32).reshape(B_, L, SSM_GROUPS, SSM_GROUP).transpose(1, 0, 2, 3)
    bu_re = jnp.einsum('gph,lbgh->lbgp', bb_re, ut)
    bu_im = jnp.einsum('gph,lbgh->lbgp', bb_im, ut)
    a_r = jnp.broadcast_to(ab_re[None, None], (L, 1, SSM_GROUPS, SSM_STATE))
    a_i = jnp.broadcast_to(ab_im[None, None], (L, 1, SSM_GROUPS, SSM_STATE))
    _, _, xr, xi = lax.associative_scan(_cplx_combine, (a_r, a_i, bu_re, bu_im), axis=0)
    y = (jnp.einsum('ghp,lbgp->blgh', c_re.astype(f32), xr)
         - jnp.einsum('ghp,lbgp->blgh', c_im.astype(f32), xi)).reshape(B_, L, W)
    y = (y + d_skip.astype(f32) * u.astype(f32)).astype(u.dtype)
    zg = jax.nn.gelu(y) @ w_glu
    return zg[..., :W] * jax.nn.sigmoid(zg[..., W:])


def memory_cross_attention(q, mem_n, w_kv):
    B_, L, _ = q.shape
    M = mem_n.shape[1]
    kv = mem_n @ w_kv
    kh = kv[..., :X_WIDTH].reshape(B_, M, X_HEADS, X_HEAD_DIM)
    vh = kv[..., X_WIDTH:].reshape(B_, M, X_HEADS, X_HEAD_DIM)
    qh = q.reshape(B_, L, X_HEADS, X_HEAD_DIM)
    s = jnp.einsum('blhe,bmhe->bhlm', qh, kh).astype(jnp.float32) * (X_HEAD_DIM ** -0.5)
    p = jax.nn.softmax(s, axis=-1)
    o = jnp.einsum('bhlm,bmhe->blhe', p.astype(vh.dtype), vh)
    return o.reshape(B_, L, X_WIDTH)


def mixing_sublayer(h, mem_n, w_in, lru_conv_w, lru_conv_b, lru_wa, lru_ba, lru_wx, lru_bx, lru_lambda,
                    ssm_a_re, ssm_a_im, ssm_log_dt, ssm_b_re, ssm_b_im, ssm_c_re, ssm_c_im, ssm_d, ssm_glu,
                    mem_wkv, proj_a, proj_b, proj_c, proj_x, w_out):
    B_, L, _ = h.shape
    z = h @ w_in
    xa, ga, q, k, v, us, xq, gates = jnp.split(z, SPLIT_IDX, axis=-1)
    ya = rg_lru_branch(xa, ga, lru_conv_w, lru_conv_b, lru_wa, lru_ba, lru_wx, lru_bx, lru_lambda)
    yb = dilated_attention(q, k, v)
    yc = s5_branch(us, ssm_a_re, ssm_a_im, ssm_log_dt, ssm_b_re, ssm_b_im, ssm_c_re, ssm_c_im, ssm_d, ssm_glu)
    yx = memory_cross_attention(xq, mem_n, mem_wkv)
    g = jax.nn.sigmoid(gates.reshape(B_, L, N_BRANCH, D_MODEL))
    m = (g[:, :, 0] * (ya @ proj_a) + g[:, :, 1] * (yb @ proj_b)
         + g[:, :, 2] * (yc @ proj_c) + g[:, :, 3] * (yx @ proj_x))
    return m @ w_out


def conv_gated_mlp(h, w_up, conv_w, conv_b, w_down):
    up = h @ w_up
    val = up[..., :D_FF]
    gate = causal_dwconv(up[..., D_FF:], conv_w, conv_b)
    return (val * jax.nn.gelu(gate)) @ w_down


def setup_inputs(seed: int = 0) -> dict:
    key = jax.random.key(seed)
    ks = iter(jax.random.split(key, 48))
    f32 = jnp.float32

    def nrm(shape, scale):
        return scale * jax.random.normal(next(ks), shape, f32)

    def gain():
        return 1.0 + nrm((DEPTH, D_MODEL), 0.02)

    x = nrm((BATCH, SEQ, D_MODEL), 1.0)
    mem = nrm((BATCH, MEM_LEN, D_MODEL), 1.0)
    g_mix_pre, g_mix_post, g_mem, g_mlp_pre, g_mlp_post = gain(), gain(), gain(), gain(), gain()
    w_in = nrm((DEPTH, D_MODEL, IN_WIDTH), D_MODEL ** -0.5)
    lru_conv_w = nrm((DEPTH, LRU_CONV, LRU_WIDTH), LRU_CONV ** -0.5)
    lru_conv_b = nrm((DEPTH, LRU_WIDTH), 0.02)
    lru_wa = nrm((DEPTH, LRU_BLOCKS, LRU_BLOCK, LRU_BLOCK), LRU_BLOCK ** -0.5)
    lru_ba = nrm((DEPTH, LRU_WIDTH), 0.02)
    lru_wx = nrm((DEPTH, LRU_BLOCKS, LRU_BLOCK, LRU_BLOCK), LRU_BLOCK ** -0.5)
    lru_bx = nrm((DEPTH, LRU_WIDTH), 0.02)
    a_c = jax.random.uniform(next(ks), (DEPTH, LRU_WIDTH), f32, 0.9, 0.999)
    a0 = a_c ** (1.0 / LRU_C)
    lru_lambda = jnp.log(a0) - jnp.log1p(-a0)
    ssm_a_re = -0.5 + nrm((DEPTH, SSM_GROUPS, SSM_STATE), 0.01)
    ssm_a_im = jnp.pi * jnp.arange(SSM_STATE, dtype=f32) + nrm((DEPTH, SSM_GROUPS, SSM_STATE), 0.01)
    ssm_log_dt = jax.random.uniform(next(ks), (DEPTH, SSM_GROUPS), f32, math.log(DT_MIN), math.log(DT_MAX))
    ssm_b_re = nrm((DEPTH, SSM_GROUPS, SSM_STATE, SSM_GROUP), (2.0 * SSM_GROUP) ** -0.5)
    ssm_b_im = nrm((DEPTH, SSM_GROUPS, SSM_STATE, SSM_GROUP), (2.0 * SSM_GROUP) ** -0.5)
    ssm_c_re = nrm((DEPTH, SSM_GROUPS, SSM_GROUP, SSM_STATE), (2.0 * SSM_STATE) ** -0.5)
    ssm_c_im = nrm((DEPTH, SSM_GROUPS, SSM_GROUP, SSM_STATE), (2.0 * SSM_STATE) ** -0.5)
    ssm_d = nrm((DEPTH, SSM_WIDTH), 1.0)
    ssm_glu = nrm((DEPTH, SSM_WIDTH, 2 * SSM_WIDTH), SSM_WIDTH ** -0.5)
    mem_wkv = nrm((DEPTH, D_MODEL, 2 * X_WIDTH), D_MODEL ** -0.5)
    proj_a = nrm((DEPTH, LRU_WIDTH, D_MODEL), LRU_WIDTH ** -0.5)
    proj_b = nrm((DEPTH, ATT_OUT, D_MODEL), ATT_OUT ** -0.5)
    proj_c = nrm((DEPTH, SSM_WIDTH, D_MODEL), SSM_WIDTH ** -0.5)
    proj_x = nrm((DEPTH, X_WIDTH, D_MODEL), X_WIDTH ** -0.5)
    w_out = nrm((DEPTH, D_MODEL, D_MODEL), D_MODEL ** -0.5)
    ffn_w_up = nrm((DEPTH, D_MODEL, 2 * D_FF), D_MODEL ** -0.5)
    ffn_conv_w = nrm((DEPTH, FFN_CONV, D_FF), FFN_CONV ** -0.5)
    ffn_conv_b = nrm((DEPTH, D_FF), 0.02)
    ffn_w_down = nrm((DEPTH, D_FF, D_MODEL), D_FF ** -0.5)
    return {'x': x, 'mem': mem, 'g_mix_pre': g_mix_pre, 'g_mix_post': g_mix_post, 'g_mem': g_mem,
            'g_mlp_pre': g_mlp_pre, 'g_mlp_post': g_mlp_post, 'w_in': w_in,
            'lru_conv_w': lru_conv_w, 'lru_conv_b': lru_conv_b, 'lru_wa': lru_wa, 'lru_ba': lru_ba,
            'lru_wx': lru_wx, 'lru_bx': lru_bx, 'lru_lambda': lru_lambda,
            'ssm_a_re': ssm_a_re, 'ssm_a_im': ssm_a_im, 'ssm_log_dt': ssm_log_dt,
            'ssm_b_re': ssm_b_re, 'ssm_b_im': ssm_b_im, 'ssm_c_re': ssm_c_re, 'ssm_c_im': ssm_c_im,
            'ssm_d': ssm_d, 'ssm_glu': ssm_glu, 'mem_wkv': mem_wkv,
            'proj_a': proj_a, 'proj_b': proj_b, 'proj_c': proj_c, 'proj_x': proj_x, 'w_out': w_out,
            'ffn_w_up': ffn_w_up, 'ffn_conv_w': ffn_conv_w, 'ffn_conv_b': ffn_conv_b, 'ffn_w_down': ffn_w_down}


def reference(x, mem, g_mix_pre, g_mix_post, g_mem, g_mlp_pre, g_mlp_post, w_in,
              lru_conv_w, lru_conv_b, lru_wa, lru_ba, lru_wx, lru_bx, lru_lambda,
              ssm_a_re, ssm_a_im, ssm_log_dt, ssm_b_re, ssm_b_im, ssm_c_re, ssm_c_im,
              ssm_d, ssm_glu, mem_wkv, proj_a, proj_b, proj_c, proj_x, w_out,
              ffn_w_up, ffn_conv_w, ffn_conv_b, ffn_w_down):
    for l in range(DEPTH):
        h = rms_norm(x, g_mix_pre[l])
        mem_n = rms_norm(mem, g_mem[l])
        y = mixing_sublayer(h, mem_n, w_in[l], lru_conv_w[l], lru_conv_b[l], lru_wa[l], lru_ba[l],
                            lru_wx[l], lru_bx[l], lru_lambda[l],
                            ssm_a_re[l], ssm_a_im[l], ssm_log_dt[l], ssm_b_re[l], ssm_b_im[l],
                            ssm_c_re[l], ssm_c_im[l], ssm_d[l], ssm_glu[l],
                            mem_wkv[l], proj_a[l], proj_b[l], proj_c[l], proj_x[l], w_out[l])
        x = x + rms_norm(y, g_mix_post[l])
        h = rms_norm(x, g_mlp_pre[l])
        y = conv_gated_mlp(h, ffn_w_up[l], ffn_conv_w[l], ffn_conv_b[l], ffn_w_down[l])
        x = x + rms_norm(y, g_mlp_post[l])
    return x
```

```python
import math
from contextlib import ExitStack
import numpy as np
import concourse.bass as bass
import concourse.mybir as mybir
from concourse.bass_utils import run_bass_kernel_spmd

AF = mybir.ActivationFunctionType
ALU = mybir.AluOpType
F32 = mybir.dt.float32
BF16 = mybir.dt.bfloat16

D = 1024
NL = 2
SEQ = 16384
MEM = 256
IN_W = 9472
DFF = 3072
T = 512
EPS = 1e-6
ALIBI = [2.0 ** (-8.0 * (h + 1) / 12) for h in range(12)]
DILS = (1, 4, 16)
import os
P3STOP = int(os.environ.get("P3STOP", "0"))


class Sched:
    NSLOT = 8

    def __init__(self, nc, stack):
        self.nc = nc
        self.engs = {'pe': nc.tensor, 'act': nc.scalar, 'dve': nc.vector,
                     'pool': nc.gpsimd, 'sp': nc.sync}
        self.sem = {}
        self.cnt = {}
        for n in ['pe', 'act', 'dve', 'pool']:
            self.sem[n] = stack.enter_context(nc.semaphore("s_" + n))
            self.cnt[n] = 0
        self.dq = {}
        for q in ['sp', 'pool', 'act']:
            for i in range(self.NSLOT):
                self.sem[('dma', q, i)] = stack.enter_context(nc.semaphore("d_%s%d" % (q, i)))
            self.dq[q] = 0
        self.seen = {e: {} for e in self.engs}
        self.lastw = {}
        self.readers = {}

    def _deps(self, reads, writes):
        deps = []
        for r in reads:
            t = self.lastw.get(r)
            if t is not None:
                deps.append(t)
        for w in writes:
            t = self.lastw.get(w)
            if t is not None:
                deps.append(t)
            deps.extend(self.readers.get(w, ()))
        return deps

    def _wait(self, ename, deps):
        best = {}
        for (src, val) in deps:
            if best.get(src, 0) < val:
                best[src] = val
        seen = self.seen[ename]
        eng = self.engs[ename]
        for src, val in best.items():
            if src == 'pe' and ename == 'pe':
                continue
            if seen.get(src, 0) >= val:
                continue
            eng.wait_ge(self.sem[src], val)
            seen[src] = val

    def _record(self, ticket, reads, writes):
        for r in reads:
            self.readers.setdefault(r, []).append(ticket)
        for w in writes:
            self.lastw[w] = ticket
            self.readers[w] = []

    def op(self, ename, fn, reads=(), writes=(), inc=True):
        self._wait(ename, self._deps(reads, writes))
        ins = fn(self.engs[ename])
        if inc:
            self.cnt[ename] += 1
            ins.then_inc(self.sem[ename], 1)
            ticket = (ename, self.cnt[ename])
        else:
            ticket = (ename, self.cnt[ename] + 1)
        self._record(ticket, reads, writes)
        return ticket

    def dma(self, q, out, in_, reads=(), writes=(), **kw):
        i = self.dq[q]
        slot = i % self.NSLOT
        rnd = i // self.NSLOT
        src = ('dma', q, slot)
        deps = self._deps(reads, writes)
        if rnd > 0:
            deps.append((src, 16 * rnd))
        self._wait(q, deps)
        ins = self.engs[q].dma_start(out=out, in_=in_, **kw)
        ins.then_inc(self.sem[src], 16)
        self.dq[q] = i + 1
        ticket = (src, 16 * (rnd + 1))
        self._record(ticket, reads, writes)
        return ticket

    def coll(self, kind, src, dst, groups, reads=(), writes=()):
        q = 'pool'
        i = self.dq[q]
        slot = i % self.NSLOT
        rnd = i // self.NSLOT
        srck = ('dma', q, slot)
        deps = self._deps(reads, writes)
        if rnd > 0:
            deps.append((srck, 16 * rnd))
        self._wait(q, deps)
        ins = self.engs[q].collective_compute(kind, ALU.bypass, replica_groups=groups, ins=[src], outs=[dst])
        ins.then_inc(self.sem[srck], 16)
        self.dq[q] = i + 1
        ticket = (srck, 16 * (rnd + 1))
        self._record(ticket, reads, writes)
        return ticket

    def finish(self, ename='sp'):
        deps = list(self.lastw.values())
        for l in self.readers.values():
            deps.extend(l)
        self._wait(ename, deps)

    def barrier(self):
        for e in self.engs:
            self.finish(e)
        self.lastw = {}
        self.readers = {}


_UID = [0]


class Ring:
    def __init__(self, nc, stack, name, n, shape, dtype, psum=False):
        self.bufs = []
        _UID[0] += 1
        for i in range(n):
            nm = "%s_%d_%d" % (name, _UID[0], i)
            if psum:
                t = stack.enter_context(nc.psum_tensor(nm, shape, dtype))
            else:
                t = stack.enter_context(nc.sbuf_tensor(nm, shape, dtype))
            self.bufs.append((nm, t))
        self.i = 0

    def next(self):
        b = self.bufs[self.i % len(self.bufs)]
        self.i += 1
        return b


class K:
    def __init__(self, L, nl, dbg=False):
        self.L = L
        self.nl = nl
        self.dbg = dbg
        self.nc = bass.Bass("TRN2", target_bir_lowering=False)
        self.ins = {}
        self.scr = {}

    def inp(self, name, shape, dt=F32):
        t = self.nc.dram_tensor(name, list(shape), dt, kind="ExternalInput").ap()
        self.ins[name] = t
        return t

    def scratch(self, name, shape, dt):
        kind = "ExternalOutput" if self.dbg else "Internal"
        t = self.nc.dram_tensor(name, list(shape), dt, kind=kind).ap()
        self.scr[name] = t
        return t

    def sb(self, st, name, shape, dt):
        _UID[0] += 1
        return st.enter_context(self.nc.sbuf_tensor("%s_%d" % (name, _UID[0]), list(shape), dt))

    def build(self, phases):
        nc = self.nc
        L, nl = self.L, self.nl
        x = self.inp("x", [L, D])
        mem = self.inp("mem", [MEM, D])
        ident = self.inp("ident", [128, 128])
        gains = self.inp("gains", [128, NL, 5, 8])
        w_in = self.inp("w_in", [NL, D, IN_W])
        ffn_w_up = self.inp("ffn_w_up", [NL, D, 2 * DFF])
        ffn_w_down = self.inp("ffn_w_down", [NL, DFF, D])
        ffnp = self.inp("ffnp", [128, NL, 4, 24])
        lrup = self.inp("lrup", [128, NL, 8, 6])
        lru_bd = self.inp("lru_bd", [NL, 2, 128, 6, 128])
        ssmd = self.inp("ssmd", [128, NL, 6])
        wsrc = {}
        for nm, shp in (("ssm_glu", [NL, 768, 1536]), ("mem_wkv", [NL, D, 1536]), ("proj_a", [NL, 768, D]),
                        ("proj_b", [NL, 256, D]), ("proj_c", [NL, 768, D]), ("proj_x", [NL, 768, D]), ("w_out", [NL, D, D])):
            wsrc[nm] = (self.inp(nm, shp), self.scratch(nm + "B", shp, BF16), shp[1])
        self.wB = {nm: v[1] for nm, v in wsrc.items()}
        yA = self.scratch("yA", [768, L], BF16)
        yC = self.scratch("yC", [768, L], BF16)
        yX = self.scratch("yX", [768, L], BF16)
        oG = self.scratch("oG", [3, L, 260], F32)
        self.phases = phases
        amask = self.inp("amask", [128, 2, 12, 256])
        s5A = self.inp("s5A", [128, NL, 3, 24])
        s5R = self.inp("s5R", [128, NL, 3, 3072])
        s5B = self.inp("s5B", [NL, 2, 128, 24, 128])
        s5C = self.inp("s5C", [NL, 2, 128, 24, 128])
        out = self.nc.dram_tensor("out", [L, D], F32, kind="ExternalOutput").ap()
        self.out = out

        xT = self.scratch("xT", [D, L], F32)
        xmT = self.scratch("xmT", [D, L], F32)
        zF = self.scratch("zF", [56 * 128, L], BF16)
        zQKV = self.scratch("zQKV", [L, 2304], BF16)
        wInB = self.scratch("wInB", [NL, D, IN_W], BF16)
        wUpB = self.scratch("wUpB", [NL, D, 2 * DFF], BF16)
        wDnB = self.scratch("wDnB", [NL, DFF, D], BF16)

        with ExitStack() as st0:
            S = Sched(nc, st0)
            self.S = S
            ident_f = self.sb(st0, "ident_f", [128, 128], F32)
            ones_b = self.sb(st0, "ones_b", [128, 128], BF16)
            gains_s = self.sb(st0, "gains_s", [128, NL, 5, 8], F32)
            eps_c = self.sb(st0, "eps_c", [128, 1], F32)
            self.ident_f, self.ones_b, self.gains_s, self.eps_c = ident_f, ones_b, gains_s, eps_c
            one_c = self.sb(st0, "one_c", [128, 1], F32)
            self.one_c = one_c
            S.op('dve', lambda e: e.memset(one_c[:], 1.0), writes=['one_c'])
            S.dma('sp', ident_f[:], ident, writes=['ident_f'])
            S.dma('sp', gains_s[:], gains, writes=['gains_s'])
            S.op('dve', lambda e: e.memset(ones_b[:], 1.0), writes=['ones_b'])
            S.op('dve', lambda e: e.memset(eps_c[:], EPS), writes=['eps_c'])
            self.psum = Ring(nc, st0, "ps", 6, [128, 512], F32, psum=True)

            if 'prepass' in phases:
                for l in range(nl):
                    for (src, dst, rows) in [(w_in, wInB, D), (ffn_w_up, wUpB, D), (ffn_w_down, wDnB, DFF)] + list(wsrc.values()):
                        for r in range(0, rows, 128):
                            S.dma('pool', dst[l, r:r + 128, :], src[l, r:r + 128, :],
                                  writes=[(dst.tensor.name, l)])
            if 'prologue' in phases:
                self.transpose_in(x, xT, L)
                S.barrier()
            for l in range(nl):
                if 'p1' in phases:
                    self.phase_inproj(l, xT, wInB, zF, zQKV)
                    S.barrier()
                if 'p2' in phases:
                    self.phase_lru(l, zF, yA, lrup, lru_bd)
                    S.barrier()
                if 'p3' in phases:
                    self.phase_attn(l, zQKV, oG, amask)
                    S.barrier()
                if 'p4' in phases:
                    self.phase_s5(l, zF, yC, s5A, s5R, s5B, s5C, ssmd)
                    S.barrier()
                if 'p5' in phases:
                    self.phase_xattn(l, mem, zF, yX)
                    S.barrier()
                if 'p6' in phases:
                    self.phase_merge(l, xT, xmT, zF, yA, yC, yX, oG)
                    S.barrier()
                if 'p7' in phases:
                    self.phase_ffn(l, xmT if 'p6' in phases else xT, xT, wUpB, wDnB, ffnp)
                    S.barrier()
            if 'epilogue' in phases:
                self.transpose_out(xT, out, L)
            S.barrier()
        return nc

    def transpose_in(self, x, xT, L):
        nc, S = self.nc, self.S
        with ExitStack() as st:
            xin = Ring(nc, st, "ti_x", 2, [128, D], F32)
            xo = Ring(nc, st, "ti_o", 2, [128, 8, 128], F32)
            for b in range(L // 128):
                nm, xt = xin.next()
                S.dma('sp', xt[:], x[b * 128:(b + 1) * 128, :], writes=[nm])
                no, ot = xo.next()
                for half in range(2):
                    pn, ps = self.psum.next()
                    for j in range(4):
                        c = half * 4 + j
                        S.op('pe', lambda e: e.transpose(ps[:, j * 128:(j + 1) * 128], xt[:, c * 128:(c + 1) * 128], self.ident_f[:]),
                             reads=[nm, 'ident_f'], writes=[pn], inc=(j == 3))
                    eng = 'act' if half == 0 else 'dve'
                    if eng == 'act':
                        S.op('act', lambda e: e.copy(ot[:, half * 4:(half + 1) * 4, :], ps[:, :].rearrange("p (a b) -> p a b", a=4)),
                             reads=[pn], writes=[no])
                    else:
                        S.op('dve', lambda e: e.tensor_copy(ot[:, half * 4:(half + 1) * 4, :], ps[:, :].rearrange("p (a b) -> p a b", a=4)),
                             reads=[pn], writes=[no])
                S.dma('sp', xT.rearrange("(c p) l -> p c l", p=128)[:, :, b * 128:(b + 1) * 128], ot[:],
                      reads=[no], writes=[('xT', b // 4)])

    def transpose_out(self, xT, out, L):
        nc, S = self.nc, self.S
        with ExitStack() as st:
            xin = Ring(nc, st, "to_x", 2, [128, 8, 128], F32)
            xo = Ring(nc, st, "to_o", 2, [128, D], F32)
            for b in range(L // 128):
                nm, xt = xin.next()
                S.dma('sp', xt[:], xT.rearrange("(c p) l -> p c l", p=128)[:, :, b * 128:(b + 1) * 128],
                      reads=[('xT', b // 4)], writes=[nm])
                no, ot = xo.next()
                for half in range(2):
                    pn, ps = self.psum.next()
                    for j in range(4):
                        c = half * 4 + j
                        S.op('pe', lambda e: e.transpose(ps[:, j * 128:(j + 1) * 128], xt[:, c, :], self.ident_f[:]),
                             reads=[nm, 'ident_f'], writes=[pn], inc=(j == 3))
                    if half == 0:
                        S.op('act', lambda e: e.copy(ot[:, 0:512], ps[:, :]), reads=[pn], writes=[no])
                    else:
                        S.op('dve', lambda e: e.tensor_copy(ot[:, 512:1024], ps[:, :]), reads=[pn], writes=[no])
                S.dma('sp', out[b * 128:(b + 1) * 128, :], ot[:], reads=[no], writes=['out'])

    def rms_stats(self, sqname, sq, rstd_name, rstd, Tn):
        S = self.S
        pn, ps = self.psum.next()
        for c in range(8):
            S.op('pe', lambda e: e.matmul(ps[:, :Tn], lhsT=self.ones_b[:], rhs=sq[:, c, :], start=(c == 0), stop=(c == 7)),
                 reads=[sqname, 'ones_b'], writes=[pn], inc=(c == 7))
        S.op('act', lambda e: e.activation(rstd[:, :Tn], ps[:, :Tn], AF.Sqrt, bias=self.eps_c[:], scale=1.0 / D),
             reads=[pn, 'eps_c'], writes=[rstd_name])
        S.op('dve', lambda e: e.reciprocal(rstd[:, :Tn], rstd[:, :Tn]), reads=[rstd_name], writes=[rstd_name])

    def prenorm(self, l, which, xt_name, xt, h_name, h, sq_name, sq, rstd_name, rstd):
        S = self.S
        for c in range(8):
            S.op('act', lambda e: e.activation(sq[:, c, :], xt[:, c, :], AF.Square), reads=[xt_name], writes=[sq_name])
        self.rms_stats(sq_name, sq, rstd_name, rstd, T)
        for c in range(8):
            S.op('dve', lambda e: e.scalar_tensor_tensor(h[:, c, :], xt[:, c, :], self.gains_s[:, l, which, c:c + 1], rstd[:, :],
                                                         ALU.mult, ALU.mult),
                 reads=[xt_name, rstd_name, 'gains_s'], writes=[h_name])

    def phase_inproj(self, l, xT, wInB, zF, zQKV):
        nc, S, L = self.nc, self.S, self.L
        FM_COLS = list(range(0, 1536, 128)) + list(range(3840, IN_W, 128))
        NT = 2
        with ExitStack() as st:
            xr = Ring(nc, st, "p1_x", 2, [128, 8, T], F32)
            sq = self.sb(st, "p1_sq", [128, 8, T], BF16)
            rstd = self.sb(st, "p1_rstd", [128, T], F32)
            hr = Ring(nc, st, "p1_h", 4, [128, 8, T], BF16)
            wr = Ring(nc, st, "p1_w", 2, [128, 8, 1024], BF16)
            zo = Ring(nc, st, "p1_zo", 2, [128, 8, T], BF16)
            qo = Ring(nc, st, "p1_qo", 2, [128, 4, 2304], BF16)
            wv = wInB[l].rearrange("(k p) n -> p k n", p=128)
            zFv = zF.rearrange("(c p) l -> p c l", p=128)
            ev = 0
            wq = 0
            for tp in range(L // (NT * T)):
                hs = []
                for ti in range(tp * NT, (tp + 1) * NT):
                    t0 = ti * T
                    xn, xt = xr.next()
                    S.dma('sp', xt[:], xT.rearrange("(c p) l -> p c l", p=128)[:, :, t0:t0 + T], reads=[('xT', ti)], writes=[xn])
                    hn, h = hr.next()
                    self.prenorm(l, 0, xn, xt, hn, h, "p1_sq", sq, "p1_rstd", rstd)
                    hs.append((t0, hn, h))
                for blk in range(7):
                    wn, w = wr.next()
                    wq += 1
                    for j in range(8):
                        c0 = FM_COLS[blk * 8 + j]
                        if j == 0 or FM_COLS[blk * 8 + j - 1] + 128 != c0:
                            j2 = j
                            while j2 + 1 < 8 and FM_COLS[blk * 8 + j2 + 1] == FM_COLS[blk * 8 + j2] + 128:
                                j2 += 1
                            S.dma('sp' if wq % 2 == 0 else 'pool', w[:, :, j * 128:(j2 + 1) * 128], wv[:, :, c0:c0 + (j2 - j + 1) * 128],
                                  reads=[('wInB', l)], writes=[wn])
                    for (t0, hn, h) in hs:
                        zn, z = zo.next()
                        for j in range(8):
                            ch = blk * 8 + j
                            pn, ps = self.psum.next()
                            for k in range(8):
                                S.op('pe', lambda e: e.matmul(ps[:, :], lhsT=w[:, k, j * 128:(j + 1) * 128], rhs=h[:, k, :], start=(k == 0), stop=(k == 7)),
                                     reads=[wn, hn], writes=[pn], inc=(k == 7))
                            if 6 <= ch < 12:
                                S.op('act', lambda e: e.activation(z[:, j, :], ps[:, :], AF.Gelu_apprx_tanh), reads=[pn], writes=[zn])
                            elif ch >= 24:
                                S.op('act', lambda e: e.activation(z[:, j, :], ps[:, :], AF.Sigmoid), reads=[pn], writes=[zn])
                            else:
                                S.op('dve', lambda e: e.tensor_copy(z[:, j, :], ps[:, :]), reads=[pn], writes=[zn])
                        S.dma('sp', zFv[:, blk * 8:(blk + 1) * 8, t0:t0 + T], z[:], reads=[zn], writes=['zF'])
                qs = [qo.next() for _ in hs]
                for blk in range(3):
                    c0 = 1536 + blk * 1024
                    ncol = min(1024, 3840 - c0)
                    wn, w = wr.next()
                    wq += 1
                    S.dma('sp' if wq % 2 == 0 else 'pool', w[:, :, :ncol], wv[:, :, c0:c0 + ncol], reads=[('wInB', l)], writes=[wn])
                    for (t0, hn, h), (qn, q) in zip(hs, qs):
                        for tb in range(4):
                            for n0 in range(0, ncol, 512):
                                nn = min(512, ncol - n0)
                                pn, ps = self.psum.next()
                                for k in range(8):
                                    S.op('pe', lambda e: e.matmul(ps[:, :nn], lhsT=h[:, k, tb * 128:(tb + 1) * 128], rhs=w[:, k, n0:n0 + nn], start=(k == 0), stop=(k == 7)),
                                         reads=[wn, hn], writes=[pn], inc=(k == 7))
                                dst = q[:, tb, blk * 1024 + n0: blk * 1024 + n0 + nn]
                                ev += 1
                                if ev % 2:
                                    S.op('dve', lambda e: e.tensor_copy(dst, ps[:, :nn]), reads=[pn], writes=[qn])
                                else:
                                    S.op('act', lambda e: e.copy(dst, ps[:, :nn]), reads=[pn], writes=[qn])
                for (t0, hn, h), (qn, q) in zip(hs, qs):
                    S.dma('sp', zQKV[t0:t0 + T, :].rearrange("(tb p) n -> p tb n", p=128), q[:], reads=[qn], writes=['zQKV'])

    def phase_lru(self, l, zF, yA, lrup, lru_bd):
        nc, S, L = self.nc, self.S, self.L
        with ExitStack() as st:
            lp = self.sb(st, "p2_lp", [128, 8, 6], F32)
            kap = self.sb(st, "p2_kap", [128, 2, 6], F32)
            bdA = self.sb(st, "p2_bdA", [128, 6, 128], BF16)
            bdX = self.sb(st, "p2_bdX", [128, 6, 128], BF16)
            state = self.sb(st, "p2_state", [128, 6], F32)
            xar = Ring(nc, st, "p2_xa", 3, [128, T + 3], BF16)
            ggr = Ring(nc, st, "p2_gg", 3, [128, T], BF16)
            xcr = Ring(nc, st, "p2_xc", 2, [128, T], F32)
            xcbr = Ring(nc, st, "p2_xcb", 2, [128, T], BF16)
            rr = Ring(nc, st, "p2_r", 2, [128, T], F32)
            ir = Ring(nc, st, "p2_i", 2, [128, T], F32)
            ar = Ring(nc, st, "p2_a", 2, [128, T], F32)
            a2r = Ring(nc, st, "p2_a2", 2, [128, T], F32)
            hr = Ring(nc, st, "p2_h", 2, [128, T], F32)
            yr = Ring(nc, st, "p2_y", 3, [128, T], BF16)
            S.dma('sp', lp[:], lrup[:, l, :, :], writes=['p2_lp'])
            S.dma('pool', bdA[:], lru_bd[l, 0], writes=['p2_bdA'])
            S.dma('pool', bdX[:], lru_bd[l, 1], writes=['p2_bdX'])
            S.op('dve', lambda e: e.memset(state[:], 0.0), writes=[('p2_state', c) for c in range(6)])
            S.op('act', lambda e: e.activation(kap[:, 0, :], lp[:, 7, :], AF.Exp, scale=-1.0), reads=['p2_lp'], writes=['p2_kap'])
            S.op('act', lambda e: e.activation(kap[:, 0, :], kap[:, 0, :], AF.Ln, bias=self.one_c[:]), reads=['p2_kap', 'one_c'], writes=['p2_kap'])
            S.op('dve', lambda e: e.tensor_scalar(kap[:, 1, :], kap[:, 0, :], -16.0, None, ALU.mult), reads=['p2_kap'], writes=['p2_kap'])
            S.op('dve', lambda e: e.tensor_scalar(kap[:, 0, :], kap[:, 0, :], -8.0, None, ALU.mult), reads=['p2_kap'], writes=['p2_kap'])
            zFv = zF.rearrange("(c p) l -> p c l", p=128)
            yAv = yA.rearrange("(c p) l -> p c l", p=128)
            for ti in range(L // T):
                t0 = ti * T
                for c in range(6):
                    xn, xa = xar.next()
                    if t0 == 0:
                        S.op('pool', lambda e: e.memset(xa[:, 0:3], 0.0), writes=[xn])
                        S.dma('sp', xa[:, 3:T + 3], zFv[:, c, 0:T], reads=['zF'], writes=[xn])
                    else:
                        S.dma('sp', xa[:, :], zFv[:, c, t0 - 3:t0 + T], reads=['zF'], writes=[xn])
                    gn, gg = ggr.next()
                    S.dma('sp', gg[:, :], zFv[:, 6 + c, t0:t0 + T], reads=['zF'], writes=[gn])
                    xcn, xc = xcr.next()
                    S.op('dve', lambda e: e.tensor_scalar(xc[:, :], xa[:, 0:T], lp[:, 0, c:c + 1], lp[:, 4, c:c + 1], ALU.mult, ALU.add),
                         reads=[xn, 'p2_lp'], writes=[xcn])
                    for j in range(1, 4):
                        S.op('dve', lambda e: e.scalar_tensor_tensor(xc[:, :], xa[:, j:j + T], lp[:, j, c:c + 1], xc[:, :], ALU.mult, ALU.add),
                             reads=[xn, xcn, 'p2_lp'], writes=[xcn])
                    xbn, xcb = xcbr.next()
                    S.op('pool', lambda e: e.tensor_copy(xcb[:, :], xc[:, :]), reads=[xcn], writes=[xbn])
                    prn, pr = self.psum.next()
                    S.op('pe', lambda e: e.matmul(pr[:, :], lhsT=bdA[:, c, :], rhs=xcb[:, :], start=True, stop=True), reads=['p2_bdA', xbn], writes=[prn])
                    pin, pi = self.psum.next()
                    S.op('pe', lambda e: e.matmul(pi[:, :], lhsT=bdX[:, c, :], rhs=xcb[:, :], start=True, stop=True), reads=['p2_bdX', xbn], writes=[pin])
                    rn, r = rr.next()
                    S.op('act', lambda e: e.activation(r[:, :], pr[:, :], AF.Sigmoid, bias=lp[:, 5, c:c + 1]), reads=[prn, 'p2_lp'], writes=[rn])
                    inn, iv = ir.next()
                    S.op('act', lambda e: e.activation(iv[:, :], pi[:, :], AF.Sigmoid, bias=lp[:, 6, c:c + 1]), reads=[pin, 'p2_lp'], writes=[inn])
                    an, a = ar.next()
                    S.op('act', lambda e: e.activation(a[:, :], r[:, :], AF.Exp, scale=kap[:, 0, c:c + 1]), reads=[rn, 'p2_kap'], writes=[an])
                    a2n, a2 = a2r.next()
                    S.op('act', lambda e: e.activation(a2[:, :], r[:, :], AF.Exp, scale=kap[:, 1, c:c + 1]), reads=[rn, 'p2_kap'], writes=[a2n])
                    S.op('dve', lambda e: e.tensor_scalar(a2[:, :], a2[:, :], -1.0, 1.0, ALU.mult, ALU.add), reads=[a2n], writes=[a2n])
                    S.op('act', lambda e: e.activation(a2[:, :], a2[:, :], AF.Sqrt), reads=[a2n], writes=[a2n])
                    S.op('dve', lambda e: e.tensor_tensor(iv[:, :], iv[:, :], a2[:, :], ALU.mult), reads=[inn, a2n], writes=[inn])
                    S.op('dve', lambda e: e.tensor_tensor(iv[:, :], iv[:, :], xc[:, :], ALU.mult), reads=[inn, xcn], writes=[inn])
                    hn, h = hr.next()
                    S.op('dve', lambda e: e.tensor_tensor_scan(h[:, :], a[:, :], iv[:, :], state[:, c:c + 1], ALU.mult, ALU.add),
                         reads=[an, inn, ('p2_state', c)], writes=[hn])
                    S.op('act', lambda e: e.copy(state[:, c:c + 1], h[:, T - 1:T]), reads=[hn], writes=[('p2_state', c)])
                    yn, y = yr.next()
                    S.op('dve', lambda e: e.tensor_tensor(y[:, :], h[:, :], gg[:, :], ALU.mult), reads=[hn, gn], writes=[yn])
                    S.dma('sp', yAv[:, c, t0:t0 + T], y[:, :], reads=[yn], writes=['yA'])


    def phase_attn(self, l, zQKV, oG, amask):
        nc, S, L = self.nc, self.S, self.L
        with ExitStack() as st:
            mk = self.sb(st, "p3_mk", [128, 2, 12, 256], F32)
            idb = self.sb(st, "p3_idb", [128, 128], BF16)
            S.dma('sp', mk[:], amask, writes=['p3_mk'])
            S.op('dve', lambda e: e.tensor_copy(idb[:], self.ident_f[:]), reads=['ident_f'], writes=['p3_idb'])
            qr = Ring(nc, st, "p3_q", 3, [128, 256], BF16)
            kr = Ring(nc, st, "p3_k", 3, [128, 256], BF16)
            vr = Ring(nc, st, "p3_v", 4, [128, 4, 128], BF16)
            qkr = Ring(nc, st, "p3_qk", 3, [128, 8, 128], BF16)
            scr = Ring(nc, st, "p3_sc", 3, [128, 512], F32)
            ptr = Ring(nc, st, "p3_pt", 4, [128, 2, 2, 128], BF16)
            osr = Ring(nc, st, "p3_os", 3, [128, 260], F32)
            for (vn, v) in vr.bufs:
                S.op('pool', lambda e: e.memset(v[:], 1.0), writes=[vn])
            for g, d in enumerate(DILS):
                if P3STOP == -1:
                    break
                nb = L // (128 * d)
                zv = zQKV.rearrange("(n d) c -> d n c", d=d)
                ov = oG[g].rearrange("(n d) c -> d n c", d=d)
                for r in range(d):
                    prev = None
                    for blk in range(nb):
                        rows = slice(blk * 128, (blk + 1) * 128)
                        qn, q = qr.next()
                        kn, k_ = kr.next()
                        vn, v = vr.next()
                        S.dma('sp', q[:, :], zv[r, rows, 256 * g:256 * g + 256], reads=['zQKV'], writes=[qn])
                        S.dma('sp', k_[:, :], zv[r, rows, 768 + 256 * g:768 + 256 * g + 256], reads=['zQKV'], writes=[kn])
                        S.dma('sp', v[:, :, 0:64], zv[r, rows, 1536 + 256 * g:1536 + 256 * g + 256].rearrange("p (h e) -> p h e", e=64),
                              reads=['zQKV'], writes=[vn])
                        if P3STOP == -2:
                            continue
                        ptn, pT = self.psum.next()
                        for j in range(4):
                            S.op('pe', lambda e: e.matmul(pT[0:64, j * 128:(j + 1) * 128], lhsT=q[:, j * 64:(j + 1) * 64], rhs=idb[:], start=True, stop=True),
                                 reads=[qn, 'p3_idb'], writes=[ptn], inc=(j == 3))
                        ptn2, pT2 = self.psum.next()
                        for j in range(4):
                            S.op('pe', lambda e: e.matmul(pT2[0:64, j * 128:(j + 1) * 128], lhsT=k_[:, j * 64:(j + 1) * 64], rhs=idb[:], start=True, stop=True),
                                 reads=[kn, 'p3_idb'], writes=[ptn2], inc=(j == 3))
                        qkn, qk = qkr.next()
                        S.op('dve', lambda e: e.tensor_copy(qk[0:64, 0:4, :], pT[0:64, 0:512].rearrange("p (a b) -> p a b", a=4)), reads=[ptn], writes=[qkn])
                        S.op('dve', lambda e: e.tensor_copy(qk[0:64, 4:8, :], pT2[0:64, 0:512].rearrange("p (a b) -> p a b", a=4)), reads=[ptn2], writes=[qkn])
                        qTn, qT = qkn, qk[:, 0:4, :]
                        kTn, kT = qkn, qk[:, 4:8, :]
                        first = prev is None
                        if P3STOP == 1:
                            continue
                        if first:
                            kTp_n, kTp, vp_n, vp = kTn, kT, vn, v
                        else:
                            kTp_n, kTp, vp_n, vp = prev
                        pts = []
                        for pair in range(2):
                            pn, ps = self.psum.next()
                            for h2 in range(2):
                                hx = 2 * pair + h2
                                col = h2 * 256
                                if not first:
                                    S.op('pe', lambda e: e.matmul(ps[:, col:col + 128], lhsT=kTp[0:64, hx, :], rhs=qT[0:64, hx, :], start=True, stop=True),
                                         reads=[kTp_n, qTn], writes=[pn], inc=False)
                                S.op('pe', lambda e: e.matmul(ps[:, col + 128:col + 256], lhsT=kT[0:64, hx, :], rhs=qT[0:64, hx, :], start=True, stop=True),
                                     reads=[kTn, qTn], writes=[pn], inc=(h2 == 1))
                            scn, sc = scr.next()
                            hh0 = 4 * g + 2 * pair
                            mview = mk[:, 1 if first else 0, hh0:hh0 + 2, :].rearrange("p a b -> p (a b)")
                            if first:
                                S.op('pool', lambda e: e.memset(sc[:, :], 0.0), writes=[scn])
                                for h2 in range(2):
                                    col = h2 * 256
                                    S.op('dve', lambda e: e.scalar_tensor_tensor(sc[:, col + 128:col + 256], ps[:, col + 128:col + 256], 0.125, mview[:, col + 128:col + 256], ALU.mult, ALU.add),
                                         reads=[pn, 'p3_mk'], writes=[scn])
                                    S.op('dve', lambda e: e.tensor_copy(sc[:, col:col + 128], mview[:, col:col + 128]), reads=['p3_mk'], writes=[scn])
                            else:
                                S.op('dve', lambda e: e.scalar_tensor_tensor(sc[:, :], ps[:, :], 0.125, mview, ALU.mult, ALU.add),
                                     reads=[pn, 'p3_mk'], writes=[scn])
                            pn2, pt = ptr.next()
                            S.op('act', lambda e: e.activation(pt[:, :, :, :].rearrange("p a b c -> p (a b c)"), sc[:, :], AF.Exp), reads=[scn], writes=[pn2])
                            pts.append((pn2, pt))
                        if P3STOP == 2:
                            prev = (kTn, kT, vn, v)
                            continue
                        pn, ps = self.psum.next()
                        for hh in range(4):
                            pn2, pt = pts[hh // 2]
                            S.op('pe', lambda e: e.matmul(ps[:, hh * 128:hh * 128 + 65], lhsT=pt[:, hh % 2, 0, :], rhs=vp[:, hh, 0:65], start=True, stop=False),
                                 reads=[pn2, vp_n], writes=[pn], inc=False)
                            S.op('pe', lambda e: e.matmul(ps[:, hh * 128:hh * 128 + 65], lhsT=pt[:, hh % 2, 1, :], rhs=v[:, hh, 0:65], start=False, stop=True),
                                 reads=[pn2, vn], writes=[pn], inc=(hh == 3))
                        on, o = osr.next()
                        S.op('act', lambda e: e.copy(o[:, :].rearrange("p (h e) -> p h e", e=65), ps[:, :].rearrange("p (h e) -> p h e", e=128)[:, :, 0:65]), reads=[pn], writes=[on])
                        S.dma('sp', ov[r, rows, :], o[:, :], reads=[on], writes=['oG'])
                        prev = (kTn, kT, vn, v)

    def sincos(self, st, th_name, th, n, out_s, out_c, key):
        S = self.S
        TWO_PI = 2.0 * math.pi
        ti = self.sb(st, "sc_i", [128, n], mybir.dt.int32)
        tf = self.sb(st, "sc_f", [128, n], F32)
        ph = self.sb(st, "sc_p", [128, n], F32)
        mm = self.sb(st, "sc_m", [128, n], F32)
        for (shift, outt) in ((0.0, out_s), (math.pi / 2, out_c)):
            S.op('dve', lambda e: e.tensor_scalar(ph[:, :], th, 1.0 / TWO_PI, shift / TWO_PI, ALU.mult, ALU.add), reads=[th_name], writes=[key + 'ph'])
            S.op('dve', lambda e: e.tensor_copy(ti[:, :], ph[:, :]), reads=[key + 'ph'], writes=[key + 'ti'])
            S.op('dve', lambda e: e.tensor_copy(tf[:, :], ti[:, :]), reads=[key + 'ti'], writes=[key + 'tf'])
            S.op('dve', lambda e: e.tensor_tensor(ph[:, :], ph[:, :], tf[:, :], ALU.subtract), reads=[key + 'ph', key + 'tf'], writes=[key + 'ph'])
            S.op('dve', lambda e: e.tensor_scalar(mm[:, :], ph[:, :], 0.5, None, ALU.is_gt), reads=[key + 'ph'], writes=[key + 'mm'])
            S.op('dve', lambda e: e.tensor_tensor(ph[:, :], ph[:, :], mm[:, :], ALU.subtract), reads=[key + 'ph', key + 'mm'], writes=[key + 'ph'])
            S.op('dve', lambda e: e.tensor_scalar(mm[:, :], ph[:, :], -0.5, None, ALU.is_lt), reads=[key + 'ph'], writes=[key + 'mm'])
            S.op('dve', lambda e: e.tensor_tensor(ph[:, :], ph[:, :], mm[:, :], ALU.add), reads=[key + 'ph', key + 'mm'], writes=[key + 'ph'])
            S.op('dve', lambda e: e.tensor_scalar(ph[:, :], ph[:, :], -0.4999, 0.4999, ALU.max, ALU.min), reads=[key + 'ph'], writes=[key + 'ph'])
            S.op('act', lambda e: e.activation(outt, ph[:, :], AF.Sin, scale=TWO_PI), reads=[key + 'ph'], writes=[key + 'out'])

    def phase_s5(self, l, zF, yC, s5A, s5R, s5B, s5C, ssmd):
        nc, S, L = self.nc, self.S, self.L
        with ExitStack() as st:
            rho = self.sb(st, "p4_rho", [128, 24], F32)
            BbR = self.sb(st, "p4_BbR", [128, 24, 128], BF16)
            BbI = self.sb(st, "p4_BbI", [128, 24, 128], BF16)
            CtR = self.sb(st, "p4_CtR", [128, 24, 128], BF16)
            CtI = self.sb(st, "p4_CtI", [128, 24, 128], BF16)
            dsk = self.sb(st, "p4_d", [128, 6], F32)
            xst = self.sb(st, "p4_xst", [128, 2, 24], F32)
            S.dma('sp', dsk[:], ssmd[:, l, :], writes=['p4_d'])
            S.op('dve', lambda e: e.memset(xst[:], 0.0), writes=[('p4_xst', p) for p in range(24)])
            S.dma('pool', CtR[:], s5C[l, 0], writes=['p4_CtR'])
            with ExitStack() as s2:
                N = 3072
                par = self.sb(s2, "p4s_par", [128, 3, N], F32)
                S.dma('sp', par[:], s5R[:, l, :, :], writes=['p4s_par'])
                ar, ai, ld = par[:, 0, :], par[:, 1, :], par[:, 2, :]
                dt = self.sb(s2, "p4s_dt", [128, N], F32)
                mag = self.sb(st if False else s2, "p4s_mag", [128, N], F32)
                th = self.sb(s2, "p4s_th", [128, N], F32)
                sn = self.sb(s2, "p4s_sn", [128, N], F32)
                cs = self.sb(s2, "p4s_cs", [128, N], F32)
                S.op('act', lambda e: e.activation(dt[:, :], ld, AF.Exp), reads=['p4s_par'], writes=['p4s_dt'])
                S.op('dve', lambda e: e.tensor_tensor(mag[:, :], ar, dt[:, :], ALU.mult), reads=['p4s_par', 'p4s_dt'], writes=['p4s_mag'])
                S.op('act', lambda e: e.activation(mag[:, :], mag[:, :], AF.Exp), reads=['p4s_mag'], writes=['p4s_mag'])
                S.op('dve', lambda e: e.tensor_tensor(th[:, :], ai, dt[:, :], ALU.mult), reads=['p4s_par', 'p4s_dt'], writes=['p4s_th'])
                for hf in range(2):
                    with ExitStack() as s3:
                        sl = slice(hf * 1536, (hf + 1) * 1536)
                        self.sincos(s3, 'p4s_th', th[:, sl], 1536, sn[:, sl], cs[:, sl], 'scB')
                        S.barrier()
                S.op('dve', lambda e: e.tensor_tensor(cs[:, :], cs[:, :], mag[:, :], ALU.mult), reads=['scBout', 'p4s_mag'], writes=['p4s_cs'])
                S.op('dve', lambda e: e.tensor_scalar(cs[:, :], cs[:, :], -1.0, None, ALU.add), reads=['p4s_cs'], writes=['p4s_cs'])
                S.op('dve', lambda e: e.tensor_tensor(sn[:, :], sn[:, :], mag[:, :], ALU.mult), reads=['scBout', 'p4s_mag'], writes=['p4s_sn'])
                S.op('dve', lambda e: e.tensor_tensor(dt[:, :], ar, ar, ALU.mult), reads=['p4s_par'], writes=['p4s_dt'])
                S.op('dve', lambda e: e.tensor_tensor(th[:, :], ai, ai, ALU.mult), reads=['p4s_par'], writes=['p4s_th'])
                S.op('dve', lambda e: e.tensor_tensor(dt[:, :], dt[:, :], th[:, :], ALU.add), reads=['p4s_dt', 'p4s_th'], writes=['p4s_dt'])
                S.op('dve', lambda e: e.reciprocal(dt[:, :], dt[:, :]), reads=['p4s_dt'], writes=['p4s_dt'])
                t1 = self.sb(s2, "p4s_t1", [128, N], F32)
                S.op('dve', lambda e: e.tensor_tensor(mag[:, :], cs[:, :], ar, ALU.mult), reads=['p4s_cs', 'p4s_par'], writes=['p4s_mag'])
                S.op('dve', lambda e: e.tensor_tensor(t1[:, :], sn[:, :], ai, ALU.mult), reads=['p4s_sn', 'p4s_par'], writes=['p4s_t1'])
                S.op('dve', lambda e: e.tensor_tensor(mag[:, :], mag[:, :], t1[:, :], ALU.add), reads=['p4s_mag', 'p4s_t1'], writes=['p4s_mag'])
                S.op('dve', lambda e: e.tensor_tensor(mag[:, :], mag[:, :], dt[:, :], ALU.mult), reads=['p4s_mag', 'p4s_dt'], writes=['p4s_mag'])
                S.op('dve', lambda e: e.tensor_tensor(th[:, :], sn[:, :], ar, ALU.mult), reads=['p4s_sn', 'p4s_par'], writes=['p4s_th'])
                S.op('dve', lambda e: e.tensor_tensor(t1[:, :], cs[:, :], ai, ALU.mult), reads=['p4s_cs', 'p4s_par'], writes=['p4s_t1'])
                S.op('dve', lambda e: e.tensor_tensor(th[:, :], th[:, :], t1[:, :], ALU.subtract), reads=['p4s_th', 'p4s_t1'], writes=['p4s_th'])
                S.op('dve', lambda e: e.tensor_tensor(th[:, :], th[:, :], dt[:, :], ALU.mult), reads=['p4s_th', 'p4s_dt'], writes=['p4s_th'])
                zr, zi = mag, th
                bre = self.sb(s2, "p4s_bre", [128, N], F32)
                bim = self.sb(s2, "p4s_bim", [128, N], F32)
                S.dma('sp', bre[:, :], s5B[l, 0].rearrange("p a b -> p (a b)"), writes=['p4s_bre'])
                S.dma('sp', bim[:, :], s5B[l, 1].rearrange("p a b -> p (a b)"), writes=['p4s_bim'])
                S.op('dve', lambda e: e.tensor_tensor(t1[:, :], zr[:, :], bre[:, :], ALU.mult), reads=['p4s_mag', 'p4s_bre'], writes=['p4s_t1'])
                S.op('dve', lambda e: e.tensor_tensor(cs[:, :], zi[:, :], bim[:, :], ALU.mult), reads=['p4s_th', 'p4s_bim'], writes=['p4s_cs'])
                S.op('dve', lambda e: e.tensor_tensor(BbR[:, :, :].rearrange("p a b -> p (a b)"), t1[:, :], cs[:, :], ALU.subtract), reads=['p4s_t1', 'p4s_cs'], writes=['p4_BbR'])
                S.op('dve', lambda e: e.tensor_tensor(t1[:, :], zr[:, :], bim[:, :], ALU.mult), reads=['p4s_mag', 'p4s_bim'], writes=['p4s_t1'])
                S.op('dve', lambda e: e.tensor_tensor(cs[:, :], zi[:, :], bre[:, :], ALU.mult), reads=['p4s_th', 'p4s_bre'], writes=['p4s_cs'])
                S.op('dve', lambda e: e.tensor_tensor(BbI[:, :, :].rearrange("p a b -> p (a b)"), t1[:, :], cs[:, :], ALU.add), reads=['p4s_t1', 'p4s_cs'], writes=['p4_BbI'])
                S.dma('sp', bre[:, :], s5C[l, 1].rearrange("p a b -> p (a b)"), reads=['p4s_bre'], writes=['p4s_bre'])
                S.op('act', lambda e: e.mul(CtI[:, :, :].rearrange("p a b -> p (a b)"), bre[:, :], -1.0), reads=['p4s_bre'], writes=['p4_CtI'])
                S.barrier()
            cosT = self.sb(st, "p4_cos", [128, 24, T], F32)
            sinT = self.sb(st, "p4_sin", [128, 24, T], F32)
            with ExitStack() as s2:
                pa = self.sb(s2, "p4a_par", [128, 3, 24], F32)
                S.dma('sp', pa[:], s5A[:, l, :, :], writes=['p4a_par'])
                dt = self.sb(s2, "p4a_dt", [128, 24], F32)
                th = self.sb(s2, "p4a_th", [128, 24], F32)
                sn = self.sb(s2, "p4a_sn", [128, 24], F32)
                cs = self.sb(s2, "p4a_cs", [128, 24], F32)
                S.op('act', lambda e: e.activation(dt[:, :], pa[:, 2, :], AF.Exp), reads=['p4a_par'], writes=['p4a_dt'])
                S.op('dve', lambda e: e.tensor_tensor(rho[:, :], pa[:, 0, :], dt[:, :], ALU.mult), reads=['p4a_par', 'p4a_dt'], writes=['p4_rho'])
                S.op('act', lambda e: e.activation(rho[:, :], rho[:, :], AF.Exp), reads=['p4_rho'], writes=['p4_rho'])
                S.op('dve', lambda e: e.tensor_tensor(th[:, :], pa[:, 1, :], dt[:, :], ALU.mult), reads=['p4a_par', 'p4a_dt'], writes=['p4a_th'])
                self.sincos(s2, 'p4a_th', th[:, :], 24, sn[:, :], cs[:, :], 'scA')
                tmp = self.sb(s2, "p4a_tmp", [128, T // 2], F32)
                for p in range(24):
                    S.op('act', lambda e: e.copy(cosT[:, p, 0:1], cs[:, p:p + 1]), reads=['scAout'], writes=[('p4_cos', p)])
                    S.op('act', lambda e: e.copy(sinT[:, p, 0:1], sn[:, p:p + 1]), reads=['scAout'], writes=[('p4_sin', p)])
                    w = 1
                    while w < T:
                        cw, sw = cosT[:, p, w - 1:w], sinT[:, p, w - 1:w]
                        S.op('dve', lambda e: e.tensor_scalar(tmp[:, 0:w], sinT[:, p, 0:w], sw, None, ALU.mult), reads=[('p4_sin', p)], writes=['p4a_tmp'])
                        S.op('dve', lambda e: e.scalar_tensor_tensor(cosT[:, p, w:2 * w], cosT[:, p, 0:w], cw, tmp[:, 0:w], ALU.mult, ALU.subtract),
                             reads=[('p4_cos', p), 'p4a_tmp'], writes=[('p4_cos', p)])
                        S.op('dve', lambda e: e.tensor_scalar(tmp[:, 0:w], cosT[:, p, 0:w], sw, None, ALU.mult), reads=[('p4_cos', p)], writes=['p4a_tmp'])
                        S.op('dve', lambda e: e.scalar_tensor_tensor(sinT[:, p, w:2 * w], sinT[:, p, 0:w], cw, tmp[:, 0:w], ALU.mult, ALU.add),
                             reads=[('p4_sin', p), ('p4_cos', p), 'p4a_tmp'], writes=[('p4_sin', p)])
                        w *= 2
                S.barrier()
            wg = self.sb(st, "p4_wg", [128, 6, 1536], BF16)
            S.dma('sp', wg[:], self.wB["ssm_glu"][l].rearrange("(k p) n -> p k n", p=128), reads=[('ssm_gluB', l)], writes=['p4_wg'])
            psy = Ring(nc, st, "p4_psy", 2, [128, 512], F32, psum=True)
            usr = Ring(nc, st, "p4_us", 2, [128, 6, T], BF16)
            tr = Ring(nc, st, "p4_t", 6, [128, T], F32)
            br = Ring(nc, st, "p4_b", 4, [128, T], F32)
            xr = Ring(nc, st, "p4_x", 4, [128, T], F32)
            xbr = Ring(nc, st, "p4_xb", 4, [128, T], BF16)
            yfr = Ring(nc, st, "p4_yf", 2, [128, T], F32)
            gy = self.sb(st, "p4_gy", [128, 6, T], BF16)
            sgr = Ring(nc, st, "p4_sg", 2, [128, T], F32)
            ycr = Ring(nc, st, "p4_yc", 2, [128, T], BF16)
            zFv = zF.rearrange("(c p) l -> p c l", p=128)
            yCv = yC.rearrange("(c p) l -> p c l", p=128)
            for ti in range(L // T):
                t0 = ti * T
                un, us = usr.next()
                S.dma('sp', us[:], zFv[:, 12:18, t0:t0 + T], reads=['zF'], writes=[un])
                for p in range(24):
                    ch = p // 4
                    c_, s_ = cosT[:, p, :], sinT[:, p, :]
                    prn, pr = self.psum.next()
                    S.op('pe', lambda e: e.matmul(pr[:, :], lhsT=BbR[:, p, :], rhs=us[:, ch, :], start=True, stop=True), reads=['p4_BbR', un], writes=[prn])
                    pin, pi = self.psum.next()
                    S.op('pe', lambda e: e.matmul(pi[:, :], lhsT=BbI[:, p, :], rhs=us[:, ch, :], start=True, stop=True), reads=['p4_BbI', un], writes=[pin])
                    t1n, t1 = tr.next(); t2n, t2 = tr.next(); t3n, t3 = tr.next(); t4n, t4 = tr.next()
                    S.op('dve', lambda e: e.tensor_tensor(t1[:, :], pr[:, :], c_, ALU.mult), reads=[prn, ('p4_cos', p)], writes=[t1n])
                    S.op('dve', lambda e: e.tensor_tensor(t2[:, :], pi[:, :], s_, ALU.mult), reads=[pin, ('p4_sin', p)], writes=[t2n])
                    S.op('dve', lambda e: e.tensor_tensor(t3[:, :], pi[:, :], c_, ALU.mult), reads=[pin, ('p4_cos', p)], writes=[t3n])
                    S.op('dve', lambda e: e.tensor_tensor(t4[:, :], pr[:, :], s_, ALU.mult), reads=[prn, ('p4_sin', p)], writes=[t4n])
                    brn, bre = br.next(); bin_, bim = br.next()
                    S.op('pool', lambda e: e.tensor_tensor(bre[:, :], t1[:, :], t2[:, :], ALU.add), reads=[t1n, t2n], writes=[brn])
                    S.op('pool', lambda e: e.tensor_tensor(bim[:, :], t3[:, :], t4[:, :], ALU.subtract), reads=[t3n, t4n], writes=[bin_])
                    rb = rho[:, p:p + 1].to_broadcast([128, T])
                    xrn, xre = xr.next(); xin, xim = xr.next()
                    S.op('dve', lambda e: e.tensor_tensor_scan(xre[:, :], rb, bre[:, :], xst[:, 0, p:p + 1], ALU.mult, ALU.add),
                         reads=['p4_rho', brn, ('p4_xst', p)], writes=[xrn])
                    S.op('dve', lambda e: e.tensor_tensor_scan(xim[:, :], rb, bim[:, :], xst[:, 1, p:p + 1], ALU.mult, ALU.add),
                         reads=['p4_rho', bin_, ('p4_xst', p)], writes=[xin])
                    u1n, u1 = tr.next(); u2n, u2 = tr.next()
                    S.op('pool', lambda e: e.tensor_tensor(u1[:, :], xre[:, :], c_, ALU.mult), reads=[xrn, ('p4_cos', p)], writes=[u1n])
                    S.op('pool', lambda e: e.tensor_tensor(u2[:, :], xim[:, :], s_, ALU.mult), reads=[xin, ('p4_sin', p)], writes=[u2n])
                    S.op('pool', lambda e: e.tensor_tensor(u1[:, :], u1[:, :], u2[:, :], ALU.subtract), reads=[u1n, u2n], writes=[u1n])
                    u3n, u3 = tr.next(); u4n, u4 = tr.next()
                    S.op('pool', lambda e: e.tensor_tensor(u3[:, :], xre[:, :], s_, ALU.mult), reads=[xrn, ('p4_sin', p)], writes=[u3n])
                    S.op('pool', lambda e: e.tensor_tensor(u4[:, :], xim[:, :], c_, ALU.mult), reads=[xin, ('p4_cos', p)], writes=[u4n])
                    S.op('pool', lambda e: e.tensor_tensor(u3[:, :], u3[:, :], u4[:, :], ALU.add), reads=[u3n, u4n], writes=[u3n])
                    S.op('act', lambda e: e.copy(xst[:, 0, p:p + 1], u1[:, T - 1:T]), reads=[u1n], writes=[('p4_xst', p)])
                    S.op('act', lambda e: e.copy(xst[:, 1, p:p + 1], u3[:, T - 1:T]), reads=[u3n], writes=[('p4_xst', p)])
                    xbrn, xbre = xbr.next(); xbin, xbim = xbr.next()
                    S.op('act', lambda e: e.copy(xbre[:, :], u1[:, :]), reads=[u1n], writes=[xbrn])
                    S.op('act', lambda e: e.copy(xbim[:, :], u3[:, :]), reads=[u3n], writes=[xbin])
                    if p % 4 == 0:
                        pyn, py = psy.next()
                    S.op('pe', lambda e: e.matmul(py[:, :], lhsT=CtR[:, p, :], rhs=xbre[:, :], start=(p % 4 == 0), stop=False), reads=['p4_CtR', xbrn], writes=[pyn], inc=False)
                    S.op('pe', lambda e: e.matmul(py[:, :], lhsT=CtI[:, p, :], rhs=xbim[:, :], start=False, stop=(p % 4 == 3)), reads=['p4_CtI', xbin], writes=[pyn])
                    if p % 4 == 3:
                        oc = p // 4
                        yfn, yf = yfr.next()
                        S.op('dve', lambda e: e.scalar_tensor_tensor(yf[:, :], us[:, oc, :], dsk[:, oc:oc + 1], py[:, :], ALU.mult, ALU.add),
                             reads=[un, 'p4_d', pyn], writes=[yfn])
                        S.op('act', lambda e: e.activation(gy[:, oc, :], yf[:, :], AF.Gelu_apprx_tanh), reads=[yfn], writes=[('p4_gy', oc)])
                for j in range(6):
                    pan, pa_ = self.psum.next()
                    for k in range(6):
                        S.op('pe', lambda e: e.matmul(pa_[:, :], lhsT=wg[:, k, j * 128:(j + 1) * 128], rhs=gy[:, k, :], start=(k == 0), stop=(k == 5)),
                             reads=['p4_wg', ('p4_gy', k)], writes=[pan], inc=(k == 5))
                    pbn, pb_ = self.psum.next()
                    for k in range(6):
                        S.op('pe', lambda e: e.matmul(pb_[:, :], lhsT=wg[:, k, 768 + j * 128:768 + (j + 1) * 128], rhs=gy[:, k, :], start=(k == 0), stop=(k == 5)),
                             reads=['p4_wg', ('p4_gy', k)], writes=[pbn], inc=(k == 5))
                    sgn, sg = sgr.next()
                    S.op('act', lambda e: e.activation(sg[:, :], pb_[:, :], AF.Sigmoid), reads=[pbn], writes=[sgn])
                    ycn, yc = ycr.next()
                    S.op('dve', lambda e: e.tensor_tensor(yc[:, :], pa_[:, :], sg[:, :], ALU.mult), reads=[pan, sgn], writes=[ycn])
                    S.dma('sp', yCv[:, j, t0:t0 + T], yc[:, :], reads=[ycn], writes=['yC'])

    def phase_xattn(self, l, mem, zF, yX):
        nc, S, L = self.nc, self.S, self.L
        with ExitStack() as st:
            kT = self.sb(st, "p5_kT", [128, 4, 2, 256], BF16)
            vm = self.sb(st, "p5_vm", [128, 2, 768], BF16)
            with ExitStack() as st2:
                memt = self.sb(st2, "p5_mem", [128, 2, D], F32)
                memT = self.sb(st2, "p5_memT", [128, 8, 256], F32)
                sq = self.sb(st2, "p5_sq", [128, 8, 256], BF16)
                rstd = self.sb(st2, "p5_rstd", [128, 256], F32)
                mn = self.sb(st2, "p5_mn", [128, 8, 256], BF16)
                wkv = self.sb(st2, "p5_wkv", [128, 8, 1536], BF16)
                S.dma('sp', memt[:], mem.rearrange("(b p) d -> p b d", p=128), writes=['p5_mem'])
                S.dma('sp', wkv[:], self.wB["mem_wkv"][l].rearrange("(k p) n -> p k n", p=128), reads=[('mem_wkvB', l)], writes=['p5_wkv'])
                for b in range(2):
                    for half in range(2):
                        pn, ps = self.psum.next()
                        for j in range(4):
                            c = half * 4 + j
                            S.op('pe', lambda e: e.transpose(ps[:, j * 128:(j + 1) * 128], memt[:, b, c * 128:(c + 1) * 128], self.ident_f[:]),
                                 reads=['p5_mem', 'ident_f'], writes=[pn], inc=(j == 3))
                        S.op('dve', lambda e: e.tensor_copy(memT[:, half * 4:(half + 1) * 4, b * 128:(b + 1) * 128], ps[:, :].rearrange("p (a b) -> p a b", a=4)),
                             reads=[pn], writes=['p5_memT'])
                for c in range(8):
                    S.op('act', lambda e: e.activation(sq[:, c, :], memT[:, c, :], AF.Square), reads=['p5_memT'], writes=['p5_sq'])
                self.rms_stats('p5_sq', sq, 'p5_rstd', rstd, 256)
                for c in range(8):
                    S.op('dve', lambda e: e.scalar_tensor_tensor(mn[:, c, :], memT[:, c, :], self.gains_s[:, l, 2, c:c + 1], rstd[:, :], ALU.mult, ALU.mult),
                         reads=['p5_memT', 'p5_rstd', 'gains_s'], writes=['p5_mn'])
                for h in range(4):
                    for part, (off, M) in enumerate(((0, 128), (128, 64))):
                        col = 192 * h + off
                        pn, ps = self.psum.next()
                        for k in range(8):
                            S.op('pe', lambda e: e.matmul(ps[:M, :256], lhsT=wkv[:, k, col:col + M], rhs=mn[:, k, :], start=(k == 0), stop=(k == 7)),
                                 reads=['p5_wkv', 'p5_mn'], writes=[pn], inc=(k == 7))
                        S.op('dve', lambda e: e.tensor_copy(kT[:M, h, part, :], ps[:M, :256]), reads=[pn], writes=['p5_kT'])
                for mc in range(2):
                    for (n0, nn) in ((0, 512), (512, 256)):
                        pn, ps = self.psum.next()
                        for k in range(8):
                            S.op('pe', lambda e: e.matmul(ps[:, :nn], lhsT=mn[:, k, mc * 128:(mc + 1) * 128], rhs=wkv[:, k, 768 + n0:768 + n0 + nn], start=(k == 0), stop=(k == 7)),
                                 reads=['p5_wkv', 'p5_mn'], writes=[pn], inc=(k == 7))
                        S.op('dve', lambda e: e.tensor_copy(vm[:, mc, n0:n0 + nn], ps[:, :nn]), reads=[pn], writes=['p5_vm'])
                S.barrier()
            xq0r = Ring(nc, st, "p5_xq0", 2, [128, T], BF16)
            xq1r = Ring(nc, st, "p5_xq1", 2, [128, T], BF16)
            ptr = Ring(nc, st, "p5_pt", 4, [128, T], BF16)
            rcr = Ring(nc, st, "p5_rc", 2, [128, T], F32)
            y0r = Ring(nc, st, "p5_y0", 2, [128, T], BF16)
            y1r = Ring(nc, st, "p5_y1", 2, [128, T], BF16)
            XQ0 = 18 * 128
            sc = 192.0 ** -0.5
            for ti in range(L // T):
                t0 = ti * T
                for h in range(4):
                    q0n, q0 = xq0r.next()
                    q1n, q1 = xq1r.next()
                    r0 = XQ0 + 192 * h
                    S.dma('sp', q0[:, :], zF[r0:r0 + 128, t0:t0 + T], reads=['zF'], writes=[q0n])
                    S.dma('sp', q1[0:64, :], zF[r0 + 128:r0 + 192, t0:t0 + T], reads=['zF'], writes=[q1n])
                    pts = []
                    for mc in range(2):
                        pn, ps = self.psum.next()
                        S.op('pe', lambda e: e.matmul(ps[:, :], lhsT=kT[:, h, 0, mc * 128:(mc + 1) * 128], rhs=q0[:, :], start=True, stop=False),
                             reads=['p5_kT', q0n], writes=[pn], inc=False)
                        S.op('pe', lambda e: e.matmul(ps[:, :], lhsT=kT[0:64, h, 1, mc * 128:(mc + 1) * 128], rhs=q1[0:64, :], start=False, stop=True),
                             reads=['p5_kT', q1n], writes=[pn])
                        ptn, pt = ptr.next()
                        S.op('act', lambda e: e.activation(pt[:, :], ps[:, :], AF.Exp, scale=sc), reads=[pn], writes=[ptn])
                        pts.append((ptn, pt))
                    pn, ps = self.psum.next()
                    for mc in range(2):
                        S.op('pe', lambda e: e.matmul(ps[:, :], lhsT=self.ones_b[:], rhs=pts[mc][1][:, :], start=(mc == 0), stop=(mc == 1)),
                             reads=['ones_b', pts[mc][0]], writes=[pn], inc=(mc == 1))
                    rcn, rc = rcr.next()
                    S.op('dve', lambda e: e.reciprocal(rc[:, :], ps[:, :]), reads=[pn], writes=[rcn])
                    for part, (off, M, yr_) in enumerate(((0, 128, y0r), (128, 64, y1r))):
                        pn, ps = self.psum.next()
                        for mc in range(2):
                            S.op('pe', lambda e: e.matmul(ps[:M, :], lhsT=vm[:, mc, 192 * h + off:192 * h + off + M], rhs=pts[mc][1][:, :], start=(mc == 0), stop=(mc == 1)),
                                 reads=['p5_vm', pts[mc][0]], writes=[pn], inc=(mc == 1))
                        yn, y = yr_.next()
                        S.op('dve', lambda e: e.tensor_tensor(y[:M, :], ps[:M, :], rc[:M, :], ALU.mult), reads=[pn, rcn], writes=[yn])
                        S.dma('sp', yX[192 * h + off:192 * h + off + M, t0:t0 + T], y[:M, :], reads=[yn], writes=['yX'])

    def phase_merge(self, l, xT, xmT, zF, yA, yC, yX, oG):
        nc, S, L = self.nc, self.S, self.L
        ph = self.phases
        branches = []
        if 'p2' in ph:
            branches.append((0, 'proj_a', 6))
        if 'p3' in ph:
            branches.append((1, 'proj_b', 2))
        if 'p4' in ph:
            branches.append((2, 'proj_c', 6))
        if 'p5' in ph:
            branches.append((3, 'proj_x', 6))
        with ExitStack() as st:
            wp = {}
            for (b, nm, nk) in branches:
                wp[b] = self.sb(st, "p6_" + nm, [128, nk, D], BF16)
                S.dma('sp', wp[b][:], self.wB[nm][l].rearrange("(k p) n -> p k n", p=128), reads=[(nm + 'B', l)], writes=['p6_w%d' % b])
            wo = self.sb(st, "p6_wo", [128, 8, D], BF16)
            S.dma('sp', wo[:], self.wB["w_out"][l].rearrange("(k p) n -> p k n", p=128), reads=[('w_outB', l)], writes=['p6_wo'])
            yr = {b: Ring(nc, st, "p6_y%d" % b, 1, [128, nk, T], BF16) for (b, nm, nk) in branches}
            sgr = Ring(nc, st, "p6_sg", 2, [128, 4, T], BF16)
            xr = Ring(nc, st, "p6_x", 1, [128, 8, T], F32)
            xor_ = Ring(nc, st, "p6_xo", 1, [128, 8, T], F32)
            macc = Ring(nc, st, "p6_macc", 2, [128, T], F32)
            mtmp = Ring(nc, st, "p6_mtmp", 2, [128, T], F32)
            mb = self.sb(st, "p6_mb", [128, 8, T], BF16)
            sq = self.sb(st, "p6_sq", [128, 8, T], BF16)
            rstd = self.sb(st, "p6_rstd", [128, T], F32)
            y = self.sb(st, "p6_yy", [128, 8, T], F32)
            og = Ring(nc, st, "p6_og", 2, [128, 3, 260], F32)
            ybt = Ring(nc, st, "p6_ybt", 2, [128, 256], F32)
            rl = Ring(nc, st, "p6_rl", 2, [128, 4], F32)
            srcs = {0: yA, 2: yC, 3: yX}
            for ti in range(L // T):
                t0 = ti * T
                ys = {}
                for (b, nm, nk) in branches:
                    yn, yt = yr[b].next()
                    ys[b] = (yn, yt)
                    if b != 1:
                        S.dma('sp', yt[:], srcs[b].rearrange("(c p) l -> p c l", p=128)[:, :, t0:t0 + T], reads=[srcs[b].tensor.name], writes=[yn])
                    else:
                        for tb in range(4):
                            on, o = og.next()
                            S.dma('sp', o[:], oG[:, t0 + tb * 128:t0 + (tb + 1) * 128, :].rearrange("g p n -> p g n"), reads=['oG'], writes=[on])
                            S.op('dve', lambda e: e.tensor_tensor(o[:, 0, :], o[:, 0, :], o[:, 1, :], ALU.add), reads=[on], writes=[on])
                            S.op('dve', lambda e: e.tensor_tensor(o[:, 0, :], o[:, 0, :], o[:, 2, :], ALU.add), reads=[on], writes=[on])
                            rn, r = rl.next()
                            ov = o[:, 0, :].rearrange("p (h e) -> p h e", e=65)
                            S.op('dve', lambda e: e.reciprocal(r[:, :], ov[:, :, 64]), reads=[on], writes=[rn])
                            bn, bt = ybt.next()
                            for hh in range(4):
                                S.op('dve', lambda e: e.tensor_scalar(bt[:, hh * 64:(hh + 1) * 64], ov[:, hh, 0:64], r[:, hh:hh + 1], None, ALU.mult),
                                     reads=[on, rn], writes=[bn])
                            pn, ps = self.psum.next()
                            for half in range(2):
                                S.op('pe', lambda e: e.transpose(ps[:, half * 128:(half + 1) * 128], bt[:, half * 128:(half + 1) * 128], self.ident_f[:]),
                                     reads=[bn, 'ident_f'], writes=[pn], inc=(half == 1))
                            S.op('act', lambda e: e.copy(yt[:, :, tb * 128:(tb + 1) * 128], ps[:, 0:256].rearrange("p (a b) -> p a b", a=2)),
                                 reads=[pn], writes=[yn])
                xn, xt = xr.next()
                S.dma('sp', xt[:], xT.rearrange("(c p) l -> p c l", p=128)[:, :, t0:t0 + T], reads=[('xT', ti)], writes=[xn])
                for c in range(8):
                    mn_, m = macc.next()
                    sn, sg = sgr.next()
                    S.dma('sp', sg[:], zF.rearrange("(b c p) l -> p b c l", p=128, c=8)[:, 3:7, c, t0:t0 + T], reads=['zF'], writes=[sn])
                    for bi, (b, nm, nk) in enumerate(branches):
                        yn, yt = ys[b]
                        pn, ps = self.psum.next()
                        for k in range(nk):
                            S.op('pe', lambda e: e.matmul(ps[:, :], lhsT=wp[b][:, k, c * 128:(c + 1) * 128], rhs=yt[:, k, :], start=(k == 0), stop=(k == nk - 1)),
                                 reads=['p6_w%d' % b, yn], writes=[pn], inc=(k == nk - 1))
                        if bi == 0:
                            S.op('dve', lambda e: e.tensor_tensor(m[:, :], ps[:, :], sg[:, b, :], ALU.mult), reads=[pn, sn], writes=[mn_])
                        else:
                            tn, tm = mtmp.next()
                            S.op('dve', lambda e: e.tensor_tensor(tm[:, :], ps[:, :], sg[:, b, :], ALU.mult), reads=[pn, sn], writes=[tn])
                            S.op('pool', lambda e: e.tensor_tensor(m[:, :], m[:, :], tm[:, :], ALU.add), reads=[tn, mn_], writes=[mn_])
                    S.op('act', lambda e: e.copy(mb[:, c, :], m[:, :]), reads=[mn_], writes=[('p6_mb', c)])
                for c in range(8):
                    pn, ps = self.psum.next()
                    for k in range(8):
                        S.op('pe', lambda e: e.matmul(ps[:, :], lhsT=wo[:, k, c * 128:(c + 1) * 128], rhs=mb[:, k, :], start=(k == 0), stop=(k == 7)),
                             reads=['p6_wo', ('p6_mb', k)], writes=[pn], inc=(k == 7))
                    S.op('act', lambda e: e.activation(sq[:, c, :], ps[:, :], AF.Square), reads=[pn], writes=['p6_sq'])
                    S.op('act', lambda e: e.copy(y[:, c, :], ps[:, :]), reads=[pn], writes=['p6_yy'])
                on_, xo_t = xor_.next()
                self.postnorm_residual(l, 1, 'p6_yy', y, 'p6_sq', sq, 'p6_rstd', rstd, xn, xt, on_, xo_t)
                S.dma('sp', xmT.rearrange("(c p) l -> p c l", p=128)[:, :, t0:t0 + T], xo_t[:], reads=[on_], writes=[('xmT', ti)])

    def postnorm_residual(self, l, which, y_name, y, sq_name, sq, rstd_name, rstd, xres_name, xres, xo_name, xo):
        S = self.S
        self.rms_stats(sq_name, sq, rstd_name, rstd, T)
        for c in range(8):
            S.op('dve', lambda e: e.scalar_tensor_tensor(y[:, c, :], y[:, c, :], self.gains_s[:, l, which, c:c + 1], rstd[:, :],
                                                         ALU.mult, ALU.mult),
                 reads=[y_name, rstd_name, 'gains_s'], writes=[y_name])
            S.op('pool', lambda e: e.tensor_tensor(xo[:, c, :], y[:, c, :], xres[:, c, :], ALU.add),
                 reads=[y_name, xres_name], writes=[xo_name])

    def phase_ffn(self, l, xsrc, xdst, wUpB, wDnB, ffnp):
        nc, S, L = self.nc, self.S, self.L
        NT = 2
        with ExitStack() as st:
            xr = Ring(nc, st, "p7_x", 2, [128, 8, T], F32)
            sq = self.sb(st, "p7_sq", [128, 8, T], BF16)
            rstd = self.sb(st, "p7_rstd", [128, T], F32)
            hr = Ring(nc, st, "p7_h", 2, [128, 8, T], BF16)
            wr = Ring(nc, st, "p7_w", 2, [128, 8, 1024], BF16)
            actr = Ring(nc, st, "p7_act", 2, [128, 24, T], BF16)
            gsb = Ring(nc, st, "p7_g", 2, [128, T + 2], F32)
            gtmp = Ring(nc, st, "p7_gt", 2, [128, T], F32)
            tails = self.sb(st, "p7_tail", [128, 24, 2], F32)
            fp = self.sb(st, "p7_fp", [128, 4, 24], F32)
            self.ys = [self.sb(st, "p7_y%d" % i, [128, 8, T], F32) for i in range(2)]
            self.sqs = [self.sb(st, "p7_sqo%d" % i, [128, 8, T], BF16) for i in range(2)]
            S.dma('sp', fp[:], ffnp[:, l, :, :], writes=['p7_fp'])
            S.op('dve', lambda e: e.memset(tails[:], 0.0), writes=[('p7_tail', c) for c in range(24)])
            wup = wUpB[l].rearrange("(k p) n -> p k n", p=128)
            wdn = wDnB[l].rearrange("(k p) n -> p k n", p=128)
            wq = 0
            for tp in range(L // (NT * T)):
                tiles = []
                for ti in range(tp * NT, (tp + 1) * NT):
                    t0 = ti * T
                    xn, xt = xr.next()
                    S.dma('sp', xt[:], xsrc.rearrange("(c p) l -> p c l", p=128)[:, :, t0:t0 + T], reads=[(xsrc.tensor.name, ti)], writes=[xn])
                    hn, h = hr.next()
                    self.prenorm(l, 3, xn, xt, hn, h, "p7_sq", sq, "p7_rstd", rstd)
                    an, act = actr.next()
                    tiles.append((ti, t0, xn, xt, hn, h, an, act))
                for blk in range(6):
                    wn, w = wr.next()
                    wq += 1
                    q_ = 'sp' if wq % 2 == 0 else 'pool'
                    S.dma(q_, w[:, :, 0:512], wup[:, :, blk * 512:(blk + 1) * 512], reads=[('wUpB', l)], writes=[wn])
                    S.dma(q_, w[:, :, 512:1024], wup[:, :, DFF + blk * 512:DFF + (blk + 1) * 512], reads=[('wUpB', l)], writes=[wn])
                    for (ti, t0, xn, xt, hn, h, an, act) in tiles:
                        for j in range(4):
                            ch = blk * 4 + j
                            pgn, pg = self.psum.next()
                            for k in range(8):
                                S.op('pe', lambda e: e.matmul(pg[:, :], lhsT=w[:, k, 512 + j * 128:512 + (j + 1) * 128], rhs=h[:, k, :], start=(k == 0), stop=(k == 7)),
                                     reads=[wn, hn], writes=[pgn], inc=(k == 7))
                            pvn, pv = self.psum.next()
                            for k in range(8):
                                S.op('pe', lambda e: e.matmul(pv[:, :], lhsT=w[:, k, j * 128:(j + 1) * 128], rhs=h[:, k, :], start=(k == 0), stop=(k == 7)),
                                     reads=[wn, hn], writes=[pvn], inc=(k == 7))
                            gn, g = gsb.next()
                            S.op('act', lambda e: e.copy(g[:, 2:T + 2], pg[:, :]), reads=[pgn], writes=[gn])
                            S.op('act', lambda e: e.copy(g[:, 0:2], tails[:, ch, :]), reads=[('p7_tail', ch)], writes=[gn])
                            S.op('act', lambda e: e.copy(tails[:, ch, :], g[:, T:T + 2]), reads=[gn], writes=[('p7_tail', ch)])
                            tn, tm = gtmp.next()
                            S.op('dve', lambda e: e.tensor_scalar(tm[:, :], g[:, 0:T], fp[:, 0, ch:ch + 1], fp[:, 3, ch:ch + 1], ALU.mult, ALU.add),
                                 reads=[gn, 'p7_fp'], writes=[tn])
                            S.op('dve', lambda e: e.scalar_tensor_tensor(tm[:, :], g[:, 1:T + 1], fp[:, 1, ch:ch + 1], tm[:, :], ALU.mult, ALU.add),
                                 reads=[gn, tn, 'p7_fp'], writes=[tn])
                            S.op('dve', lambda e: e.scalar_tensor_tensor(tm[:, :], g[:, 2:T + 2], fp[:, 2, ch:ch + 1], tm[:, :], ALU.mult, ALU.add),
                                 reads=[gn, tn, 'p7_fp'], writes=[tn])
                            S.op('act', lambda e: e.activation(tm[:, :], tm[:, :], AF.Gelu_apprx_tanh), reads=[tn], writes=[tn])
                            S.op('dve', lambda e: e.tensor_tensor(act[:, ch, :], pv[:, :], tm[:, :], ALU.mult), reads=[pvn, tn], writes=[(an, ch)])
                for half in range(2):
                    banks = {}
                    for cpair in range(2):
                        for (ti, t0, xn, xt, hn, h, an, act) in tiles:
                            banks[ti] = [self.psum.next() for _ in range(2)]
                        for kb in range(3):
                            wn, w = wr.next()
                            wq += 1
                            q_ = 'sp' if wq % 2 == 0 else 'pool'
                            c00 = half * 512 + cpair * 256
                            S.dma(q_, w[:, :, 0:256], wdn[:, kb * 8:(kb + 1) * 8, c00:c00 + 256], reads=[('wDnB', l)], writes=[wn])
                            for (ti, t0, xn, xt, hn, h, an, act) in tiles:
                                for cc in range(2):
                                    pn, ps = banks[ti][cc]
                                    for k in range(8):
                                        kk = kb * 8 + k
                                        S.op('pe', lambda e: e.matmul(ps[:, :], lhsT=w[:, k, cc * 128:(cc + 1) * 128], rhs=act[:, kk, :], start=(kk == 0), stop=(kk == 23)),
                                             reads=[wn, (an, kk)], writes=[pn], inc=(k == 7))
                        for (ti, t0, xn, xt, hn, h, an, act) in tiles:
                            for cc in range(2):
                                c = half * 4 + cpair * 2 + cc
                                pn, ps = banks[ti][cc]
                                S.op('act', lambda e: e.activation(self.sqs[ti % 2][:, c, :], ps[:, :], AF.Square), reads=[pn], writes=[('p7_sq2', ti % 2)])
                                S.op('act', lambda e: e.copy(self.ys[ti % 2][:, c, :], ps[:, :]), reads=[pn], writes=[('p7_y2', ti % 2)])
                for (ti, t0, xn, xt, hn, h, an, act) in tiles:
                    on, xo_t = ('p7_y2', ti % 2), self.ys[ti % 2]
                    self.postnorm_residual(l, 4, ('p7_y2', ti % 2), self.ys[ti % 2], ('p7_sq2', ti % 2), self.sqs[ti % 2], 'p7_rstd', rstd, xn, xt, on, xo_t)
                    S.dma('sp', xdst.rearrange("(c p) l -> p c l", p=128)[:, :, t0:t0 + T], xo_t[:], reads=[on], writes=[(xdst.tensor.name, ti)])


def host_inputs(inputs, b, L):
    f = np.float32
    d = {}
    d["x"] = np.ascontiguousarray(inputs["x"][b, :L])
    d["mem"] = np.ascontiguousarray(inputs["mem"][b])
    d["ident"] = np.eye(128, dtype=f)
    gs = np.stack([inputs[k] for k in ("g_mix_pre", "g_mix_post", "g_mem", "g_mlp_pre", "g_mlp_post")], axis=1)
    d["gains"] = np.ascontiguousarray(gs.reshape(NL, 5, 8, 128).transpose(3, 0, 1, 2)).astype(f)
    d["w_in"] = inputs["w_in"]
    d["ffn_w_up"] = inputs["ffn_w_up"]
    d["ffn_w_down"] = inputs["ffn_w_down"]
    fp = np.concatenate([inputs["ffn_conv_w"], inputs["ffn_conv_b"][:, None, :]], axis=1)
    lp = np.concatenate([inputs["lru_conv_w"], inputs["lru_conv_b"][:, None], inputs["lru_ba"][:, None],
                         inputs["lru_bx"][:, None], inputs["lru_lambda"][:, None]], axis=1)
    d["lrup"] = np.ascontiguousarray(lp.reshape(NL, 8, 6, 128).transpose(3, 0, 1, 2)).astype(f)
    bd = np.zeros((NL, 2, 128, 6, 128), f)
    for wi, nm in enumerate(("lru_wa", "lru_wx")):
        w = inputs[nm]
        for c in range(6):
            bd[:, wi, 0:64, c, 0:64] = w[:, 2 * c]
            bd[:, wi, 64:128, c, 64:128] = w[:, 2 * c + 1]
    d["lru_bd"] = bd
    d["ssmd"] = np.ascontiguousarray(inputs["ssm_d"].reshape(NL, 6, 128).transpose(2, 0, 1)).astype(f)
    for nm in ("ssm_glu", "mem_wkv", "proj_a", "proj_b", "proj_c", "proj_x", "w_out"):
        d[nm] = inputs[nm]
    am = np.full((128, 2, 12, 256), -30000.0, f)
    kk = np.arange(128)[:, None]
    qq = np.arange(128)[None, :]
    for hd in range(12):
        dd = DILS[hd // 4]
        dp = (qq + 128 - kk).astype(f)
        dc = (qq - kk).astype(f)
        mp = np.where(kk >= qq, -ALIBI[hd] * dd * dp, -30000.0)
        mc = np.where(kk <= qq, -ALIBI[hd] * dd * dc, -30000.0)
        am[:, 0, hd, 0:128] = mp
        am[:, 0, hd, 128:256] = mc
        am[:, 1, hd, 128:256] = mc
    d["amask"] = am
    def lay_a(a):
        return a.reshape(NL, 24, 2, 64).transpose(2, 3, 0, 1).reshape(128, NL, 24)
    ld_full = np.repeat(inputs["ssm_log_dt"][:, :, None], 64, axis=2)
    d["s5A"] = np.ascontiguousarray(np.stack([lay_a(inputs["ssm_a_re"]), lay_a(inputs["ssm_a_im"]), lay_a(ld_full)], axis=2)).astype(f)
    def lay_r(a):
        return a.reshape(NL, 3072)
    rr = np.stack([lay_r(inputs["ssm_a_re"]), lay_r(inputs["ssm_a_im"]), lay_r(ld_full)], axis=1)
    d["s5R"] = np.ascontiguousarray(np.broadcast_to(rr[None], (128, NL, 3, 3072))).astype(f)
    sB = np.zeros((NL, 2, 128, 24, 128), f)
    sC = np.zeros((NL, 2, 128, 24, 128), f)
    for ri, (bn, cn) in enumerate((("ssm_b_re", "ssm_c_re"), ("ssm_b_im", "ssm_c_im"))):
        Bm = inputs[bn]
        Cm = inputs[cn]
        for p in range(24):
            for gl in range(2):
                r0 = 32 * (p % 4) + gl * 16
                sB[:, ri, r0:r0 + 16, p, gl * 64:(gl + 1) * 64] = Bm[:, 2 * p + gl].transpose(0, 2, 1)
                sC[:, ri, gl * 64:(gl + 1) * 64, p, r0:r0 + 16] = Cm[:, 2 * p + gl].transpose(0, 2, 1)
    d["s5B"] = sB
    d["s5C"] = sC
    d["ffnp"] = np.ascontiguousarray(fp.reshape(NL, 4, 24, 128).transpose(3, 0, 1, 2)).astype(f)
    return d


ALL_PHASES = ('prepass', 'prologue', 'p1', 'p2', 'p3', 'p4', 'p5', 'p6', 'p7', 'epilogue')


def kernel(**inputs):
    inputs = {k: np.asarray(v) for k, v in inputs.items()}
    L = inputs["x"].shape[1]
    kb = K(L, NL)
    nc = kb.build(ALL_PHASES)
    in_maps = []
    for b in range(2):
        hi = host_inputs(inputs, b, L)
        in_maps.append({k: hi[k] for k in kb.ins})
    res = run_bass_kernel_spmd(nc, in_maps, core_ids=[0, 1])
    return np.stack([res.results[b]["out"] for b in range(2)], axis=0)
```

```python
import math
from contextlib import ExitStack
import numpy as np
import concourse.bass as bass
import concourse.mybir as mybir
from concourse.bass_utils import run_bass_kernel_spmd

AF = mybir.ActivationFunctionType
ALU = mybir.AluOpType
F32 = mybir.dt.float32
BF16 = mybir.dt.bfloat16

D = 1024
NL = 2
SEQ = 16384
MEM = 256
IN_W = 9472
DFF = 3072
T = 512
EPS = 1e-6
ALIBI = [2.0 ** (-8.0 * (h + 1) / 12) for h in range(12)]
DILS = (1, 4, 16)
import os
P3STOP = int(os.environ.get("P3STOP", "0"))


class Sched:
    NSLOT = 8

    def __init__(self, nc, stack):
        self.nc = nc
        self.engs = {'pe': nc.tensor, 'act': nc.scalar, 'dve': nc.vector,
                     'pool': nc.gpsimd, 'sp': nc.sync}
        self.sem = {}
        self.cnt = {}
        for n in ['pe', 'act', 'dve', 'pool']:
            self.sem[n] = stack.enter_context(nc.semaphore("s_" + n))
            self.cnt[n] = 0
        self.dq = {}
        for q in ['sp', 'pool', 'act']:
            for i in range(self.NSLOT):
                self.sem[('dma', q, i)] = stack.enter_context(nc.semaphore("d_%s%d" % (q, i)))
            self.dq[q] = 0
        self.seen = {e: {} for e in self.engs}
        self.lastw = {}
        self.readers = {}

    def _deps(self, reads, writes):
        deps = []
        for r in reads:
            t = self.lastw.get(r)
            if t is not None:
                deps.append(t)
        for w in writes:
            t = self.lastw.get(w)
            if t is not None:
                deps.append(t)
            deps.extend(self.readers.get(w, ()))
        return deps

    def _wait(self, ename, deps):
        best = {}
        for (src, val) in deps:
            if best.get(src, 0) < val:
                best[src] = val
        seen = self.seen[ename]
        eng = self.engs[ename]
        for src, val in best.items():
            if src == 'pe' and ename == 'pe':
                continue
            if seen.get(src, 0) >= val:
                continue
            eng.wait_ge(self.sem[src], val)
            seen[src] = val

    def _record(self, ticket, reads, writes):
        for r in reads:
            self.readers.setdefault(r, []).append(ticket)
        for w in writes:
            self.lastw[w] = ticket
            self.readers[w] = []

    defer = None

    def op(self, ename, fn, reads=(), writes=(), inc=True):
        if self.defer is not None:
            self.defer.append((ename, fn, reads, writes, inc))
            return None
        self._wait(ename, self._deps(reads, writes))
        ins = fn(self.engs[ename])
        if inc:
            self.cnt[ename] += 1
            ins.then_inc(self.sem[ename], 1)
            ticket = (ename, self.cnt[ename])
        else:
            ticket = (ename, self.cnt[ename] + 1)
        self._record(ticket, reads, writes)
        return ticket

    def dma(self, q, out, in_, reads=(), writes=(), **kw):
        i = self.dq[q]
        slot = i % self.NSLOT
        rnd = i // self.NSLOT
        src = ('dma', q, slot)
        deps = self._deps(reads, writes)
        if rnd > 0:
            deps.append((src, 16 * rnd))
        self._wait(q, deps)
        ins = self.engs[q].dma_start(out=out, in_=in_, **kw)
        ins.then_inc(self.sem[src], 16)
        self.dq[q] = i + 1
        ticket = (src, 16 * (rnd + 1))
        self._record(ticket, reads, writes)
        return ticket

    def coll(self, kind, src, dst, groups, reads=(), writes=()):
        q = 'pool'
        i = self.dq[q]
        slot = i % self.NSLOT
        rnd = i // self.NSLOT
        srck = ('dma', q, slot)
        deps = self._deps(reads, writes)
        if rnd > 0:
            deps.append((srck, 16 * rnd))
        self._wait(q, deps)
        ins = self.engs[q].collective_compute(kind, ALU.bypass, replica_groups=groups, ins=[src], outs=[dst])
        ins.then_inc(self.sem[srck], 16)
        self.dq[q] = i + 1
        ticket = (srck, 16 * (rnd + 1))
        self._record(ticket, reads, writes)
        return ticket

    def finish(self, ename='sp'):
        deps = list(self.lastw.values())
        for l in self.readers.values():
            deps.extend(l)
        self._wait(ename, deps)

    def barrier(self):
        for e in self.engs:
            self.finish(e)
        self.lastw = {}
        self.readers = {}


_UID = [0]


class Ring:
    def __init__(self, nc, stack, name, n, shape, dtype, psum=False):
        self.bufs = []
        _UID[0] += 1
        for i in range(n):
            nm = "%s_%d_%d" % (name, _UID[0], i)
            if psum:
                t = stack.enter_context(nc.psum_tensor(nm, shape, dtype))
            else:
                t = stack.enter_context(nc.sbuf_tensor(nm, shape, dtype))
            self.bufs.append((nm, t))
        self.i = 0

    def next(self):
        b = self.bufs[self.i % len(self.bufs)]
        self.i += 1
        return b


class K:
    def __init__(self, L, nl, dbg=False):
        self.L = L
        self.nl = nl
        self.dbg = dbg
        self.nc = bass.Bass("TRN2", target_bir_lowering=False)
        self.ins = {}
        self.scr = {}

    def inp(self, name, shape, dt=F32):
        t = self.nc.dram_tensor(name, list(shape), dt, kind="ExternalInput").ap()
        self.ins[name] = t
        return t

    def scratch(self, name, shape, dt):
        kind = "ExternalOutput" if self.dbg else "Internal"
        t = self.nc.dram_tensor(name, list(shape), dt, kind=kind).ap()
        self.scr[name] = t
        return t

    def sb(self, st, name, shape, dt):
        _UID[0] += 1
        return st.enter_context(self.nc.sbuf_tensor("%s_%d" % (name, _UID[0]), list(shape), dt))

    def build(self, phases):
        nc = self.nc
        L, nl = self.L, self.nl
        x = self.inp("x", [L, D])
        mem = self.inp("mem", [MEM, D])
        ident = self.inp("ident", [128, 128])
        gains = self.inp("gains", [128, NL, 5, 8])
        w_in = self.inp("w_in", [NL, D, IN_W])
        ffn_w_up = self.inp("ffn_w_up", [NL, D, 2 * DFF])
        ffn_w_down = self.inp("ffn_w_down", [NL, DFF, D])
        ffnp = self.inp("ffnp", [128, NL, 4, 24])
        lrup = self.inp("lrup", [128, NL, 8, 6])
        lru_bd = self.inp("lru_bd", [NL, 2, 128, 6, 128])
        ssmd = self.inp("ssmd", [128, NL, 6])
        wsrc = {}
        for nm, shp in (("ssm_glu", [NL, 768, 1536]), ("mem_wkv", [NL, D, 1536]), ("proj_a", [NL, 768, D]),
                        ("proj_b", [NL, 256, D]), ("proj_c", [NL, 768, D]), ("proj_x", [NL, 768, D]), ("w_out", [NL, D, D])):
            wsrc[nm] = (self.inp(nm, shp), self.scratch(nm + "B", shp, BF16), shp[1])
        self.wB = {nm: v[1] for nm, v in wsrc.items()}
        yA = self.scratch("yA", [768, L], BF16)
        yC = self.scratch("yC", [768, L], BF16)
        yX = self.scratch("yX", [768, L], BF16)
        oG = self.scratch("oG", [3, L, 260], F32)
        self.phases = phases
        amask = self.inp("amask", [128, 2, 12, 256])
        s5A = self.inp("s5A", [128, NL, 3, 24])
        s5R = self.inp("s5R", [128, NL, 3, 3072])
        s5B = self.inp("s5B", [NL, 2, 128, 24, 128])
        s5C = self.inp("s5C", [NL, 2, 128, 24, 128])
        out = self.nc.dram_tensor("out", [L, D], F32, kind="ExternalOutput").ap()
        self.out = out

        xT = self.scratch("xT", [D, L], F32)
        xmT = self.scratch("xmT", [D, L], F32)
        zF = self.scratch("zF", [56 * 128, L], BF16)
        zQKV = self.scratch("zQKV", [L, 2304], BF16)
        wInB = self.scratch("wInB", [NL, D, IN_W], BF16)
        wUpB = self.scratch("wUpB", [NL, D, 2 * DFF], BF16)
        wDnB = self.scratch("wDnB", [NL, DFF, D], BF16)

        with ExitStack() as st0:
            S = Sched(nc, st0)
            self.S = S
            ident_f = self.sb(st0, "ident_f", [128, 128], F32)
            ones_b = self.sb(st0, "ones_b", [128, 128], BF16)
            gains_s = self.sb(st0, "gains_s", [128, NL, 5, 8], F32)
            eps_c = self.sb(st0, "eps_c", [128, 1], F32)
            self.ident_f, self.ones_b, self.gains_s, self.eps_c = ident_f, ones_b, gains_s, eps_c
            one_c = self.sb(st0, "one_c", [128, 1], F32)
            self.one_c = one_c
            S.op('dve', lambda e: e.memset(one_c[:], 1.0), writes=['one_c'])
            S.dma('sp', ident_f[:], ident, writes=['ident_f'])
            S.dma('sp', gains_s[:], gains, writes=['gains_s'])
            S.op('dve', lambda e: e.memset(ones_b[:], 1.0), writes=['ones_b'])
            S.op('dve', lambda e: e.memset(eps_c[:], EPS), writes=['eps_c'])
            self.psum = Ring(nc, st0, "ps", 6, [128, 512], F32, psum=True)

            if 'prepass' in phases:
                for l in range(nl):
                    for (src, dst, rows) in [(w_in, wInB, D), (ffn_w_up, wUpB, D), (ffn_w_down, wDnB, DFF)] + list(wsrc.values()):
                        for r in range(0, rows, 128):
                            S.dma('pool', dst[l, r:r + 128, :], src[l, r:r + 128, :],
                                  writes=[(dst.tensor.name, l)])
            if 'prologue' in phases:
                self.transpose_in(x, xT, L)
                S.barrier()
            for l in range(nl):
                if 'p1' in phases:
                    self.phase_inproj(l, xT, wInB, zF, zQKV)
                    S.barrier()
                if 'p2' in phases:
                    self.phase_lru(l, zF, yA, lrup, lru_bd)
                    S.barrier()
                if 'p3' in phases:
                    self.phase_attn(l, zQKV, oG, amask)
                    S.barrier()
                if 'p4' in phases:
                    self.phase_s5(l, zF, yC, s5A, s5R, s5B, s5C, ssmd)
                    S.barrier()
                if 'p5' in phases:
                    self.phase_xattn(l, mem, zF, yX)
                    S.barrier()
                if 'p6' in phases:
                    self.phase_merge(l, xT, xmT, zF, yA, yC, yX, oG)
                    S.barrier()
                if 'p7' in phases:
                    self.phase_ffn(l, xmT if 'p6' in phases else xT, xT, wUpB, wDnB, ffnp)
                    S.barrier()
            if 'epilogue' in phases:
                self.transpose_out(xT, out, L)
            S.barrier()
        return nc

    def transpose_in(self, x, xT, L):
        nc, S = self.nc, self.S
        with ExitStack() as st:
            xin = Ring(nc, st, "ti_x", 2, [128, D], F32)
            xo = Ring(nc, st, "ti_o", 2, [128, 8, 128], F32)
            for b in range(L // 128):
                nm, xt = xin.next()
                S.dma('sp', xt[:], x[b * 128:(b + 1) * 128, :], writes=[nm])
                no, ot = xo.next()
                for half in range(2):
                    pn, ps = self.psum.next()
                    for j in range(4):
                        c = half * 4 + j
                        S.op('pe', lambda e: e.transpose(ps[:, j * 128:(j + 1) * 128], xt[:, c * 128:(c + 1) * 128], self.ident_f[:]),
                             reads=[nm, 'ident_f'], writes=[pn], inc=(j == 3))
                    eng = 'act' if half == 0 else 'dve'
                    if eng == 'act':
                        S.op('act', lambda e: e.copy(ot[:, half * 4:(half + 1) * 4, :], ps[:, :].rearrange("p (a b) -> p a b", a=4)),
                             reads=[pn], writes=[no])
                    else:
                        S.op('dve', lambda e: e.tensor_copy(ot[:, half * 4:(half + 1) * 4, :], ps[:, :].rearrange("p (a b) -> p a b", a=4)),
                             reads=[pn], writes=[no])
                S.dma('sp', xT.rearrange("(c p) l -> p c l", p=128)[:, :, b * 128:(b + 1) * 128], ot[:],
                      reads=[no], writes=[('xT', b // 4)])

    def transpose_out(self, xT, out, L):
        nc, S = self.nc, self.S
        with ExitStack() as st:
            xin = Ring(nc, st, "to_x", 2, [128, 8, 128], F32)
            xo = Ring(nc, st, "to_o", 2, [128, D], F32)
            for b in range(L // 128):
                nm, xt = xin.next()
                S.dma('sp', xt[:], xT.rearrange("(c p) l -> p c l", p=128)[:, :, b * 128:(b + 1) * 128],
                      reads=[('xT', b // 4)], writes=[nm])
                no, ot = xo.next()
                for half in range(2):
                    pn, ps = self.psum.next()
                    for j in range(4):
                        c = half * 4 + j
                        S.op('pe', lambda e: e.transpose(ps[:, j * 128:(j + 1) * 128], xt[:, c, :], self.ident_f[:]),
                             reads=[nm, 'ident_f'], writes=[pn], inc=(j == 3))
                    if half == 0:
                        S.op('act', lambda e: e.copy(ot[:, 0:512], ps[:, :]), reads=[pn], writes=[no])
                    else:
                        S.op('dve', lambda e: e.tensor_copy(ot[:, 512:1024], ps[:, :]), reads=[pn], writes=[no])
                S.dma('sp', out[b * 128:(b + 1) * 128, :], ot[:], reads=[no], writes=['out'])

    def rms_stats(self, sqname, sq, rstd_name, rstd, Tn):
        S = self.S
        pn, ps = self.psum.next()
        for c in range(8):
            S.op('pe', lambda e: e.matmul(ps[:, :Tn], lhsT=self.ones_b[:], rhs=sq[:, c, :], start=(c == 0), stop=(c == 7)),
                 reads=[sqname, 'ones_b'], writes=[pn], inc=(c == 7))
        S.op('act', lambda e: e.activation(rstd[:, :Tn], ps[:, :Tn], AF.Sqrt, bias=self.eps_c[:], scale=1.0 / D),
             reads=[pn, 'eps_c'], writes=[rstd_name])
        S.op('dve', lambda e: e.reciprocal(rstd[:, :Tn], rstd[:, :Tn]), reads=[rstd_name], writes=[rstd_name])

    def prenorm(self, l, which, xt_name, xt, h_name, h, sq_name, sq, rstd_name, rstd):
        S = self.S
        for c in range(8):
            S.op('act', lambda e: e.activation(sq[:, c, :], xt[:, c, :], AF.Square), reads=[xt_name], writes=[sq_name])
        self.rms_stats(sq_name, sq, rstd_name, rstd, T)
        for c in range(8):
            S.op('dve', lambda e: e.scalar_tensor_tensor(h[:, c, :], xt[:, c, :], self.gains_s[:, l, which, c:c + 1], rstd[:, :],
                                                         ALU.mult, ALU.mult),
                 reads=[xt_name, rstd_name, 'gains_s'], writes=[h_name])

    def phase_inproj(self, l, xT, wInB, zF, zQKV):
        nc, S, L = self.nc, self.S, self.L
        FM_COLS = list(range(0, 1536, 128)) + list(range(3840, IN_W, 128))
        NT = 2
        with ExitStack() as st:
            xr = Ring(nc, st, "p1_x", 2, [128, 8, T], F32)
            sq = self.sb(st, "p1_sq", [128, 8, T], BF16)
            rstd = self.sb(st, "p1_rstd", [128, T], F32)
            hr = Ring(nc, st, "p1_h", 4, [128, 8, T], BF16)
            wr = Ring(nc, st, "p1_w", 2, [128, 8, 1024], BF16)
            zo = Ring(nc, st, "p1_zo", 2, [128, 8, T], BF16)
            qo = Ring(nc, st, "p1_qo", 2, [128, 4, 2304], BF16)
            wv = wInB[l].rearrange("(k p) n -> p k n", p=128)
            zFv = zF.rearrange("(c p) l -> p c l", p=128)
            ev = 0
            wq = 0
            for tp in range(L // (NT * T)):
                hs = []
                for ti in range(tp * NT, (tp + 1) * NT):
                    t0 = ti * T
                    xn, xt = xr.next()
                    S.dma('sp', xt[:], xT.rearrange("(c p) l -> p c l", p=128)[:, :, t0:t0 + T], reads=[('xT', ti)], writes=[xn])
                    hn, h = hr.next()
                    self.prenorm(l, 0, xn, xt, hn, h, "p1_sq", sq, "p1_rstd", rstd)
                    hs.append((t0, hn, h))
                for blk in range(7):
                    wn, w = wr.next()
                    wq += 1
                    for j in range(8):
                        c0 = FM_COLS[blk * 8 + j]
                        if j == 0 or FM_COLS[blk * 8 + j - 1] + 128 != c0:
                            j2 = j
                            while j2 + 1 < 8 and FM_COLS[blk * 8 + j2 + 1] == FM_COLS[blk * 8 + j2] + 128:
                                j2 += 1
                            S.dma('sp' if wq % 2 == 0 else 'pool', w[:, :, j * 128:(j2 + 1) * 128], wv[:, :, c0:c0 + (j2 - j + 1) * 128],
                                  reads=[('wInB', l)], writes=[wn])
                    for (t0, hn, h) in hs:
                        zn, z = zo.next()
                        for j in range(8):
                            ch = blk * 8 + j
                            pn, ps = self.psum.next()
                            for k in range(8):
                                S.op('pe', lambda e: e.matmul(ps[:, :], lhsT=w[:, k, j * 128:(j + 1) * 128], rhs=h[:, k, :], start=(k == 0), stop=(k == 7)),
                                     reads=[wn, hn], writes=[pn], inc=(k == 7))
                            if 6 <= ch < 12:
                                S.op('act', lambda e: e.activation(z[:, j, :], ps[:, :], AF.Gelu_apprx_tanh), reads=[pn], writes=[zn])
                            elif ch >= 24:
                                S.op('act', lambda e: e.activation(z[:, j, :], ps[:, :], AF.Sigmoid), reads=[pn], writes=[zn])
                            else:
                                S.op('dve', lambda e: e.tensor_copy(z[:, j, :], ps[:, :]), reads=[pn], writes=[zn])
                        S.dma('sp', zFv[:, blk * 8:(blk + 1) * 8, t0:t0 + T], z[:], reads=[zn], writes=['zF'])
                qs = [qo.next() for _ in hs]
                for blk in range(3):
                    c0 = 1536 + blk * 1024
                    ncol = min(1024, 3840 - c0)
                    wn, w = wr.next()
                    wq += 1
                    S.dma('sp' if wq % 2 == 0 else 'pool', w[:, :, :ncol], wv[:, :, c0:c0 + ncol], reads=[('wInB', l)], writes=[wn])
                    for (t0, hn, h), (qn, q) in zip(hs, qs):
                        for tb in range(4):
                            for n0 in range(0, ncol, 512):
                                nn = min(512, ncol - n0)
                                pn, ps = self.psum.next()
                                for k in range(8):
                                    S.op('pe', lambda e: e.matmul(ps[:, :nn], lhsT=h[:, k, tb * 128:(tb + 1) * 128], rhs=w[:, k, n0:n0 + nn], start=(k == 0), stop=(k == 7)),
                                         reads=[wn, hn], writes=[pn], inc=(k == 7))
                                dst = q[:, tb, blk * 1024 + n0: blk * 1024 + n0 + nn]
                                ev += 1
                                if ev % 2:
                                    S.op('dve', lambda e: e.tensor_copy(dst, ps[:, :nn]), reads=[pn], writes=[qn])
                                else:
                                    S.op('act', lambda e: e.copy(dst, ps[:, :nn]), reads=[pn], writes=[qn])
                for (t0, hn, h), (qn, q) in zip(hs, qs):
                    S.dma('sp', zQKV[t0:t0 + T, :].rearrange("(tb p) n -> p tb n", p=128), q[:], reads=[qn], writes=['zQKV'])

    def phase_lru(self, l, zF, yA, lrup, lru_bd):
        nc, S, L = self.nc, self.S, self.L
        with ExitStack() as st:
            lp = self.sb(st, "p2_lp", [128, 8, 6], F32)
            kap = self.sb(st, "p2_kap", [128, 2, 6], F32)
            bdA = self.sb(st, "p2_bdA", [128, 6, 128], BF16)
            bdX = self.sb(st, "p2_bdX", [128, 6, 128], BF16)
            state = self.sb(st, "p2_state", [128, 6], F32)
            xar = Ring(nc, st, "p2_xa", 3, [128, T + 3], BF16)
            ggr = Ring(nc, st, "p2_gg", 3, [128, T], BF16)
            xcr = Ring(nc, st, "p2_xc", 2, [128, T], F32)
            xcbr = Ring(nc, st, "p2_xcb", 2, [128, T], BF16)
            rr = Ring(nc, st, "p2_r", 2, [128, T], F32)
            ir = Ring(nc, st, "p2_i", 2, [128, T], F32)
            ar = Ring(nc, st, "p2_a", 2, [128, T], F32)
            a2r = Ring(nc, st, "p2_a2", 2, [128, T], F32)
            hr = Ring(nc, st, "p2_h", 2, [128, T], F32)
            yr = Ring(nc, st, "p2_y", 3, [128, T], BF16)
            S.dma('sp', lp[:], lrup[:, l, :, :], writes=['p2_lp'])
            S.dma('pool', bdA[:], lru_bd[l, 0], writes=['p2_bdA'])
            S.dma('pool', bdX[:], lru_bd[l, 1], writes=['p2_bdX'])
            S.op('dve', lambda e: e.memset(state[:], 0.0), writes=[('p2_state', c) for c in range(6)])
            S.op('act', lambda e: e.activation(kap[:, 0, :], lp[:, 7, :], AF.Exp, scale=-1.0), reads=['p2_lp'], writes=['p2_kap'])
            S.op('act', lambda e: e.activation(kap[:, 0, :], kap[:, 0, :], AF.Ln, bias=self.one_c[:]), reads=['p2_kap', 'one_c'], writes=['p2_kap'])
            S.op('dve', lambda e: e.tensor_scalar(kap[:, 1, :], kap[:, 0, :], -16.0, None, ALU.mult), reads=['p2_kap'], writes=['p2_kap'])
            S.op('dve', lambda e: e.tensor_scalar(kap[:, 0, :], kap[:, 0, :], -8.0, None, ALU.mult), reads=['p2_kap'], writes=['p2_kap'])
            zFv = zF.rearrange("(c p) l -> p c l", p=128)
            yAv = yA.rearrange("(c p) l -> p c l", p=128)
            for ti in range(L // T):
                t0 = ti * T
                for c in range(6):
                    xn, xa = xar.next()
                    if t0 == 0:
                        S.op('pool', lambda e: e.memset(xa[:, 0:3], 0.0), writes=[xn])
                        S.dma('sp', xa[:, 3:T + 3], zFv[:, c, 0:T], reads=['zF'], writes=[xn])
                    else:
                        S.dma('sp', xa[:, :], zFv[:, c, t0 - 3:t0 + T], reads=['zF'], writes=[xn])
                    gn, gg = ggr.next()
                    S.dma('sp', gg[:, :], zFv[:, 6 + c, t0:t0 + T], reads=['zF'], writes=[gn])
                    xcn, xc = xcr.next()
                    S.op('dve', lambda e: e.tensor_scalar(xc[:, :], xa[:, 0:T], lp[:, 0, c:c + 1], lp[:, 4, c:c + 1], ALU.mult, ALU.add),
                         reads=[xn, 'p2_lp'], writes=[xcn])
                    for j in range(1, 4):
                        S.op('dve', lambda e: e.scalar_tensor_tensor(xc[:, :], xa[:, j:j + T], lp[:, j, c:c + 1], xc[:, :], ALU.mult, ALU.add),
                             reads=[xn, xcn, 'p2_lp'], writes=[xcn])
                    xbn, xcb = xcbr.next()
                    S.op('dve', lambda e: e.tensor_copy(xcb[:, :], xc[:, :]), reads=[xcn], writes=[xbn])
                    prn, pr = self.psum.next()
                    S.op('pe', lambda e: e.matmul(pr[:, :], lhsT=bdA[:, c, :], rhs=xcb[:, :], start=True, stop=True), reads=['p2_bdA', xbn], writes=[prn])
                    pin, pi = self.psum.next()
                    S.op('pe', lambda e: e.matmul(pi[:, :], lhsT=bdX[:, c, :], rhs=xcb[:, :], start=True, stop=True), reads=['p2_bdX', xbn], writes=[pin])
                    rn, r = rr.next()
                    S.op('act', lambda e: e.activation(r[:, :], pr[:, :], AF.Sigmoid, bias=lp[:, 5, c:c + 1]), reads=[prn, 'p2_lp'], writes=[rn])
                    inn, iv = ir.next()
                    S.op('act', lambda e: e.activation(iv[:, :], pi[:, :], AF.Sigmoid, bias=lp[:, 6, c:c + 1]), reads=[pin, 'p2_lp'], writes=[inn])
                    an, a = ar.next()
                    S.op('act', lambda e: e.activation(a[:, :], r[:, :], AF.Exp, scale=kap[:, 0, c:c + 1]), reads=[rn, 'p2_kap'], writes=[an])
                    a2n, a2 = a2r.next()
                    S.op('act', lambda e: e.activation(a2[:, :], r[:, :], AF.Exp, scale=kap[:, 1, c:c + 1]), reads=[rn, 'p2_kap'], writes=[a2n])
                    S.op('dve', lambda e: e.tensor_scalar(a2[:, :], a2[:, :], -1.0, 1.0, ALU.mult, ALU.add), reads=[a2n], writes=[a2n])
                    S.op('act', lambda e: e.activation(a2[:, :], a2[:, :], AF.Sqrt), reads=[a2n], writes=[a2n])
                    S.op('dve', lambda e: e.tensor_tensor(iv[:, :], iv[:, :], a2[:, :], ALU.mult), reads=[inn, a2n], writes=[inn])
                    S.op('dve', lambda e: e.tensor_tensor(iv[:, :], iv[:, :], xc[:, :], ALU.mult), reads=[inn, xcn], writes=[inn])
                    hn, h = hr.next()
                    S.op('dve', lambda e: e.tensor_tensor_scan(h[:, :], a[:, :], iv[:, :], state[:, c:c + 1], ALU.mult, ALU.add),
                         reads=[an, inn, ('p2_state', c)], writes=[hn])
                    S.op('act', lambda e: e.copy(state[:, c:c + 1], h[:, T - 1:T]), reads=[hn], writes=[('p2_state', c)])
                    yn, y = yr.next()
                    S.op('dve', lambda e: e.tensor_tensor(y[:, :], h[:, :], gg[:, :], ALU.mult), reads=[hn, gn], writes=[yn])
                    S.dma('sp', yAv[:, c, t0:t0 + T], y[:, :], reads=[yn], writes=['yA'])


    def phase_attn(self, l, zQKV, oG, amask):
        nc, S, L = self.nc, self.S, self.L
        with ExitStack() as st:
            mk = self.sb(st, "p3_mk", [128, 2, 12, 256], F32)
            idb = self.sb(st, "p3_idb", [128, 128], BF16)
            S.dma('sp', mk[:], amask, writes=['p3_mk'])
            S.op('dve', lambda e: e.tensor_copy(idb[:], self.ident_f[:]), reads=['ident_f'], writes=['p3_idb'])
            qr = Ring(nc, st, "p3_q", 3, [128, 256], BF16)
            kr = Ring(nc, st, "p3_k", 3, [128, 256], BF16)
            vr = Ring(nc, st, "p3_v", 4, [128, 4, 128], BF16)
            qkr = Ring(nc, st, "p3_qk", 3, [128, 8, 128], BF16)
            scr = Ring(nc, st, "p3_sc", 3, [128, 512], F32)
            ptr = Ring(nc, st, "p3_pt", 4, [128, 2, 2, 128], BF16)
            osr = Ring(nc, st, "p3_os", 3, [128, 260], F32)
            for (vn, v) in vr.bufs:
                S.op('pool', lambda e: e.memset(v[:], 1.0), writes=[vn])
            for g, d in enumerate(DILS):
                if P3STOP == -1:
                    break
                nb = L // (128 * d)
                zv = zQKV.rearrange("(n d) c -> d n c", d=d)
                ov = oG[g].rearrange("(n d) c -> d n c", d=d)
                for r in range(d):
                    prev = None
                    for blk in range(nb):
                        rows = slice(blk * 128, (blk + 1) * 128)
                        qn, q = qr.next()
                        kn, k_ = kr.next()
                        vn, v = vr.next()
                        S.dma('sp', q[:, :], zv[r, rows, 256 * g:256 * g + 256], reads=['zQKV'], writes=[qn])
                        S.dma('sp', k_[:, :], zv[r, rows, 768 + 256 * g:768 + 256 * g + 256], reads=['zQKV'], writes=[kn])
                        S.dma('sp', v[:, :, 0:64], zv[r, rows, 1536 + 256 * g:1536 + 256 * g + 256].rearrange("p (h e) -> p h e", e=64),
                              reads=['zQKV'], writes=[vn])
                        if P3STOP == -2:
                            continue
                        ptn, pT = self.psum.next()
                        for j in range(4):
                            S.op('pe', lambda e: e.matmul(pT[0:64, j * 128:(j + 1) * 128], lhsT=q[:, j * 64:(j + 1) * 64], rhs=idb[:], start=True, stop=True),
                                 reads=[qn, 'p3_idb'], writes=[ptn], inc=(j == 3))
                        ptn2, pT2 = self.psum.next()
                        for j in range(4):
                            S.op('pe', lambda e: e.matmul(pT2[0:64, j * 128:(j + 1) * 128], lhsT=k_[:, j * 64:(j + 1) * 64], rhs=idb[:], start=True, stop=True),
                                 reads=[kn, 'p3_idb'], writes=[ptn2], inc=(j == 3))
                        qkn, qk = qkr.next()
                        S.op('dve', lambda e: e.tensor_copy(qk[0:64, 0:4, :], pT[0:64, 0:512].rearrange("p (a b) -> p a b", a=4)), reads=[ptn], writes=[qkn])
                        S.op('dve', lambda e: e.tensor_copy(qk[0:64, 4:8, :], pT2[0:64, 0:512].rearrange("p (a b) -> p a b", a=4)), reads=[ptn2], writes=[qkn])
                        qTn, qT = qkn, qk[:, 0:4, :]
                        kTn, kT = qkn, qk[:, 4:8, :]
                        first = prev is None
                        if P3STOP == 1:
                            continue
                        if first:
                            kTp_n, kTp, vp_n, vp = kTn, kT, vn, v
                        else:
                            kTp_n, kTp, vp_n, vp = prev
                        pts = []
                        for pair in range(2):
                            pn, ps = self.psum.next()
                            for h2 in range(2):
                                hx = 2 * pair + h2
                                col = h2 * 256
                                if not first:
                                    S.op('pe', lambda e: e.matmul(ps[:, col:col + 128], lhsT=kTp[0:64, hx, :], rhs=qT[0:64, hx, :], start=True, stop=True),
                                         reads=[kTp_n, qTn], writes=[pn], inc=False)
                                S.op('pe', lambda e: e.matmul(ps[:, col + 128:col + 256], lhsT=kT[0:64, hx, :], rhs=qT[0:64, hx, :], start=True, stop=True),
                                     reads=[kTn, qTn], writes=[pn], inc=(h2 == 1))
                            scn, sc = scr.next()
                            hh0 = 4 * g + 2 * pair
                            mview = mk[:, 1 if first else 0, hh0:hh0 + 2, :].rearrange("p a b -> p (a b)")
                            if first:
                                S.op('pool', lambda e: e.memset(sc[:, :], 0.0), writes=[scn])
                                for h2 in range(2):
                                    col = h2 * 256
                                    S.op('dve', lambda e: e.scalar_tensor_tensor(sc[:, col + 128:col + 256], ps[:, col + 128:col + 256], 0.125, mview[:, col + 128:col + 256], ALU.mult, ALU.add),
                                         reads=[pn, 'p3_mk'], writes=[scn])
                                    S.op('dve', lambda e: e.tensor_copy(sc[:, col:col + 128], mview[:, col:col + 128]), reads=['p3_mk'], writes=[scn])
                            else:
                                S.op('dve', lambda e: e.scalar_tensor_tensor(sc[:, :], ps[:, :], 0.125, mview, ALU.mult, ALU.add),
                                     reads=[pn, 'p3_mk'], writes=[scn])
                            pn2, pt = ptr.next()
                            S.op('act', lambda e: e.activation(pt[:, :, :, :].rearrange("p a b c -> p (a b c)"), sc[:, :], AF.Exp), reads=[scn], writes=[pn2])
                            pts.append((pn2, pt))
                        if P3STOP == 2:
                            prev = (kTn, kT, vn, v)
                            continue
                        pn, ps = self.psum.next()
                        for hh in range(4):
                            pn2, pt = pts[hh // 2]
                            S.op('pe', lambda e: e.matmul(ps[:, hh * 128:hh * 128 + 65], lhsT=pt[:, hh % 2, 0, :], rhs=vp[:, hh, 0:65], start=True, stop=False),
                                 reads=[pn2, vp_n], writes=[pn], inc=False)
                            S.op('pe', lambda e: e.matmul(ps[:, hh * 128:hh * 128 + 65], lhsT=pt[:, hh % 2, 1, :], rhs=v[:, hh, 0:65], start=False, stop=True),
                                 reads=[pn2, vn], writes=[pn], inc=(hh == 3))
                        on, o = osr.next()
                        S.op('act', lambda e: e.copy(o[:, :].rearrange("p (h e) -> p h e", e=65), ps[:, :].rearrange("p (h e) -> p h e", e=128)[:, :, 0:65]), reads=[pn], writes=[on])
                        S.dma('sp', ov[r, rows, :], o[:, :], reads=[on], writes=['oG'])
                        prev = (kTn, kT, vn, v)

    def sincos(self, st, th_name, th, n, out_s, out_c, key):
        S = self.S
        TWO_PI = 2.0 * math.pi
        ti = self.sb(st, "sc_i", [128, n], mybir.dt.int32)
        tf = self.sb(st, "sc_f", [128, n], F32)
        ph = self.sb(st, "sc_p", [128, n], F32)
        mm = self.sb(st, "sc_m", [128, n], F32)
        for (shift, outt) in ((0.0, out_s), (math.pi / 2, out_c)):
            S.op('dve', lambda e: e.tensor_scalar(ph[:, :], th, 1.0 / TWO_PI, shift / TWO_PI, ALU.mult, ALU.add), reads=[th_name], writes=[key + 'ph'])
            S.op('dve', lambda e: e.tensor_copy(ti[:, :], ph[:, :]), reads=[key + 'ph'], writes=[key + 'ti'])
            S.op('dve', lambda e: e.tensor_copy(tf[:, :], ti[:, :]), reads=[key + 'ti'], writes=[key + 'tf'])
            S.op('dve', lambda e: e.tensor_tensor(ph[:, :], ph[:, :], tf[:, :], ALU.subtract), reads=[key + 'ph', key + 'tf'], writes=[key + 'ph'])
            S.op('dve', lambda e: e.tensor_scalar(mm[:, :], ph[:, :], 0.5, None, ALU.is_gt), reads=[key + 'ph'], writes=[key + 'mm'])
            S.op('dve', lambda e: e.tensor_tensor(ph[:, :], ph[:, :], mm[:, :], ALU.subtract), reads=[key + 'ph', key + 'mm'], writes=[key + 'ph'])
            S.op('dve', lambda e: e.tensor_scalar(mm[:, :], ph[:, :], -0.5, None, ALU.is_lt), reads=[key + 'ph'], writes=[key + 'mm'])
            S.op('dve', lambda e: e.tensor_tensor(ph[:, :], ph[:, :], mm[:, :], ALU.add), reads=[key + 'ph', key + 'mm'], writes=[key + 'ph'])
            S.op('dve', lambda e: e.tensor_scalar(ph[:, :], ph[:, :], -0.4999, 0.4999, ALU.max, ALU.min), reads=[key + 'ph'], writes=[key + 'ph'])
            S.op('act', lambda e: e.activation(outt, ph[:, :], AF.Sin, scale=TWO_PI), reads=[key + 'ph'], writes=[key + 'out'])

    def phase_s5(self, l, zF, yC, s5A, s5R, s5B, s5C, ssmd):
        nc, S, L = self.nc, self.S, self.L
        with ExitStack() as st:
            rho = self.sb(st, "p4_rho", [128, 24], F32)
            BbR = self.sb(st, "p4_BbR", [128, 24, 128], BF16)
            BbI = self.sb(st, "p4_BbI", [128, 24, 128], BF16)
            CtR = self.sb(st, "p4_CtR", [128, 24, 128], BF16)
            CtI = self.sb(st, "p4_CtI", [128, 24, 128], BF16)
            dsk = self.sb(st, "p4_d", [128, 6], F32)
            xst = self.sb(st, "p4_xst", [128, 2, 24], F32)
            S.dma('sp', dsk[:], ssmd[:, l, :], writes=['p4_d'])
            S.op('dve', lambda e: e.memset(xst[:], 0.0), writes=[('p4_xst', p) for p in range(24)])
            S.dma('pool', CtR[:], s5C[l, 0], writes=['p4_CtR'])
            with ExitStack() as s2:
                N = 3072
                par = self.sb(s2, "p4s_par", [128, 3, N], F32)
                S.dma('sp', par[:], s5R[:, l, :, :], writes=['p4s_par'])
                ar, ai, ld = par[:, 0, :], par[:, 1, :], par[:, 2, :]
                dt = self.sb(s2, "p4s_dt", [128, N], F32)
                mag = self.sb(st if False else s2, "p4s_mag", [128, N], F32)
                th = self.sb(s2, "p4s_th", [128, N], F32)
                sn = self.sb(s2, "p4s_sn", [128, N], F32)
                cs = self.sb(s2, "p4s_cs", [128, N], F32)
                S.op('act', lambda e: e.activation(dt[:, :], ld, AF.Exp), reads=['p4s_par'], writes=['p4s_dt'])
                S.op('dve', lambda e: e.tensor_tensor(mag[:, :], ar, dt[:, :], ALU.mult), reads=['p4s_par', 'p4s_dt'], writes=['p4s_mag'])
                S.op('act', lambda e: e.activation(mag[:, :], mag[:, :], AF.Exp), reads=['p4s_mag'], writes=['p4s_mag'])
                S.op('dve', lambda e: e.tensor_tensor(th[:, :], ai, dt[:, :], ALU.mult), reads=['p4s_par', 'p4s_dt'], writes=['p4s_th'])
                for hf in range(2):
                    with ExitStack() as s3:
                        sl = slice(hf * 1536, (hf + 1) * 1536)
                        self.sincos(s3, 'p4s_th', th[:, sl], 1536, sn[:, sl], cs[:, sl], 'scB')
                        S.barrier()
                S.op('dve', lambda e: e.tensor_tensor(cs[:, :], cs[:, :], mag[:, :], ALU.mult), reads=['scBout', 'p4s_mag'], writes=['p4s_cs'])
                S.op('dve', lambda e: e.tensor_scalar(cs[:, :], cs[:, :], -1.0, None, ALU.add), reads=['p4s_cs'], writes=['p4s_cs'])
                S.op('dve', lambda e: e.tensor_tensor(sn[:, :], sn[:, :], mag[:, :], ALU.mult), reads=['scBout', 'p4s_mag'], writes=['p4s_sn'])
                S.op('dve', lambda e: e.tensor_tensor(dt[:, :], ar, ar, ALU.mult), reads=['p4s_par'], writes=['p4s_dt'])
                S.op('dve', lambda e: e.tensor_tensor(th[:, :], ai, ai, ALU.mult), reads=['p4s_par'], writes=['p4s_th'])
                S.op('dve', lambda e: e.tensor_tensor(dt[:, :], dt[:, :], th[:, :], ALU.add), reads=['p4s_dt', 'p4s_th'], writes=['p4s_dt'])
                S.op('dve', lambda e: e.reciprocal(dt[:, :], dt[:, :]), reads=['p4s_dt'], writes=['p4s_dt'])
                t1 = self.sb(s2, "p4s_t1", [128, N], F32)
                S.op('dve', lambda e: e.tensor_tensor(mag[:, :], cs[:, :], ar, ALU.mult), reads=['p4s_cs', 'p4s_par'], writes=['p4s_mag'])
                S.op('dve', lambda e: e.tensor_tensor(t1[:, :], sn[:, :], ai, ALU.mult), reads=['p4s_sn', 'p4s_par'], writes=['p4s_t1'])
                S.op('dve', lambda e: e.tensor_tensor(mag[:, :], mag[:, :], t1[:, :], ALU.add), reads=['p4s_mag', 'p4s_t1'], writes=['p4s_mag'])
                S.op('dve', lambda e: e.tensor_tensor(mag[:, :], mag[:, :], dt[:, :], ALU.mult), reads=['p4s_mag', 'p4s_dt'], writes=['p4s_mag'])
                S.op('dve', lambda e: e.tensor_tensor(th[:, :], sn[:, :], ar, ALU.mult), reads=['p4s_sn', 'p4s_par'], writes=['p4s_th'])
                S.op('dve', lambda e: e.tensor_tensor(t1[:, :], cs[:, :], ai, ALU.mult), reads=['p4s_cs', 'p4s_par'], writes=['p4s_t1'])
                S.op('dve', lambda e: e.tensor_tensor(th[:, :], th[:, :], t1[:, :], ALU.subtract), reads=['p4s_th', 'p4s_t1'], writes=['p4s_th'])
                S.op('dve', lambda e: e.tensor_tensor(th[:, :], th[:, :], dt[:, :], ALU.mult), reads=['p4s_th', 'p4s_dt'], writes=['p4s_th'])
                zr, zi = mag, th
                bre = self.sb(s2, "p4s_bre", [128, N], F32)
                bim = self.sb(s2, "p4s_bim", [128, N], F32)
                S.dma('sp', bre[:, :], s5B[l, 0].rearrange("p a b -> p (a b)"), writes=['p4s_bre'])
                S.dma('sp', bim[:, :], s5B[l, 1].rearrange("p a b -> p (a b)"), writes=['p4s_bim'])
                S.op('dve', lambda e: e.tensor_tensor(t1[:, :], zr[:, :], bre[:, :], ALU.mult), reads=['p4s_mag', 'p4s_bre'], writes=['p4s_t1'])
                S.op('dve', lambda e: e.tensor_tensor(cs[:, :], zi[:, :], bim[:, :], ALU.mult), reads=['p4s_th', 'p4s_bim'], writes=['p4s_cs'])
                S.op('dve', lambda e: e.tensor_tensor(BbR[:, :, :].rearrange("p a b -> p (a b)"), t1[:, :], cs[:, :], ALU.subtract), reads=['p4s_t1', 'p4s_cs'], writes=['p4_BbR'])
                S.op('dve', lambda e: e.tensor_tensor(t1[:, :], zr[:, :], bim[:, :], ALU.mult), reads=['p4s_mag', 'p4s_bim'], writes=['p4s_t1'])
                S.op('dve', lambda e: e.tensor_tensor(cs[:, :], zi[:, :], bre[:, :], ALU.mult), reads=['p4s_th', 'p4s_bre'], writes=['p4s_cs'])
                S.op('dve', lambda e: e.tensor_tensor(BbI[:, :, :].rearrange("p a b -> p (a b)"), t1[:, :], cs[:, :], ALU.add), reads=['p4s_t1', 'p4s_cs'], writes=['p4_BbI'])
                S.dma('sp', bre[:, :], s5C[l, 1].rearrange("p a b -> p (a b)"), reads=['p4s_bre'], writes=['p4s_bre'])
                S.op('act', lambda e: e.mul(CtI[:, :, :].rearrange("p a b -> p (a b)"), bre[:, :], -1.0), reads=['p4s_bre'], writes=['p4_CtI'])
                S.barrier()
            cosT = self.sb(st, "p4_cos", [128, 24, T], F32)
            sinT = self.sb(st, "p4_sin", [128, 24, T], F32)
            with ExitStack() as s2:
                pa = self.sb(s2, "p4a_par", [128, 3, 24], F32)
                S.dma('sp', pa[:], s5A[:, l, :, :], writes=['p4a_par'])
                dt = self.sb(s2, "p4a_dt", [128, 24], F32)
                th = self.sb(s2, "p4a_th", [128, 24], F32)
                sn = self.sb(s2, "p4a_sn", [128, 24], F32)
                cs = self.sb(s2, "p4a_cs", [128, 24], F32)
                S.op('act', lambda e: e.activation(dt[:, :], pa[:, 2, :], AF.Exp), reads=['p4a_par'], writes=['p4a_dt'])
                S.op('dve', lambda e: e.tensor_tensor(rho[:, :], pa[:, 0, :], dt[:, :], ALU.mult), reads=['p4a_par', 'p4a_dt'], writes=['p4_rho'])
                S.op('act', lambda e: e.activation(rho[:, :], rho[:, :], AF.Exp), reads=['p4_rho'], writes=['p4_rho'])
                S.op('dve', lambda e: e.tensor_tensor(th[:, :], pa[:, 1, :], dt[:, :], ALU.mult), reads=['p4a_par', 'p4a_dt'], writes=['p4a_th'])
                self.sincos(s2, 'p4a_th', th[:, :], 24, sn[:, :], cs[:, :], 'scA')
                tmp = self.sb(s2, "p4a_tmp", [128, T // 2], F32)
                for p in range(24):
                    S.op('act', lambda e: e.copy(cosT[:, p, 0:1], cs[:, p:p + 1]), reads=['scAout'], writes=[('p4_cos', p)])
                    S.op('act', lambda e: e.copy(sinT[:, p, 0:1], sn[:, p:p + 1]), reads=['scAout'], writes=[('p4_sin', p)])
                    w = 1
                    while w < T:
                        cw, sw = cosT[:, p, w - 1:w], sinT[:, p, w - 1:w]
                        S.op('dve', lambda e: e.tensor_scalar(tmp[:, 0:w], sinT[:, p, 0:w], sw, None, ALU.mult), reads=[('p4_sin', p)], writes=['p4a_tmp'])
                        S.op('dve', lambda e: e.scalar_tensor_tensor(cosT[:, p, w:2 * w], cosT[:, p, 0:w], cw, tmp[:, 0:w], ALU.mult, ALU.subtract),
                             reads=[('p4_cos', p), 'p4a_tmp'], writes=[('p4_cos', p)])
                        S.op('dve', lambda e: e.tensor_scalar(tmp[:, 0:w], cosT[:, p, 0:w], sw, None, ALU.mult), reads=[('p4_cos', p)], writes=['p4a_tmp'])
                        S.op('dve', lambda e: e.scalar_tensor_tensor(sinT[:, p, w:2 * w], sinT[:, p, 0:w], cw, tmp[:, 0:w], ALU.mult, ALU.add),
                             reads=[('p4_sin', p), ('p4_cos', p), 'p4a_tmp'], writes=[('p4_sin', p)])
                        w *= 2
                S.barrier()
            wg = self.sb(st, "p4_wg", [128, 6, 1536], BF16)
            S.dma('sp', wg[:], self.wB["ssm_glu"][l].rearrange("(k p) n -> p k n", p=128), reads=[('ssm_gluB', l)], writes=['p4_wg'])
            psy = Ring(nc, st, "p4_psy", 2, [128, 512], F32, psum=True)
            usr = Ring(nc, st, "p4_us", 1, [128, 6, T], BF16)
            tr = Ring(nc, st, "p4_t", 16, [128, T], F32)
            stmp = self.sb(st, "p4_stmp", [128, 24, 2], F32)
            xr = Ring(nc, st, "p4_x", 4, [128, T], F32)
            xbr = Ring(nc, st, "p4_xb", 4, [128, T], BF16)
            yfr = Ring(nc, st, "p4_yf", 2, [128, T], F32)
            gy = self.sb(st, "p4_gy", [128, 6, T], BF16)
            sgr = Ring(nc, st, "p4_sg", 2, [128, T], F32)
            ycr = Ring(nc, st, "p4_yc", 2, [128, T], BF16)
            zFv = zF.rearrange("(c p) l -> p c l", p=128)
            yCv = yC.rearrange("(c p) l -> p c l", p=128)
            for ti in range(L // T):
                t0 = ti * T
                un, us = usr.next()
                S.dma('sp', us[:], zFv[:, 12:18, t0:t0 + T], reads=['zF'], writes=[un])
                PP = {}

                def stageA0(p):
                    ch = p // 4
                    prn, pr = self.psum.next()
                    S.op('pe', lambda e: e.matmul(pr[:, :], lhsT=BbR[:, p, :], rhs=us[:, ch, :], start=True, stop=True), reads=['p4_BbR', un], writes=[prn])
                    pin, pi = self.psum.next()
                    S.op('pe', lambda e: e.matmul(pi[:, :], lhsT=BbI[:, p, :], rhs=us[:, ch, :], start=True, stop=True), reads=['p4_BbI', un], writes=[pin])
                    PP[p] = dict(pr=(prn, pr), pi=(pin, pi))

                def stageA(p):
                    c_, s_ = cosT[:, p, :], sinT[:, p, :]
                    prn, pr = PP[p]['pr']; pin, pi = PP[p]['pi']
                    t1n, t1 = tr.next(); t2n, t2 = tr.next(); t3n, t3 = tr.next(); t4n, t4 = tr.next()
                    S.op('dve', lambda e: e.tensor_tensor(t1[:, :], pr[:, :], c_, ALU.mult), reads=[prn, ('p4_cos', p)], writes=[t1n])
                    S.op('dve', lambda e: e.tensor_tensor(t2[:, :], pi[:, :], s_, ALU.mult), reads=[pin, ('p4_sin', p)], writes=[t2n])
                    S.op('dve', lambda e: e.tensor_tensor(t3[:, :], pi[:, :], c_, ALU.mult), reads=[pin, ('p4_cos', p)], writes=[t3n])
                    S.op('dve', lambda e: e.tensor_tensor(t4[:, :], pr[:, :], s_, ALU.mult), reads=[prn, ('p4_sin', p)], writes=[t4n])
                    S.op('dve', lambda e: e.tensor_tensor(t1[:, :], t1[:, :], t2[:, :], ALU.add), reads=[t1n, t2n], writes=[t1n])
                    S.op('dve', lambda e: e.tensor_tensor(t3[:, :], t3[:, :], t4[:, :], ALU.subtract), reads=[t3n, t4n], writes=[t3n])
                    PP[p].update(t1=(t1n, t1), t3=(t3n, t3))

                def stageC(p):
                    c_, s_ = cosT[:, p, :], sinT[:, p, :]
                    t1n, t1 = PP[p]['t1']; t3n, t3 = PP[p]['t3']
                    rb = rho[:, p:p + 1].to_broadcast([128, T])
                    xrn, xre = xr.next(); xin, xim = xr.next()
                    S.op('dve', lambda e: e.tensor_tensor_scan(xre[:, :], rb, t1[:, :], xst[:, 0, p:p + 1], ALU.mult, ALU.add),
                         reads=['p4_rho', t1n, ('p4_xst', p)], writes=[xrn])
                    S.op('dve', lambda e: e.tensor_tensor_scan(xim[:, :], rb, t3[:, :], xst[:, 1, p:p + 1], ALU.mult, ALU.add),
                         reads=['p4_rho', t3n, ('p4_xst', p)], writes=[xin])
                    cl, sl = cosT[:, p, T - 1:T], sinT[:, p, T - 1:T]
                    S.op('act', lambda e: e.activation(stmp[:, p, 0:1], xim[:, T - 1:T], AF.Copy, scale=sl), reads=[xin, ('p4_sin', p)], writes=[('p4_stmp', p)])
                    S.op('act', lambda e: e.activation(stmp[:, p, 1:2], xim[:, T - 1:T], AF.Copy, scale=cl), reads=[xin, ('p4_cos', p)], writes=[('p4_stmp', p)])
                    u1n, u1 = tr.next(); u2n, u2 = tr.next(); u3n, u3 = tr.next(); u4n, u4 = tr.next()
                    S.op('dve', lambda e: e.tensor_tensor(u1[:, :], xre[:, :], c_, ALU.mult), reads=[xrn, ('p4_cos', p)], writes=[u1n])
                    S.op('dve', lambda e: e.tensor_tensor(u2[:, :], xim[:, :], s_, ALU.mult), reads=[xin, ('p4_sin', p)], writes=[u2n])
                    S.op('dve', lambda e: e.tensor_tensor(u3[:, :], xre[:, :], s_, ALU.mult), reads=[xrn, ('p4_sin', p)], writes=[u3n])
                    S.op('dve', lambda e: e.tensor_tensor(u4[:, :], xim[:, :], c_, ALU.mult), reads=[xin, ('p4_cos', p)], writes=[u4n])
                    S.op('dve', lambda e: e.scalar_tensor_tensor(xst[:, 0, p:p + 1], xre[:, T - 1:T], cl, stmp[:, p, 0:1], ALU.mult, ALU.subtract),
                         reads=[xrn, ('p4_stmp', p), ('p4_cos', p)], writes=[('p4_xst', p)])
                    S.op('dve', lambda e: e.scalar_tensor_tensor(xst[:, 1, p:p + 1], xre[:, T - 1:T], sl, stmp[:, p, 1:2], ALU.mult, ALU.add),
                         reads=[xrn, ('p4_stmp', p), ('p4_sin', p)], writes=[('p4_xst', p)])
                    PP[p].update(u1=(u1n, u1), u2=(u2n, u2), u3=(u3n, u3), u4=(u4n, u4))

                def stageE(p):
                    u1n, u1 = PP[p]['u1']; u2n, u2 = PP[p]['u2']; u3n, u3 = PP[p]['u3']; u4n, u4 = PP[p]['u4']
                    xbrn, xbre = xbr.next(); xbin, xbim = xbr.next()
                    S.op('dve', lambda e: e.tensor_tensor(xbre[:, :], u1[:, :], u2[:, :], ALU.subtract), reads=[u1n, u2n], writes=[xbrn])
                    S.op('dve', lambda e: e.tensor_tensor(xbim[:, :], u3[:, :], u4[:, :], ALU.add), reads=[u3n, u4n], writes=[xbin])
                    if p % 4 == 0:
                        PP['py'] = psy.next()
                    pyn, py = PP['py']
                    S.op('pe', lambda e: e.matmul(py[:, :], lhsT=CtR[:, p, :], rhs=xbre[:, :], start=(p % 4 == 0), stop=False), reads=['p4_CtR', xbrn], writes=[pyn], inc=False)
                    S.op('pe', lambda e: e.matmul(py[:, :], lhsT=CtI[:, p, :], rhs=xbim[:, :], start=False, stop=(p % 4 == 3)), reads=['p4_CtI', xbin], writes=[pyn])
                    if p % 4 == 3:
                        oc = p // 4
                        yfn, yf = yfr.next()
                        S.op('dve', lambda e: e.scalar_tensor_tensor(yf[:, :], us[:, oc, :], dsk[:, oc:oc + 1], py[:, :], ALU.mult, ALU.add),
                             reads=[un, 'p4_d', pyn], writes=[yfn])
                        S.op('act', lambda e: e.activation(gy[:, oc, :], yf[:, :], AF.Gelu_apprx_tanh), reads=[yfn], writes=[('p4_gy', oc)])
                    del PP[p]

                stageA0(0)
                for it in range(24 + 2):
                    if it + 1 < 24:
                        stageA0(it + 1)
                    lists = []
                    for (fn_, arg, ok) in ((stageA, it, it < 24), (stageC, it - 1, 1 <= it <= 24), (stageE, it - 2, it >= 2)):
                        if ok:
                            S.defer = []
                            fn_(arg)
                            lists.append(S.defer)
                            S.defer = None
                    while any(lists):
                        for lst in lists:
                            if lst:
                                S.op(*lst.pop(0))
                for j in range(6):
                    pan, pa_ = self.psum.next()
                    for k in range(6):
                        S.op('pe', lambda e: e.matmul(pa_[:, :], lhsT=wg[:, k, j * 128:(j + 1) * 128], rhs=gy[:, k, :], start=(k == 0), stop=(k == 5)),
                             reads=['p4_wg', ('p4_gy', k)], writes=[pan], inc=(k == 5))
                    pbn, pb_ = self.psum.next()
                    for k in range(6):
                        S.op('pe', lambda e: e.matmul(pb_[:, :], lhsT=wg[:, k, 768 + j * 128:768 + (j + 1) * 128], rhs=gy[:, k, :], start=(k == 0), stop=(k == 5)),
                             reads=['p4_wg', ('p4_gy', k)], writes=[pbn], inc=(k == 5))
                    sgn, sg = sgr.next()
                    S.op('act', lambda e: e.activation(sg[:, :], pb_[:, :], AF.Sigmoid), reads=[pbn], writes=[sgn])
                    ycn, yc = ycr.next()
                    S.op('dve', lambda e: e.tensor_tensor(yc[:, :], pa_[:, :], sg[:, :], ALU.mult), reads=[pan, sgn], writes=[ycn])
                    S.dma('sp', yCv[:, j, t0:t0 + T], yc[:, :], reads=[ycn], writes=['yC'])

    def phase_xattn(self, l, mem, zF, yX):
        nc, S, L = self.nc, self.S, self.L
        with ExitStack() as st:
            kT = self.sb(st, "p5_kT", [128, 4, 2, 256], BF16)
            vm = self.sb(st, "p5_vm", [128, 2, 768], BF16)
            with ExitStack() as st2:
                memt = self.sb(st2, "p5_mem", [128, 2, D], F32)
                memT = self.sb(st2, "p5_memT", [128, 8, 256], F32)
                sq = self.sb(st2, "p5_sq", [128, 8, 256], BF16)
                rstd = self.sb(st2, "p5_rstd", [128, 256], F32)
                mn = self.sb(st2, "p5_mn", [128, 8, 256], BF16)
                wkv = self.sb(st2, "p5_wkv", [128, 8, 1536], BF16)
                S.dma('sp', memt[:], mem.rearrange("(b p) d -> p b d", p=128), writes=['p5_mem'])
                S.dma('sp', wkv[:], self.wB["mem_wkv"][l].rearrange("(k p) n -> p k n", p=128), reads=[('mem_wkvB', l)], writes=['p5_wkv'])
                for b in range(2):
                    for half in range(2):
                        pn, ps = self.psum.next()
                        for j in range(4):
                            c = half * 4 + j
                            S.op('pe', lambda e: e.transpose(ps[:, j * 128:(j + 1) * 128], memt[:, b, c * 128:(c + 1) * 128], self.ident_f[:]),
                                 reads=['p5_mem', 'ident_f'], writes=[pn], inc=(j == 3))
                        S.op('dve', lambda e: e.tensor_copy(memT[:, half * 4:(half + 1) * 4, b * 128:(b + 1) * 128], ps[:, :].rearrange("p (a b) -> p a b", a=4)),
                             reads=[pn], writes=['p5_memT'])
                for c in range(8):
                    S.op('act', lambda e: e.activation(sq[:, c, :], memT[:, c, :], AF.Square), reads=['p5_memT'], writes=['p5_sq'])
                self.rms_stats('p5_sq', sq, 'p5_rstd', rstd, 256)
                for c in range(8):
                    S.op('dve', lambda e: e.scalar_tensor_tensor(mn[:, c, :], memT[:, c, :], self.gains_s[:, l, 2, c:c + 1], rstd[:, :], ALU.mult, ALU.mult),
                         reads=['p5_memT', 'p5_rstd', 'gains_s'], writes=['p5_mn'])
                for h in range(4):
                    for part, (off, M) in enumerate(((0, 128), (128, 64))):
                        col = 192 * h + off
                        pn, ps = self.psum.next()
                        for k in range(8):
                            S.op('pe', lambda e: e.matmul(ps[:M, :256], lhsT=wkv[:, k, col:col + M], rhs=mn[:, k, :], start=(k == 0), stop=(k == 7)),
                                 reads=['p5_wkv', 'p5_mn'], writes=[pn], inc=(k == 7))
                        S.op('dve', lambda e: e.tensor_copy(kT[:M, h, part, :], ps[:M, :256]), reads=[pn], writes=['p5_kT'])
                for mc in range(2):
                    for (n0, nn) in ((0, 512), (512, 256)):
                        pn, ps = self.psum.next()
                        for k in range(8):
                            S.op('pe', lambda e: e.matmul(ps[:, :nn], lhsT=mn[:, k, mc * 128:(mc + 1) * 128], rhs=wkv[:, k, 768 + n0:768 + n0 + nn], start=(k == 0), stop=(k == 7)),
                                 reads=['p5_wkv', 'p5_mn'], writes=[pn], inc=(k == 7))
                        S.op('dve', lambda e: e.tensor_copy(vm[:, mc, n0:n0 + nn], ps[:, :nn]), reads=[pn], writes=['p5_vm'])
                S.barrier()
            xq0r = Ring(nc, st, "p5_xq0", 2, [128, T], BF16)
            xq1r = Ring(nc, st, "p5_xq1", 2, [128, T], BF16)
            ptr = Ring(nc, st, "p5_pt", 4, [128, T], BF16)
            rcr = Ring(nc, st, "p5_rc", 2, [128, T], F32)
            y0r = Ring(nc, st, "p5_y0", 2, [128, T], BF16)
            y1r = Ring(nc, st, "p5_y1", 2, [128, T], BF16)
            XQ0 = 18 * 128
            sc = 192.0 ** -0.5
            for ti in range(L // T):
                t0 = ti * T
                for h in range(4):
                    q0n, q0 = xq0r.next()
                    q1n, q1 = xq1r.next()
                    r0 = XQ0 + 192 * h
                    S.dma('sp', q0[:, :], zF[r0:r0 + 128, t0:t0 + T], reads=['zF'], writes=[q0n])
                    S.dma('sp', q1[0:64, :], zF[r0 + 128:r0 + 192, t0:t0 + T], reads=['zF'], writes=[q1n])
                    pts = []
                    for mc in range(2):
                        pn, ps = self.psum.next()
                        S.op('pe', lambda e: e.matmul(ps[:, :], lhsT=kT[:, h, 0, mc * 128:(mc + 1) * 128], rhs=q0[:, :], start=True, stop=False),
                             reads=['p5_kT', q0n], writes=[pn], inc=False)
                        S.op('pe', lambda e: e.matmul(ps[:, :], lhsT=kT[0:64, h, 1, mc * 128:(mc + 1) * 128], rhs=q1[0:64, :], start=False, stop=True),
                             reads=['p5_kT', q1n], writes=[pn])
                        ptn, pt = ptr.next()
                        S.op('act', lambda e: e.activation(pt[:, :], ps[:, :], AF.Exp, scale=sc), reads=[pn], writes=[ptn])
                        pts.append((ptn, pt))
                    pn, ps = self.psum.next()
                    for mc in range(2):
                        S.op('pe', lambda e: e.matmul(ps[:, :], lhsT=self.ones_b[:], rhs=pts[mc][1][:, :], start=(mc == 0), stop=(mc == 1)),
                             reads=['ones_b', pts[mc][0]], writes=[pn], inc=(mc == 1))
                    rcn, rc = rcr.next()
                    S.op('dve', lambda e: e.reciprocal(rc[:, :], ps[:, :]), reads=[pn], writes=[rcn])
                    for part, (off, M, yr_) in enumerate(((0, 128, y0r), (128, 64, y1r))):
                        pn, ps = self.psum.next()
                        for mc in range(2):
                            S.op('pe', lambda e: e.matmul(ps[:M, :], lhsT=vm[:, mc, 192 * h + off:192 * h + off + M], rhs=pts[mc][1][:, :], start=(mc == 0), stop=(mc == 1)),
                                 reads=['p5_vm', pts[mc][0]], writes=[pn], inc=(mc == 1))
                        yn, y = yr_.next()
                        S.op('dve', lambda e: e.tensor_tensor(y[:M, :], ps[:M, :], rc[:M, :], ALU.mult), reads=[pn, rcn], writes=[yn])
                        S.dma('sp', yX[192 * h + off:192 * h + off + M, t0:t0 + T], y[:M, :], reads=[yn], writes=['yX'])

    def phase_merge(self, l, xT, xmT, zF, yA, yC, yX, oG):
        nc, S, L = self.nc, self.S, self.L
        ph = self.phases
        branches = []
        if 'p2' in ph:
            branches.append((0, 'proj_a', 6))
        if 'p3' in ph:
            branches.append((1, 'proj_b', 2))
        if 'p4' in ph:
            branches.append((2, 'proj_c', 6))
        if 'p5' in ph:
            branches.append((3, 'proj_x', 6))
        with ExitStack() as st:
            wp = {}
            for (b, nm, nk) in branches:
                wp[b] = self.sb(st, "p6_" + nm, [128, nk, D], BF16)
                S.dma('sp', wp[b][:], self.wB[nm][l].rearrange("(k p) n -> p k n", p=128), reads=[(nm + 'B', l)], writes=['p6_w%d' % b])
            wo = self.sb(st, "p6_wo", [128, 8, D], BF16)
            S.dma('sp', wo[:], self.wB["w_out"][l].rearrange("(k p) n -> p k n", p=128), reads=[('w_outB', l)], writes=['p6_wo'])
            yr = {b: Ring(nc, st, "p6_y%d" % b, 1, [128, nk, T], BF16) for (b, nm, nk) in branches}
            sgr = Ring(nc, st, "p6_sg", 2, [128, 4, T], BF16)
            xr = Ring(nc, st, "p6_x", 1, [128, 8, T], F32)
            xor_ = Ring(nc, st, "p6_xo", 1, [128, 8, T], F32)
            macc = Ring(nc, st, "p6_macc", 2, [128, T], F32)
            mtmp = Ring(nc, st, "p6_mtmp", 2, [128, T], F32)
            mb = self.sb(st, "p6_mb", [128, 8, T], BF16)
            sq = self.sb(st, "p6_sq", [128, 8, T], BF16)
            rstd = self.sb(st, "p6_rstd", [128, T], F32)
            y = self.sb(st, "p6_yy", [128, 8, T], F32)
            og = Ring(nc, st, "p6_og", 2, [128, 3, 260], F32)
            ybt = Ring(nc, st, "p6_ybt", 2, [128, 256], F32)
            rl = Ring(nc, st, "p6_rl", 2, [128, 4], F32)
            srcs = {0: yA, 2: yC, 3: yX}
            for ti in range(L // T):
                t0 = ti * T
                ys = {}
                for (b, nm, nk) in branches:
                    yn, yt = yr[b].next()
                    ys[b] = (yn, yt)
                    if b != 1:
                        S.dma('sp', yt[:], srcs[b].rearrange("(c p) l -> p c l", p=128)[:, :, t0:t0 + T], reads=[srcs[b].tensor.name], writes=[yn])
                    else:
                        for tb in range(4):
                            on, o = og.next()
                            S.dma('sp', o[:], oG[:, t0 + tb * 128:t0 + (tb + 1) * 128, :].rearrange("g p n -> p g n"), reads=['oG'], writes=[on])
                            S.op('dve', lambda e: e.tensor_tensor(o[:, 0, :], o[:, 0, :], o[:, 1, :], ALU.add), reads=[on], writes=[on])
                            S.op('dve', lambda e: e.tensor_tensor(o[:, 0, :], o[:, 0, :], o[:, 2, :], ALU.add), reads=[on], writes=[on])
                            rn, r = rl.next()
                            ov = o[:, 0, :].rearrange("p (h e) -> p h e", e=65)
                            S.op('dve', lambda e: e.reciprocal(r[:, :], ov[:, :, 64]), reads=[on], writes=[rn])
                            bn, bt = ybt.next()
                            for hh in range(4):
                                S.op('dve', lambda e: e.tensor_scalar(bt[:, hh * 64:(hh + 1) * 64], ov[:, hh, 0:64], r[:, hh:hh + 1], None, ALU.mult),
                                     reads=[on, rn], writes=[bn])
                            pn, ps = self.psum.next()
                            for half in range(2):
                                S.op('pe', lambda e: e.transpose(ps[:, half * 128:(half + 1) * 128], bt[:, half * 128:(half + 1) * 128], self.ident_f[:]),
                                     reads=[bn, 'ident_f'], writes=[pn], inc=(half == 1))
                            S.op('act', lambda e: e.copy(yt[:, :, tb * 128:(tb + 1) * 128], ps[:, 0:256].rearrange("p (a b) -> p a b", a=2)),
                                 reads=[pn], writes=[yn])
                xn, xt = xr.next()
                S.dma('sp', xt[:], xT.rearrange("(c p) l -> p c l", p=128)[:, :, t0:t0 + T], reads=[('xT', ti)], writes=[xn])
                for c in range(8):
                    mn_, m = macc.next()
                    sn, sg = sgr.next()
                    S.dma('sp', sg[:], zF.rearrange("(b c p) l -> p b c l", p=128, c=8)[:, 3:7, c, t0:t0 + T], reads=['zF'], writes=[sn])
                    for bi, (b, nm, nk) in enumerate(branches):
                        yn, yt = ys[b]
                        pn, ps = self.psum.next()
                        for k in range(nk):
                            S.op('pe', lambda e: e.matmul(ps[:, :], lhsT=wp[b][:, k, c * 128:(c + 1) * 128], rhs=yt[:, k, :], start=(k == 0), stop=(k == nk - 1)),
                                 reads=['p6_w%d' % b, yn], writes=[pn], inc=(k == nk - 1))
                        if bi == 0:
                            S.op('dve', lambda e: e.tensor_tensor(m[:, :], ps[:, :], sg[:, b, :], ALU.mult), reads=[pn, sn], writes=[mn_])
                        else:
                            tn, tm = mtmp.next()
                            S.op('dve', lambda e: e.tensor_tensor(tm[:, :], ps[:, :], sg[:, b, :], ALU.mult), reads=[pn, sn], writes=[tn])
                            S.op('dve', lambda e: e.tensor_tensor(m[:, :], m[:, :], tm[:, :], ALU.add), reads=[tn, mn_], writes=[mn_])
                    S.op('act', lambda e: e.copy(mb[:, c, :], m[:, :]), reads=[mn_], writes=[('p6_mb', c)])
                for c in range(8):
                    pn, ps = self.psum.next()
                    for k in range(8):
                        S.op('pe', lambda e: e.matmul(ps[:, :], lhsT=wo[:, k, c * 128:(c + 1) * 128], rhs=mb[:, k, :], start=(k == 0), stop=(k == 7)),
                             reads=['p6_wo', ('p6_mb', k)], writes=[pn], inc=(k == 7))
                    S.op('act', lambda e: e.activation(sq[:, c, :], ps[:, :], AF.Square), reads=[pn], writes=['p6_sq'])
                    S.op('act', lambda e: e.copy(y[:, c, :], ps[:, :]), reads=[pn], writes=['p6_yy'])
                on_, xo_t = xor_.next()
                self.postnorm_residual(l, 1, 'p6_yy', y, 'p6_sq', sq, 'p6_rstd', rstd, xn, xt, on_, xo_t)
                S.dma('sp', xmT.rearrange("(c p) l -> p c l", p=128)[:, :, t0:t0 + T], xo_t[:], reads=[on_], writes=[('xmT', ti)])

    def postnorm_residual(self, l, which, y_name, y, sq_name, sq, rstd_name, rstd, xres_name, xres, xo_name, xo):
        S = self.S
        self.rms_stats(sq_name, sq, rstd_name, rstd, T)
        for c in range(8):
            S.op('dve', lambda e: e.scalar_tensor_tensor(y[:, c, :], y[:, c, :], self.gains_s[:, l, which, c:c + 1], rstd[:, :],
                                                         ALU.mult, ALU.mult),
                 reads=[y_name, rstd_name, 'gains_s'], writes=[y_name])
            S.op('dve', lambda e: e.tensor_tensor(xo[:, c, :], y[:, c, :], xres[:, c, :], ALU.add),
                 reads=[y_name, xres_name], writes=[xo_name])

    def phase_ffn(self, l, xsrc, xdst, wUpB, wDnB, ffnp):
        nc, S, L = self.nc, self.S, self.L
        NT = 2
        with ExitStack() as st:
            xr = Ring(nc, st, "p7_x", 2, [128, 8, T], F32)
            sq = self.sb(st, "p7_sq", [128, 8, T], BF16)
            rstd = self.sb(st, "p7_rstd", [128, T], F32)
            hr = Ring(nc, st, "p7_h", 2, [128, 8, T], BF16)
            wr = Ring(nc, st, "p7_w", 2, [128, 8, 1024], BF16)
            actr = Ring(nc, st, "p7_act", 2, [128, 24, T], BF16)
            gsb = Ring(nc, st, "p7_g", 2, [128, T + 2], F32)
            gtmp = Ring(nc, st, "p7_gt", 2, [128, T], F32)
            tails = self.sb(st, "p7_tail", [128, 24, 2], F32)
            fp = self.sb(st, "p7_fp", [128, 4, 24], F32)
            self.ys = [self.sb(st, "p7_y%d" % i, [128, 8, T], F32) for i in range(2)]
            self.sqs = [self.sb(st, "p7_sqo%d" % i, [128, 8, T], BF16) for i in range(2)]
            S.dma('sp', fp[:], ffnp[:, l, :, :], writes=['p7_fp'])
            S.op('dve', lambda e: e.memset(tails[:], 0.0), writes=[('p7_tail', c) for c in range(24)])
            wup = wUpB[l].rearrange("(k p) n -> p k n", p=128)
            wdn = wDnB[l].rearrange("(k p) n -> p k n", p=128)
            wq = 0
            for tp in range(L // (NT * T)):
                tiles = []
                for ti in range(tp * NT, (tp + 1) * NT):
                    t0 = ti * T
                    xn, xt = xr.next()
                    S.dma('sp', xt[:], xsrc.rearrange("(c p) l -> p c l", p=128)[:, :, t0:t0 + T], reads=[(xsrc.tensor.name, ti)], writes=[xn])
                    hn, h = hr.next()
                    self.prenorm(l, 3, xn, xt, hn, h, "p7_sq", sq, "p7_rstd", rstd)
                    an, act = actr.next()
                    tiles.append((ti, t0, xn, xt, hn, h, an, act))
                for blk in range(6):
                    wn, w = wr.next()
                    wq += 1
                    q_ = 'sp' if wq % 2 == 0 else 'pool'
                    S.dma(q_, w[:, :, 0:512], wup[:, :, blk * 512:(blk + 1) * 512], reads=[('wUpB', l)], writes=[wn])
                    S.dma(q_, w[:, :, 512:1024], wup[:, :, DFF + blk * 512:DFF + (blk + 1) * 512], reads=[('wUpB', l)], writes=[wn])
                    for (ti, t0, xn, xt, hn, h, an, act) in tiles:
                        for j in range(4):
                            ch = blk * 4 + j
                            pgn, pg = self.psum.next()
                            for k in range(8):
                                S.op('pe', lambda e: e.matmul(pg[:, :], lhsT=w[:, k, 512 + j * 128:512 + (j + 1) * 128], rhs=h[:, k, :], start=(k == 0), stop=(k == 7)),
                                     reads=[wn, hn], writes=[pgn], inc=(k == 7))
                            pvn, pv = self.psum.next()
                            for k in range(8):
                                S.op('pe', lambda e: e.matmul(pv[:, :], lhsT=w[:, k, j * 128:(j + 1) * 128], rhs=h[:, k, :], start=(k == 0), stop=(k == 7)),
                                     reads=[wn, hn], writes=[pvn], inc=(k == 7))
                            gn, g = gsb.next()
                            S.op('act', lambda e: e.copy(g[:, 2:T + 2], pg[:, :]), reads=[pgn], writes=[gn])
                            S.op('act', lambda e: e.copy(g[:, 0:2], tails[:, ch, :]), reads=[('p7_tail', ch)], writes=[gn])
                            S.op('act', lambda e: e.copy(tails[:, ch, :], g[:, T:T + 2]), reads=[gn], writes=[('p7_tail', ch)])
                            tn, tm = gtmp.next()
                            S.op('dve', lambda e: e.tensor_scalar(tm[:, :], g[:, 0:T], fp[:, 0, ch:ch + 1], fp[:, 3, ch:ch + 1], ALU.mult, ALU.add),
                                 reads=[gn, 'p7_fp'], writes=[tn])
                            S.op('dve', lambda e: e.scalar_tensor_tensor(tm[:, :], g[:, 1:T + 1], fp[:, 1, ch:ch + 1], tm[:, :], ALU.mult, ALU.add),
                                 reads=[gn, tn, 'p7_fp'], writes=[tn])
                            S.op('dve', lambda e: e.scalar_tensor_tensor(tm[:, :], g[:, 2:T + 2], fp[:, 2, ch:ch + 1], tm[:, :], ALU.mult, ALU.add),
                                 reads=[gn, tn, 'p7_fp'], writes=[tn])
                            S.op('act', lambda e: e.activation(tm[:, :], tm[:, :], AF.Gelu_apprx_tanh), reads=[tn], writes=[tn])
                            S.op('dve', lambda e: e.tensor_tensor(act[:, ch, :], pv[:, :], tm[:, :], ALU.mult), reads=[pvn, tn], writes=[(an, ch)])
                for half in range(2):
                    banks = {}
                    for cpair in range(2):
                        for (ti, t0, xn, xt, hn, h, an, act) in tiles:
                            banks[ti] = [self.psum.next() for _ in range(2)]
                        for kb in range(3):
                            wn, w = wr.next()
                            wq += 1
                            q_ = 'sp' if wq % 2 == 0 else 'pool'
                            c00 = half * 512 + cpair * 256
                            S.dma(q_, w[:, :, 0:256], wdn[:, kb * 8:(kb + 1) * 8, c00:c00 + 256], reads=[('wDnB', l)], writes=[wn])
                            for (ti, t0, xn, xt, hn, h, an, act) in tiles:
                                for cc in range(2):
                                    pn, ps = banks[ti][cc]
                                    for k in range(8):
                                        kk = kb * 8 + k
                                        S.op('pe', lambda e: e.matmul(ps[:, :], lhsT=w[:, k, cc * 128:(cc + 1) * 128], rhs=act[:, kk, :], start=(kk == 0), stop=(kk == 23)),
                                             reads=[wn, (an, kk)], writes=[pn], inc=(k == 7))
                        for (ti, t0, xn, xt, hn, h, an, act) in tiles:
                            for cc in range(2):
                                c = half * 4 + cpair * 2 + cc
                                pn, ps = banks[ti][cc]
                                S.op('act', lambda e: e.activation(self.sqs[ti % 2][:, c, :], ps[:, :], AF.Square), reads=[pn], writes=[('p7_sq2', ti % 2)])
                                S.op('act', lambda e: e.copy(self.ys[ti % 2][:, c, :], ps[:, :]), reads=[pn], writes=[('p7_y2', ti % 2)])
                for (ti, t0, xn, xt, hn, h, an, act) in tiles:
                    on, xo_t = ('p7_y2', ti % 2), self.ys[ti % 2]
                    self.postnorm_residual(l, 4, ('p7_y2', ti % 2), self.ys[ti % 2], ('p7_sq2', ti % 2), self.sqs[ti % 2], 'p7_rstd', rstd, xn, xt, on, xo_t)
                    S.dma('sp', xdst.rearrange("(c p) l -> p c l", p=128)[:, :, t0:t0 + T], xo_t[:], reads=[on], writes=[(xdst.tensor.name, ti)])


def host_inputs(inputs, b, L):
    f = np.float32
    d = {}
    d["x"] = np.ascontiguousarray(inputs["x"][b, :L])
    d["mem"] = np.ascontiguousarray(inputs["mem"][b])
    d["ident"] = np.eye(128, dtype=f)
    gs = np.stack([inputs[k] for k in ("g_mix_pre", "g_mix_post", "g_mem", "g_mlp_pre", "g_mlp_post")], axis=1)
    d["gains"] = np.ascontiguousarray(gs.reshape(NL, 5, 8, 128).transpose(3, 0, 1, 2)).astype(f)
    d["w_in"] = inputs["w_in"]
    d["ffn_w_up"] = inputs["ffn_w_up"]
    d["ffn_w_down"] = inputs["ffn_w_down"]
    fp = np.concatenate([inputs["ffn_conv_w"], inputs["ffn_conv_b"][:, None, :]], axis=1)
    lp = np.concatenate([inputs["lru_conv_w"], inputs["lru_conv_b"][:, None], inputs["lru_ba"][:, None],
                         inputs["lru_bx"][:, None], inputs["lru_lambda"][:, None]], axis=1)
    d["lrup"] = np.ascontiguousarray(lp.reshape(NL, 8, 6, 128).transpose(3, 0, 1, 2)).astype(f)
    bd = np.zeros((NL, 2, 128, 6, 128), f)
    for wi, nm in enumerate(("lru_wa", "lru_wx")):
        w = inputs[nm]
        for c in range(6):
            bd[:, wi, 0:64, c, 0:64] = w[:, 2 * c]
            bd[:, wi, 64:128, c, 64:128] = w[:, 2 * c + 1]
    d["lru_bd"] = bd
    d["ssmd"] = np.ascontiguousarray(inputs["ssm_d"].reshape(NL, 6, 128).transpose(2, 0, 1)).astype(f)
    for nm in ("ssm_glu", "mem_wkv", "proj_a", "proj_b", "proj_c", "proj_x", "w_out"):
        d[nm] = inputs[nm]
    am = np.full((128, 2, 12, 256), -30000.0, f)
    kk = np.arange(128)[:, None]
    qq = np.arange(128)[None, :]
    for hd in range(12):
        dd = DILS[hd // 4]
        dp = (qq + 128 - kk).astype(f)
        dc = (qq - kk).astype(f)
        mp = np.where(kk >= qq, -ALIBI[hd] * dd * dp, -30000.0)
        mc = np.where(kk <= qq, -ALIBI[hd] * dd * dc, -30000.0)
        am[:, 0, hd, 0:128] = mp
        am[:, 0, hd, 128:256] = mc
        am[:, 1, hd, 128:256] = mc
    d["amask"] = am
    def lay_a(a):
        return a.reshape(NL, 24, 2, 64).transpose(2, 3, 0, 1).reshape(128, NL, 24)
    ld_full = np.repeat(inputs["ssm_log_dt"][:, :, None], 64, axis=2)
    d["s5A"] = np.ascontiguousarray(np.stack([lay_a(inputs["ssm_a_re"]), lay_a(inputs["ssm_a_im"]), lay_a(ld_full)], axis=2)).astype(f)
    def lay_r(a):
        return a.reshape(NL, 3072)
    rr = np.stack([lay_r(inputs["ssm_a_re"]), lay_r(inputs["ssm_a_im"]), lay_r(ld_full)], axis=1)
    d["s5R"] = np.ascontiguousarray(np.broadcast_to(rr[None], (128, NL, 3, 3072))).astype(f)
    sB = np.zeros((NL, 2, 128, 24, 128), f)
    sC = np.zeros((NL, 2, 128, 24, 128), f)
    for ri, (bn, cn) in enumerate((("ssm_b_re", "ssm_c_re"), ("ssm_b_im", "ssm_c_im"))):
        Bm = inputs[bn]
        Cm = inputs[cn]
        for p in range(24):
            for gl in range(2):
                r0 = 32 * (p % 4) + gl * 16
                sB[:, ri, r0:r0 + 16, p, gl * 64:(gl + 1) * 64] = Bm[:, 2 * p + gl].transpose(0, 2, 1)
                sC[:, ri, gl * 64:(gl + 1) * 64, p, r0:r0 + 16] = Cm[:, 2 * p + gl].transpose(0, 2, 1)
    d["s5B"] = sB
    d["s5C"] = sC
    d["ffnp"] = np.ascontiguousarray(fp.reshape(NL, 4, 24, 128).transpose(3, 0, 1, 2)).astype(f)
    return d


ALL_PHASES = ('prepass', 'prologue', 'p1', 'p2', 'p3', 'p4', 'p5', 'p6', 'p7', 'epilogue')


def kernel(**inputs):
    inputs = {k: np.asarray(v) for k, v in inputs.items()}
    L = inputs["x"].shape[1]
    kb = K(L, NL)
    nc = kb.build(ALL_PHASES)
    in_maps = []
    for b in range(2):
        hi = host_inputs(inputs, b, L)
        in_maps.append({k: hi[k] for k in kb.ins})
    res = run_bass_kernel_spmd(nc, in_maps, core_ids=[0, 1])
    return np.stack([res.results[b]["out"] for b in range(2)], axis=0)
```

```python
import math
from contextlib import ExitStack
import numpy as np
import concourse.bass as bass
import concourse.mybir as mybir
from concourse.bass_utils import run_bass_kernel_spmd

AF = mybir.ActivationFunctionType
ALU = mybir.AluOpType
F32 = mybir.dt.float32
BF16 = mybir.dt.bfloat16

D = 1024
NL = 2
SEQ = 16384
MEM = 256
IN_W = 9472
DFF = 3072
T = 512
EPS = 1e-6
ALIBI = [2.0 ** (-8.0 * (h + 1) / 12) for h in range(12)]
DILS = (1, 4, 16)
import os
P3STOP = int(os.environ.get("P3STOP", "0"))


class Sched:
    NSLOT = 8

    def __init__(self, nc, stack):
        self.nc = nc
        self.engs = {'pe': nc.tensor, 'act': nc.scalar, 'dve': nc.vector,
                     'pool': nc.gpsimd, 'sp': nc.sync}
        self.sem = {}
        self.cnt = {}
        for n in ['pe', 'act', 'dve', 'pool']:
            self.sem[n] = stack.enter_context(nc.semaphore("s_" + n))
            self.cnt[n] = 0
        self.dq = {}
        for q in ['sp', 'pool', 'act']:
            for i in range(self.NSLOT):
                self.sem[('dma', q, i)] = stack.enter_context(nc.semaphore("d_%s%d" % (q, i)))
            self.dq[q] = 0
        self.seen = {e: {} for e in self.engs}
        self.lastw = {}
        self.readers = {}

    def _deps(self, reads, writes):
        deps = []
        for r in reads:
            t = self.lastw.get(r)
            if t is not None:
                deps.append(t)
        for w in writes:
            t = self.lastw.get(w)
            if t is not None:
                deps.append(t)
            deps.extend(self.readers.get(w, ()))
        return deps

    def _wait(self, ename, deps):
        best = {}
        for (src, val) in deps:
            if best.get(src, 0) < val:
                best[src] = val
        seen = self.seen[ename]
        eng = self.engs[ename]
        for src, val in best.items():
            if src == 'pe' and ename == 'pe':
                continue
            if seen.get(src, 0) >= val:
                continue
            eng.wait_ge(self.sem[src], val)
            seen[src] = val

    def _record(self, ticket, reads, writes):
        for r in reads:
            self.readers.setdefault(r, []).append(ticket)
        for w in writes:
            self.lastw[w] = ticket
            self.readers[w] = []

    defer = None

    def op(self, ename, fn, reads=(), writes=(), inc=True):
        if self.defer is not None:
            self.defer.append(('__op__', (ename, fn, reads, writes, inc)))
            return None
        self._wait(ename, self._deps(reads, writes))
        ins = fn(self.engs[ename])
        if inc:
            self.cnt[ename] += 1
            ins.then_inc(self.sem[ename], 1)
            ticket = (ename, self.cnt[ename])
        else:
            ticket = (ename, self.cnt[ename] + 1)
        self._record(ticket, reads, writes)
        return ticket

    def dma(self, q, out, in_, reads=(), writes=(), **kw):
        if self.defer is not None:
            self.defer.append(('__dma__', (q, out, in_, reads, writes, kw)))
            return None
        i = self.dq[q]
        slot = i % self.NSLOT
        rnd = i // self.NSLOT
        src = ('dma', q, slot)
        deps = self._deps(reads, writes)
        if rnd > 0:
            deps.append((src, 16 * rnd))
        self._wait(q, deps)
        ins = self.engs[q].dma_start(out=out, in_=in_, **kw)
        ins.then_inc(self.sem[src], 16)
        self.dq[q] = i + 1
        ticket = (src, 16 * (rnd + 1))
        self._record(ticket, reads, writes)
        return ticket

    def coll(self, kind, src, dst, groups, reads=(), writes=()):
        q = 'pool'
        i = self.dq[q]
        slot = i % self.NSLOT
        rnd = i // self.NSLOT
        srck = ('dma', q, slot)
        deps = self._deps(reads, writes)
        if rnd > 0:
            deps.append((srck, 16 * rnd))
        self._wait(q, deps)
        ins = self.engs[q].collective_compute(kind, ALU.bypass, replica_groups=groups, ins=[src], outs=[dst])
        ins.then_inc(self.sem[srck], 16)
        self.dq[q] = i + 1
        ticket = (srck, 16 * (rnd + 1))
        self._record(ticket, reads, writes)
        return ticket

    def run_deferred(self, item):
        kind, a = item
        if kind == '__op__':
            self.op(*a)
        else:
            q, out, in_, reads, writes, kw = a
            self.dma(q, out, in_, reads=reads, writes=writes, **kw)

    def interleave(self, fns):
        lists = []
        for f in fns:
            self.defer = []
            f()
            lists.append(self.defer)
            self.defer = None
        while any(lists):
            for lst in lists:
                if lst:
                    self.run_deferred(lst.pop(0))

    def finish(self, ename='sp'):
        deps = list(self.lastw.values())
        for l in self.readers.values():
            deps.extend(l)
        self._wait(ename, deps)

    def barrier(self):
        for e in self.engs:
            self.finish(e)
        self.lastw = {}
        self.readers = {}


_UID = [0]


class Ring:
    def __init__(self, nc, stack, name, n, shape, dtype, psum=False):
        self.bufs = []
        _UID[0] += 1
        for i in range(n):
            nm = "%s_%d_%d" % (name, _UID[0], i)
            if psum:
                t = stack.enter_context(nc.psum_tensor(nm, shape, dtype))
            else:
                t = stack.enter_context(nc.sbuf_tensor(nm, shape, dtype))
            self.bufs.append((nm, t))
        self.i = 0

    def next(self):
        b = self.bufs[self.i % len(self.bufs)]
        self.i += 1
        return b


class K:
    def __init__(self, L, nl, dbg=False):
        self.L = L
        self.nl = nl
        self.dbg = dbg
        self.nc = bass.Bass("TRN2", target_bir_lowering=False)
        self.ins = {}
        self.scr = {}

    def inp(self, name, shape, dt=F32):
        t = self.nc.dram_tensor(name, list(shape), dt, kind="ExternalInput").ap()
        self.ins[name] = t
        return t

    def scratch(self, name, shape, dt):
        kind = "ExternalOutput" if self.dbg else "Internal"
        t = self.nc.dram_tensor(name, list(shape), dt, kind=kind).ap()
        self.scr[name] = t
        return t

    def sb(self, st, name, shape, dt):
        _UID[0] += 1
        return st.enter_context(self.nc.sbuf_tensor("%s_%d" % (name, _UID[0]), list(shape), dt))

    def build(self, phases):
        nc = self.nc
        L, nl = self.L, self.nl
        x = self.inp("x", [L, D])
        mem = self.inp("mem", [MEM, D])
        ident = self.inp("ident", [128, 128])
        gains = self.inp("gains", [128, NL, 5, 8])
        w_in = self.inp("w_in", [NL, D, IN_W])
        ffn_w_up = self.inp("ffn_w_up", [NL, D, 2 * DFF])
        ffn_w_down = self.inp("ffn_w_down", [NL, DFF, D])
        ffnp = self.inp("ffnp", [128, NL, 4, 24])
        lrup = self.inp("lrup", [128, NL, 8, 6])
        lru_bd = self.inp("lru_bd", [NL, 2, 128, 6, 128])
        ssmd = self.inp("ssmd", [128, NL, 6])
        wsrc = {}
        for nm, shp in (("ssm_glu", [NL, 768, 1536]), ("mem_wkv", [NL, D, 1536]), ("proj_a", [NL, 768, D]),
                        ("proj_b", [NL, 256, D]), ("proj_c", [NL, 768, D]), ("proj_x", [NL, 768, D]), ("w_out", [NL, D, D])):
            wsrc[nm] = (self.inp(nm, shp), self.scratch(nm + "B", shp, BF16), shp[1])
        self.wB = {nm: v[1] for nm, v in wsrc.items()}
        yA = self.scratch("yA", [768, L], BF16)
        yC = self.scratch("yC", [768, L], BF16)
        yX = self.scratch("yX", [768, L], BF16)
        oG = self.scratch("oG", [3, L, 260], F32)
        self.phases = phases
        amask = self.inp("amask", [128, 2, 12, 256])
        s5A = self.inp("s5A", [128, NL, 3, 24])
        s5R = self.inp("s5R", [128, NL, 3, 3072])
        s5B = self.inp("s5B", [NL, 2, 128, 24, 128])
        s5C = self.inp("s5C", [NL, 2, 128, 24, 128])
        out = self.nc.dram_tensor("out", [L, D], F32, kind="ExternalOutput").ap()
        self.out = out

        xT = self.scratch("xT", [D, L], F32)
        xmT = self.scratch("xmT", [D, L], F32)
        zF = self.scratch("zF", [56 * 128, L], BF16)
        zQKV = self.scratch("zQKV", [L, 2304], BF16)
        wInB = self.scratch("wInB", [NL, D, IN_W], BF16)
        wUpB = self.scratch("wUpB", [NL, D, 2 * DFF], BF16)
        wDnB = self.scratch("wDnB", [NL, DFF, D], BF16)

        with ExitStack() as st0:
            S = Sched(nc, st0)
            self.S = S
            ident_f = self.sb(st0, "ident_f", [128, 128], F32)
            ones_b = self.sb(st0, "ones_b", [128, 128], BF16)
            gains_s = self.sb(st0, "gains_s", [128, NL, 5, 8], F32)
            eps_c = self.sb(st0, "eps_c", [128, 1], F32)
            self.ident_f, self.ones_b, self.gains_s, self.eps_c = ident_f, ones_b, gains_s, eps_c
            one_c = self.sb(st0, "one_c", [128, 1], F32)
            self.one_c = one_c
            S.op('dve', lambda e: e.memset(one_c[:], 1.0), writes=['one_c'])
            S.dma('sp', ident_f[:], ident, writes=['ident_f'])
            S.dma('sp', gains_s[:], gains, writes=['gains_s'])
            S.op('dve', lambda e: e.memset(ones_b[:], 1.0), writes=['ones_b'])
            S.op('dve', lambda e: e.memset(eps_c[:], EPS), writes=['eps_c'])
            self.psum = Ring(nc, st0, "ps", 6, [128, 512], F32, psum=True)

            if 'prepass' in phases:
                for l in range(nl):
                    for (src, dst, rows) in [(w_in, wInB, D), (ffn_w_up, wUpB, D), (ffn_w_down, wDnB, DFF)] + list(wsrc.values()):
                        for r in range(0, rows, 128):
                            S.dma('pool', dst[l, r:r + 128, :], src[l, r:r + 128, :],
                                  writes=[(dst.tensor.name, l)])
            if 'prologue' in phases:
                self.transpose_in(x, xT, L)
                S.barrier()
            for l in range(nl):
                if 'p1' in phases:
                    self.phase_inproj(l, xT, wInB, zF, zQKV)
                    S.barrier()
                if 'p2' in phases:
                    self.phase_lru(l, zF, yA, lrup, lru_bd)
                    S.barrier()
                if 'p3' in phases:
                    self.phase_attn(l, zQKV, oG, amask)
                    S.barrier()
                if 'p4' in phases:
                    self.phase_s5(l, zF, yC, s5A, s5R, s5B, s5C, ssmd)
                    S.barrier()
                if 'p5' in phases:
                    self.phase_xattn(l, mem, zF, yX)
                    S.barrier()
                if 'p6' in phases:
                    self.phase_merge(l, xT, xmT, zF, yA, yC, yX, oG)
                    S.barrier()
                if 'p7' in phases:
                    self.phase_ffn(l, xmT if 'p6' in phases else xT, xT, wUpB, wDnB, ffnp)
                    S.barrier()
            if 'epilogue' in phases:
                self.transpose_out(xT, out, L)
            S.barrier()
        return nc

    def transpose_in(self, x, xT, L):
        nc, S = self.nc, self.S
        with ExitStack() as st:
            xin = Ring(nc, st, "ti_x", 2, [128, D], F32)
            xo = Ring(nc, st, "ti_o", 2, [128, 8, 128], F32)
            for b in range(L // 128):
                nm, xt = xin.next()
                S.dma('sp', xt[:], x[b * 128:(b + 1) * 128, :], writes=[nm])
                no, ot = xo.next()
                for half in range(2):
                    pn, ps = self.psum.next()
                    for j in range(4):
                        c = half * 4 + j
                        S.op('pe', lambda e: e.transpose(ps[:, j * 128:(j + 1) * 128], xt[:, c * 128:(c + 1) * 128], self.ident_f[:]),
                             reads=[nm, 'ident_f'], writes=[pn], inc=(j == 3))
                    eng = 'act' if half == 0 else 'dve'
                    if eng == 'act':
                        S.op('act', lambda e: e.copy(ot[:, half * 4:(half + 1) * 4, :], ps[:, :].rearrange("p (a b) -> p a b", a=4)),
                             reads=[pn], writes=[no])
                    else:
                        S.op('dve', lambda e: e.tensor_copy(ot[:, half * 4:(half + 1) * 4, :], ps[:, :].rearrange("p (a b) -> p a b", a=4)),
                             reads=[pn], writes=[no])
                S.dma('sp', xT.rearrange("(c p) l -> p c l", p=128)[:, :, b * 128:(b + 1) * 128], ot[:],
                      reads=[no], writes=[('xT', b // 4)])

    def transpose_out(self, xT, out, L):
        nc, S = self.nc, self.S
        with ExitStack() as st:
            xin = Ring(nc, st, "to_x", 2, [128, 8, 128], F32)
            xo = Ring(nc, st, "to_o", 2, [128, D], F32)
            for b in range(L // 128):
                nm, xt = xin.next()
                S.dma('sp', xt[:], xT.rearrange("(c p) l -> p c l", p=128)[:, :, b * 128:(b + 1) * 128],
                      reads=[('xT', b // 4)], writes=[nm])
                no, ot = xo.next()
                for half in range(2):
                    pn, ps = self.psum.next()
                    for j in range(4):
                        c = half * 4 + j
                        S.op('pe', lambda e: e.transpose(ps[:, j * 128:(j + 1) * 128], xt[:, c, :], self.ident_f[:]),
                             reads=[nm, 'ident_f'], writes=[pn], inc=(j == 3))
                    if half == 0:
                        S.op('act', lambda e: e.copy(ot[:, 0:512], ps[:, :]), reads=[pn], writes=[no])
                    else:
                        S.op('dve', lambda e: e.tensor_copy(ot[:, 512:1024], ps[:, :]), reads=[pn], writes=[no])
                S.dma('sp', out[b * 128:(b + 1) * 128, :], ot[:], reads=[no], writes=['out'])

    def rms_stats(self, sqname, sq, rstd_name, rstd, Tn):
        S = self.S
        pn, ps = self.psum.next()
        for c in range(8):
            S.op('pe', lambda e: e.matmul(ps[:, :Tn], lhsT=self.ones_b[:], rhs=sq[:, c, :], start=(c == 0), stop=(c == 7)),
                 reads=[sqname, 'ones_b'], writes=[pn], inc=(c == 7))
        S.op('act', lambda e: e.activation(rstd[:, :Tn], ps[:, :Tn], AF.Sqrt, bias=self.eps_c[:], scale=1.0 / D),
             reads=[pn, 'eps_c'], writes=[rstd_name])
        S.op('dve', lambda e: e.reciprocal(rstd[:, :Tn], rstd[:, :Tn]), reads=[rstd_name], writes=[rstd_name])

    def prenorm(self, l, which, xt_name, xt, h_name, h, sq_name, sq, rstd_name, rstd):
        S = self.S
        for c in range(8):
            S.op('act', lambda e: e.activation(sq[:, c, :], xt[:, c, :], AF.Square), reads=[xt_name], writes=[sq_name])
        self.rms_stats(sq_name, sq, rstd_name, rstd, T)
        for c in range(8):
            S.op('dve', lambda e: e.scalar_tensor_tensor(h[:, c, :], xt[:, c, :], self.gains_s[:, l, which, c:c + 1], rstd[:, :],
                                                         ALU.mult, ALU.mult),
                 reads=[xt_name, rstd_name, 'gains_s'], writes=[h_name])

    def phase_inproj(self, l, xT, wInB, zF, zQKV):
        nc, S, L = self.nc, self.S, self.L
        FM_COLS = list(range(0, 1536, 128)) + list(range(3840, IN_W, 128))
        NT = 2
        with ExitStack() as st:
            xr = Ring(nc, st, "p1_x", 2, [128, 8, T], F32)
            sq = self.sb(st, "p1_sq", [128, 8, T], BF16)
            rstd = self.sb(st, "p1_rstd", [128, T], F32)
            hr = Ring(nc, st, "p1_h", 4, [128, 8, T], BF16)
            wr = Ring(nc, st, "p1_w", 2, [128, 8, 1024], BF16)
            zo = Ring(nc, st, "p1_zo", 2, [128, 8, T], BF16)
            qo = Ring(nc, st, "p1_qo", 2, [128, 4, 2304], BF16)
            wv = wInB[l].rearrange("(k p) n -> p k n", p=128)
            zFv = zF.rearrange("(c p) l -> p c l", p=128)
            ev = 0
            wq = 0
            for tp in range(L // (NT * T)):
                hs = []
                for ti in range(tp * NT, (tp + 1) * NT):
                    t0 = ti * T
                    xn, xt = xr.next()
                    S.dma('sp', xt[:], xT.rearrange("(c p) l -> p c l", p=128)[:, :, t0:t0 + T], reads=[('xT', ti)], writes=[xn])
                    hn, h = hr.next()
                    self.prenorm(l, 0, xn, xt, hn, h, "p1_sq", sq, "p1_rstd", rstd)
                    hs.append((t0, hn, h))
                for blk in range(7):
                    wn, w = wr.next()
                    wq += 1
                    for j in range(8):
                        c0 = FM_COLS[blk * 8 + j]
                        if j == 0 or FM_COLS[blk * 8 + j - 1] + 128 != c0:
                            j2 = j
                            while j2 + 1 < 8 and FM_COLS[blk * 8 + j2 + 1] == FM_COLS[blk * 8 + j2] + 128:
                                j2 += 1
                            S.dma('sp' if wq % 2 == 0 else 'pool', w[:, :, j * 128:(j2 + 1) * 128], wv[:, :, c0:c0 + (j2 - j + 1) * 128],
                                  reads=[('wInB', l)], writes=[wn])
                    for (t0, hn, h) in hs:
                        zn, z = zo.next()
                        for j in range(8):
                            ch = blk * 8 + j
                            pn, ps = self.psum.next()
                            for k in range(8):
                                S.op('pe', lambda e: e.matmul(ps[:, :], lhsT=w[:, k, j * 128:(j + 1) * 128], rhs=h[:, k, :], start=(k == 0), stop=(k == 7)),
                                     reads=[wn, hn], writes=[pn], inc=(k == 7))
                            if 6 <= ch < 12:
                                S.op('act', lambda e: e.activation(z[:, j, :], ps[:, :], AF.Gelu_apprx_tanh), reads=[pn], writes=[zn])
                            elif ch >= 24:
                                S.op('act', lambda e: e.activation(z[:, j, :], ps[:, :], AF.Sigmoid), reads=[pn], writes=[zn])
                            else:
                                S.op('dve', lambda e: e.tensor_copy(z[:, j, :], ps[:, :]), reads=[pn], writes=[zn])
                        S.dma('sp', zFv[:, blk * 8:(blk + 1) * 8, t0:t0 + T], z[:], reads=[zn], writes=['zF'])
                qs = [qo.next() for _ in hs]
                for blk in range(3):
                    c0 = 1536 + blk * 1024
                    ncol = min(1024, 3840 - c0)
                    wn, w = wr.next()
                    wq += 1
                    S.dma('sp' if wq % 2 == 0 else 'pool', w[:, :, :ncol], wv[:, :, c0:c0 + ncol], reads=[('wInB', l)], writes=[wn])
                    for (t0, hn, h), (qn, q) in zip(hs, qs):
                        for tb in range(4):
                            for n0 in range(0, ncol, 512):
                                nn = min(512, ncol - n0)
                                pn, ps = self.psum.next()
                                for k in range(8):
                                    S.op('pe', lambda e: e.matmul(ps[:, :nn], lhsT=h[:, k, tb * 128:(tb + 1) * 128], rhs=w[:, k, n0:n0 + nn], start=(k == 0), stop=(k == 7)),
                                         reads=[wn, hn], writes=[pn], inc=(k == 7))
                                dst = q[:, tb, blk * 1024 + n0: blk * 1024 + n0 + nn]
                                ev += 1
                                if ev % 2:
                                    S.op('dve', lambda e: e.tensor_copy(dst, ps[:, :nn]), reads=[pn], writes=[qn])
                                else:
                                    S.op('act', lambda e: e.copy(dst, ps[:, :nn]), reads=[pn], writes=[qn])
                for (t0, hn, h), (qn, q) in zip(hs, qs):
                    S.dma('sp', zQKV[t0:t0 + T, :].rearrange("(tb p) n -> p tb n", p=128), q[:], reads=[qn], writes=['zQKV'])

    def phase_lru(self, l, zF, yA, lrup, lru_bd):
        nc, S, L = self.nc, self.S, self.L
        with ExitStack() as st:
            lp = self.sb(st, "p2_lp", [128, 8, 6], F32)
            kap = self.sb(st, "p2_kap", [128, 2, 6], F32)
            bdA = self.sb(st, "p2_bdA", [128, 6, 128], BF16)
            bdX = self.sb(st, "p2_bdX", [128, 6, 128], BF16)
            state = self.sb(st, "p2_state", [128, 6], F32)
            xar = Ring(nc, st, "p2_xa", 3, [128, T + 3], BF16)
            ggr = Ring(nc, st, "p2_gg", 3, [128, T], BF16)
            xcr = Ring(nc, st, "p2_xc", 3, [128, T], F32)
            xcbr = Ring(nc, st, "p2_xcb", 3, [128, T], BF16)
            rr = Ring(nc, st, "p2_r", 3, [128, T], F32)
            ir = Ring(nc, st, "p2_i", 3, [128, T], F32)
            ar = Ring(nc, st, "p2_a", 3, [128, T], F32)
            a2r = Ring(nc, st, "p2_a2", 3, [128, T], F32)
            hr = Ring(nc, st, "p2_h", 3, [128, T], F32)
            yr = Ring(nc, st, "p2_y", 3, [128, T], BF16)
            S.dma('sp', lp[:], lrup[:, l, :, :], writes=['p2_lp'])
            S.dma('pool', bdA[:], lru_bd[l, 0], writes=['p2_bdA'])
            S.dma('pool', bdX[:], lru_bd[l, 1], writes=['p2_bdX'])
            S.op('dve', lambda e: e.memset(state[:], 0.0), writes=[('p2_state', c) for c in range(6)])
            S.op('act', lambda e: e.activation(kap[:, 0, :], lp[:, 7, :], AF.Exp, scale=-1.0), reads=['p2_lp'], writes=['p2_kap'])
            S.op('act', lambda e: e.activation(kap[:, 0, :], kap[:, 0, :], AF.Ln, bias=self.one_c[:]), reads=['p2_kap', 'one_c'], writes=['p2_kap'])
            S.op('dve', lambda e: e.tensor_scalar(kap[:, 1, :], kap[:, 0, :], -16.0, None, ALU.mult), reads=['p2_kap'], writes=['p2_kap'])
            S.op('dve', lambda e: e.tensor_scalar(kap[:, 0, :], kap[:, 0, :], -8.0, None, ALU.mult), reads=['p2_kap'], writes=['p2_kap'])
            zFv = zF.rearrange("(c p) l -> p c l", p=128)
            yAv = yA.rearrange("(c p) l -> p c l", p=128)
            for ti in range(L // T):
                t0 = ti * T
                def chunk(c):
                    xn, xa = xar.next()
                    if t0 == 0:
                        S.op('pool', lambda e: e.memset(xa[:, 0:3], 0.0), writes=[xn])
                        S.dma('sp', xa[:, 3:T + 3], zFv[:, c, 0:T], reads=['zF'], writes=[xn])
                    else:
                        S.dma('sp', xa[:, :], zFv[:, c, t0 - 3:t0 + T], reads=['zF'], writes=[xn])
                    gn, gg = ggr.next()
                    S.dma('sp', gg[:, :], zFv[:, 6 + c, t0:t0 + T], reads=['zF'], writes=[gn])
                    xcn, xc = xcr.next()
                    S.op('dve', lambda e: e.tensor_scalar(xc[:, :], xa[:, 0:T], lp[:, 0, c:c + 1], lp[:, 4, c:c + 1], ALU.mult, ALU.add),
                         reads=[xn, 'p2_lp'], writes=[xcn])
                    for j in range(1, 4):
                        S.op('dve', lambda e, j=j: e.scalar_tensor_tensor(xc[:, :], xa[:, j:j + T], lp[:, j, c:c + 1], xc[:, :], ALU.mult, ALU.add),
                             reads=[xn, xcn, 'p2_lp'], writes=[xcn])
                    xbn, xcb = xcbr.next()
                    S.op('dve', lambda e: e.tensor_copy(xcb[:, :], xc[:, :]), reads=[xcn], writes=[xbn])
                    prn, pr = self.psum.next()
                    S.op('pe', lambda e: e.matmul(pr[:, :], lhsT=bdA[:, c, :], rhs=xcb[:, :], start=True, stop=True), reads=['p2_bdA', xbn], writes=[prn])
                    pin, pi = self.psum.next()
                    S.op('pe', lambda e: e.matmul(pi[:, :], lhsT=bdX[:, c, :], rhs=xcb[:, :], start=True, stop=True), reads=['p2_bdX', xbn], writes=[pin])
                    rn, r = rr.next()
                    S.op('act', lambda e: e.activation(r[:, :], pr[:, :], AF.Sigmoid, bias=lp[:, 5, c:c + 1]), reads=[prn, 'p2_lp'], writes=[rn])
                    inn, iv = ir.next()
                    S.op('act', lambda e: e.activation(iv[:, :], pi[:, :], AF.Sigmoid, bias=lp[:, 6, c:c + 1]), reads=[pin, 'p2_lp'], writes=[inn])
                    an, a = ar.next()
                    S.op('act', lambda e: e.activation(a[:, :], r[:, :], AF.Exp, scale=kap[:, 0, c:c + 1]), reads=[rn, 'p2_kap'], writes=[an])
                    a2n, a2 = a2r.next()
                    S.op('act', lambda e: e.activation(a2[:, :], r[:, :], AF.Exp, scale=kap[:, 1, c:c + 1]), reads=[rn, 'p2_kap'], writes=[a2n])
                    S.op('dve', lambda e: e.tensor_scalar(a2[:, :], a2[:, :], -1.0, 1.0, ALU.mult, ALU.add), reads=[a2n], writes=[a2n])
                    S.op('act', lambda e: e.activation(a2[:, :], a2[:, :], AF.Sqrt), reads=[a2n], writes=[a2n])
                    S.op('dve', lambda e: e.tensor_tensor(iv[:, :], iv[:, :], a2[:, :], ALU.mult), reads=[inn, a2n], writes=[inn])
                    S.op('dve', lambda e: e.tensor_tensor(iv[:, :], iv[:, :], xc[:, :], ALU.mult), reads=[inn, xcn], writes=[inn])
                    hn, h = hr.next()
                    S.op('dve', lambda e: e.tensor_tensor_scan(h[:, :], a[:, :], iv[:, :], state[:, c:c + 1], ALU.mult, ALU.add),
                         reads=[an, inn, ('p2_state', c)], writes=[hn])
                    S.op('act', lambda e: e.copy(state[:, c:c + 1], h[:, T - 1:T]), reads=[hn], writes=[('p2_state', c)])
                    yn, y = yr.next()
                    S.op('dve', lambda e: e.tensor_tensor(y[:, :], h[:, :], gg[:, :], ALU.mult), reads=[hn, gn], writes=[yn])
                    S.dma('sp', yAv[:, c, t0:t0 + T], y[:, :], reads=[yn], writes=['yA'])
                for c in range(0, 6, 3):
                    S.interleave([lambda c=c: chunk(c), lambda c=c: chunk(c + 1), lambda c=c: chunk(c + 2)])

    def phase_attn(self, l, zQKV, oG, amask):
        nc, S, L = self.nc, self.S, self.L
        with ExitStack() as st:
            mk = self.sb(st, "p3_mk", [128, 2, 12, 256], F32)
            idb = self.sb(st, "p3_idb", [128, 128], BF16)
            S.dma('sp', mk[:], amask, writes=['p3_mk'])
            S.op('dve', lambda e: e.tensor_copy(idb[:], self.ident_f[:]), reads=['ident_f'], writes=['p3_idb'])
            extra = Ring(nc, st, "p3_psx", 2, [128, 512], F32, psum=True)
            psum = Ring.__new__(Ring)
            psum.bufs = list(self.psum.bufs) + list(extra.bufs)
            psum.i = 0
            qr = Ring(nc, st, "p3_q", 4, [128, 256], BF16)
            kr = Ring(nc, st, "p3_k", 4, [128, 256], BF16)
            vr = Ring(nc, st, "p3_v", 5, [128, 4, 128], BF16)
            qkr = Ring(nc, st, "p3_qk", 4, [128, 8, 128], BF16)
            scr = Ring(nc, st, "p3_sc", 4, [128, 512], F32)
            ptr = Ring(nc, st, "p3_pt", 6, [128, 2, 2, 128], BF16)
            osr = Ring(nc, st, "p3_os", 4, [128, 260], F32)
            for (vn, v) in vr.bufs:
                S.op('pool', lambda e: e.memset(v[:], 1.0), writes=[vn])
            box = {}

            def unit(g, d, r, blk):
                zv = zQKV.rearrange("(n d) c -> d n c", d=d)
                ov = oG[g].rearrange("(n d) c -> d n c", d=d)
                rows = slice(blk * 128, (blk + 1) * 128)
                qn, q = qr.next()
                kn, k_ = kr.next()
                vn, v = vr.next()
                S.dma('sp', q[:, :], zv[r, rows, 256 * g:256 * g + 256], reads=['zQKV'], writes=[qn])
                S.dma('sp', k_[:, :], zv[r, rows, 768 + 256 * g:768 + 256 * g + 256], reads=['zQKV'], writes=[kn])
                S.dma('sp', v[:, :, 0:64], zv[r, rows, 1536 + 256 * g:1536 + 256 * g + 256].rearrange("p (h e) -> p h e", e=64),
                      reads=['zQKV'], writes=[vn])
                ptn, pT = psum.next()
                for j in range(4):
                    S.op('pe', lambda e, j=j: e.matmul(pT[0:64, j * 128:(j + 1) * 128], lhsT=q[:, j * 64:(j + 1) * 64], rhs=idb[:], start=True, stop=True),
                         reads=[qn, 'p3_idb'], writes=[ptn], inc=(j == 3))
                ptn2, pT2 = psum.next()
                for j in range(4):
                    S.op('pe', lambda e, j=j: e.matmul(pT2[0:64, j * 128:(j + 1) * 128], lhsT=k_[:, j * 64:(j + 1) * 64], rhs=idb[:], start=True, stop=True),
                         reads=[kn, 'p3_idb'], writes=[ptn2], inc=(j == 3))
                qkn, qk = qkr.next()
                S.op('dve', lambda e: e.tensor_copy(qk[0:64, 0:4, :], pT[0:64, 0:512].rearrange("p (a b) -> p a b", a=4)), reads=[ptn], writes=[qkn])
                S.op('dve', lambda e: e.tensor_copy(qk[0:64, 4:8, :], pT2[0:64, 0:512].rearrange("p (a b) -> p a b", a=4)), reads=[ptn2], writes=[qkn])
                qTn, qT = qkn, qk[:, 0:4, :]
                kTn, kT = qkn, qk[:, 4:8, :]
                first = blk == 0
                if first:
                    kTp_n, kTp, vp_n, vp = kTn, kT, vn, v
                else:
                    kTp_n, kTp, vp_n, vp = box['prev']
                box['prev'] = (kTn, kT, vn, v)
                pts = []
                for pair in range(2):
                    pn, ps = psum.next()
                    for h2 in range(2):
                        hx = 2 * pair + h2
                        col = h2 * 256
                        if not first:
                            S.op('pe', lambda e, hx=hx, col=col, ps=ps: e.matmul(ps[:, col:col + 128], lhsT=kTp[0:64, hx, :], rhs=qT[0:64, hx, :], start=True, stop=True),
                                 reads=[kTp_n, qTn], writes=[pn], inc=False)
                        S.op('pe', lambda e, hx=hx, col=col, ps=ps: e.matmul(ps[:, col + 128:col + 256], lhsT=kT[0:64, hx, :], rhs=qT[0:64, hx, :], start=True, stop=True),
                             reads=[kTn, qTn], writes=[pn], inc=(h2 == 1))
                    scn, sc = scr.next()
                    hh0 = 4 * g + 2 * pair
                    mview = mk[:, 1 if first else 0, hh0:hh0 + 2, :].rearrange("p a b -> p (a b)")
                    if first:
                        S.op('pool', lambda e, sc=sc: e.memset(sc[:, :], 0.0), writes=[scn])
                        for h2 in range(2):
                            col = h2 * 256
                            S.op('dve', lambda e, col=col, sc=sc, ps=ps, mview=mview: e.scalar_tensor_tensor(sc[:, col + 128:col + 256], ps[:, col + 128:col + 256], 0.125, mview[:, col + 128:col + 256], ALU.mult, ALU.add),
                                 reads=[pn, 'p3_mk'], writes=[scn])
                            S.op('dve', lambda e, col=col, sc=sc, mview=mview: e.tensor_copy(sc[:, col:col + 128], mview[:, col:col + 128]), reads=['p3_mk'], writes=[scn])
                    else:
                        S.op('dve', lambda e, sc=sc, ps=ps, mview=mview: e.scalar_tensor_tensor(sc[:, :], ps[:, :], 0.125, mview, ALU.mult, ALU.add),
                             reads=[pn, 'p3_mk'], writes=[scn])
                    pn2, pt = ptr.next()
                    S.op('act', lambda e, pt=pt, sc=sc: e.activation(pt[:, :, :, :].rearrange("p a b c -> p (a b c)"), sc[:, :], AF.Exp), reads=[scn], writes=[pn2])
                    pts.append((pn2, pt))
                pn, ps = psum.next()
                for hh in range(4):
                    pn2, pt = pts[hh // 2]
                    S.op('pe', lambda e, hh=hh, pt=pt: e.matmul(ps[:, hh * 128:hh * 128 + 65], lhsT=pt[:, hh % 2, 0, :], rhs=vp[:, hh, 0:65], start=True, stop=False),
                         reads=[pn2, vp_n], writes=[pn], inc=False)
                    S.op('pe', lambda e, hh=hh, pt=pt: e.matmul(ps[:, hh * 128:hh * 128 + 65], lhsT=pt[:, hh % 2, 1, :], rhs=v[:, hh, 0:65], start=False, stop=True),
                         reads=[pn2, vn], writes=[pn], inc=(hh == 3))
                on, o = osr.next()
                S.op('act', lambda e: e.copy(o[:, :].rearrange("p (h e) -> p h e", e=65), ps[:, :].rearrange("p (h e) -> p h e", e=128)[:, :, 0:65]), reads=[pn], writes=[on])
                S.dma('sp', ov[r, rows, :], o[:, :], reads=[on], writes=['oG'])

            for g, d in enumerate(DILS):
                nb = L // (128 * d)
                for r in range(d):
                    if nb % 2 == 0:
                        for blk in range(0, nb, 2):
                            S.interleave([lambda blk=blk: unit(g, d, r, blk), lambda blk=blk: unit(g, d, r, blk + 1)])
                    else:
                        for blk in range(nb):
                            unit(g, d, r, blk)

    def sincos(self, st, th_name, th, n, out_s, out_c, key):
        S = self.S
        TWO_PI = 2.0 * math.pi
        ti = self.sb(st, "sc_i", [128, n], mybir.dt.int32)
        tf = self.sb(st, "sc_f", [128, n], F32)
        ph = self.sb(st, "sc_p", [128, n], F32)
        mm = self.sb(st, "sc_m", [128, n], F32)
        for (shift, outt) in ((0.0, out_s), (math.pi / 2, out_c)):
            S.op('dve', lambda e: e.tensor_scalar(ph[:, :], th, 1.0 / TWO_PI, shift / TWO_PI, ALU.mult, ALU.add), reads=[th_name], writes=[key + 'ph'])
            S.op('dve', lambda e: e.tensor_copy(ti[:, :], ph[:, :]), reads=[key + 'ph'], writes=[key + 'ti'])
            S.op('dve', lambda e: e.tensor_copy(tf[:, :], ti[:, :]), reads=[key + 'ti'], writes=[key + 'tf'])
            S.op('dve', lambda e: e.tensor_tensor(ph[:, :], ph[:, :], tf[:, :], ALU.subtract), reads=[key + 'ph', key + 'tf'], writes=[key + 'ph'])
            S.op('dve', lambda e: e.tensor_scalar(mm[:, :], ph[:, :], 0.5, None, ALU.is_gt), reads=[key + 'ph'], writes=[key + 'mm'])
            S.op('dve', lambda e: e.tensor_tensor(ph[:, :], ph[:, :], mm[:, :], ALU.subtract), reads=[key + 'ph', key + 'mm'], writes=[key + 'ph'])
            S.op('dve', lambda e: e.tensor_scalar(mm[:, :], ph[:, :], -0.5, None, ALU.is_lt), reads=[key + 'ph'], writes=[key + 'mm'])
            S.op('dve', lambda e: e.tensor_tensor(ph[:, :], ph[:, :], mm[:, :], ALU.add), reads=[key + 'ph', key + 'mm'], writes=[key + 'ph'])
            S.op('dve', lambda e: e.tensor_scalar(ph[:, :], ph[:, :], -0.4999, 0.4999, ALU.max, ALU.min), reads=[key + 'ph'], writes=[key + 'ph'])
            S.op('act', lambda e: e.activation(outt, ph[:, :], AF.Sin, scale=TWO_PI), reads=[key + 'ph'], writes=[key + 'out'])

    def phase_s5(self, l, zF, yC, s5A, s5R, s5B, s5C, ssmd):
        nc, S, L = self.nc, self.S, self.L
        with ExitStack() as st:
            rho = self.sb(st, "p4_rho", [128, 24], F32)
            BbR = self.sb(st, "p4_BbR", [128, 24, 128], BF16)
            BbI = self.sb(st, "p4_BbI", [128, 24, 128], BF16)
            CtR = self.sb(st, "p4_CtR", [128, 24, 128], BF16)
            CtI = self.sb(st, "p4_CtI", [128, 24, 128], BF16)
            dsk = self.sb(st, "p4_d", [128, 6], F32)
            xst = self.sb(st, "p4_xst", [128, 2, 24], F32)
            S.dma('sp', dsk[:], ssmd[:, l, :], writes=['p4_d'])
            S.op('dve', lambda e: e.memset(xst[:], 0.0), writes=[('p4_xst', p) for p in range(24)])
            S.dma('pool', CtR[:], s5C[l, 0], writes=['p4_CtR'])
            with ExitStack() as s2:
                N = 3072
                par = self.sb(s2, "p4s_par", [128, 3, N], F32)
                S.dma('sp', par[:], s5R[:, l, :, :], writes=['p4s_par'])
                ar, ai, ld = par[:, 0, :], par[:, 1, :], par[:, 2, :]
                dt = self.sb(s2, "p4s_dt", [128, N], F32)
                mag = self.sb(st if False else s2, "p4s_mag", [128, N], F32)
                th = self.sb(s2, "p4s_th", [128, N], F32)
                sn = self.sb(s2, "p4s_sn", [128, N], F32)
                cs = self.sb(s2, "p4s_cs", [128, N], F32)
                S.op('act', lambda e: e.activation(dt[:, :], ld, AF.Exp), reads=['p4s_par'], writes=['p4s_dt'])
                S.op('dve', lambda e: e.tensor_tensor(mag[:, :], ar, dt[:, :], ALU.mult), reads=['p4s_par', 'p4s_dt'], writes=['p4s_mag'])
                S.op('act', lambda e: e.activation(mag[:, :], mag[:, :], AF.Exp), reads=['p4s_mag'], writes=['p4s_mag'])
                S.op('dve', lambda e: e.tensor_tensor(th[:, :], ai, dt[:, :], ALU.mult), reads=['p4s_par', 'p4s_dt'], writes=['p4s_th'])
                for hf in range(2):
                    with ExitStack() as s3:
                        sl = slice(hf * 1536, (hf + 1) * 1536)
                        self.sincos(s3, 'p4s_th', th[:, sl], 1536, sn[:, sl], cs[:, sl], 'scB')
                        S.barrier()
                S.op('dve', lambda e: e.tensor_tensor(cs[:, :], cs[:, :], mag[:, :], ALU.mult), reads=['scBout', 'p4s_mag'], writes=['p4s_cs'])
                S.op('dve', lambda e: e.tensor_scalar(cs[:, :], cs[:, :], -1.0, None, ALU.add), reads=['p4s_cs'], writes=['p4s_cs'])
                S.op('dve', lambda e: e.tensor_tensor(sn[:, :], sn[:, :], mag[:, :], ALU.mult), reads=['scBout', 'p4s_mag'], writes=['p4s_sn'])
                S.op('dve', lambda e: e.tensor_tensor(dt[:, :], ar, ar, ALU.mult), reads=['p4s_par'], writes=['p4s_dt'])
                S.op('dve', lambda e: e.tensor_tensor(th[:, :], ai, ai, ALU.mult), reads=['p4s_par'], writes=['p4s_th'])
                S.op('dve', lambda e: e.tensor_tensor(dt[:, :], dt[:, :], th[:, :], ALU.add), reads=['p4s_dt', 'p4s_th'], writes=['p4s_dt'])
                S.op('dve', lambda e: e.reciprocal(dt[:, :], dt[:, :]), reads=['p4s_dt'], writes=['p4s_dt'])
                t1 = self.sb(s2, "p4s_t1", [128, N], F32)
                S.op('dve', lambda e: e.tensor_tensor(mag[:, :], cs[:, :], ar, ALU.mult), reads=['p4s_cs', 'p4s_par'], writes=['p4s_mag'])
                S.op('dve', lambda e: e.tensor_tensor(t1[:, :], sn[:, :], ai, ALU.mult), reads=['p4s_sn', 'p4s_par'], writes=['p4s_t1'])
                S.op('dve', lambda e: e.tensor_tensor(mag[:, :], mag[:, :], t1[:, :], ALU.add), reads=['p4s_mag', 'p4s_t1'], writes=['p4s_mag'])
                S.op('dve', lambda e: e.tensor_tensor(mag[:, :], mag[:, :], dt[:, :], ALU.mult), reads=['p4s_mag', 'p4s_dt'], writes=['p4s_mag'])
                S.op('dve', lambda e: e.tensor_tensor(th[:, :], sn[:, :], ar, ALU.mult), reads=['p4s_sn', 'p4s_par'], writes=['p4s_th'])
                S.op('dve', lambda e: e.tensor_tensor(t1[:, :], cs[:, :], ai, ALU.mult), reads=['p4s_cs', 'p4s_par'], writes=['p4s_t1'])
                S.op('dve', lambda e: e.tensor_tensor(th[:, :], th[:, :], t1[:, :], ALU.subtract), reads=['p4s_th', 'p4s_t1'], writes=['p4s_th'])
                S.op('dve', lambda e: e.tensor_tensor(th[:, :], th[:, :], dt[:, :], ALU.mult), reads=['p4s_th', 'p4s_dt'], writes=['p4s_th'])
                zr, zi = mag, th
                bre = self.sb(s2, "p4s_bre", [128, N], F32)
                bim = self.sb(s2, "p4s_bim", [128, N], F32)
                S.dma('sp', bre[:, :], s5B[l, 0].rearrange("p a b -> p (a b)"), writes=['p4s_bre'])
                S.dma('sp', bim[:, :], s5B[l, 1].rearrange("p a b -> p (a b)"), writes=['p4s_bim'])
                S.op('dve', lambda e: e.tensor_tensor(t1[:, :], zr[:, :], bre[:, :], ALU.mult), reads=['p4s_mag', 'p4s_bre'], writes=['p4s_t1'])
                S.op('dve', lambda e: e.tensor_tensor(cs[:, :], zi[:, :], bim[:, :], ALU.mult), reads=['p4s_th', 'p4s_bim'], writes=['p4s_cs'])
                S.op('dve', lambda e: e.tensor_tensor(BbR[:, :, :].rearrange("p a b -> p (a b)"), t1[:, :], cs[:, :], ALU.subtract), reads=['p4s_t1', 'p4s_cs'], writes=['p4_BbR'])
                S.op('dve', lambda e: e.tensor_tensor(t1[:, :], zr[:, :], bim[:, :], ALU.mult), reads=['p4s_mag', 'p4s_bim'], writes=['p4s_t1'])
                S.op('dve', lambda e: e.tensor_tensor(cs[:, :], zi[:, :], bre[:, :], ALU.mult), reads=['p4s_th', 'p4s_bre'], writes=['p4s_cs'])
                S.op('dve', lambda e: e.tensor_tensor(BbI[:, :, :].rearrange("p a b -> p (a b)"), t1[:, :], cs[:, :], ALU.add), reads=['p4s_t1', 'p4s_cs'], writes=['p4_BbI'])
                S.dma('sp', bre[:, :], s5C[l, 1].rearrange("p a b -> p (a b)"), reads=['p4s_bre'], writes=['p4s_bre'])
                S.op('act', lambda e: e.mul(CtI[:, :, :].rearrange("p a b -> p (a b)"), bre[:, :], -1.0), reads=['p4s_bre'], writes=['p4_CtI'])
                S.barrier()
            cosT = self.sb(st, "p4_cos", [128, 24, T], F32)
            sinT = self.sb(st, "p4_sin", [128, 24, T], F32)
            with ExitStack() as s2:
                pa = self.sb(s2, "p4a_par", [128, 3, 24], F32)
                S.dma('sp', pa[:], s5A[:, l, :, :], writes=['p4a_par'])
                dt = self.sb(s2, "p4a_dt", [128, 24], F32)
                th = self.sb(s2, "p4a_th", [128, 24], F32)
                sn = self.sb(s2, "p4a_sn", [128, 24], F32)
                cs = self.sb(s2, "p4a_cs", [128, 24], F32)
                S.op('act', lambda e: e.activation(dt[:, :], pa[:, 2, :], AF.Exp), reads=['p4a_par'], writes=['p4a_dt'])
                S.op('dve', lambda e: e.tensor_tensor(rho[:, :], pa[:, 0, :], dt[:, :], ALU.mult), reads=['p4a_par', 'p4a_dt'], writes=['p4_rho'])
                S.op('act', lambda e: e.activation(rho[:, :], rho[:, :], AF.Exp), reads=['p4_rho'], writes=['p4_rho'])
                S.op('dve', lambda e: e.tensor_tensor(th[:, :], pa[:, 1, :], dt[:, :], ALU.mult), reads=['p4a_par', 'p4a_dt'], writes=['p4a_th'])
                self.sincos(s2, 'p4a_th', th[:, :], 24, sn[:, :], cs[:, :], 'scA')
                tmp = self.sb(s2, "p4a_tmp", [128, T // 2], F32)
                for p in range(24):
                    S.op('act', lambda e: e.copy(cosT[:, p, 0:1], cs[:, p:p + 1]), reads=['scAout'], writes=[('p4_cos', p)])
                    S.op('act', lambda e: e.copy(sinT[:, p, 0:1], sn[:, p:p + 1]), reads=['scAout'], writes=[('p4_sin', p)])
                    w = 1
                    while w < T:
                        cw, sw = cosT[:, p, w - 1:w], sinT[:, p, w - 1:w]
                        S.op('dve', lambda e: e.tensor_scalar(tmp[:, 0:w], sinT[:, p, 0:w], sw, None, ALU.mult), reads=[('p4_sin', p)], writes=['p4a_tmp'])
                        S.op('dve', lambda e: e.scalar_tensor_tensor(cosT[:, p, w:2 * w], cosT[:, p, 0:w], cw, tmp[:, 0:w], ALU.mult, ALU.subtract),
                             reads=[('p4_cos', p), 'p4a_tmp'], writes=[('p4_cos', p)])
                        S.op('dve', lambda e: e.tensor_scalar(tmp[:, 0:w], cosT[:, p, 0:w], sw, None, ALU.mult), reads=[('p4_cos', p)], writes=['p4a_tmp'])
                        S.op('dve', lambda e: e.scalar_tensor_tensor(sinT[:, p, w:2 * w], sinT[:, p, 0:w], cw, tmp[:, 0:w], ALU.mult, ALU.add),
                             reads=[('p4_sin', p), ('p4_cos', p), 'p4a_tmp'], writes=[('p4_sin', p)])
                        w *= 2
                S.barrier()
            wg = self.sb(st, "p4_wg", [128, 6, 1536], BF16)
            S.dma('sp', wg[:], self.wB["ssm_glu"][l].rearrange("(k p) n -> p k n", p=128), reads=[('ssm_gluB', l)], writes=['p4_wg'])
            psy = Ring(nc, st, "p4_psy", 2, [128, 512], F32, psum=True)
            usr = Ring(nc, st, "p4_us", 1, [128, 6, T], BF16)
            tr = Ring(nc, st, "p4_t", 16, [128, T], F32)
            stmp = self.sb(st, "p4_stmp", [128, 24, 2], F32)
            xr = Ring(nc, st, "p4_x", 4, [128, T], F32)
            xbr = Ring(nc, st, "p4_xb", 4, [128, T], BF16)
            yfr = Ring(nc, st, "p4_yf", 2, [128, T], F32)
            gy = self.sb(st, "p4_gy", [128, 6, T], BF16)
            sgr = Ring(nc, st, "p4_sg", 2, [128, T], F32)
            ycr = Ring(nc, st, "p4_yc", 2, [128, T], BF16)
            zFv = zF.rearrange("(c p) l -> p c l", p=128)
            yCv = yC.rearrange("(c p) l -> p c l", p=128)
            for ti in range(L // T):
                t0 = ti * T
                un, us = usr.next()
                S.dma('sp', us[:], zFv[:, 12:18, t0:t0 + T], reads=['zF'], writes=[un])
                PP = {}

                def stageA0(p):
                    ch = p // 4
                    prn, pr = self.psum.next()
                    S.op('pe', lambda e: e.matmul(pr[:, :], lhsT=BbR[:, p, :], rhs=us[:, ch, :], start=True, stop=True), reads=['p4_BbR', un], writes=[prn])
                    pin, pi = self.psum.next()
                    S.op('pe', lambda e: e.matmul(pi[:, :], lhsT=BbI[:, p, :], rhs=us[:, ch, :], start=True, stop=True), reads=['p4_BbI', un], writes=[pin])
                    PP[p] = dict(pr=(prn, pr), pi=(pin, pi))

                def stageA(p):
                    c_, s_ = cosT[:, p, :], sinT[:, p, :]
                    prn, pr = PP[p]['pr']; pin, pi = PP[p]['pi']
                    t1n, t1 = tr.next(); t2n, t2 = tr.next(); t3n, t3 = tr.next(); t4n, t4 = tr.next()
                    S.op('dve', lambda e: e.tensor_tensor(t1[:, :], pr[:, :], c_, ALU.mult), reads=[prn, ('p4_cos', p)], writes=[t1n])
                    S.op('dve', lambda e: e.tensor_tensor(t2[:, :], pi[:, :], s_, ALU.mult), reads=[pin, ('p4_sin', p)], writes=[t2n])
                    S.op('dve', lambda e: e.tensor_tensor(t3[:, :], pi[:, :], c_, ALU.mult), reads=[pin, ('p4_cos', p)], writes=[t3n])
                    S.op('dve', lambda e: e.tensor_tensor(t4[:, :], pr[:, :], s_, ALU.mult), reads=[prn, ('p4_sin', p)], writes=[t4n])
                    S.op('dve', lambda e: e.tensor_tensor(t1[:, :], t1[:, :], t2[:, :], ALU.add), reads=[t1n, t2n], writes=[t1n])
                    S.op('dve', lambda e: e.tensor_tensor(t3[:, :], t3[:, :], t4[:, :], ALU.subtract), reads=[t3n, t4n], writes=[t3n])
                    PP[p].update(t1=(t1n, t1), t3=(t3n, t3))

                def stageC(p):
                    c_, s_ = cosT[:, p, :], sinT[:, p, :]
                    t1n, t1 = PP[p]['t1']; t3n, t3 = PP[p]['t3']
                    rb = rho[:, p:p + 1].to_broadcast([128, T])
                    xrn, xre = xr.next(); xin, xim = xr.next()
                    S.op('dve', lambda e: e.tensor_tensor_scan(xre[:, :], rb, t1[:, :], xst[:, 0, p:p + 1], ALU.mult, ALU.add),
                         reads=['p4_rho', t1n, ('p4_xst', p)], writes=[xrn])
                    S.op('dve', lambda e: e.tensor_tensor_scan(xim[:, :], rb, t3[:, :], xst[:, 1, p:p + 1], ALU.mult, ALU.add),
                         reads=['p4_rho', t3n, ('p4_xst', p)], writes=[xin])
                    cl, sl = cosT[:, p, T - 1:T], sinT[:, p, T - 1:T]
                    S.op('act', lambda e: e.activation(stmp[:, p, 0:1], xim[:, T - 1:T], AF.Copy, scale=sl), reads=[xin, ('p4_sin', p)], writes=[('p4_stmp', p)])
                    S.op('act', lambda e: e.activation(stmp[:, p, 1:2], xim[:, T - 1:T], AF.Copy, scale=cl), reads=[xin, ('p4_cos', p)], writes=[('p4_stmp', p)])
                    u1n, u1 = tr.next(); u2n, u2 = tr.next(); u3n, u3 = tr.next(); u4n, u4 = tr.next()
                    S.op('dve', lambda e: e.tensor_tensor(u1[:, :], xre[:, :], c_, ALU.mult), reads=[xrn, ('p4_cos', p)], writes=[u1n])
                    S.op('dve', lambda e: e.tensor_tensor(u2[:, :], xim[:, :], s_, ALU.mult), reads=[xin, ('p4_sin', p)], writes=[u2n])
                    S.op('dve', lambda e: e.tensor_tensor(u3[:, :], xre[:, :], s_, ALU.mult), reads=[xrn, ('p4_sin', p)], writes=[u3n])
                    S.op('dve', lambda e: e.tensor_tensor(u4[:, :], xim[:, :], c_, ALU.mult), reads=[xin, ('p4_cos', p)], writes=[u4n])
                    S.op('dve', lambda e: e.scalar_tensor_tensor(xst[:, 0, p:p + 1], xre[:, T - 1:T], cl, stmp[:, p, 0:1], ALU.mult, ALU.subtract),
                         reads=[xrn, ('p4_stmp', p), ('p4_cos', p)], writes=[('p4_xst', p)])
                    S.op('dve', lambda e: e.scalar_tensor_tensor(xst[:, 1, p:p + 1], xre[:, T - 1:T], sl, stmp[:, p, 1:2], ALU.mult, ALU.add),
                         reads=[xrn, ('p4_stmp', p), ('p4_sin', p)], writes=[('p4_xst', p)])
                    PP[p].update(u1=(u1n, u1), u2=(u2n, u2), u3=(u3n, u3), u4=(u4n, u4))

                def stageE(p):
                    u1n, u1 = PP[p]['u1']; u2n, u2 = PP[p]['u2']; u3n, u3 = PP[p]['u3']; u4n, u4 = PP[p]['u4']
                    xbrn, xbre = xbr.next(); xbin, xbim = xbr.next()
                    S.op('dve', lambda e: e.tensor_tensor(xbre[:, :], u1[:, :], u2[:, :], ALU.subtract), reads=[u1n, u2n], writes=[xbrn])
                    S.op('dve', lambda e: e.tensor_tensor(xbim[:, :], u3[:, :], u4[:, :], ALU.add), reads=[u3n, u4n], writes=[xbin])
                    if p % 4 == 0:
                        PP['py'] = psy.next()
                    pyn, py = PP['py']
                    S.op('pe', lambda e: e.matmul(py[:, :], lhsT=CtR[:, p, :], rhs=xbre[:, :], start=(p % 4 == 0), stop=False), reads=['p4_CtR', xbrn], writes=[pyn], inc=False)
                    S.op('pe', lambda e: e.matmul(py[:, :], lhsT=CtI[:, p, :], rhs=xbim[:, :], start=False, stop=(p % 4 == 3)), reads=['p4_CtI', xbin], writes=[pyn])
                    if p % 4 == 3:
                        oc = p // 4
                        yfn, yf = yfr.next()
                        S.op('dve', lambda e: e.scalar_tensor_tensor(yf[:, :], us[:, oc, :], dsk[:, oc:oc + 1], py[:, :], ALU.mult, ALU.add),
                             reads=[un, 'p4_d', pyn], writes=[yfn])
                        S.op('act', lambda e: e.activation(gy[:, oc, :], yf[:, :], AF.Gelu_apprx_tanh), reads=[yfn], writes=[('p4_gy', oc)])
                    del PP[p]

                stageA0(0)
                for it in range(24 + 2):
                    if it + 1 < 24:
                        stageA0(it + 1)
                    lists = []
                    for (fn_, arg, ok) in ((stageA, it, it < 24), (stageC, it - 1, 1 <= it <= 24), (stageE, it - 2, it >= 2)):
                        if ok:
                            S.defer = []
                            fn_(arg)
                            lists.append(S.defer)
                            S.defer = None
                    while any(lists):
                        for lst in lists:
                            if lst:
                                S.run_deferred(lst.pop(0))
                for j in range(6):
                    pan, pa_ = self.psum.next()
                    for k in range(6):
                        S.op('pe', lambda e: e.matmul(pa_[:, :], lhsT=wg[:, k, j * 128:(j + 1) * 128], rhs=gy[:, k, :], start=(k == 0), stop=(k == 5)),
                             reads=['p4_wg', ('p4_gy', k)], writes=[pan], inc=(k == 5))
                    pbn, pb_ = self.psum.next()
                    for k in range(6):
                        S.op('pe', lambda e: e.matmul(pb_[:, :], lhsT=wg[:, k, 768 + j * 128:768 + (j + 1) * 128], rhs=gy[:, k, :], start=(k == 0), stop=(k == 5)),
                             reads=['p4_wg', ('p4_gy', k)], writes=[pbn], inc=(k == 5))
                    sgn, sg = sgr.next()
                    S.op('act', lambda e: e.activation(sg[:, :], pb_[:, :], AF.Sigmoid), reads=[pbn], writes=[sgn])
                    ycn, yc = ycr.next()
                    S.op('dve', lambda e: e.tensor_tensor(yc[:, :], pa_[:, :], sg[:, :], ALU.mult), reads=[pan, sgn], writes=[ycn])
                    S.dma('sp', yCv[:, j, t0:t0 + T], yc[:, :], reads=[ycn], writes=['yC'])

    def phase_xattn(self, l, mem, zF, yX):
        nc, S, L = self.nc, self.S, self.L
        with ExitStack() as st:
            kT = self.sb(st, "p5_kT", [128, 4, 2, 256], BF16)
            vm = self.sb(st, "p5_vm", [128, 2, 768], BF16)
            with ExitStack() as st2:
                memt = self.sb(st2, "p5_mem", [128, 2, D], F32)
                memT = self.sb(st2, "p5_memT", [128, 8, 256], F32)
                sq = self.sb(st2, "p5_sq", [128, 8, 256], BF16)
                rstd = self.sb(st2, "p5_rstd", [128, 256], F32)
                mn = self.sb(st2, "p5_mn", [128, 8, 256], BF16)
                wkv = self.sb(st2, "p5_wkv", [128, 8, 1536], BF16)
                S.dma('sp', memt[:], mem.rearrange("(b p) d -> p b d", p=128), writes=['p5_mem'])
                S.dma('sp', wkv[:], self.wB["mem_wkv"][l].rearrange("(k p) n -> p k n", p=128), reads=[('mem_wkvB', l)], writes=['p5_wkv'])
                for b in range(2):
                    for half in range(2):
                        pn, ps = self.psum.next()
                        for j in range(4):
                            c = half * 4 + j
                            S.op('pe', lambda e: e.transpose(ps[:, j * 128:(j + 1) * 128], memt[:, b, c * 128:(c + 1) * 128], self.ident_f[:]),
                                 reads=['p5_mem', 'ident_f'], writes=[pn], inc=(j == 3))
                        S.op('dve', lambda e: e.tensor_copy(memT[:, half * 4:(half + 1) * 4, b * 128:(b + 1) * 128], ps[:, :].rearrange("p (a b) -> p a b", a=4)),
                             reads=[pn], writes=['p5_memT'])
                for c in range(8):
                    S.op('act', lambda e: e.activation(sq[:, c, :], memT[:, c, :], AF.Square), reads=['p5_memT'], writes=['p5_sq'])
                self.rms_stats('p5_sq', sq, 'p5_rstd', rstd, 256)
                for c in range(8):
                    S.op('dve', lambda e: e.scalar_tensor_tensor(mn[:, c, :], memT[:, c, :], self.gains_s[:, l, 2, c:c + 1], rstd[:, :], ALU.mult, ALU.mult),
                         reads=['p5_memT', 'p5_rstd', 'gains_s'], writes=['p5_mn'])
                for h in range(4):
                    for part, (off, M) in enumerate(((0, 128), (128, 64))):
                        col = 192 * h + off
                        pn, ps = self.psum.next()
                        for k in range(8):
                            S.op('pe', lambda e: e.matmul(ps[:M, :256], lhsT=wkv[:, k, col:col + M], rhs=mn[:, k, :], start=(k == 0), stop=(k == 7)),
                                 reads=['p5_wkv', 'p5_mn'], writes=[pn], inc=(k == 7))
                        S.op('dve', lambda e: e.tensor_copy(kT[:M, h, part, :], ps[:M, :256]), reads=[pn], writes=['p5_kT'])
                for mc in range(2):
                    for (n0, nn) in ((0, 512), (512, 256)):
                        pn, ps = self.psum.next()
                        for k in range(8):
                            S.op('pe', lambda e: e.matmul(ps[:, :nn], lhsT=mn[:, k, mc * 128:(mc + 1) * 128], rhs=wkv[:, k, 768 + n0:768 + n0 + nn], start=(k == 0), stop=(k == 7)),
                                 reads=['p5_wkv', 'p5_mn'], writes=[pn], inc=(k == 7))
                        S.op('dve', lambda e: e.tensor_copy(vm[:, mc, n0:n0 + nn], ps[:, :nn]), reads=[pn], writes=['p5_vm'])
                S.barrier()
            xq0r = Ring(nc, st, "p5_xq0", 2, [128, T], BF16)
            xq1r = Ring(nc, st, "p5_xq1", 2, [128, T], BF16)
            ptr = Ring(nc, st, "p5_pt", 4, [128, T], BF16)
            rcr = Ring(nc, st, "p5_rc", 2, [128, T], F32)
            y0r = Ring(nc, st, "p5_y0", 2, [128, T], BF16)
            y1r = Ring(nc, st, "p5_y1", 2, [128, T], BF16)
            XQ0 = 18 * 128
            sc = 192.0 ** -0.5
            for ti in range(L // T):
                t0 = ti * T
                for h in range(4):
                    q0n, q0 = xq0r.next()
                    q1n, q1 = xq1r.next()
                    r0 = XQ0 + 192 * h
                    S.dma('sp', q0[:, :], zF[r0:r0 + 128, t0:t0 + T], reads=['zF'], writes=[q0n])
                    S.dma('sp', q1[0:64, :], zF[r0 + 128:r0 + 192, t0:t0 + T], reads=['zF'], writes=[q1n])
                    pts = []
                    for mc in range(2):
                        pn, ps = self.psum.next()
                        S.op('pe', lambda e: e.matmul(ps[:, :], lhsT=kT[:, h, 0, mc * 128:(mc + 1) * 128], rhs=q0[:, :], start=True, stop=False),
                             reads=['p5_kT', q0n], writes=[pn], inc=False)
                        S.op('pe', lambda e: e.matmul(ps[:, :], lhsT=kT[0:64, h, 1, mc * 128:(mc + 1) * 128], rhs=q1[0:64, :], start=False, stop=True),
                             reads=['p5_kT', q1n], writes=[pn])
                        ptn, pt = ptr.next()
                        S.op('act', lambda e: e.activation(pt[:, :], ps[:, :], AF.Exp, scale=sc), reads=[pn], writes=[ptn])
                        pts.append((ptn, pt))
                    pn, ps = self.psum.next()
                    for mc in range(2):
                        S.op('pe', lambda e: e.matmul(ps[:, :], lhsT=self.ones_b[:], rhs=pts[mc][1][:, :], start=(mc == 0), stop=(mc == 1)),
                             reads=['ones_b', pts[mc][0]], writes=[pn], inc=(mc == 1))
                    rcn, rc = rcr.next()
                    S.op('dve', lambda e: e.reciprocal(rc[:, :], ps[:, :]), reads=[pn], writes=[rcn])
                    for part, (off, M, yr_) in enumerate(((0, 128, y0r), (128, 64, y1r))):
                        pn, ps = self.psum.next()
                        for mc in range(2):
                            S.op('pe', lambda e: e.matmul(ps[:M, :], lhsT=vm[:, mc, 192 * h + off:192 * h + off + M], rhs=pts[mc][1][:, :], start=(mc == 0), stop=(mc == 1)),
                                 reads=['p5_vm', pts[mc][0]], writes=[pn], inc=(mc == 1))
                        yn, y = yr_.next()
                        S.op('dve', lambda e: e.tensor_tensor(y[:M, :], ps[:M, :], rc[:M, :], ALU.mult), reads=[pn, rcn], writes=[yn])
                        S.dma('sp', yX[192 * h + off:192 * h + off + M, t0:t0 + T], y[:M, :], reads=[yn], writes=['yX'])

    def phase_merge(self, l, xT, xmT, zF, yA, yC, yX, oG):
        nc, S, L = self.nc, self.S, self.L
        ph = self.phases
        branches = []
        if 'p2' in ph:
            branches.append((0, 'proj_a', 6))
        if 'p3' in ph:
            branches.append((1, 'proj_b', 2))
        if 'p4' in ph:
            branches.append((2, 'proj_c', 6))
        if 'p5' in ph:
            branches.append((3, 'proj_x', 6))
        with ExitStack() as st:
            wp = {}
            for (b, nm, nk) in branches:
                wp[b] = self.sb(st, "p6_" + nm, [128, nk, D], BF16)
                S.dma('sp', wp[b][:], self.wB[nm][l].rearrange("(k p) n -> p k n", p=128), reads=[(nm + 'B', l)], writes=['p6_w%d' % b])
            wo = self.sb(st, "p6_wo", [128, 8, D], BF16)
            S.dma('sp', wo[:], self.wB["w_out"][l].rearrange("(k p) n -> p k n", p=128), reads=[('w_outB', l)], writes=['p6_wo'])
            yr = {b: Ring(nc, st, "p6_y%d" % b, 1, [128, nk, T], BF16) for (b, nm, nk) in branches}
            sgr = Ring(nc, st, "p6_sg", 3, [128, 4, T], BF16)
            xr = Ring(nc, st, "p6_x", 1, [128, 8, T], F32)
            xor_ = Ring(nc, st, "p6_xo", 1, [128, 8, T], F32)
            macc = Ring(nc, st, "p6_macc", 2, [128, T], F32)
            mtmp = Ring(nc, st, "p6_mtmp", 4, [128, T], F32)
            mb = self.sb(st, "p6_mb", [128, 8, T], BF16)
            sq = self.sb(st, "p6_sq", [128, 8, T], BF16)
            rstd = self.sb(st, "p6_rstd", [128, T], F32)
            y = self.sb(st, "p6_yy", [128, 8, T], F32)
            og = Ring(nc, st, "p6_og", 2, [128, 3, 260], F32)
            ybt = Ring(nc, st, "p6_ybt", 2, [128, 256], F32)
            rl = Ring(nc, st, "p6_rl", 2, [128, 4], F32)
            srcs = {0: yA, 2: yC, 3: yX}
            for ti in range(L // T):
                t0 = ti * T
                ys = {}
                for (b, nm, nk) in branches:
                    yn, yt = yr[b].next()
                    ys[b] = (yn, yt)
                    if b != 1:
                        S.dma('sp', yt[:], srcs[b].rearrange("(c p) l -> p c l", p=128)[:, :, t0:t0 + T], reads=[srcs[b].tensor.name], writes=[yn])
                    else:
                        for tb in range(4):
                            on, o = og.next()
                            S.dma('sp', o[:], oG[:, t0 + tb * 128:t0 + (tb + 1) * 128, :].rearrange("g p n -> p g n"), reads=['oG'], writes=[on])
                            S.op('dve', lambda e: e.tensor_tensor(o[:, 0, :], o[:, 0, :], o[:, 1, :], ALU.add), reads=[on], writes=[on])
                            S.op('dve', lambda e: e.tensor_tensor(o[:, 0, :], o[:, 0, :], o[:, 2, :], ALU.add), reads=[on], writes=[on])
                            rn, r = rl.next()
                            ov = o[:, 0, :].rearrange("p (h e) -> p h e", e=65)
                            S.op('dve', lambda e: e.reciprocal(r[:, :], ov[:, :, 64]), reads=[on], writes=[rn])
                            bn, bt = ybt.next()
                            for hh in range(4):
                                S.op('dve', lambda e: e.tensor_scalar(bt[:, hh * 64:(hh + 1) * 64], ov[:, hh, 0:64], r[:, hh:hh + 1], None, ALU.mult),
                                     reads=[on, rn], writes=[bn])
                            pn, ps = self.psum.next()
                            for half in range(2):
                                S.op('pe', lambda e: e.transpose(ps[:, half * 128:(half + 1) * 128], bt[:, half * 128:(half + 1) * 128], self.ident_f[:]),
                                     reads=[bn, 'ident_f'], writes=[pn], inc=(half == 1))
                            S.op('act', lambda e: e.copy(yt[:, :, tb * 128:(tb + 1) * 128], ps[:, 0:256].rearrange("p (a b) -> p a b", a=2)),
                                 reads=[pn], writes=[yn])
                xn, xt = xr.next()
                S.dma('sp', xt[:], xT.rearrange("(c p) l -> p c l", p=128)[:, :, t0:t0 + T], reads=[('xT', ti)], writes=[xn])
                def mchunk(c):
                    mn_, m = macc.next()
                    sn, sg = sgr.next()
                    S.dma('sp', sg[:], zF.rearrange("(b c p) l -> p b c l", p=128, c=8)[:, 3:7, c, t0:t0 + T], reads=['zF'], writes=[sn])
                    for bi, (b, nm, nk) in enumerate(branches):
                        yn, yt = ys[b]
                        pn, ps = self.psum.next()
                        for k in range(nk):
                            S.op('pe', lambda e, b=b, k=k, ps=ps, yt=yt, nk=nk: e.matmul(ps[:, :], lhsT=wp[b][:, k, c * 128:(c + 1) * 128], rhs=yt[:, k, :], start=(k == 0), stop=(k == nk - 1)),
                                 reads=['p6_w%d' % b, yn], writes=[pn], inc=(k == nk - 1))
                        if bi == 0:
                            S.op('dve', lambda e, b=b, ps=ps: e.tensor_tensor(m[:, :], ps[:, :], sg[:, b, :], ALU.mult), reads=[pn, sn], writes=[mn_])
                        else:
                            tn, tm = mtmp.next()
                            S.op('dve', lambda e, b=b, ps=ps, tm=tm: e.tensor_tensor(tm[:, :], ps[:, :], sg[:, b, :], ALU.mult), reads=[pn, sn], writes=[tn])
                            S.op('dve', lambda e, tm=tm: e.tensor_tensor(m[:, :], m[:, :], tm[:, :], ALU.add), reads=[tn, mn_], writes=[mn_])
                    S.op('act', lambda e: e.copy(mb[:, c, :], m[:, :]), reads=[mn_], writes=[('p6_mb', c)])
                for c in range(0, 8, 2):
                    S.interleave([lambda c=c: mchunk(c), lambda c=c: mchunk(c + 1)])
                for c in range(8):
                    pn, ps = self.psum.next()
                    for k in range(8):
                        S.op('pe', lambda e: e.matmul(ps[:, :], lhsT=wo[:, k, c * 128:(c + 1) * 128], rhs=mb[:, k, :], start=(k == 0), stop=(k == 7)),
                             reads=['p6_wo', ('p6_mb', k)], writes=[pn], inc=(k == 7))
                    S.op('act', lambda e: e.activation(sq[:, c, :], ps[:, :], AF.Square), reads=[pn], writes=['p6_sq'])
                    S.op('act', lambda e: e.copy(y[:, c, :], ps[:, :]), reads=[pn], writes=['p6_yy'])
                on_, xo_t = xor_.next()
                self.postnorm_residual(l, 1, 'p6_yy', y, 'p6_sq', sq, 'p6_rstd', rstd, xn, xt, on_, xo_t)
                S.dma('sp', xmT.rearrange("(c p) l -> p c l", p=128)[:, :, t0:t0 + T], xo_t[:], reads=[on_], writes=[('xmT', ti)])

    def postnorm_residual(self, l, which, y_name, y, sq_name, sq, rstd_name, rstd, xres_name, xres, xo_name, xo):
        S = self.S
        self.rms_stats(sq_name, sq, rstd_name, rstd, T)
        for c in range(8):
            S.op('dve', lambda e: e.scalar_tensor_tensor(y[:, c, :], y[:, c, :], self.gains_s[:, l, which, c:c + 1], rstd[:, :],
                                                         ALU.mult, ALU.mult),
                 reads=[y_name, rstd_name, 'gains_s'], writes=[y_name])
            S.op('dve', lambda e: e.tensor_tensor(xo[:, c, :], y[:, c, :], xres[:, c, :], ALU.add),
                 reads=[y_name, xres_name], writes=[xo_name])

    def phase_ffn(self, l, xsrc, xdst, wUpB, wDnB, ffnp):
        nc, S, L = self.nc, self.S, self.L
        NT = 2
        with ExitStack() as st:
            xr = Ring(nc, st, "p7_x", 2, [128, 8, T], F32)
            sq = self.sb(st, "p7_sq", [128, 8, T], BF16)
            rstd = self.sb(st, "p7_rstd", [128, T], F32)
            hr = Ring(nc, st, "p7_h", 2, [128, 8, T], BF16)
            wr = Ring(nc, st, "p7_w", 2, [128, 8, 1024], BF16)
            actr = Ring(nc, st, "p7_act", 2, [128, 24, T], BF16)
            gsb = Ring(nc, st, "p7_g", 2, [128, T + 2], F32)
            gtmp = Ring(nc, st, "p7_gt", 2, [128, T], F32)
            tails = self.sb(st, "p7_tail", [128, 24, 2], F32)
            fp = self.sb(st, "p7_fp", [128, 4, 24], F32)
            self.ys = [self.sb(st, "p7_y%d" % i, [128, 8, T], F32) for i in range(2)]
            self.sqs = [self.sb(st, "p7_sqo%d" % i, [128, 8, T], BF16) for i in range(2)]
            S.dma('sp', fp[:], ffnp[:, l, :, :], writes=['p7_fp'])
            S.op('dve', lambda e: e.memset(tails[:], 0.0), writes=[('p7_tail', c) for c in range(24)])
            wup = wUpB[l].rearrange("(k p) n -> p k n", p=128)
            wdn = wDnB[l].rearrange("(k p) n -> p k n", p=128)
            wq = 0
            for tp in range(L // (NT * T)):
                tiles = []
                for ti in range(tp * NT, (tp + 1) * NT):
                    t0 = ti * T
                    xn, xt = xr.next()
                    S.dma('sp', xt[:], xsrc.rearrange("(c p) l -> p c l", p=128)[:, :, t0:t0 + T], reads=[(xsrc.tensor.name, ti)], writes=[xn])
                    hn, h = hr.next()
                    self.prenorm(l, 3, xn, xt, hn, h, "p7_sq", sq, "p7_rstd", rstd)
                    an, act = actr.next()
                    tiles.append((ti, t0, xn, xt, hn, h, an, act))
                for blk in range(6):
                    wn, w = wr.next()
                    wq += 1
                    q_ = 'sp' if wq % 2 == 0 else 'pool'
                    S.dma(q_, w[:, :, 0:512], wup[:, :, blk * 512:(blk + 1) * 512], reads=[('wUpB', l)], writes=[wn])
                    S.dma(q_, w[:, :, 512:1024], wup[:, :, DFF + blk * 512:DFF + (blk + 1) * 512], reads=[('wUpB', l)], writes=[wn])
                    for (ti, t0, xn, xt, hn, h, an, act) in tiles:
                        for j in range(4):
                            ch = blk * 4 + j
                            pgn, pg = self.psum.next()
                            for k in range(8):
                                S.op('pe', lambda e: e.matmul(pg[:, :], lhsT=w[:, k, 512 + j * 128:512 + (j + 1) * 128], rhs=h[:, k, :], start=(k == 0), stop=(k == 7)),
                                     reads=[wn, hn], writes=[pgn], inc=(k == 7))
                            pvn, pv = self.psum.next()
                            for k in range(8):
                                S.op('pe', lambda e: e.matmul(pv[:, :], lhsT=w[:, k, j * 128:(j + 1) * 128], rhs=h[:, k, :], start=(k == 0), stop=(k == 7)),
                                     reads=[wn, hn], writes=[pvn], inc=(k == 7))
                            gn, g = gsb.next()
                            S.op('act', lambda e: e.copy(g[:, 2:T + 2], pg[:, :]), reads=[pgn], writes=[gn])
                            S.op('act', lambda e: e.copy(g[:, 0:2], tails[:, ch, :]), reads=[('p7_tail', ch)], writes=[gn])
                            S.op('act', lambda e: e.copy(tails[:, ch, :], g[:, T:T + 2]), reads=[gn], writes=[('p7_tail', ch)])
                            tn, tm = gtmp.next()
                            S.op('dve', lambda e: e.tensor_scalar(tm[:, :], g[:, 0:T], fp[:, 0, ch:ch + 1], fp[:, 3, ch:ch + 1], ALU.mult, ALU.add),
                                 reads=[gn, 'p7_fp'], writes=[tn])
                            S.op('dve', lambda e: e.scalar_tensor_tensor(tm[:, :], g[:, 1:T + 1], fp[:, 1, ch:ch + 1], tm[:, :], ALU.mult, ALU.add),
                                 reads=[gn, tn, 'p7_fp'], writes=[tn])
                            S.op('dve', lambda e: e.scalar_tensor_tensor(tm[:, :], g[:, 2:T + 2], fp[:, 2, ch:ch + 1], tm[:, :], ALU.mult, ALU.add),
                                 reads=[gn, tn, 'p7_fp'], writes=[tn])
                            S.op('act', lambda e: e.activation(tm[:, :], tm[:, :], AF.Gelu_apprx_tanh), reads=[tn], writes=[tn])
                            S.op('dve', lambda e: e.tensor_tensor(act[:, ch, :], pv[:, :], tm[:, :], ALU.mult), reads=[pvn, tn], writes=[(an, ch)])
                for half in range(2):
                    banks = {}
                    for cpair in range(2):
                        for (ti, t0, xn, xt, hn, h, an, act) in tiles:
                            banks[ti] = [self.psum.next() for _ in range(2)]
                        for kb in range(3):
                            wn, w = wr.next()
                            wq += 1
                            q_ = 'sp' if wq % 2 == 0 else 'pool'
                            c00 = half * 512 + cpair * 256
                            S.dma(q_, w[:, :, 0:256], wdn[:, kb * 8:(kb + 1) * 8, c00:c00 + 256], reads=[('wDnB', l)], writes=[wn])
                            for (ti, t0, xn, xt, hn, h, an, act) in tiles:
                                for cc in range(2):
                                    pn, ps = banks[ti][cc]
                                    for k in range(8):
                                        kk = kb * 8 + k
                                        S.op('pe', lambda e: e.matmul(ps[:, :], lhsT=w[:, k, cc * 128:(cc + 1) * 128], rhs=act[:, kk, :], start=(kk == 0), stop=(kk == 23)),
                                             reads=[wn, (an, kk)], writes=[pn], inc=(k == 7))
                        for (ti, t0, xn, xt, hn, h, an, act) in tiles:
                            for cc in range(2):
                                c = half * 4 + cpair * 2 + cc
                                pn, ps = banks[ti][cc]
                                S.op('act', lambda e: e.activation(self.sqs[ti % 2][:, c, :], ps[:, :], AF.Square), reads=[pn], writes=[('p7_sq2', ti % 2)])
                                S.op('act', lambda e: e.copy(self.ys[ti % 2][:, c, :], ps[:, :]), reads=[pn], writes=[('p7_y2', ti % 2)])
                for (ti, t0, xn, xt, hn, h, an, act) in tiles:
                    on, xo_t = ('p7_y2', ti % 2), self.ys[ti % 2]
                    self.postnorm_residual(l, 4, ('p7_y2', ti % 2), self.ys[ti % 2], ('p7_sq2', ti % 2), self.sqs[ti % 2], 'p7_rstd', rstd, xn, xt, on, xo_t)
                    S.dma('sp', xdst.rearrange("(c p) l -> p c l", p=128)[:, :, t0:t0 + T], xo_t[:], reads=[on], writes=[(xdst.tensor.name, ti)])


def host_inputs(inputs, b, L):
    f = np.float32
    d = {}
    d["x"] = np.ascontiguousarray(inputs["x"][b, :L])
    d["mem"] = np.ascontiguousarray(inputs["mem"][b])
    d["ident"] = np.eye(128, dtype=f)
    gs = np.stack([inputs[k] for k in ("g_mix_pre", "g_mix_post", "g_mem", "g_mlp_pre", "g_mlp_post")], axis=1)
    d["gains"] = np.ascontiguousarray(gs.reshape(NL, 5, 8, 128).transpose(3, 0, 1, 2)).astype(f)
    d["w_in"] = inputs["w_in"]
    d["ffn_w_up"] = inputs["ffn_w_up"]
    d["ffn_w_down"] = inputs["ffn_w_down"]
    fp = np.concatenate([inputs["ffn_conv_w"], inputs["ffn_conv_b"][:, None, :]], axis=1)
    lp = np.concatenate([inputs["lru_conv_w"], inputs["lru_conv_b"][:, None], inputs["lru_ba"][:, None],
                         inputs["lru_bx"][:, None], inputs["lru_lambda"][:, None]], axis=1)
    d["lrup"] = np.ascontiguousarray(lp.reshape(NL, 8, 6, 128).transpose(3, 0, 1, 2)).astype(f)
    bd = np.zeros((NL, 2, 128, 6, 128), f)
    for wi, nm in enumerate(("lru_wa", "lru_wx")):
        w = inputs[nm]
        for c in range(6):
            bd[:, wi, 0:64, c, 0:64] = w[:, 2 * c]
            bd[:, wi, 64:128, c, 64:128] = w[:, 2 * c + 1]
    d["lru_bd"] = bd
    d["ssmd"] = np.ascontiguousarray(inputs["ssm_d"].reshape(NL, 6, 128).transpose(2, 0, 1)).astype(f)
    for nm in ("ssm_glu", "mem_wkv", "proj_a", "proj_b", "proj_c", "proj_x", "w_out"):
        d[nm] = inputs[nm]
    am = np.full((128, 2, 12, 256), -30000.0, f)
    kk = np.arange(128)[:, None]
    qq = np.arange(128)[None, :]
    for hd in range(12):
        dd = DILS[hd // 4]
        dp = (qq + 128 - kk).astype(f)
        dc = (qq - kk).astype(f)
        mp = np.where(kk >= qq, -ALIBI[hd] * dd * dp, -30000.0)
        mc = np.where(kk <= qq, -ALIBI[hd] * dd * dc, -30000.0)
        am[:, 0, hd, 0:128] = mp
        am[:, 0, hd, 128:256] = mc
        am[:, 1, hd, 128:256] = mc
    d["amask"] = am
    def lay_a(a):
        return a.reshape(NL, 24, 2, 64).transpose(2, 3, 0, 1).reshape(128, NL, 24)
    ld_full = np.repeat(inputs["ssm_log_dt"][:, :, None], 64, axis=2)
    d["s5A"] = np.ascontiguousarray(np.stack([lay_a(inputs["ssm_a_re"]), lay_a(inputs["ssm_a_im"]), lay_a(ld_full)], axis=2)).astype(f)
    def lay_r(a):
        return a.reshape(NL, 3072)
    rr = np.stack([lay_r(inputs["ssm_a_re"]), lay_r(inputs["ssm_a_im"]), lay_r(ld_full)], axis=1)
    d["s5R"] = np.ascontiguousarray(np.broadcast_to(rr[None], (128, NL, 3, 3072))).astype(f)
    sB = np.zeros((NL, 2, 128, 24, 128), f)
    sC = np.zeros((NL, 2, 128, 24, 128), f)
    for ri, (bn, cn) in enumerate((("ssm_b_re", "ssm_c_re"), ("ssm_b_im", "ssm_c_im"))):
        Bm = inputs[bn]
        Cm = inputs[cn]
        for p in range(24):
            for gl in range(2):
                r0 = 32 * (p % 4) + gl * 16
                sB[:, ri, r0:r0 + 16, p, gl * 64:(gl + 1) * 64] = Bm[:, 2 * p + gl].transpose(0, 2, 1)
                sC[:, ri, gl * 64:(gl + 1) * 64, p, r0:r0 + 16] = Cm[:, 2 * p + gl].transpose(0, 2, 1)
    d["s5B"] = sB
    d["s5C"] = sC
    d["ffnp"] = np.ascontiguousarray(fp.reshape(NL, 4, 24, 128).transpose(3, 0, 1, 2)).astype(f)
    return d


ALL_PHASES = ('prepass', 'prologue', 'p1', 'p2', 'p3', 'p4', 'p5', 'p6', 'p7', 'epilogue')


def kernel(**inputs):
    inputs = {k: np.asarray(v) for k, v in inputs.items()}
    L = inputs["x"].shape[1]
    kb = K(L, NL)
    nc = kb.build(ALL_PHASES)
    in_maps = []
    for b in range(2):
        hi = host_inputs(inputs, b, L)
        in_maps.append({k: hi[k] for k in kb.ins})
    res = run_bass_kernel_spmd(nc, in_maps, core_ids=[0, 1])
    return np.stack([res.results[b]["out"] for b in range(2)], axis=0)
```

```python
import math
from contextlib import ExitStack
import numpy as np
import concourse.bass as bass
import concourse.mybir as mybir
from concourse.bass_utils import run_bass_kernel_spmd

AF = mybir.ActivationFunctionType
ALU = mybir.AluOpType
F32 = mybir.dt.float32
BF16 = mybir.dt.bfloat16

D = 1024
NL = 2
SEQ = 16384
MEM = 256
IN_W = 9472
DFF = 3072
T = 512
EPS = 1e-6
ALIBI = [2.0 ** (-8.0 * (h + 1) / 12) for h in range(12)]
DILS = (1, 4, 16)
import os
P3STOP = int(os.environ.get("P3STOP", "0"))


class Sched:
    NSLOT = 8

    def __init__(self, nc, stack):
        self.nc = nc
        self.engs = {'pe': nc.tensor, 'act': nc.scalar, 'dve': nc.vector,
                     'pool': nc.gpsimd, 'sp': nc.sync}
        self.sem = {}
        self.cnt = {}
        for n in ['pe', 'act', 'dve', 'pool']:
            self.sem[n] = stack.enter_context(nc.semaphore("s_" + n))
            self.cnt[n] = 0
        self.dq = {}
        for q in ['sp', 'pool', 'act']:
            for i in range(self.NSLOT):
                self.sem[('dma', q, i)] = stack.enter_context(nc.semaphore("d_%s%d" % (q, i)))
            self.dq[q] = 0
        self.seen = {e: {} for e in self.engs}
        self.lastw = {}
        self.readers = {}

    def _deps(self, reads, writes):
        deps = []
        for r in reads:
            t = self.lastw.get(r)
            if t is not None:
                deps.append(t)
        for w in writes:
            t = self.lastw.get(w)
            if t is not None:
                deps.append(t)
            deps.extend(self.readers.get(w, ()))
        return deps

    def _wait(self, ename, deps):
        best = {}
        for (src, val) in deps:
            if best.get(src, 0) < val:
                best[src] = val
        seen = self.seen[ename]
        eng = self.engs[ename]
        for src, val in best.items():
            if src == 'pe' and ename == 'pe':
                continue
            if seen.get(src, 0) >= val:
                continue
            eng.wait_ge(self.sem[src], val)
            seen[src] = val

    def _record(self, ticket, reads, writes):
        for r in reads:
            self.readers.setdefault(r, []).append(ticket)
        for w in writes:
            self.lastw[w] = ticket
            self.readers[w] = []

    defer = None

    def op(self, ename, fn, reads=(), writes=(), inc=True):
        if self.defer is not None:
            self.defer.append(('__op__', (ename, fn, reads, writes, inc)))
            return None
        self._wait(ename, self._deps(reads, writes))
        ins = fn(self.engs[ename])
        if inc:
            self.cnt[ename] += 1
            ins.then_inc(self.sem[ename], 1)
            ticket = (ename, self.cnt[ename])
        else:
            ticket = (ename, self.cnt[ename] + 1)
        self._record(ticket, reads, writes)
        return ticket

    def dma(self, q, out, in_, reads=(), writes=(), **kw):
        if self.defer is not None:
            self.defer.append(('__dma__', (q, out, in_, reads, writes, kw)))
            return None
        i = self.dq[q]
        slot = i % self.NSLOT
        rnd = i // self.NSLOT
        src = ('dma', q, slot)
        deps = self._deps(reads, writes)
        if rnd > 0:
            deps.append((src, 16 * rnd))
        self._wait(q, deps)
        ins = self.engs[q].dma_start(out=out, in_=in_, **kw)
        ins.then_inc(self.sem[src], 16)
        self.dq[q] = i + 1
        ticket = (src, 16 * (rnd + 1))
        self._record(ticket, reads, writes)
        return ticket

    def coll(self, kind, src, dst, groups, reads=(), writes=()):
        q = 'pool'
        i = self.dq[q]
        slot = i % self.NSLOT
        rnd = i // self.NSLOT
        srck = ('dma', q, slot)
        deps = self._deps(reads, writes)
        if rnd > 0:
            deps.append((srck, 16 * rnd))
        self._wait(q, deps)
        ins = self.engs[q].collective_compute(kind, ALU.bypass, replica_groups=groups, ins=[src], outs=[dst])
        ins.then_inc(self.sem[srck], 16)
        self.dq[q] = i + 1
        ticket = (srck, 16 * (rnd + 1))
        self._record(ticket, reads, writes)
        return ticket

    def run_deferred(self, item):
        kind, a = item
        if kind == '__op__':
            self.op(*a)
        else:
            q, out, in_, reads, writes, kw = a
            self.dma(q, out, in_, reads=reads, writes=writes, **kw)

    def interleave(self, fns):
        lists = []
        for f in fns:
            self.defer = []
            f()
            lists.append(self.defer)
            self.defer = None
        while any(lists):
            for lst in lists:
                if lst:
                    self.run_deferred(lst.pop(0))

    def finish(self, ename='sp'):
        deps = list(self.lastw.values())
        for l in self.readers.values():
            deps.extend(l)
        self._wait(ename, deps)

    def barrier(self):
        for e in self.engs:
            self.finish(e)
        self.lastw = {}
        self.readers = {}


_UID = [0]


class Ring:
    def __init__(self, nc, stack, name, n, shape, dtype, psum=False):
        self.bufs = []
        _UID[0] += 1
        for i in range(n):
            nm = "%s_%d_%d" % (name, _UID[0], i)
            if psum:
                t = stack.enter_context(nc.psum_tensor(nm, shape, dtype))
            else:
                t = stack.enter_context(nc.sbuf_tensor(nm, shape, dtype))
            self.bufs.append((nm, t))
        self.i = 0

    def next(self):
        b = self.bufs[self.i % len(self.bufs)]
        self.i += 1
        return b


class K:
    def __init__(self, L, nl, dbg=False):
        self.L = L
        self.nl = nl
        self.dbg = dbg
        self.nc = bass.Bass("TRN2", target_bir_lowering=False)
        self.ins = {}
        self.scr = {}

    def inp(self, name, shape, dt=F32):
        t = self.nc.dram_tensor(name, list(shape), dt, kind="ExternalInput").ap()
        self.ins[name] = t
        return t

    def scratch(self, name, shape, dt):
        kind = "ExternalOutput" if self.dbg else "Internal"
        t = self.nc.dram_tensor(name, list(shape), dt, kind=kind).ap()
        self.scr[name] = t
        return t

    def sb(self, st, name, shape, dt):
        _UID[0] += 1
        return st.enter_context(self.nc.sbuf_tensor("%s_%d" % (name, _UID[0]), list(shape), dt))

    def build(self, phases):
        nc = self.nc
        L, nl = self.L, self.nl
        x = self.inp("x", [L, D])
        mem = self.inp("mem", [MEM, D])
        ident = self.inp("ident", [128, 128])
        gains = self.inp("gains", [128, NL, 5, 8])
        w_in = self.inp("w_in", [NL, D, IN_W])
        ffn_w_up = self.inp("ffn_w_up", [NL, D, 2 * DFF])
        ffn_w_down = self.inp("ffn_w_down", [NL, DFF, D])
        ffnp = self.inp("ffnp", [128, NL, 4, 24])
        lrup = self.inp("lrup", [128, NL, 8, 6])
        lru_bd = self.inp("lru_bd", [NL, 2, 128, 6, 128])
        ssmd = self.inp("ssmd", [128, NL, 6])
        wsrc = {}
        for nm, shp in (("ssm_glu", [NL, 768, 1536]), ("mem_wkv", [NL, D, 1536]), ("proj_a", [NL, 768, D]),
                        ("proj_b", [NL, 256, D]), ("proj_c", [NL, 768, D]), ("proj_x", [NL, 768, D]), ("w_out", [NL, D, D])):
            wsrc[nm] = (self.inp(nm, shp), self.scratch(nm + "B", shp, BF16), shp[1])
        self.wB = {nm: v[1] for nm, v in wsrc.items()}
        yA = self.scratch("yA", [768, L], BF16)
        yC = self.scratch("yC", [768, L], BF16)
        yX = self.scratch("yX", [768, L], BF16)
        oG = self.scratch("oG", [3, L, 260], F32)
        self.phases = phases
        amask = self.inp("amask", [128, 2, 12, 256])
        s5A = self.inp("s5A", [128, NL, 3, 24])
        s5R = self.inp("s5R", [128, NL, 3, 3072])
        s5B = self.inp("s5B", [NL, 2, 128, 24, 128])
        s5C = self.inp("s5C", [NL, 2, 128, 24, 128])
        out = self.nc.dram_tensor("out", [L, D], F32, kind="ExternalOutput").ap()
        self.out = out

        xT = self.scratch("xT", [D, L], F32)
        xmT = self.scratch("xmT", [D, L], F32)
        zF = self.scratch("zF", [56 * 128, L], BF16)
        zQKV = self.scratch("zQKV", [L, 2304], BF16)
        wInB = self.scratch("wInB", [NL, D, IN_W], BF16)
        wUpB = self.scratch("wUpB", [NL, D, 2 * DFF], BF16)
        wDnB = self.scratch("wDnB", [NL, DFF, D], BF16)

        with ExitStack() as st0:
            S = Sched(nc, st0)
            self.S = S
            ident_f = self.sb(st0, "ident_f", [128, 128], F32)
            ones_b = self.sb(st0, "ones_b", [128, 128], BF16)
            gains_s = self.sb(st0, "gains_s", [128, NL, 5, 8], F32)
            eps_c = self.sb(st0, "eps_c", [128, 1], F32)
            self.ident_f, self.ones_b, self.gains_s, self.eps_c = ident_f, ones_b, gains_s, eps_c
            one_c = self.sb(st0, "one_c", [128, 1], F32)
            self.one_c = one_c
            S.op('dve', lambda e: e.memset(one_c[:], 1.0), writes=['one_c'])
            S.dma('sp', ident_f[:], ident, writes=['ident_f'])
            S.dma('sp', gains_s[:], gains, writes=['gains_s'])
            S.op('dve', lambda e: e.memset(ones_b[:], 1.0), writes=['ones_b'])
            S.op('dve', lambda e: e.memset(eps_c[:], EPS), writes=['eps_c'])
            self.psum = Ring(nc, st0, "ps", 6, [128, 512], F32, psum=True)

            if 'prepass' in phases:
                for l in range(nl):
                    for (src, dst, rows) in [(w_in, wInB, D), (ffn_w_up, wUpB, D), (ffn_w_down, wDnB, DFF)] + list(wsrc.values()):
                        for r in range(0, rows, 128):
                            S.dma('pool', dst[l, r:r + 128, :], src[l, r:r + 128, :],
                                  writes=[(dst.tensor.name, l)])
            if 'prologue' in phases:
                self.transpose_in(x, xT, L)
                S.barrier()
            for l in range(nl):
                if 'p1' in phases:
                    self.phase_inproj(l, xT, wInB, zF, zQKV)
                    S.barrier()
                if 'p2' in phases:
                    self.phase_lru(l, zF, yA, lrup, lru_bd)
                    S.barrier()
                if 'p3' in phases:
                    self.phase_attn(l, zQKV, oG, amask)
                    S.barrier()
                if 'p4' in phases:
                    self.phase_s5(l, zF, yC, s5A, s5R, s5B, s5C, ssmd)
                    S.barrier()
                if 'p5' in phases:
                    self.phase_xattn(l, mem, zF, yX)
                    S.barrier()
                if 'p6' in phases:
                    self.phase_merge(l, xT, xmT, zF, yA, yC, yX, oG)
                    S.barrier()
                if 'p7' in phases:
                    self.phase_ffn(l, xmT if 'p6' in phases else xT, xT, wUpB, wDnB, ffnp)
                    S.barrier()
            if 'epilogue' in phases:
                self.transpose_out(xT, out, L)
            S.barrier()
        return nc

    def transpose_in(self, x, xT, L):
        nc, S = self.nc, self.S
        with ExitStack() as st:
            xin = Ring(nc, st, "ti_x", 2, [128, D], F32)
            xo = Ring(nc, st, "ti_o", 2, [128, 8, 128], F32)
            for b in range(L // 128):
                nm, xt = xin.next()
                S.dma('sp', xt[:], x[b * 128:(b + 1) * 128, :], writes=[nm])
                no, ot = xo.next()
                for half in range(2):
                    pn, ps = self.psum.next()
                    for j in range(4):
                        c = half * 4 + j
                        S.op('pe', lambda e: e.transpose(ps[:, j * 128:(j + 1) * 128], xt[:, c * 128:(c + 1) * 128], self.ident_f[:]),
                             reads=[nm, 'ident_f'], writes=[pn], inc=(j == 3))
                    eng = 'act' if half == 0 else 'dve'
                    if eng == 'act':
                        S.op('act', lambda e: e.copy(ot[:, half * 4:(half + 1) * 4, :], ps[:, :].rearrange("p (a b) -> p a b", a=4)),
                             reads=[pn], writes=[no])
                    else:
                        S.op('dve', lambda e: e.tensor_copy(ot[:, half * 4:(half + 1) * 4, :], ps[:, :].rearrange("p (a b) -> p a b", a=4)),
                             reads=[pn], writes=[no])
                S.dma('sp', xT.rearrange("(c p) l -> p c l", p=128)[:, :, b * 128:(b + 1) * 128], ot[:],
                      reads=[no], writes=[('xT', b // 4)])

    def transpose_out(self, xT, out, L):
        nc, S = self.nc, self.S
        with ExitStack() as st:
            xin = Ring(nc, st, "to_x", 2, [128, 8, 128], F32)
            xo = Ring(nc, st, "to_o", 2, [128, D], F32)
            for b in range(L // 128):
                nm, xt = xin.next()
                S.dma('sp', xt[:], xT.rearrange("(c p) l -> p c l", p=128)[:, :, b * 128:(b + 1) * 128],
                      reads=[('xT', b // 4)], writes=[nm])
                no, ot = xo.next()
                for half in range(2):
                    pn, ps = self.psum.next()
                    for j in range(4):
                        c = half * 4 + j
                        S.op('pe', lambda e: e.transpose(ps[:, j * 128:(j + 1) * 128], xt[:, c, :], self.ident_f[:]),
                             reads=[nm, 'ident_f'], writes=[pn], inc=(j == 3))
                    if half == 0:
                        S.op('act', lambda e: e.copy(ot[:, 0:512], ps[:, :]), reads=[pn], writes=[no])
                    else:
                        S.op('dve', lambda e: e.tensor_copy(ot[:, 512:1024], ps[:, :]), reads=[pn], writes=[no])
                S.dma('sp', out[b * 128:(b + 1) * 128, :], ot[:], reads=[no], writes=['out'])

    def rms_stats(self, sqname, sq, rstd_name, rstd, Tn):
        S = self.S
        pn, ps = self.psum.next()
        for c in range(8):
            S.op('pe', lambda e: e.matmul(ps[:, :Tn], lhsT=self.ones_b[:], rhs=sq[:, c, :], start=(c == 0), stop=(c == 7)),
                 reads=[sqname, 'ones_b'], writes=[pn], inc=(c == 7))
        S.op('act', lambda e: e.activation(rstd[:, :Tn], ps[:, :Tn], AF.Sqrt, bias=self.eps_c[:], scale=1.0 / D),
             reads=[pn, 'eps_c'], writes=[rstd_name])
        S.op('dve', lambda e: e.reciprocal(rstd[:, :Tn], rstd[:, :Tn]), reads=[rstd_name], writes=[rstd_name])

    def prenorm(self, l, which, xt_name, xt, h_name, h, sq_name, sq, rstd_name, rstd):
        S = self.S
        for c in range(8):
            S.op('act', lambda e: e.activation(sq[:, c, :], xt[:, c, :], AF.Square), reads=[xt_name], writes=[sq_name])
        self.rms_stats(sq_name, sq, rstd_name, rstd, T)
        for c in range(8):
            S.op('dve', lambda e: e.scalar_tensor_tensor(h[:, c, :], xt[:, c, :], self.gains_s[:, l, which, c:c + 1], rstd[:, :],
                                                         ALU.mult, ALU.mult),
                 reads=[xt_name, rstd_name, 'gains_s'], writes=[h_name])

    def phase_inproj(self, l, xT, wInB, zF, zQKV):
        nc, S, L = self.nc, self.S, self.L
        FM_COLS = list(range(0, 1536, 128)) + list(range(3840, IN_W, 128))
        NT = 2
        with ExitStack() as st:
            xr = Ring(nc, st, "p1_x", 2, [128, 8, T], F32)
            sq = self.sb(st, "p1_sq", [128, 8, T], BF16)
            rstd = self.sb(st, "p1_rstd", [128, T], F32)
            hr = Ring(nc, st, "p1_h", 4, [128, 8, T], BF16)
            wr = Ring(nc, st, "p1_w", 2, [128, 8, 1024], BF16)
            zo = Ring(nc, st, "p1_zo", 2, [128, 8, T], BF16)
            qo = Ring(nc, st, "p1_qo", 2, [128, 4, 2304], BF16)
            wv = wInB[l].rearrange("(k p) n -> p k n", p=128)
            zFv = zF.rearrange("(c p) l -> p c l", p=128)
            ev = 0
            wq = 0
            for tp in range(L // (NT * T)):
                hs = []
                for ti in range(tp * NT, (tp + 1) * NT):
                    t0 = ti * T
                    xn, xt = xr.next()
                    S.dma('sp', xt[:], xT.rearrange("(c p) l -> p c l", p=128)[:, :, t0:t0 + T], reads=[('xT', ti)], writes=[xn])
                    hn, h = hr.next()
                    self.prenorm(l, 0, xn, xt, hn, h, "p1_sq", sq, "p1_rstd", rstd)
                    hs.append((t0, hn, h))
                for blk in range(7):
                    wn, w = wr.next()
                    wq += 1
                    for j in range(8):
                        c0 = FM_COLS[blk * 8 + j]
                        if j == 0 or FM_COLS[blk * 8 + j - 1] + 128 != c0:
                            j2 = j
                            while j2 + 1 < 8 and FM_COLS[blk * 8 + j2 + 1] == FM_COLS[blk * 8 + j2] + 128:
                                j2 += 1
                            S.dma('sp' if wq % 2 == 0 else 'pool', w[:, :, j * 128:(j2 + 1) * 128], wv[:, :, c0:c0 + (j2 - j + 1) * 128],
                                  reads=[('wInB', l)], writes=[wn])
                    for (t0, hn, h) in hs:
                        zn, z = zo.next()
                        for j in range(8):
                            ch = blk * 8 + j
                            pn, ps = self.psum.next()
                            for k in range(8):
                                S.op('pe', lambda e: e.matmul(ps[:, :], lhsT=w[:, k, j * 128:(j + 1) * 128], rhs=h[:, k, :], start=(k == 0), stop=(k == 7)),
                                     reads=[wn, hn], writes=[pn], inc=(k == 7))
                            if 6 <= ch < 12:
                                S.op('act', lambda e: e.activation(z[:, j, :], ps[:, :], AF.Gelu_apprx_tanh), reads=[pn], writes=[zn])
                            elif ch >= 24:
                                S.op('act', lambda e: e.activation(z[:, j, :], ps[:, :], AF.Sigmoid), reads=[pn], writes=[zn])
                            else:
                                S.op('dve', lambda e: e.tensor_copy(z[:, j, :], ps[:, :]), reads=[pn], writes=[zn])
                        S.dma('sp', zFv[:, blk * 8:(blk + 1) * 8, t0:t0 + T], z[:], reads=[zn], writes=['zF'])
                qs = [qo.next() for _ in hs]
                for blk in range(3):
                    c0 = 1536 + blk * 1024
                    ncol = min(1024, 3840 - c0)
                    wn, w = wr.next()
                    wq += 1
                    S.dma('sp' if wq % 2 == 0 else 'pool', w[:, :, :ncol], wv[:, :, c0:c0 + ncol], reads=[('wInB', l)], writes=[wn])
                    for (t0, hn, h), (qn, q) in zip(hs, qs):
                        for tb in range(4):
                            for n0 in range(0, ncol, 512):
                                nn = min(512, ncol - n0)
                                pn, ps = self.psum.next()
                                for k in range(8):
                                    S.op('pe', lambda e: e.matmul(ps[:, :nn], lhsT=h[:, k, tb * 128:(tb + 1) * 128], rhs=w[:, k, n0:n0 + nn], start=(k == 0), stop=(k == 7)),
                                         reads=[wn, hn], writes=[pn], inc=(k == 7))
                                dst = q[:, tb, blk * 1024 + n0: blk * 1024 + n0 + nn]
                                ev += 1
                                if ev % 2:
                                    S.op('dve', lambda e: e.tensor_copy(dst, ps[:, :nn]), reads=[pn], writes=[qn])
                                else:
                                    S.op('act', lambda e: e.copy(dst, ps[:, :nn]), reads=[pn], writes=[qn])
                for (t0, hn, h), (qn, q) in zip(hs, qs):
                    S.dma('sp', zQKV[t0:t0 + T, :].rearrange("(tb p) n -> p tb n", p=128), q[:], reads=[qn], writes=['zQKV'])

    def phase_lru(self, l, zF, yA, lrup, lru_bd):
        nc, S, L = self.nc, self.S, self.L
        with ExitStack() as st:
            lp = self.sb(st, "p2_lp", [128, 8, 6], F32)
            kap = self.sb(st, "p2_kap", [128, 2, 6], F32)
            bdA = self.sb(st, "p2_bdA", [128, 6, 128], BF16)
            bdX = self.sb(st, "p2_bdX", [128, 6, 128], BF16)
            state = self.sb(st, "p2_state", [128, 6], F32)
            xar = Ring(nc, st, "p2_xa", 3, [128, T + 3], BF16)
            ggr = Ring(nc, st, "p2_gg", 3, [128, T], BF16)
            xcr = Ring(nc, st, "p2_xc", 3, [128, T], F32)
            xcbr = Ring(nc, st, "p2_xcb", 3, [128, T], BF16)
            rr = Ring(nc, st, "p2_r", 3, [128, T], F32)
            ir = Ring(nc, st, "p2_i", 3, [128, T], F32)
            ar = Ring(nc, st, "p2_a", 3, [128, T], F32)
            a2r = Ring(nc, st, "p2_a2", 3, [128, T], F32)
            hr = Ring(nc, st, "p2_h", 3, [128, T], F32)
            yr = Ring(nc, st, "p2_y", 3, [128, T], BF16)
            S.dma('sp', lp[:], lrup[:, l, :, :], writes=['p2_lp'])
            S.dma('pool', bdA[:], lru_bd[l, 0], writes=['p2_bdA'])
            S.dma('pool', bdX[:], lru_bd[l, 1], writes=['p2_bdX'])
            S.op('dve', lambda e: e.memset(state[:], 0.0), writes=[('p2_state', c) for c in range(6)])
            S.op('act', lambda e: e.activation(kap[:, 0, :], lp[:, 7, :], AF.Exp, scale=-1.0), reads=['p2_lp'], writes=['p2_kap'])
            S.op('act', lambda e: e.activation(kap[:, 0, :], kap[:, 0, :], AF.Ln, bias=self.one_c[:]), reads=['p2_kap', 'one_c'], writes=['p2_kap'])
            S.op('dve', lambda e: e.tensor_scalar(kap[:, 1, :], kap[:, 0, :], -16.0, None, ALU.mult), reads=['p2_kap'], writes=['p2_kap'])
            S.op('dve', lambda e: e.tensor_scalar(kap[:, 0, :], kap[:, 0, :], -8.0, None, ALU.mult), reads=['p2_kap'], writes=['p2_kap'])
            zFv = zF.rearrange("(c p) l -> p c l", p=128)
            yAv = yA.rearrange("(c p) l -> p c l", p=128)
            for ti in range(L // T):
                t0 = ti * T
                def chunk(c):
                    xn, xa = xar.next()
                    if t0 == 0:
                        S.op('pool', lambda e: e.memset(xa[:, 0:3], 0.0), writes=[xn])
                        S.dma('sp', xa[:, 3:T + 3], zFv[:, c, 0:T], reads=['zF'], writes=[xn])
                    else:
                        S.dma('sp', xa[:, :], zFv[:, c, t0 - 3:t0 + T], reads=['zF'], writes=[xn])
                    gn, gg = ggr.next()
                    S.dma('sp', gg[:, :], zFv[:, 6 + c, t0:t0 + T], reads=['zF'], writes=[gn])
                    xcn, xc = xcr.next()
                    S.op('dve', lambda e: e.tensor_scalar(xc[:, :], xa[:, 0:T], lp[:, 0, c:c + 1], lp[:, 4, c:c + 1], ALU.mult, ALU.add),
                         reads=[xn, 'p2_lp'], writes=[xcn])
                    for j in range(1, 4):
                        S.op('dve', lambda e, j=j: e.scalar_tensor_tensor(xc[:, :], xa[:, j:j + T], lp[:, j, c:c + 1], xc[:, :], ALU.mult, ALU.add),
                             reads=[xn, xcn, 'p2_lp'], writes=[xcn])
                    xbn, xcb = xcbr.next()
                    S.op('dve', lambda e: e.tensor_copy(xcb[:, :], xc[:, :]), reads=[xcn], writes=[xbn])
                    prn, pr = self.psum.next()
                    S.op('pe', lambda e: e.matmul(pr[:, :], lhsT=bdA[:, c, :], rhs=xcb[:, :], start=True, stop=True), reads=['p2_bdA', xbn], writes=[prn])
                    pin, pi = self.psum.next()
                    S.op('pe', lambda e: e.matmul(pi[:, :], lhsT=bdX[:, c, :], rhs=xcb[:, :], start=True, stop=True), reads=['p2_bdX', xbn], writes=[pin])
                    rn, r = rr.next()
                    S.op('act', lambda e: e.activation(r[:, :], pr[:, :], AF.Sigmoid, bias=lp[:, 5, c:c + 1]), reads=[prn, 'p2_lp'], writes=[rn])
                    inn, iv = ir.next()
                    S.op('act', lambda e: e.activation(iv[:, :], pi[:, :], AF.Sigmoid, bias=lp[:, 6, c:c + 1]), reads=[pin, 'p2_lp'], writes=[inn])
                    an, a = ar.next()
                    S.op('act', lambda e: e.activation(a[:, :], r[:, :], AF.Exp, scale=kap[:, 0, c:c + 1]), reads=[rn, 'p2_kap'], writes=[an])
                    a2n, a2 = a2r.next()
                    S.op('act', lambda e: e.activation(a2[:, :], r[:, :], AF.Exp, scale=kap[:, 1, c:c + 1]), reads=[rn, 'p2_kap'], writes=[a2n])
                    S.op('dve', lambda e: e.tensor_scalar(a2[:, :], a2[:, :], -1.0, 1.0, ALU.mult, ALU.add), reads=[a2n], writes=[a2n])
                    S.op('act', lambda e: e.activation(a2[:, :], a2[:, :], AF.Sqrt), reads=[a2n], writes=[a2n])
                    S.op('dve', lambda e: e.tensor_tensor(iv[:, :], iv[:, :], a2[:, :], ALU.mult), reads=[inn, a2n], writes=[inn])
                    S.op('dve', lambda e: e.tensor_tensor(iv[:, :], iv[:, :], xc[:, :], ALU.mult), reads=[inn, xcn], writes=[inn])
                    hn, h = hr.next()
                    S.op('dve', lambda e: e.tensor_tensor_scan(h[:, :], a[:, :], iv[:, :], state[:, c:c + 1], ALU.mult, ALU.add),
                         reads=[an, inn, ('p2_state', c)], writes=[hn])
                    S.op('act', lambda e: e.copy(state[:, c:c + 1], h[:, T - 1:T]), reads=[hn], writes=[('p2_state', c)])
                    yn, y = yr.next()
                    S.op('dve', lambda e: e.tensor_tensor(y[:, :], h[:, :], gg[:, :], ALU.mult), reads=[hn, gn], writes=[yn])
                    S.dma('sp', yAv[:, c, t0:t0 + T], y[:, :], reads=[yn], writes=['yA'])
                for c in range(0, 6, 3):
                    S.interleave([lambda c=c: chunk(c), lambda c=c: chunk(c + 1), lambda c=c: chunk(c + 2)])

    def phase_attn(self, l, zQKV, oG, amask):
        nc, S, L = self.nc, self.S, self.L
        with ExitStack() as st:
            mk = self.sb(st, "p3_mk", [128, 2, 12, 256], F32)
            idb = self.sb(st, "p3_idb", [128, 128], BF16)
            S.dma('sp', mk[:], amask, writes=['p3_mk'])
            S.op('dve', lambda e: e.tensor_copy(idb[:], self.ident_f[:]), reads=['ident_f'], writes=['p3_idb'])
            extra = Ring(nc, st, "p3_psx", 2, [128, 512], F32, psum=True)
            psum = Ring.__new__(Ring)
            psum.bufs = list(self.psum.bufs) + list(extra.bufs)
            psum.i = 0
            qr = Ring(nc, st, "p3_q", 4, [128, 256], BF16)
            kr = Ring(nc, st, "p3_k", 4, [128, 256], BF16)
            vr = Ring(nc, st, "p3_v", 5, [128, 4, 128], BF16)
            qkr = Ring(nc, st, "p3_qk", 4, [128, 8, 128], BF16)
            scr = Ring(nc, st, "p3_sc", 4, [128, 512], F32)
            ptr = Ring(nc, st, "p3_pt", 6, [128, 2, 2, 128], BF16)
            osr = Ring(nc, st, "p3_os", 4, [128, 260], F32)
            for (vn, v) in vr.bufs:
                S.op('pool', lambda e: e.memset(v[:], 1.0), writes=[vn])
            box = {}

            def unit(g, d, r, blk):
                zv = zQKV.rearrange("(n d) c -> d n c", d=d)
                ov = oG[g].rearrange("(n d) c -> d n c", d=d)
                rows = slice(blk * 128, (blk + 1) * 128)
                qn, q = qr.next()
                kn, k_ = kr.next()
                vn, v = vr.next()
                S.dma('sp', q[:, :], zv[r, rows, 256 * g:256 * g + 256], reads=['zQKV'], writes=[qn])
                S.dma('sp', k_[:, :], zv[r, rows, 768 + 256 * g:768 + 256 * g + 256], reads=['zQKV'], writes=[kn])
                S.dma('sp', v[:, :, 0:64], zv[r, rows, 1536 + 256 * g:1536 + 256 * g + 256].rearrange("p (h e) -> p h e", e=64),
                      reads=['zQKV'], writes=[vn])
                ptn, pT = psum.next()
                for j in range(4):
                    S.op('pe', lambda e, j=j: e.matmul(pT[0:64, j * 128:(j + 1) * 128], lhsT=q[:, j * 64:(j + 1) * 64], rhs=idb[:], start=True, stop=True),
                         reads=[qn, 'p3_idb'], writes=[ptn], inc=(j == 3))
                ptn2, pT2 = psum.next()
                for j in range(4):
                    S.op('pe', lambda e, j=j: e.matmul(pT2[0:64, j * 128:(j + 1) * 128], lhsT=k_[:, j * 64:(j + 1) * 64], rhs=idb[:], start=True, stop=True),
                         reads=[kn, 'p3_idb'], writes=[ptn2], inc=(j == 3))
                qkn, qk = qkr.next()
                S.op('dve', lambda e: e.tensor_copy(qk[0:64, 0:4, :], pT[0:64, 0:512].rearrange("p (a b) -> p a b", a=4)), reads=[ptn], writes=[qkn])
                S.op('dve', lambda e: e.tensor_copy(qk[0:64, 4:8, :], pT2[0:64, 0:512].rearrange("p (a b) -> p a b", a=4)), reads=[ptn2], writes=[qkn])
                qTn, qT = qkn, qk[:, 0:4, :]
                kTn, kT = qkn, qk[:, 4:8, :]
                first = blk == 0
                if first:
                    kTp_n, kTp, vp_n, vp = kTn, kT, vn, v
                else:
                    kTp_n, kTp, vp_n, vp = box['prev']
                box['prev'] = (kTn, kT, vn, v)
                pts = []
                for pair in range(2):
                    pn, ps = psum.next()
                    for h2 in range(2):
                        hx = 2 * pair + h2
                        col = h2 * 256
                        if not first:
                            S.op('pe', lambda e, hx=hx, col=col, ps=ps: e.matmul(ps[:, col:col + 128], lhsT=kTp[0:64, hx, :], rhs=qT[0:64, hx, :], start=True, stop=True),
                                 reads=[kTp_n, qTn], writes=[pn], inc=False)
                        S.op('pe', lambda e, hx=hx, col=col, ps=ps: e.matmul(ps[:, col + 128:col + 256], lhsT=kT[0:64, hx, :], rhs=qT[0:64, hx, :], start=True, stop=True),
                             reads=[kTn, qTn], writes=[pn], inc=(h2 == 1))
                    scn, sc = scr.next()
                    hh0 = 4 * g + 2 * pair
                    mview = mk[:, 1 if first else 0, hh0:hh0 + 2, :].rearrange("p a b -> p (a b)")
                    if first:
                        S.op('pool', lambda e, sc=sc: e.memset(sc[:, :], 0.0), writes=[scn])
                        for h2 in range(2):
                            col = h2 * 256
                            S.op('dve', lambda e, col=col, sc=sc, ps=ps, mview=mview: e.scalar_tensor_tensor(sc[:, col + 128:col + 256], ps[:, col + 128:col + 256], 0.125, mview[:, col + 128:col + 256], ALU.mult, ALU.add),
                                 reads=[pn, 'p3_mk'], writes=[scn])
                            S.op('dve', lambda e, col=col, sc=sc, mview=mview: e.tensor_copy(sc[:, col:col + 128], mview[:, col:col + 128]), reads=['p3_mk'], writes=[scn])
                    else:
                        S.op('dve', lambda e, sc=sc, ps=ps, mview=mview: e.scalar_tensor_tensor(sc[:, :], ps[:, :], 0.125, mview, ALU.mult, ALU.add),
                             reads=[pn, 'p3_mk'], writes=[scn])
                    pn2, pt = ptr.next()
                    S.op('act', lambda e, pt=pt, sc=sc: e.activation(pt[:, :, :, :].rearrange("p a b c -> p (a b c)"), sc[:, :], AF.Exp), reads=[scn], writes=[pn2])
                    pts.append((pn2, pt))
                pn, ps = psum.next()
                for hh in range(4):
                    pn2, pt = pts[hh // 2]
                    S.op('pe', lambda e, hh=hh, pt=pt: e.matmul(ps[:, hh * 128:hh * 128 + 65], lhsT=pt[:, hh % 2, 0, :], rhs=vp[:, hh, 0:65], start=True, stop=False),
                         reads=[pn2, vp_n], writes=[pn], inc=False)
                    S.op('pe', lambda e, hh=hh, pt=pt: e.matmul(ps[:, hh * 128:hh * 128 + 65], lhsT=pt[:, hh % 2, 1, :], rhs=v[:, hh, 0:65], start=False, stop=True),
                         reads=[pn2, vn], writes=[pn], inc=(hh == 3))
                on, o = osr.next()
                S.op('act', lambda e: e.copy(o[:, :].rearrange("p (h e) -> p h e", e=65), ps[:, :].rearrange("p (h e) -> p h e", e=128)[:, :, 0:65]), reads=[pn], writes=[on])
                S.dma('sp', ov[r, rows, :], o[:, :], reads=[on], writes=['oG'])

            for g, d in enumerate(DILS):
                nb = L // (128 * d)
                for r in range(d):
                    if nb % 2 == 0:
                        for blk in range(0, nb, 2):
                            S.interleave([lambda blk=blk: unit(g, d, r, blk), lambda blk=blk: unit(g, d, r, blk + 1)])
                    else:
                        for blk in range(nb):
                            unit(g, d, r, blk)

    def sincos(self, st, th_name, th, n, out_s, out_c, key):
        S = self.S
        TWO_PI = 2.0 * math.pi
        ti = self.sb(st, "sc_i", [128, n], mybir.dt.int32)
        tf = self.sb(st, "sc_f", [128, n], F32)
        ph = self.sb(st, "sc_p", [128, n], F32)
        mm = self.sb(st, "sc_m", [128, n], F32)
        for (shift, outt) in ((0.0, out_s), (math.pi / 2, out_c)):
            S.op('dve', lambda e: e.tensor_scalar(ph[:, :], th, 1.0 / TWO_PI, shift / TWO_PI, ALU.mult, ALU.add), reads=[th_name], writes=[key + 'ph'])
            S.op('dve', lambda e: e.tensor_copy(ti[:, :], ph[:, :]), reads=[key + 'ph'], writes=[key + 'ti'])
            S.op('dve', lambda e: e.tensor_copy(tf[:, :], ti[:, :]), reads=[key + 'ti'], writes=[key + 'tf'])
            S.op('dve', lambda e: e.tensor_tensor(ph[:, :], ph[:, :], tf[:, :], ALU.subtract), reads=[key + 'ph', key + 'tf'], writes=[key + 'ph'])
            S.op('dve', lambda e: e.tensor_scalar(mm[:, :], ph[:, :], 0.5, None, ALU.is_gt), reads=[key + 'ph'], writes=[key + 'mm'])
            S.op('dve', lambda e: e.tensor_tensor(ph[:, :], ph[:, :], mm[:, :], ALU.subtract), reads=[key + 'ph', key + 'mm'], writes=[key + 'ph'])
            S.op('dve', lambda e: e.tensor_scalar(mm[:, :], ph[:, :], -0.5, None, ALU.is_lt), reads=[key + 'ph'], writes=[key + 'mm'])
            S.op('dve', lambda e: e.tensor_tensor(ph[:, :], ph[:, :], mm[:, :], ALU.add), reads=[key + 'ph', key + 'mm'], writes=[key + 'ph'])
            S.op('dve', lambda e: e.tensor_scalar(ph[:, :], ph[:, :], -0.4999, 0.4999, ALU.max, ALU.min), reads=[key + 'ph'], writes=[key + 'ph'])
            S.op('act', lambda e: e.activation(outt, ph[:, :], AF.Sin, scale=TWO_PI), reads=[key + 'ph'], writes=[key + 'out'])

    def phase_s5(self, l, zF, yC, s5A, s5R, s5B, s5C, ssmd):
        nc, S, L = self.nc, self.S, self.L
        with ExitStack() as st:
            rho = self.sb(st, "p4_rho", [128, 24], F32)
            BbR = self.sb(st, "p4_BbR", [128, 24, 128], BF16)
            BbI = self.sb(st, "p4_BbI", [128, 24, 128], BF16)
            CtR = self.sb(st, "p4_CtR", [128, 24, 128], BF16)
            CtI = self.sb(st, "p4_CtI", [128, 24, 128], BF16)
            CtRn = self.sb(st, "p4_CtRn", [128, 24, 128], BF16)
            dsk = self.sb(st, "p4_d", [128, 6], F32)
            xst = self.sb(st, "p4_xst", [128, 2, 24], F32)
            S.dma('sp', dsk[:], ssmd[:, l, :], writes=['p4_d'])
            S.op('dve', lambda e: e.memset(xst[:], 0.0), writes=[('p4_xst', p) for p in range(24)])
            S.dma('pool', CtR[:], s5C[l, 0], writes=['p4_CtR'])
            with ExitStack() as s2:
                N = 3072
                par = self.sb(s2, "p4s_par", [128, 3, N], F32)
                S.dma('sp', par[:], s5R[:, l, :, :], writes=['p4s_par'])
                ar, ai, ld = par[:, 0, :], par[:, 1, :], par[:, 2, :]
                dt = self.sb(s2, "p4s_dt", [128, N], F32)
                mag = self.sb(st if False else s2, "p4s_mag", [128, N], F32)
                th = self.sb(s2, "p4s_th", [128, N], F32)
                sn = self.sb(s2, "p4s_sn", [128, N], F32)
                cs = self.sb(s2, "p4s_cs", [128, N], F32)
                S.op('act', lambda e: e.activation(dt[:, :], ld, AF.Exp), reads=['p4s_par'], writes=['p4s_dt'])
                S.op('dve', lambda e: e.tensor_tensor(mag[:, :], ar, dt[:, :], ALU.mult), reads=['p4s_par', 'p4s_dt'], writes=['p4s_mag'])
                S.op('act', lambda e: e.activation(mag[:, :], mag[:, :], AF.Exp), reads=['p4s_mag'], writes=['p4s_mag'])
                S.op('dve', lambda e: e.tensor_tensor(th[:, :], ai, dt[:, :], ALU.mult), reads=['p4s_par', 'p4s_dt'], writes=['p4s_th'])
                for hf in range(2):
                    with ExitStack() as s3:
                        sl = slice(hf * 1536, (hf + 1) * 1536)
                        self.sincos(s3, 'p4s_th', th[:, sl], 1536, sn[:, sl], cs[:, sl], 'scB')
                        S.barrier()
                S.op('dve', lambda e: e.tensor_tensor(cs[:, :], cs[:, :], mag[:, :], ALU.mult), reads=['scBout', 'p4s_mag'], writes=['p4s_cs'])
                S.op('dve', lambda e: e.tensor_scalar(cs[:, :], cs[:, :], -1.0, None, ALU.add), reads=['p4s_cs'], writes=['p4s_cs'])
                S.op('dve', lambda e: e.tensor_tensor(sn[:, :], sn[:, :], mag[:, :], ALU.mult), reads=['scBout', 'p4s_mag'], writes=['p4s_sn'])
                S.op('dve', lambda e: e.tensor_tensor(dt[:, :], ar, ar, ALU.mult), reads=['p4s_par'], writes=['p4s_dt'])
                S.op('dve', lambda e: e.tensor_tensor(th[:, :], ai, ai, ALU.mult), reads=['p4s_par'], writes=['p4s_th'])
                S.op('dve', lambda e: e.tensor_tensor(dt[:, :], dt[:, :], th[:, :], ALU.add), reads=['p4s_dt', 'p4s_th'], writes=['p4s_dt'])
                S.op('dve', lambda e: e.reciprocal(dt[:, :], dt[:, :]), reads=['p4s_dt'], writes=['p4s_dt'])
                t1 = self.sb(s2, "p4s_t1", [128, N], F32)
                S.op('dve', lambda e: e.tensor_tensor(mag[:, :], cs[:, :], ar, ALU.mult), reads=['p4s_cs', 'p4s_par'], writes=['p4s_mag'])
                S.op('dve', lambda e: e.tensor_tensor(t1[:, :], sn[:, :], ai, ALU.mult), reads=['p4s_sn', 'p4s_par'], writes=['p4s_t1'])
                S.op('dve', lambda e: e.tensor_tensor(mag[:, :], mag[:, :], t1[:, :], ALU.add), reads=['p4s_mag', 'p4s_t1'], writes=['p4s_mag'])
                S.op('dve', lambda e: e.tensor_tensor(mag[:, :], mag[:, :], dt[:, :], ALU.mult), reads=['p4s_mag', 'p4s_dt'], writes=['p4s_mag'])
                S.op('dve', lambda e: e.tensor_tensor(th[:, :], sn[:, :], ar, ALU.mult), reads=['p4s_sn', 'p4s_par'], writes=['p4s_th'])
                S.op('dve', lambda e: e.tensor_tensor(t1[:, :], cs[:, :], ai, ALU.mult), reads=['p4s_cs', 'p4s_par'], writes=['p4s_t1'])
                S.op('dve', lambda e: e.tensor_tensor(th[:, :], th[:, :], t1[:, :], ALU.subtract), reads=['p4s_th', 'p4s_t1'], writes=['p4s_th'])
                S.op('dve', lambda e: e.tensor_tensor(th[:, :], th[:, :], dt[:, :], ALU.mult), reads=['p4s_th', 'p4s_dt'], writes=['p4s_th'])
                zr, zi = mag, th
                bre = self.sb(s2, "p4s_bre", [128, N], F32)
                bim = self.sb(s2, "p4s_bim", [128, N], F32)
                S.dma('sp', bre[:, :], s5B[l, 0].rearrange("p a b -> p (a b)"), writes=['p4s_bre'])
                S.dma('sp', bim[:, :], s5B[l, 1].rearrange("p a b -> p (a b)"), writes=['p4s_bim'])
                S.op('dve', lambda e: e.tensor_tensor(t1[:, :], zr[:, :], bre[:, :], ALU.mult), reads=['p4s_mag', 'p4s_bre'], writes=['p4s_t1'])
                S.op('dve', lambda e: e.tensor_tensor(cs[:, :], zi[:, :], bim[:, :], ALU.mult), reads=['p4s_th', 'p4s_bim'], writes=['p4s_cs'])
                S.op('dve', lambda e: e.tensor_tensor(BbR[:, :, :].rearrange("p a b -> p (a b)"), t1[:, :], cs[:, :], ALU.subtract), reads=['p4s_t1', 'p4s_cs'], writes=['p4_BbR'])
                S.op('dve', lambda e: e.tensor_tensor(t1[:, :], zr[:, :], bim[:, :], ALU.mult), reads=['p4s_mag', 'p4s_bim'], writes=['p4s_t1'])
                S.op('dve', lambda e: e.tensor_tensor(cs[:, :], zi[:, :], bre[:, :], ALU.mult), reads=['p4s_th', 'p4s_bre'], writes=['p4s_cs'])
                S.op('dve', lambda e: e.tensor_tensor(BbI[:, :, :].rearrange("p a b -> p (a b)"), t1[:, :], cs[:, :], ALU.add), reads=['p4s_t1', 'p4s_cs'], writes=['p4_BbI'])
                S.dma('sp', bre[:, :], s5C[l, 1].rearrange("p a b -> p (a b)"), reads=['p4s_bre'], writes=['p4s_bre'])
                S.op('act', lambda e: e.mul(CtI[:, :, :].rearrange("p a b -> p (a b)"), bre[:, :], -1.0), reads=['p4s_bre'], writes=['p4_CtI'])
                S.dma('sp', bim[:, :], s5C[l, 0].rearrange("p a b -> p (a b)"), reads=['p4s_bim'], writes=['p4s_bim'])
                S.op('act', lambda e: e.mul(CtRn[:, :, :].rearrange("p a b -> p (a b)"), bim[:, :], -1.0), reads=['p4s_bim'], writes=['p4_CtRn'])
                S.barrier()
            cosT = self.sb(st, "p4_cos", [128, 24, T], F32)
            sinT = self.sb(st, "p4_sin", [128, 24, T], F32)
            with ExitStack() as s2:
                pa = self.sb(s2, "p4a_par", [128, 3, 24], F32)
                S.dma('sp', pa[:], s5A[:, l, :, :], writes=['p4a_par'])
                dt = self.sb(s2, "p4a_dt", [128, 24], F32)
                th = self.sb(s2, "p4a_th", [128, 24], F32)
                sn = self.sb(s2, "p4a_sn", [128, 24], F32)
                cs = self.sb(s2, "p4a_cs", [128, 24], F32)
                S.op('act', lambda e: e.activation(dt[:, :], pa[:, 2, :], AF.Exp), reads=['p4a_par'], writes=['p4a_dt'])
                S.op('dve', lambda e: e.tensor_tensor(rho[:, :], pa[:, 0, :], dt[:, :], ALU.mult), reads=['p4a_par', 'p4a_dt'], writes=['p4_rho'])
                S.op('act', lambda e: e.activation(rho[:, :], rho[:, :], AF.Exp), reads=['p4_rho'], writes=['p4_rho'])
                S.op('dve', lambda e: e.tensor_tensor(th[:, :], pa[:, 1, :], dt[:, :], ALU.mult), reads=['p4a_par', 'p4a_dt'], writes=['p4a_th'])
                self.sincos(s2, 'p4a_th', th[:, :], 24, sn[:, :], cs[:, :], 'scA')
                tmp = self.sb(s2, "p4a_tmp", [128, T // 2], F32)
                for p in range(24):
                    S.op('act', lambda e: e.copy(cosT[:, p, 0:1], cs[:, p:p + 1]), reads=['scAout'], writes=[('p4_cos', p)])
                    S.op('act', lambda e: e.copy(sinT[:, p, 0:1], sn[:, p:p + 1]), reads=['scAout'], writes=[('p4_sin', p)])
                    w = 1
                    while w < T:
                        cw, sw = cosT[:, p, w - 1:w], sinT[:, p, w - 1:w]
                        S.op('dve', lambda e: e.tensor_scalar(tmp[:, 0:w], sinT[:, p, 0:w], sw, None, ALU.mult), reads=[('p4_sin', p)], writes=['p4a_tmp'])
                        S.op('dve', lambda e: e.scalar_tensor_tensor(cosT[:, p, w:2 * w], cosT[:, p, 0:w], cw, tmp[:, 0:w], ALU.mult, ALU.subtract),
                             reads=[('p4_cos', p), 'p4a_tmp'], writes=[('p4_cos', p)])
                        S.op('dve', lambda e: e.tensor_scalar(tmp[:, 0:w], cosT[:, p, 0:w], sw, None, ALU.mult), reads=[('p4_cos', p)], writes=['p4a_tmp'])
                        S.op('dve', lambda e: e.scalar_tensor_tensor(sinT[:, p, w:2 * w], sinT[:, p, 0:w], cw, tmp[:, 0:w], ALU.mult, ALU.add),
                             reads=[('p4_sin', p), ('p4_cos', p), 'p4a_tmp'], writes=[('p4_sin', p)])
                        w *= 2
                S.barrier()
            wg = self.sb(st, "p4_wg", [128, 6, 1536], BF16)
            S.dma('sp', wg[:], self.wB["ssm_glu"][l].rearrange("(k p) n -> p k n", p=128), reads=[('ssm_gluB', l)], writes=['p4_wg'])
            psy = Ring(nc, st, "p4_psy", 2, [128, 512], F32, psum=True)
            usr = Ring(nc, st, "p4_us", 1, [128, 6, T], BF16)
            tr = Ring(nc, st, "p4_t", 10, [128, T], F32)
            stmp = self.sb(st, "p4_stmp", [128, 24, 2], F32)
            xr = Ring(nc, st, "p4_x", 4, [128, T], F32)
            ubr = Ring(nc, st, "p4_ub", 12, [128, T], BF16)
            yfr = Ring(nc, st, "p4_yf", 2, [128, T], F32)
            gy = self.sb(st, "p4_gy", [128, 6, T], BF16)
            sgr = Ring(nc, st, "p4_sg", 2, [128, T], F32)
            ycr = Ring(nc, st, "p4_yc", 2, [128, T], BF16)
            zFv = zF.rearrange("(c p) l -> p c l", p=128)
            yCv = yC.rearrange("(c p) l -> p c l", p=128)
            for ti in range(L // T):
                t0 = ti * T
                un, us = usr.next()
                S.dma('sp', us[:], zFv[:, 12:18, t0:t0 + T], reads=['zF'], writes=[un])
                PP = {}

                def stageA0(p):
                    ch = p // 4
                    prn, pr = self.psum.next()
                    S.op('pe', lambda e: e.matmul(pr[:, :], lhsT=BbR[:, p, :], rhs=us[:, ch, :], start=True, stop=True), reads=['p4_BbR', un], writes=[prn])
                    pin, pi = self.psum.next()
                    S.op('pe', lambda e: e.matmul(pi[:, :], lhsT=BbI[:, p, :], rhs=us[:, ch, :], start=True, stop=True), reads=['p4_BbI', un], writes=[pin])
                    PP[p] = dict(pr=(prn, pr), pi=(pin, pi))

                def stageA(p):
                    c_, s_ = cosT[:, p, :], sinT[:, p, :]
                    prn, pr = PP[p]['pr']; pin, pi = PP[p]['pi']
                    t1n, t1 = tr.next(); t2n, t2 = tr.next(); t3n, t3 = tr.next(); t4n, t4 = tr.next()
                    S.op('dve', lambda e: e.tensor_tensor(t1[:, :], pr[:, :], c_, ALU.mult), reads=[prn, ('p4_cos', p)], writes=[t1n])
                    S.op('dve', lambda e: e.tensor_tensor(t2[:, :], pi[:, :], s_, ALU.mult), reads=[pin, ('p4_sin', p)], writes=[t2n])
                    S.op('dve', lambda e: e.tensor_tensor(t3[:, :], pi[:, :], c_, ALU.mult), reads=[pin, ('p4_cos', p)], writes=[t3n])
                    S.op('dve', lambda e: e.tensor_tensor(t4[:, :], pr[:, :], s_, ALU.mult), reads=[prn, ('p4_sin', p)], writes=[t4n])
                    S.op('pool', lambda e: e.tensor_tensor(t1[:, :], t1[:, :], t2[:, :], ALU.add), reads=[t1n, t2n], writes=[t1n])
                    S.op('pool', lambda e: e.tensor_tensor(t3[:, :], t3[:, :], t4[:, :], ALU.subtract), reads=[t3n, t4n], writes=[t3n])
                    PP[p].update(t1=(t1n, t1), t3=(t3n, t3))

                def stageC(p):
                    c_, s_ = cosT[:, p, :], sinT[:, p, :]
                    t1n, t1 = PP[p]['t1']; t3n, t3 = PP[p]['t3']
                    rb = rho[:, p:p + 1].to_broadcast([128, T])
                    xrn, xre = xr.next(); xin, xim = xr.next()
                    S.op('dve', lambda e: e.tensor_tensor_scan(xre[:, :], rb, t1[:, :], xst[:, 0, p:p + 1], ALU.mult, ALU.add),
                         reads=['p4_rho', t1n, ('p4_xst', p)], writes=[xrn])
                    S.op('dve', lambda e: e.tensor_tensor_scan(xim[:, :], rb, t3[:, :], xst[:, 1, p:p + 1], ALU.mult, ALU.add),
                         reads=['p4_rho', t3n, ('p4_xst', p)], writes=[xin])
                    cl, sl = cosT[:, p, T - 1:T], sinT[:, p, T - 1:T]
                    S.op('act', lambda e: e.activation(stmp[:, p, 0:1], xim[:, T - 1:T], AF.Copy, scale=sl), reads=[xin, ('p4_sin', p)], writes=[('p4_stmp', p)])
                    S.op('act', lambda e: e.activation(stmp[:, p, 1:2], xim[:, T - 1:T], AF.Copy, scale=cl), reads=[xin, ('p4_cos', p)], writes=[('p4_stmp', p)])
                    u1n, u1 = ubr.next(); u2n, u2 = ubr.next(); u3n, u3 = ubr.next(); u4n, u4 = ubr.next()
                    S.op('dve', lambda e: e.tensor_tensor(u1[:, :], xre[:, :], c_, ALU.mult), reads=[xrn, ('p4_cos', p)], writes=[u1n])
                    S.op('dve', lambda e: e.tensor_tensor(u2[:, :], xim[:, :], s_, ALU.mult), reads=[xin, ('p4_sin', p)], writes=[u2n])
                    S.op('dve', lambda e: e.tensor_tensor(u3[:, :], xre[:, :], s_, ALU.mult), reads=[xrn, ('p4_sin', p)], writes=[u3n])
                    S.op('dve', lambda e: e.tensor_tensor(u4[:, :], xim[:, :], c_, ALU.mult), reads=[xin, ('p4_cos', p)], writes=[u4n])
                    S.op('dve', lambda e: e.scalar_tensor_tensor(xst[:, 0, p:p + 1], xre[:, T - 1:T], cl, stmp[:, p, 0:1], ALU.mult, ALU.subtract),
                         reads=[xrn, ('p4_stmp', p), ('p4_cos', p)], writes=[('p4_xst', p)])
                    S.op('dve', lambda e: e.scalar_tensor_tensor(xst[:, 1, p:p + 1], xre[:, T - 1:T], sl, stmp[:, p, 1:2], ALU.mult, ALU.add),
                         reads=[xrn, ('p4_stmp', p), ('p4_sin', p)], writes=[('p4_xst', p)])
                    PP[p].update(u1=(u1n, u1), u2=(u2n, u2), u3=(u3n, u3), u4=(u4n, u4))

                def stageE(p):
                    u1n, u1 = PP[p]['u1']; u2n, u2 = PP[p]['u2']; u3n, u3 = PP[p]['u3']; u4n, u4 = PP[p]['u4']
                    if p % 4 == 0:
                        PP['py'] = psy.next()
                    pyn, py = PP['py']
                    S.op('pe', lambda e: e.matmul(py[:, :], lhsT=CtR[:, p, :], rhs=u1[:, :], start=(p % 4 == 0), stop=False), reads=['p4_CtR', u1n], writes=[pyn], inc=False)
                    S.op('pe', lambda e: e.matmul(py[:, :], lhsT=CtRn[:, p, :], rhs=u2[:, :], start=False, stop=False), reads=['p4_CtRn', u2n], writes=[pyn], inc=False)
                    S.op('pe', lambda e: e.matmul(py[:, :], lhsT=CtI[:, p, :], rhs=u3[:, :], start=False, stop=False), reads=['p4_CtI', u3n], writes=[pyn], inc=False)
                    S.op('pe', lambda e: e.matmul(py[:, :], lhsT=CtI[:, p, :], rhs=u4[:, :], start=False, stop=(p % 4 == 3)), reads=['p4_CtI', u4n], writes=[pyn])
                    if p % 4 == 3:
                        oc = p // 4
                        yfn, yf = yfr.next()
                        S.op('dve', lambda e: e.scalar_tensor_tensor(yf[:, :], us[:, oc, :], dsk[:, oc:oc + 1], py[:, :], ALU.mult, ALU.add),
                             reads=[un, 'p4_d', pyn], writes=[yfn])
                        S.op('act', lambda e: e.activation(gy[:, oc, :], yf[:, :], AF.Gelu_apprx_tanh), reads=[yfn], writes=[('p4_gy', oc)])
                    del PP[p]

                stageA0(0)
                for it in range(24 + 2):
                    if it + 1 < 24:
                        stageA0(it + 1)
                    lists = []
                    for (fn_, arg, ok) in ((stageA, it, it < 24), (stageC, it - 1, 1 <= it <= 24), (stageE, it - 2, it >= 2)):
                        if ok:
                            S.defer = []
                            fn_(arg)
                            lists.append(S.defer)
                            S.defer = None
                    while any(lists):
                        for lst in lists:
                            if lst:
                                S.run_deferred(lst.pop(0))
                for j in range(6):
                    pan, pa_ = self.psum.next()
                    for k in range(6):
                        S.op('pe', lambda e: e.matmul(pa_[:, :], lhsT=wg[:, k, j * 128:(j + 1) * 128], rhs=gy[:, k, :], start=(k == 0), stop=(k == 5)),
                             reads=['p4_wg', ('p4_gy', k)], writes=[pan], inc=(k == 5))
                    pbn, pb_ = self.psum.next()
                    for k in range(6):
                        S.op('pe', lambda e: e.matmul(pb_[:, :], lhsT=wg[:, k, 768 + j * 128:768 + (j + 1) * 128], rhs=gy[:, k, :], start=(k == 0), stop=(k == 5)),
                             reads=['p4_wg', ('p4_gy', k)], writes=[pbn], inc=(k == 5))
                    sgn, sg = sgr.next()
                    S.op('act', lambda e: e.activation(sg[:, :], pb_[:, :], AF.Sigmoid), reads=[pbn], writes=[sgn])
                    ycn, yc = ycr.next()
                    S.op('dve', lambda e: e.tensor_tensor(yc[:, :], pa_[:, :], sg[:, :], ALU.mult), reads=[pan, sgn], writes=[ycn])
                    S.dma('sp', yCv[:, j, t0:t0 + T], yc[:, :], reads=[ycn], writes=['yC'])

    def phase_xattn(self, l, mem, zF, yX):
        nc, S, L = self.nc, self.S, self.L
        with ExitStack() as st:
            kT = self.sb(st, "p5_kT", [128, 4, 2, 256], BF16)
            vm = self.sb(st, "p5_vm", [128, 2, 768], BF16)
            with ExitStack() as st2:
                memt = self.sb(st2, "p5_mem", [128, 2, D], F32)
                memT = self.sb(st2, "p5_memT", [128, 8, 256], F32)
                sq = self.sb(st2, "p5_sq", [128, 8, 256], BF16)
                rstd = self.sb(st2, "p5_rstd", [128, 256], F32)
                mn = self.sb(st2, "p5_mn", [128, 8, 256], BF16)
                wkv = self.sb(st2, "p5_wkv", [128, 8, 1536], BF16)
                S.dma('sp', memt[:], mem.rearrange("(b p) d -> p b d", p=128), writes=['p5_mem'])
                S.dma('sp', wkv[:], self.wB["mem_wkv"][l].rearrange("(k p) n -> p k n", p=128), reads=[('mem_wkvB', l)], writes=['p5_wkv'])
                for b in range(2):
                    for half in range(2):
                        pn, ps = self.psum.next()
                        for j in range(4):
                            c = half * 4 + j
                            S.op('pe', lambda e: e.transpose(ps[:, j * 128:(j + 1) * 128], memt[:, b, c * 128:(c + 1) * 128], self.ident_f[:]),
                                 reads=['p5_mem', 'ident_f'], writes=[pn], inc=(j == 3))
                        S.op('dve', lambda e: e.tensor_copy(memT[:, half * 4:(half + 1) * 4, b * 128:(b + 1) * 128], ps[:, :].rearrange("p (a b) -> p a b", a=4)),
                             reads=[pn], writes=['p5_memT'])
                for c in range(8):
                    S.op('act', lambda e: e.activation(sq[:, c, :], memT[:, c, :], AF.Square), reads=['p5_memT'], writes=['p5_sq'])
                self.rms_stats('p5_sq', sq, 'p5_rstd', rstd, 256)
                for c in range(8):
                    S.op('dve', lambda e: e.scalar_tensor_tensor(mn[:, c, :], memT[:, c, :], self.gains_s[:, l, 2, c:c + 1], rstd[:, :], ALU.mult, ALU.mult),
                         reads=['p5_memT', 'p5_rstd', 'gains_s'], writes=['p5_mn'])
                for h in range(4):
                    for part, (off, M) in enumerate(((0, 128), (128, 64))):
                        col = 192 * h + off
                        pn, ps = self.psum.next()
                        for k in range(8):
                            S.op('pe', lambda e: e.matmul(ps[:M, :256], lhsT=wkv[:, k, col:col + M], rhs=mn[:, k, :], start=(k == 0), stop=(k == 7)),
                                 reads=['p5_wkv', 'p5_mn'], writes=[pn], inc=(k == 7))
                        S.op('dve', lambda e: e.tensor_copy(kT[:M, h, part, :], ps[:M, :256]), reads=[pn], writes=['p5_kT'])
                for mc in range(2):
                    for (n0, nn) in ((0, 512), (512, 256)):
                        pn, ps = self.psum.next()
                        for k in range(8):
                            S.op('pe', lambda e: e.matmul(ps[:, :nn], lhsT=mn[:, k, mc * 128:(mc + 1) * 128], rhs=wkv[:, k, 768 + n0:768 + n0 + nn], start=(k == 0), stop=(k == 7)),
                                 reads=['p5_wkv', 'p5_mn'], writes=[pn], inc=(k == 7))
                        S.op('dve', lambda e: e.tensor_copy(vm[:, mc, n0:n0 + nn], ps[:, :nn]), reads=[pn], writes=['p5_vm'])
                S.barrier()
            xq0r = Ring(nc, st, "p5_xq0", 2, [128, T], BF16)
            xq1r = Ring(nc, st, "p5_xq1", 2, [128, T], BF16)
            ptr = Ring(nc, st, "p5_pt", 4, [128, T], BF16)
            rcr = Ring(nc, st, "p5_rc", 2, [128, T], F32)
            y0r = Ring(nc, st, "p5_y0", 2, [128, T], BF16)
            y1r = Ring(nc, st, "p5_y1", 2, [128, T], BF16)
            XQ0 = 18 * 128
            sc = 192.0 ** -0.5
            for ti in range(L // T):
                t0 = ti * T
                for h in range(4):
                    q0n, q0 = xq0r.next()
                    q1n, q1 = xq1r.next()
                    r0 = XQ0 + 192 * h
                    S.dma('sp', q0[:, :], zF[r0:r0 + 128, t0:t0 + T], reads=['zF'], writes=[q0n])
                    S.dma('sp', q1[0:64, :], zF[r0 + 128:r0 + 192, t0:t0 + T], reads=['zF'], writes=[q1n])
                    pts = []
                    for mc in range(2):
                        pn, ps = self.psum.next()
                        S.op('pe', lambda e: e.matmul(ps[:, :], lhsT=kT[:, h, 0, mc * 128:(mc + 1) * 128], rhs=q0[:, :], start=True, stop=False),
                             reads=['p5_kT', q0n], writes=[pn], inc=False)
                        S.op('pe', lambda e: e.matmul(ps[:, :], lhsT=kT[0:64, h, 1, mc * 128:(mc + 1) * 128], rhs=q1[0:64, :], start=False, stop=True),
                             reads=['p5_kT', q1n], writes=[pn])
                        ptn, pt = ptr.next()
                        S.op('act', lambda e: e.activation(pt[:, :], ps[:, :], AF.Exp, scale=sc), reads=[pn], writes=[ptn])
                        pts.append((ptn, pt))
                    pn, ps = self.psum.next()
                    for mc in range(2):
                        S.op('pe', lambda e: e.matmul(ps[:, :], lhsT=self.ones_b[:], rhs=pts[mc][1][:, :], start=(mc == 0), stop=(mc == 1)),
                             reads=['ones_b', pts[mc][0]], writes=[pn], inc=(mc == 1))
                    rcn, rc = rcr.next()
                    S.op('dve', lambda e: e.reciprocal(rc[:, :], ps[:, :]), reads=[pn], writes=[rcn])
                    for part, (off, M, yr_) in enumerate(((0, 128, y0r), (128, 64, y1r))):
                        pn, ps = self.psum.next()
                        for mc in range(2):
                            S.op('pe', lambda e: e.matmul(ps[:M, :], lhsT=vm[:, mc, 192 * h + off:192 * h + off + M], rhs=pts[mc][1][:, :], start=(mc == 0), stop=(mc == 1)),
                                 reads=['p5_vm', pts[mc][0]], writes=[pn], inc=(mc == 1))
                        yn, y = yr_.next()
                        S.op('dve', lambda e: e.tensor_tensor(y[:M, :], ps[:M, :], rc[:M, :], ALU.mult), reads=[pn, rcn], writes=[yn])
                        S.dma('sp', yX[192 * h + off:192 * h + off + M, t0:t0 + T], y[:M, :], reads=[yn], writes=['yX'])

    def phase_merge(self, l, xT, xmT, zF, yA, yC, yX, oG):
        nc, S, L = self.nc, self.S, self.L
        ph = self.phases
        branches = []
        if 'p2' in ph:
            branches.append((0, 'proj_a', 6))
        if 'p3' in ph:
            branches.append((1, 'proj_b', 2))
        if 'p4' in ph:
            branches.append((2, 'proj_c', 6))
        if 'p5' in ph:
            branches.append((3, 'proj_x', 6))
        with ExitStack() as st:
            wp = {}
            for (b, nm, nk) in branches:
                wp[b] = self.sb(st, "p6_" + nm, [128, nk, D], BF16)
                S.dma('sp', wp[b][:], self.wB[nm][l].rearrange("(k p) n -> p k n", p=128), reads=[(nm + 'B', l)], writes=['p6_w%d' % b])
            wo = self.sb(st, "p6_wo", [128, 8, D], BF16)
            S.dma('sp', wo[:], self.wB["w_out"][l].rearrange("(k p) n -> p k n", p=128), reads=[('w_outB', l)], writes=['p6_wo'])
            yr = {b: Ring(nc, st, "p6_y%d" % b, 1, [128, nk, T], BF16) for (b, nm, nk) in branches}
            sgr = Ring(nc, st, "p6_sg", 3, [128, 4, T], BF16)
            xr = Ring(nc, st, "p6_x", 1, [128, 8, T], F32)
            xor_ = Ring(nc, st, "p6_xo", 1, [128, 8, T], F32)
            macc = Ring(nc, st, "p6_macc", 2, [128, T], F32)
            mtmp = Ring(nc, st, "p6_mtmp", 4, [128, T], F32)
            mb = self.sb(st, "p6_mb", [128, 8, T], BF16)
            sq = self.sb(st, "p6_sq", [128, 8, T], BF16)
            rstd = self.sb(st, "p6_rstd", [128, T], F32)
            y = self.sb(st, "p6_yy", [128, 8, T], F32)
            og = Ring(nc, st, "p6_og", 2, [128, 3, 260], F32)
            ybt = Ring(nc, st, "p6_ybt", 2, [128, 256], F32)
            rl = Ring(nc, st, "p6_rl", 2, [128, 4], F32)
            srcs = {0: yA, 2: yC, 3: yX}
            for ti in range(L // T):
                t0 = ti * T
                ys = {}
                for (b, nm, nk) in branches:
                    yn, yt = yr[b].next()
                    ys[b] = (yn, yt)
                    if b != 1:
                        S.dma('sp', yt[:], srcs[b].rearrange("(c p) l -> p c l", p=128)[:, :, t0:t0 + T], reads=[srcs[b].tensor.name], writes=[yn])
                    else:
                        for tb in range(4):
                            on, o = og.next()
                            S.dma('sp', o[:], oG[:, t0 + tb * 128:t0 + (tb + 1) * 128, :].rearrange("g p n -> p g n"), reads=['oG'], writes=[on])
                            S.op('dve', lambda e: e.tensor_tensor(o[:, 0, :], o[:, 0, :], o[:, 1, :], ALU.add), reads=[on], writes=[on])
                            S.op('dve', lambda e: e.tensor_tensor(o[:, 0, :], o[:, 0, :], o[:, 2, :], ALU.add), reads=[on], writes=[on])
                            rn, r = rl.next()
                            ov = o[:, 0, :].rearrange("p (h e) -> p h e", e=65)
                            S.op('dve', lambda e: e.reciprocal(r[:, :], ov[:, :, 64]), reads=[on], writes=[rn])
                            bn, bt = ybt.next()
                            for hh in range(4):
                                S.op('dve', lambda e: e.tensor_scalar(bt[:, hh * 64:(hh + 1) * 64], ov[:, hh, 0:64], r[:, hh:hh + 1], None, ALU.mult),
                                     reads=[on, rn], writes=[bn])
                            pn, ps = self.psum.next()
                            for half in range(2):
                                S.op('pe', lambda e: e.transpose(ps[:, half * 128:(half + 1) * 128], bt[:, half * 128:(half + 1) * 128], self.ident_f[:]),
                                     reads=[bn, 'ident_f'], writes=[pn], inc=(half == 1))
                            S.op('act', lambda e: e.copy(yt[:, :, tb * 128:(tb + 1) * 128], ps[:, 0:256].rearrange("p (a b) -> p a b", a=2)),
                                 reads=[pn], writes=[yn])
                xn, xt = xr.next()
                S.dma('sp', xt[:], xT.rearrange("(c p) l -> p c l", p=128)[:, :, t0:t0 + T], reads=[('xT', ti)], writes=[xn])
                def mchunk(c):
                    mn_, m = macc.next()
                    sn, sg = sgr.next()
                    S.dma('sp', sg[:], zF.rearrange("(b c p) l -> p b c l", p=128, c=8)[:, 3:7, c, t0:t0 + T], reads=['zF'], writes=[sn])
                    for bi, (b, nm, nk) in enumerate(branches):
                        yn, yt = ys[b]
                        pn, ps = self.psum.next()
                        for k in range(nk):
                            S.op('pe', lambda e, b=b, k=k, ps=ps, yt=yt, nk=nk: e.matmul(ps[:, :], lhsT=wp[b][:, k, c * 128:(c + 1) * 128], rhs=yt[:, k, :], start=(k == 0), stop=(k == nk - 1)),
                                 reads=['p6_w%d' % b, yn], writes=[pn], inc=(k == nk - 1))
                        if bi == 0:
                            S.op('dve', lambda e, b=b, ps=ps: e.tensor_tensor(m[:, :], ps[:, :], sg[:, b, :], ALU.mult), reads=[pn, sn], writes=[mn_])
                        else:
                            tn, tm = mtmp.next()
                            S.op('dve', lambda e, b=b, ps=ps, tm=tm: e.tensor_tensor(tm[:, :], ps[:, :], sg[:, b, :], ALU.mult), reads=[pn, sn], writes=[tn])
                            S.op('dve', lambda e, tm=tm: e.tensor_tensor(m[:, :], m[:, :], tm[:, :], ALU.add), reads=[tn, mn_], writes=[mn_])
                    S.op('act', lambda e: e.copy(mb[:, c, :], m[:, :]), reads=[mn_], writes=[('p6_mb', c)])
                for c in range(0, 8, 2):
                    S.interleave([lambda c=c: mchunk(c), lambda c=c: mchunk(c + 1)])
                for c in range(8):
                    pn, ps = self.psum.next()
                    for k in range(8):
                        S.op('pe', lambda e: e.matmul(ps[:, :], lhsT=wo[:, k, c * 128:(c + 1) * 128], rhs=mb[:, k, :], start=(k == 0), stop=(k == 7)),
                             reads=['p6_wo', ('p6_mb', k)], writes=[pn], inc=(k == 7))
                    S.op('act', lambda e: e.activation(sq[:, c, :], ps[:, :], AF.Square), reads=[pn], writes=['p6_sq'])
                    S.op('act', lambda e: e.copy(y[:, c, :], ps[:, :]), reads=[pn], writes=['p6_yy'])
                on_, xo_t = xor_.next()
                self.postnorm_residual(l, 1, 'p6_yy', y, 'p6_sq', sq, 'p6_rstd', rstd, xn, xt, on_, xo_t)
                S.dma('sp', xmT.rearrange("(c p) l -> p c l", p=128)[:, :, t0:t0 + T], xo_t[:], reads=[on_], writes=[('xmT', ti)])

    def postnorm_residual(self, l, which, y_name, y, sq_name, sq, rstd_name, rstd, xres_name, xres, xo_name, xo):
        S = self.S
        self.rms_stats(sq_name, sq, rstd_name, rstd, T)
        for c in range(8):
            S.op('dve', lambda e: e.scalar_tensor_tensor(y[:, c, :], y[:, c, :], self.gains_s[:, l, which, c:c + 1], rstd[:, :],
                                                         ALU.mult, ALU.mult),
                 reads=[y_name, rstd_name, 'gains_s'], writes=[y_name])
            S.op('dve', lambda e: e.tensor_tensor(xo[:, c, :], y[:, c, :], xres[:, c, :], ALU.add),
                 reads=[y_name, xres_name], writes=[xo_name])

    def phase_ffn(self, l, xsrc, xdst, wUpB, wDnB, ffnp):
        nc, S, L = self.nc, self.S, self.L
        NT = 2
        with ExitStack() as st:
            xr = Ring(nc, st, "p7_x", 2, [128, 8, T], F32)
            sq = self.sb(st, "p7_sq", [128, 8, T], BF16)
            rstd = self.sb(st, "p7_rstd", [128, T], F32)
            hr = Ring(nc, st, "p7_h", 2, [128, 8, T], BF16)
            wr = Ring(nc, st, "p7_w", 2, [128, 8, 1024], BF16)
            actr = Ring(nc, st, "p7_act", 2, [128, 24, T], BF16)
            gsb = Ring(nc, st, "p7_g", 3, [128, T + 2], F32)
            gtmp = Ring(nc, st, "p7_gt", 3, [128, T], F32)
            tails = self.sb(st, "p7_tail", [128, 24, 2], F32)
            fp = self.sb(st, "p7_fp", [128, 4, 24], F32)
            self.ys = [self.sb(st, "p7_y%d" % i, [128, 8, T], F32) for i in range(2)]
            self.sqs = [self.sb(st, "p7_sqo%d" % i, [128, 8, T], BF16) for i in range(2)]
            S.dma('sp', fp[:], ffnp[:, l, :, :], writes=['p7_fp'])
            S.op('dve', lambda e: e.memset(tails[:], 0.0), writes=[('p7_tail', c) for c in range(24)])
            wup = wUpB[l].rearrange("(k p) n -> p k n", p=128)
            wdn = wDnB[l].rearrange("(k p) n -> p k n", p=128)
            wq = 0
            for tp in range(L // (NT * T)):
                tiles = []
                for ti in range(tp * NT, (tp + 1) * NT):
                    t0 = ti * T
                    xn, xt = xr.next()
                    S.dma('sp', xt[:], xsrc.rearrange("(c p) l -> p c l", p=128)[:, :, t0:t0 + T], reads=[(xsrc.tensor.name, ti)], writes=[xn])
                    hn, h = hr.next()
                    self.prenorm(l, 3, xn, xt, hn, h, "p7_sq", sq, "p7_rstd", rstd)
                    an, act = actr.next()
                    tiles.append((ti, t0, xn, xt, hn, h, an, act))
                for blk in range(6):
                    wn, w = wr.next()
                    wq += 1
                    q_ = 'sp' if wq % 2 == 0 else 'pool'
                    S.dma(q_, w[:, :, 0:512], wup[:, :, blk * 512:(blk + 1) * 512], reads=[('wUpB', l)], writes=[wn])
                    S.dma(q_, w[:, :, 512:1024], wup[:, :, DFF + blk * 512:DFF + (blk + 1) * 512], reads=[('wUpB', l)], writes=[wn])
                    for (ti, t0, xn, xt, hn, h, an, act) in tiles:
                        def fchunk(j, hn=hn, h=h, an=an, act=act, wn=wn, w=w):
                            ch = blk * 4 + j
                            pgn, pg = self.psum.next()
                            for k in range(8):
                                S.op('pe', lambda e, k=k: e.matmul(pg[:, :], lhsT=w[:, k, 512 + j * 128:512 + (j + 1) * 128], rhs=h[:, k, :], start=(k == 0), stop=(k == 7)),
                                     reads=[wn, hn], writes=[pgn], inc=(k == 7))
                            pvn, pv = self.psum.next()
                            for k in range(8):
                                S.op('pe', lambda e, k=k: e.matmul(pv[:, :], lhsT=w[:, k, j * 128:(j + 1) * 128], rhs=h[:, k, :], start=(k == 0), stop=(k == 7)),
                                     reads=[wn, hn], writes=[pvn], inc=(k == 7))
                            gn, g = gsb.next()
                            S.op('act', lambda e: e.copy(g[:, 2:T + 2], pg[:, :]), reads=[pgn], writes=[gn])
                            S.op('act', lambda e: e.copy(g[:, 0:2], tails[:, ch, :]), reads=[('p7_tail', ch)], writes=[gn])
                            S.op('act', lambda e: e.copy(tails[:, ch, :], g[:, T:T + 2]), reads=[gn], writes=[('p7_tail', ch)])
                            tn, tm = gtmp.next()
                            S.op('dve', lambda e: e.tensor_scalar(tm[:, :], g[:, 0:T], fp[:, 0, ch:ch + 1], fp[:, 3, ch:ch + 1], ALU.mult, ALU.add),
                                 reads=[gn, 'p7_fp'], writes=[tn])
                            S.op('dve', lambda e: e.scalar_tensor_tensor(tm[:, :], g[:, 1:T + 1], fp[:, 1, ch:ch + 1], tm[:, :], ALU.mult, ALU.add),
                                 reads=[gn, tn, 'p7_fp'], writes=[tn])
                            S.op('dve', lambda e: e.scalar_tensor_tensor(tm[:, :], g[:, 2:T + 2], fp[:, 2, ch:ch + 1], tm[:, :], ALU.mult, ALU.add),
                                 reads=[gn, tn, 'p7_fp'], writes=[tn])
                            S.op('act', lambda e: e.activation(tm[:, :], tm[:, :], AF.Gelu_apprx_tanh), reads=[tn], writes=[tn])
                            S.op('dve', lambda e: e.tensor_tensor(act[:, ch, :], pv[:, :], tm[:, :], ALU.mult), reads=[pvn, tn], writes=[(an, ch)])
                        for j in range(0, 4, 2):
                            S.interleave([lambda j=j: fchunk(j), lambda j=j: fchunk(j + 1)])
                for half in range(2):
                    banks = {}
                    for cpair in range(2):
                        for (ti, t0, xn, xt, hn, h, an, act) in tiles:
                            banks[ti] = [self.psum.next() for _ in range(2)]
                        for kb in range(3):
                            wn, w = wr.next()
                            wq += 1
                            q_ = 'sp' if wq % 2 == 0 else 'pool'
                            c00 = half * 512 + cpair * 256
                            S.dma(q_, w[:, :, 0:256], wdn[:, kb * 8:(kb + 1) * 8, c00:c00 + 256], reads=[('wDnB', l)], writes=[wn])
                            for (ti, t0, xn, xt, hn, h, an, act) in tiles:
                                for cc in range(2):
                                    pn, ps = banks[ti][cc]
                                    for k in range(8):
                                        kk = kb * 8 + k
                                        S.op('pe', lambda e: e.matmul(ps[:, :], lhsT=w[:, k, cc * 128:(cc + 1) * 128], rhs=act[:, kk, :], start=(kk == 0), stop=(kk == 23)),
                                             reads=[wn, (an, kk)], writes=[pn], inc=(k == 7))
                        for (ti, t0, xn, xt, hn, h, an, act) in tiles:
                            for cc in range(2):
                                c = half * 4 + cpair * 2 + cc
                                pn, ps = banks[ti][cc]
                                S.op('act', lambda e: e.activation(self.sqs[ti % 2][:, c, :], ps[:, :], AF.Square), reads=[pn], writes=[('p7_sq2', ti % 2)])
                                S.op('act', lambda e: e.copy(self.ys[ti % 2][:, c, :], ps[:, :]), reads=[pn], writes=[('p7_y2', ti % 2)])
                for (ti, t0, xn, xt, hn, h, an, act) in tiles:
                    on, xo_t = ('p7_y2', ti % 2), self.ys[ti % 2]
                    self.postnorm_residual(l, 4, ('p7_y2', ti % 2), self.ys[ti % 2], ('p7_sq2', ti % 2), self.sqs[ti % 2], 'p7_rstd', rstd, xn, xt, on, xo_t)
                    S.dma('sp', xdst.rearrange("(c p) l -> p c l", p=128)[:, :, t0:t0 + T], xo_t[:], reads=[on], writes=[(xdst.tensor.name, ti)])


def host_inputs(inputs, b, L):
    f = np.float32
    d = {}
    d["x"] = np.ascontiguousarray(inputs["x"][b, :L])
    d["mem"] = np.ascontiguousarray(inputs["mem"][b])
    d["ident"] = np.eye(128, dtype=f)
    gs = np.stack([inputs[k] for k in ("g_mix_pre", "g_mix_post", "g_mem", "g_mlp_pre", "g_mlp_post")], axis=1)
    d["gains"] = np.ascontiguousarray(gs.reshape(NL, 5, 8, 128).transpose(3, 0, 1, 2)).astype(f)
    d["w_in"] = inputs["w_in"]
    d["ffn_w_up"] = inputs["ffn_w_up"]
    d["ffn_w_down"] = inputs["ffn_w_down"]
    fp = np.concatenate([inputs["ffn_conv_w"], inputs["ffn_conv_b"][:, None, :]], axis=1)
    lp = np.concatenate([inputs["lru_conv_w"], inputs["lru_conv_b"][:, None], inputs["lru_ba"][:, None],
                         inputs["lru_bx"][:, None], inputs["lru_lambda"][:, None]], axis=1)
    d["lrup"] = np.ascontiguousarray(lp.reshape(NL, 8, 6, 128).transpose(3, 0, 1, 2)).astype(f)
    bd = np.zeros((NL, 2, 128, 6, 128), f)
    for wi, nm in enumerate(("lru_wa", "lru_wx")):
        w = inputs[nm]
        for c in range(6):
            bd[:, wi, 0:64, c, 0:64] = w[:, 2 * c]
            bd[:, wi, 64:128, c, 64:128] = w[:, 2 * c + 1]
    d["lru_bd"] = bd
    d["ssmd"] = np.ascontiguousarray(inputs["ssm_d"].reshape(NL, 6, 128).transpose(2, 0, 1)).astype(f)
    for nm in ("ssm_glu", "mem_wkv", "proj_a", "proj_b", "proj_c", "proj_x", "w_out"):
        d[nm] = inputs[nm]
    am = np.full((128, 2, 12, 256), -30000.0, f)
    kk = np.arange(128)[:, None]
    qq = np.arange(128)[None, :]
    for hd in range(12):
        dd = DILS[hd // 4]
        dp = (qq + 128 - kk).astype(f)
        dc = (qq - kk).astype(f)
        mp = np.where(kk >= qq, -ALIBI[hd] * dd * dp, -30000.0)
        mc = np.where(kk <= qq, -ALIBI[hd] * dd * dc, -30000.0)
        am[:, 0, hd, 0:128] = mp
        am[:, 0, hd, 128:256] = mc
        am[:, 1, hd, 128:256] = mc
    d["amask"] = am
    def lay_a(a):
        return a.reshape(NL, 24, 2, 64).transpose(2, 3, 0, 1).reshape(128, NL, 24)
    ld_full = np.repeat(inputs["ssm_log_dt"][:, :, None], 64, axis=2)
    d["s5A"] = np.ascontiguousarray(np.stack([lay_a(inputs["ssm_a_re"]), lay_a(inputs["ssm_a_im"]), lay_a(ld_full)], axis=2)).astype(f)
    def lay_r(a):
        return a.reshape(NL, 3072)
    rr = np.stack([lay_r(inputs["ssm_a_re"]), lay_r(inputs["ssm_a_im"]), lay_r(ld_full)], axis=1)
    d["s5R"] = np.ascontiguousarray(np.broadcast_to(rr[None], (128, NL, 3, 3072))).astype(f)
    sB = np.zeros((NL, 2, 128, 24, 128), f)
    sC = np.zeros((NL, 2, 128, 24, 128), f)
    for ri, (bn, cn) in enumerate((("ssm_b_re", "ssm_c_re"), ("ssm_b_im", "ssm_c_im"))):
        Bm = inputs[bn]
        Cm = inputs[cn]
        for p in range(24):
            for gl in range(2):
                r0 = 32 * (p % 4) + gl * 16
                sB[:, ri, r0:r0 + 16, p, gl * 64:(gl + 1) * 64] = Bm[:, 2 * p + gl].transpose(0, 2, 1)
                sC[:, ri, gl * 64:(gl + 1) * 64, p, r0:r0 + 16] = Cm[:, 2 * p + gl].transpose(0, 2, 1)
    d["s5B"] = sB
    d["s5C"] = sC
    d["ffnp"] = np.ascontiguousarray(fp.reshape(NL, 4, 24, 128).transpose(3, 0, 1, 2)).astype(f)
    return d


ALL_PHASES = ('prepass', 'prologue', 'p1', 'p2', 'p3', 'p4', 'p5', 'p6', 'p7', 'epilogue')


def kernel(**inputs):
    inputs = {k: np.asarray(v) for k, v in inputs.items()}
    L = inputs["x"].shape[1]
    kb = K(L, NL)
    nc = kb.build(ALL_PHASES)
    in_maps = []
    for b in range(2):
        hi = host_inputs(inputs, b, L)
        in_maps.append({k: hi[k] for k in kb.ins})
    res = run_bass_kernel_spmd(nc, in_maps, core_ids=[0, 1])
    return np.stack([res.results[b]["out"] for b in range(2)], axis=0)
```

```python
import math
from contextlib import ExitStack
import numpy as np
import concourse.bass as bass
import concourse.mybir as mybir
from concourse.bass_utils import run_bass_kernel_spmd

AF = mybir.ActivationFunctionType
ALU = mybir.AluOpType
F32 = mybir.dt.float32
BF16 = mybir.dt.bfloat16

D = 1024
NL = 2
SEQ = 16384
MEM = 256
IN_W = 9472
DFF = 3072
T = 512
EPS = 1e-6
ALIBI = [2.0 ** (-8.0 * (h + 1) / 12) for h in range(12)]
DILS = (1, 4, 16)
import os
P3STOP = int(os.environ.get("P3STOP", "0"))


class Sched:
    NSLOT = 8

    def __init__(self, nc, stack):
        self.nc = nc
        self.engs = {'pe': nc.tensor, 'act': nc.scalar, 'dve': nc.vector,
                     'pool': nc.gpsimd, 'sp': nc.sync}
        self.sem = {}
        self.cnt = {}
        for n in ['pe', 'act', 'dve', 'pool']:
            self.sem[n] = stack.enter_context(nc.semaphore("s_" + n))
            self.cnt[n] = 0
        self.dq = {}
        for q in ['sp', 'pool', 'act']:
            for i in range(self.NSLOT):
                self.sem[('dma', q, i)] = stack.enter_context(nc.semaphore("d_%s%d" % (q, i)))
            self.dq[q] = 0
        self.seen = {e: {} for e in self.engs}
        self.lastw = {}
        self.readers = {}

    def _deps(self, reads, writes):
        deps = []
        for r in reads:
            t = self.lastw.get(r)
            if t is not None:
                deps.append(t)
        for w in writes:
            t = self.lastw.get(w)
            if t is not None:
                deps.append(t)
            deps.extend(self.readers.get(w, ()))
        return deps

    def _wait(self, ename, deps):
        best = {}
        for (src, val) in deps:
            if best.get(src, 0) < val:
                best[src] = val
        seen = self.seen[ename]
        eng = self.engs[ename]
        for src, val in best.items():
            if src == 'pe' and ename == 'pe':
                continue
            if seen.get(src, 0) >= val:
                continue
            eng.wait_ge(self.sem[src], val)
            seen[src] = val

    def _record(self, ticket, reads, writes):
        for r in reads:
            self.readers.setdefault(r, []).append(ticket)
        for w in writes:
            self.lastw[w] = ticket
            self.readers[w] = []

    defer = None

    def op(self, ename, fn, reads=(), writes=(), inc=True):
        if self.defer is not None:
            self.defer.append(('__op__', (ename, fn, reads, writes, inc)))
            return None
        self._wait(ename, self._deps(reads, writes))
        ins = fn(self.engs[ename])
        if inc:
            self.cnt[ename] += 1
            ins.then_inc(self.sem[ename], 1)
            ticket = (ename, self.cnt[ename])
        else:
            ticket = (ename, self.cnt[ename] + 1)
        self._record(ticket, reads, writes)
        return ticket

    def dma(self, q, out, in_, reads=(), writes=(), **kw):
        if self.defer is not None:
            self.defer.append(('__dma__', (q, out, in_, reads, writes, kw)))
            return None
        i = self.dq[q]
        slot = i % self.NSLOT
        rnd = i // self.NSLOT
        src = ('dma', q, slot)
        deps = self._deps(reads, writes)
        if rnd > 0:
            deps.append((src, 16 * rnd))
        self._wait(q, deps)
        ins = self.engs[q].dma_start(out=out, in_=in_, **kw)
        ins.then_inc(self.sem[src], 16)
        self.dq[q] = i + 1
        ticket = (src, 16 * (rnd + 1))
        self._record(ticket, reads, writes)
        return ticket

    def coll(self, kind, src, dst, groups, reads=(), writes=()):
        q = 'pool'
        i = self.dq[q]
        slot = i % self.NSLOT
        rnd = i // self.NSLOT
        srck = ('dma', q, slot)
        deps = self._deps(reads, writes)
        if rnd > 0:
            deps.append((srck, 16 * rnd))
        self._wait(q, deps)
        ins = self.engs[q].collective_compute(kind, ALU.bypass, replica_groups=groups, ins=[src], outs=[dst])
        ins.then_inc(self.sem[srck], 16)
        self.dq[q] = i + 1
        ticket = (srck, 16 * (rnd + 1))
        self._record(ticket, reads, writes)
        return ticket

    def run_deferred(self, item):
        kind, a = item
        if kind == '__op__':
            self.op(*a)
        else:
            q, out, in_, reads, writes, kw = a
            self.dma(q, out, in_, reads=reads, writes=writes, **kw)

    def interleave(self, fns):
        lists = []
        for f in fns:
            self.defer = []
            f()
            lists.append(self.defer)
            self.defer = None
        while any(lists):
            for lst in lists:
                if lst:
                    self.run_deferred(lst.pop(0))

    def finish(self, ename='sp'):
        deps = list(self.lastw.values())
        for l in self.readers.values():
            deps.extend(l)
        self._wait(ename, deps)

    def barrier(self):
        for e in self.engs:
            self.finish(e)
        self.lastw = {}
        self.readers = {}


_UID = [0]


class Ring:
    def __init__(self, nc, stack, name, n, shape, dtype, psum=False):
        self.bufs = []
        _UID[0] += 1
        for i in range(n):
            nm = "%s_%d_%d" % (name, _UID[0], i)
            if psum:
                t = stack.enter_context(nc.psum_tensor(nm, shape, dtype))
            else:
                t = stack.enter_context(nc.sbuf_tensor(nm, shape, dtype))
            self.bufs.append((nm, t))
        self.i = 0

    def next(self):
        b = self.bufs[self.i % len(self.bufs)]
        self.i += 1
        return b


class K:
    def __init__(self, L, nl, dbg=False):
        self.L = L
        self.nl = nl
        self.dbg = dbg
        self.nc = bass.Bass("TRN2", target_bir_lowering=False)
        self.ins = {}
        self.scr = {}

    def inp(self, name, shape, dt=F32):
        t = self.nc.dram_tensor(name, list(shape), dt, kind="ExternalInput").ap()
        self.ins[name] = t
        return t

    def scratch(self, name, shape, dt):
        kind = "ExternalOutput" if self.dbg else "Internal"
        t = self.nc.dram_tensor(name, list(shape), dt, kind=kind).ap()
        self.scr[name] = t
        return t

    def sb(self, st, name, shape, dt):
        _UID[0] += 1
        return st.enter_context(self.nc.sbuf_tensor("%s_%d" % (name, _UID[0]), list(shape), dt))

    def build(self, phases):
        nc = self.nc
        L, nl = self.L, self.nl
        x = self.inp("x", [L, D])
        mem = self.inp("mem", [MEM, D])
        ident = self.inp("ident", [128, 128])
        gains = self.inp("gains", [128, NL, 5, 8])
        w_in = self.inp("w_in", [NL, D, IN_W])
        ffn_w_up = self.inp("ffn_w_up", [NL, D, 2 * DFF])
        ffn_w_down = self.inp("ffn_w_down", [NL, DFF, D])
        ffnp = self.inp("ffnp", [128, NL, 4, 24])
        lrup = self.inp("lrup", [128, NL, 8, 6])
        lru_bd = self.inp("lru_bd", [NL, 2, 128, 6, 128])
        ssmd = self.inp("ssmd", [128, NL, 6])
        wsrc = {}
        for nm, shp in (("ssm_glu", [NL, 768, 1536]), ("mem_wkv", [NL, D, 1536]), ("proj_a", [NL, 768, D]),
                        ("proj_b", [NL, 256, D]), ("proj_c", [NL, 768, D]), ("proj_x", [NL, 768, D]), ("w_out", [NL, D, D])):
            wsrc[nm] = (self.inp(nm, shp), self.scratch(nm + "B", shp, BF16), shp[1])
        self.wB = {nm: v[1] for nm, v in wsrc.items()}
        yA = self.scratch("yA", [768, L], BF16)
        yC = self.scratch("yC", [768, L], BF16)
        yX = self.scratch("yX", [768, L], BF16)
        oG = self.scratch("oG", [3, L, 260], F32)
        self.phases = phases
        amask = self.inp("amask", [128, 2, 12, 256])
        s5A = self.inp("s5A", [128, NL, 3, 24])
        s5R = self.inp("s5R", [128, NL, 3, 3072])
        s5B = self.inp("s5B", [NL, 2, 128, 24, 128])
        s5C = self.inp("s5C", [NL, 2, 128, 24, 128])
        out = self.nc.dram_tensor("out", [L, D], F32, kind="ExternalOutput").ap()
        self.out = out

        xT = self.scratch("xT", [D, L], F32)
        xmT = self.scratch("xmT", [D, L], F32)
        zF = self.scratch("zF", [56 * 128, L], BF16)
        zQKV = self.scratch("zQKV", [L, 2304], BF16)
        wInB = self.scratch("wInB", [NL, D, IN_W], BF16)
        wUpB = self.scratch("wUpB", [NL, D, 2 * DFF], BF16)
        wDnB = self.scratch("wDnB", [NL, DFF, D], BF16)

        with ExitStack() as st0:
            S = Sched(nc, st0)
            self.S = S
            ident_f = self.sb(st0, "ident_f", [128, 128], F32)
            ones_b = self.sb(st0, "ones_b", [128, 128], BF16)
            gains_s = self.sb(st0, "gains_s", [128, NL, 5, 8], F32)
            eps_c = self.sb(st0, "eps_c", [128, 1], F32)
            self.ident_f, self.ones_b, self.gains_s, self.eps_c = ident_f, ones_b, gains_s, eps_c
            one_c = self.sb(st0, "one_c", [128, 1], F32)
            self.one_c = one_c
            S.op('dve', lambda e: e.memset(one_c[:], 1.0), writes=['one_c'])
            S.dma('sp', ident_f[:], ident, writes=['ident_f'])
            S.dma('sp', gains_s[:], gains, writes=['gains_s'])
            S.op('dve', lambda e: e.memset(ones_b[:], 1.0), writes=['ones_b'])
            S.op('dve', lambda e: e.memset(eps_c[:], EPS), writes=['eps_c'])
            self.psum = Ring(nc, st0, "ps", 6, [128, 512], F32, psum=True)

            if 'prepass' in phases:
                for l in range(nl):
                    for (src, dst, rows) in [(w_in, wInB, D), (ffn_w_up, wUpB, D), (ffn_w_down, wDnB, DFF)] + list(wsrc.values()):
                        for r in range(0, rows, 128):
                            S.dma('pool', dst[l, r:r + 128, :], src[l, r:r + 128, :],
                                  writes=[(dst.tensor.name, l)])
            if 'prologue' in phases:
                self.transpose_in(x, xT, L)
                S.barrier()
            for l in range(nl):
                if 'p1' in phases:
                    self.phase_inproj(l, xT, wInB, zF, zQKV)
                    S.barrier()
                if 'p2' in phases:
                    self.phase_lru(l, zF, yA, lrup, lru_bd)
                    S.barrier()
                if 'p3' in phases:
                    self.phase_attn(l, zQKV, oG, amask)
                    S.barrier()
                if 'p4' in phases:
                    self.phase_s5(l, zF, yC, s5A, s5R, s5B, s5C, ssmd)
                    S.barrier()
                if 'p5' in phases:
                    self.phase_xattn(l, mem, zF, yX)
                    S.barrier()
                if 'p6' in phases:
                    self.phase_merge(l, xT, xmT, zF, yA, yC, yX, oG)
                    S.barrier()
                if 'p7' in phases:
                    self.phase_ffn(l, xmT if 'p6' in phases else xT, xT, wUpB, wDnB, ffnp)
                    S.barrier()
            if 'epilogue' in phases:
                self.transpose_out(xT, out, L)
            S.barrier()
        return nc

    def transpose_in(self, x, xT, L):
        nc, S = self.nc, self.S
        with ExitStack() as st:
            xin = Ring(nc, st, "ti_x", 2, [128, D], F32)
            xo = Ring(nc, st, "ti_o", 2, [128, 8, 128], F32)
            for b in range(L // 128):
                nm, xt = xin.next()
                S.dma('sp', xt[:], x[b * 128:(b + 1) * 128, :], writes=[nm])
                no, ot = xo.next()
                for half in range(2):
                    pn, ps = self.psum.next()
                    for j in range(4):
                        c = half * 4 + j
                        S.op('pe', lambda e: e.transpose(ps[:, j * 128:(j + 1) * 128], xt[:, c * 128:(c + 1) * 128], self.ident_f[:]),
                             reads=[nm, 'ident_f'], writes=[pn], inc=(j == 3))
                    eng = 'act' if half == 0 else 'dve'
                    if eng == 'act':
                        S.op('act', lambda e: e.copy(ot[:, half * 4:(half + 1) * 4, :], ps[:, :].rearrange("p (a b) -> p a b", a=4)),
                             reads=[pn], writes=[no])
                    else:
                        S.op('dve', lambda e: e.tensor_copy(ot[:, half * 4:(half + 1) * 4, :], ps[:, :].rearrange("p (a b) -> p a b", a=4)),
                             reads=[pn], writes=[no])
                S.dma('sp', xT.rearrange("(c p) l -> p c l", p=128)[:, :, b * 128:(b + 1) * 128], ot[:],
                      reads=[no], writes=[('xT', b // 4)])

    def transpose_out(self, xT, out, L):
        nc, S = self.nc, self.S
        with ExitStack() as st:
            xin = Ring(nc, st, "to_x", 2, [128, 8, 128], F32)
            xo = Ring(nc, st, "to_o", 2, [128, D], F32)
            for b in range(L // 128):
                nm, xt = xin.next()
                S.dma('sp', xt[:], xT.rearrange("(c p) l -> p c l", p=128)[:, :, b * 128:(b + 1) * 128],
                      reads=[('xT', b // 4)], writes=[nm])
                no, ot = xo.next()
                for half in range(2):
                    pn, ps = self.psum.next()
                    for j in range(4):
                        c = half * 4 + j
                        S.op('pe', lambda e: e.transpose(ps[:, j * 128:(j + 1) * 128], xt[:, c, :], self.ident_f[:]),
                             reads=[nm, 'ident_f'], writes=[pn], inc=(j == 3))
                    if half == 0:
                        S.op('act', lambda e: e.copy(ot[:, 0:512], ps[:, :]), reads=[pn], writes=[no])
                    else:
                        S.op('dve', lambda e: e.tensor_copy(ot[:, 512:1024], ps[:, :]), reads=[pn], writes=[no])
                S.dma('sp', out[b * 128:(b + 1) * 128, :], ot[:], reads=[no], writes=['out'])

    def rms_stats(self, sqname, sq, rstd_name, rstd, Tn):
        S = self.S
        pn, ps = self.psum.next()
        for c in range(8):
            S.op('pe', lambda e: e.matmul(ps[:, :Tn], lhsT=self.ones_b[:], rhs=sq[:, c, :], start=(c == 0), stop=(c == 7)),
                 reads=[sqname, 'ones_b'], writes=[pn], inc=(c == 7))
        S.op('act', lambda e: e.activation(rstd[:, :Tn], ps[:, :Tn], AF.Sqrt, bias=self.eps_c[:], scale=1.0 / D),
             reads=[pn, 'eps_c'], writes=[rstd_name])
        S.op('dve', lambda e: e.reciprocal(rstd[:, :Tn], rstd[:, :Tn]), reads=[rstd_name], writes=[rstd_name])

    def prenorm(self, l, which, xt_name, xt, h_name, h, sq_name, sq, rstd_name, rstd):
        S = self.S
        for c in range(8):
            S.op('act', lambda e: e.activation(sq[:, c, :], xt[:, c, :], AF.Square), reads=[xt_name], writes=[sq_name])
        self.rms_stats(sq_name, sq, rstd_name, rstd, T)
        for c in range(8):
            S.op('dve', lambda e: e.scalar_tensor_tensor(h[:, c, :], xt[:, c, :], self.gains_s[:, l, which, c:c + 1], rstd[:, :],
                                                         ALU.mult, ALU.mult),
                 reads=[xt_name, rstd_name, 'gains_s'], writes=[h_name])

    def phase_inproj(self, l, xT, wInB, zF, zQKV):
        nc, S, L = self.nc, self.S, self.L
        FM_COLS = list(range(0, 1536, 128)) + list(range(3840, IN_W, 128))
        NT = 4
        with ExitStack() as st:
            xr = Ring(nc, st, "p1_x", 2, [128, 8, T], F32)
            sq = self.sb(st, "p1_sq", [128, 8, T], BF16)
            rstd = self.sb(st, "p1_rstd", [128, T], F32)
            hr = Ring(nc, st, "p1_h", 5, [128, 8, T], BF16)
            wr = Ring(nc, st, "p1_w", 2, [128, 8, 1024], BF16)
            zo = Ring(nc, st, "p1_zo", 2, [128, 8, T], BF16)
            qo = Ring(nc, st, "p1_qo", 3, [128, 4, 1024], BF16)
            wv = wInB[l].rearrange("(k p) n -> p k n", p=128)
            zFv = zF.rearrange("(c p) l -> p c l", p=128)
            ev = 0
            wq = 0
            for tp in range(L // (NT * T)):
                hs = []
                for ti in range(tp * NT, (tp + 1) * NT):
                    t0 = ti * T
                    xn, xt = xr.next()
                    S.dma('sp', xt[:], xT.rearrange("(c p) l -> p c l", p=128)[:, :, t0:t0 + T], reads=[('xT', ti)], writes=[xn])
                    hn, h = hr.next()
                    self.prenorm(l, 0, xn, xt, hn, h, "p1_sq", sq, "p1_rstd", rstd)
                    hs.append((t0, hn, h))
                for blk in range(7):
                    wn, w = wr.next()
                    wq += 1
                    for j in range(8):
                        c0 = FM_COLS[blk * 8 + j]
                        if j == 0 or FM_COLS[blk * 8 + j - 1] + 128 != c0:
                            j2 = j
                            while j2 + 1 < 8 and FM_COLS[blk * 8 + j2 + 1] == FM_COLS[blk * 8 + j2] + 128:
                                j2 += 1
                            S.dma('sp' if wq % 2 == 0 else 'pool', w[:, :, j * 128:(j2 + 1) * 128], wv[:, :, c0:c0 + (j2 - j + 1) * 128],
                                  reads=[('wInB', l)], writes=[wn])
                    for (t0, hn, h) in hs:
                        zn, z = zo.next()
                        for j in range(8):
                            ch = blk * 8 + j
                            pn, ps = self.psum.next()
                            for k in range(8):
                                S.op('pe', lambda e: e.matmul(ps[:, :], lhsT=w[:, k, j * 128:(j + 1) * 128], rhs=h[:, k, :], start=(k == 0), stop=(k == 7)),
                                     reads=[wn, hn], writes=[pn], inc=(k == 7))
                            if 6 <= ch < 12:
                                S.op('act', lambda e: e.activation(z[:, j, :], ps[:, :], AF.Gelu_apprx_tanh), reads=[pn], writes=[zn])
                            elif ch >= 24:
                                S.op('act', lambda e: e.activation(z[:, j, :], ps[:, :], AF.Sigmoid), reads=[pn], writes=[zn])
                            else:
                                S.op('dve', lambda e: e.tensor_copy(z[:, j, :], ps[:, :]), reads=[pn], writes=[zn])
                        S.dma('sp', zFv[:, blk * 8:(blk + 1) * 8, t0:t0 + T], z[:], reads=[zn], writes=['zF'])
                for blk in range(3):
                    c0 = 1536 + blk * 1024
                    ncol = min(1024, 3840 - c0)
                    wn, w = wr.next()
                    wq += 1
                    S.dma('sp' if wq % 2 == 0 else 'pool', w[:, :, :ncol], wv[:, :, c0:c0 + ncol], reads=[('wInB', l)], writes=[wn])
                    for (t0, hn, h) in hs:
                        qn, q = qo.next()
                        for tb in range(4):
                            for n0 in range(0, ncol, 512):
                                nn = min(512, ncol - n0)
                                pn, ps = self.psum.next()
                                for k in range(8):
                                    S.op('pe', lambda e: e.matmul(ps[:, :nn], lhsT=h[:, k, tb * 128:(tb + 1) * 128], rhs=w[:, k, n0:n0 + nn], start=(k == 0), stop=(k == 7)),
                                         reads=[wn, hn], writes=[pn], inc=(k == 7))
                                dst = q[:, tb, n0:n0 + nn]
                                ev += 1
                                if ev % 2:
                                    S.op('dve', lambda e: e.tensor_copy(dst, ps[:, :nn]), reads=[pn], writes=[qn])
                                else:
                                    S.op('act', lambda e: e.copy(dst, ps[:, :nn]), reads=[pn], writes=[qn])
                        S.dma('sp', zQKV[t0:t0 + T, blk * 1024:blk * 1024 + ncol].rearrange("(tb p) n -> p tb n", p=128), q[:, :, :ncol], reads=[qn], writes=['zQKV'])

    def phase_lru(self, l, zF, yA, lrup, lru_bd):
        nc, S, L = self.nc, self.S, self.L
        with ExitStack() as st:
            lp = self.sb(st, "p2_lp", [128, 8, 6], F32)
            kap = self.sb(st, "p2_kap", [128, 2, 6], F32)
            bdA = self.sb(st, "p2_bdA", [128, 6, 128], BF16)
            bdX = self.sb(st, "p2_bdX", [128, 6, 128], BF16)
            state = self.sb(st, "p2_state", [128, 6], F32)
            xar = Ring(nc, st, "p2_xa", 3, [128, T + 3], BF16)
            ggr = Ring(nc, st, "p2_gg", 3, [128, T], BF16)
            xcr = Ring(nc, st, "p2_xc", 3, [128, T], F32)
            xcbr = Ring(nc, st, "p2_xcb", 3, [128, T], BF16)
            rr = Ring(nc, st, "p2_r", 3, [128, T], F32)
            ir = Ring(nc, st, "p2_i", 3, [128, T], F32)
            ar = Ring(nc, st, "p2_a", 3, [128, T], F32)
            a2r = Ring(nc, st, "p2_a2", 3, [128, T], F32)
            hr = Ring(nc, st, "p2_h", 3, [128, T], F32)
            yr = Ring(nc, st, "p2_y", 3, [128, T], BF16)
            S.dma('sp', lp[:], lrup[:, l, :, :], writes=['p2_lp'])
            S.dma('pool', bdA[:], lru_bd[l, 0], writes=['p2_bdA'])
            S.dma('pool', bdX[:], lru_bd[l, 1], writes=['p2_bdX'])
            S.op('dve', lambda e: e.memset(state[:], 0.0), writes=[('p2_state', c) for c in range(6)])
            S.op('act', lambda e: e.activation(kap[:, 0, :], lp[:, 7, :], AF.Exp, scale=-1.0), reads=['p2_lp'], writes=['p2_kap'])
            S.op('act', lambda e: e.activation(kap[:, 0, :], kap[:, 0, :], AF.Ln, bias=self.one_c[:]), reads=['p2_kap', 'one_c'], writes=['p2_kap'])
            S.op('dve', lambda e: e.tensor_scalar(kap[:, 1, :], kap[:, 0, :], -16.0, None, ALU.mult), reads=['p2_kap'], writes=['p2_kap'])
            S.op('dve', lambda e: e.tensor_scalar(kap[:, 0, :], kap[:, 0, :], -8.0, None, ALU.mult), reads=['p2_kap'], writes=['p2_kap'])
            zFv = zF.rearrange("(c p) l -> p c l", p=128)
            yAv = yA.rearrange("(c p) l -> p c l", p=128)
            for ti in range(L // T):
                t0 = ti * T
                def chunk(c):
                    xn, xa = xar.next()
                    if t0 == 0:
                        S.op('pool', lambda e: e.memset(xa[:, 0:3], 0.0), writes=[xn])
                        S.dma('sp', xa[:, 3:T + 3], zFv[:, c, 0:T], reads=['zF'], writes=[xn])
                    else:
                        S.dma('sp', xa[:, :], zFv[:, c, t0 - 3:t0 + T], reads=['zF'], writes=[xn])
                    gn, gg = ggr.next()
                    S.dma('sp', gg[:, :], zFv[:, 6 + c, t0:t0 + T], reads=['zF'], writes=[gn])
                    xcn, xc = xcr.next()
                    S.op('dve', lambda e: e.tensor_scalar(xc[:, :], xa[:, 0:T], lp[:, 0, c:c + 1], lp[:, 4, c:c + 1], ALU.mult, ALU.add),
                         reads=[xn, 'p2_lp'], writes=[xcn])
                    for j in range(1, 4):
                        S.op('dve', lambda e, j=j: e.scalar_tensor_tensor(xc[:, :], xa[:, j:j + T], lp[:, j, c:c + 1], xc[:, :], ALU.mult, ALU.add),
                             reads=[xn, xcn, 'p2_lp'], writes=[xcn])
                    xbn, xcb = xcbr.next()
                    S.op('dve', lambda e: e.tensor_copy(xcb[:, :], xc[:, :]), reads=[xcn], writes=[xbn])
                    prn, pr = self.psum.next()
                    S.op('pe', lambda e: e.matmul(pr[:, :], lhsT=bdA[:, c, :], rhs=xcb[:, :], start=True, stop=True), reads=['p2_bdA', xbn], writes=[prn])
                    pin, pi = self.psum.next()
                    S.op('pe', lambda e: e.matmul(pi[:, :], lhsT=bdX[:, c, :], rhs=xcb[:, :], start=True, stop=True), reads=['p2_bdX', xbn], writes=[pin])
                    rn, r = rr.next()
                    S.op('act', lambda e: e.activation(r[:, :], pr[:, :], AF.Sigmoid, bias=lp[:, 5, c:c + 1]), reads=[prn, 'p2_lp'], writes=[rn])
                    inn, iv = ir.next()
                    S.op('act', lambda e: e.activation(iv[:, :], pi[:, :], AF.Sigmoid, bias=lp[:, 6, c:c + 1]), reads=[pin, 'p2_lp'], writes=[inn])
                    an, a = ar.next()
                    S.op('act', lambda e: e.activation(a[:, :], r[:, :], AF.Exp, scale=kap[:, 0, c:c + 1]), reads=[rn, 'p2_kap'], writes=[an])
                    a2n, a2 = a2r.next()
                    S.op('act', lambda e: e.activation(a2[:, :], r[:, :], AF.Exp, scale=kap[:, 1, c:c + 1]), reads=[rn, 'p2_kap'], writes=[a2n])
                    S.op('dve', lambda e: e.tensor_scalar(a2[:, :], a2[:, :], -1.0, 1.0, ALU.mult, ALU.add), reads=[a2n], writes=[a2n])
                    S.op('act', lambda e: e.activation(a2[:, :], a2[:, :], AF.Sqrt), reads=[a2n], writes=[a2n])
                    S.op('dve', lambda e: e.tensor_tensor(iv[:, :], iv[:, :], a2[:, :], ALU.mult), reads=[inn, a2n], writes=[inn])
                    S.op('dve', lambda e: e.tensor_tensor(iv[:, :], iv[:, :], xc[:, :], ALU.mult), reads=[inn, xcn], writes=[inn])
                    hn, h = hr.next()
                    S.op('dve', lambda e: e.tensor_tensor_scan(h[:, :], a[:, :], iv[:, :], state[:, c:c + 1], ALU.mult, ALU.add),
                         reads=[an, inn, ('p2_state', c)], writes=[hn])
                    S.op('act', lambda e: e.copy(state[:, c:c + 1], h[:, T - 1:T]), reads=[hn], writes=[('p2_state', c)])
                    yn, y = yr.next()
                    S.op('dve', lambda e: e.tensor_tensor(y[:, :], h[:, :], gg[:, :], ALU.mult), reads=[hn, gn], writes=[yn])
                    S.dma('sp', yAv[:, c, t0:t0 + T], y[:, :], reads=[yn], writes=['yA'])
                for c in range(0, 6, 3):
                    S.interleave([lambda c=c: chunk(c), lambda c=c: chunk(c + 1), lambda c=c: chunk(c + 2)])

    def phase_attn(self, l, zQKV, oG, amask):
        nc, S, L = self.nc, self.S, self.L
        with ExitStack() as st:
            mk = self.sb(st, "p3_mk", [128, 2, 12, 256], F32)
            idb = self.sb(st, "p3_idb", [128, 128], BF16)
            S.dma('sp', mk[:], amask, writes=['p3_mk'])
            S.op('dve', lambda e: e.tensor_copy(idb[:], self.ident_f[:]), reads=['ident_f'], writes=['p3_idb'])
            extra = Ring(nc, st, "p3_psx", 2, [128, 512], F32, psum=True)
            psum = Ring.__new__(Ring)
            psum.bufs = list(self.psum.bufs) + list(extra.bufs)
            psum.i = 0
            qr = Ring(nc, st, "p3_q", 4, [128, 256], BF16)
            kr = Ring(nc, st, "p3_k", 4, [128, 256], BF16)
            vr = Ring(nc, st, "p3_v", 5, [128, 4, 128], BF16)
            qkr = Ring(nc, st, "p3_qk", 4, [128, 8, 128], BF16)
            scr = Ring(nc, st, "p3_sc", 4, [128, 512], F32)
            ptr = Ring(nc, st, "p3_pt", 6, [128, 2, 2, 128], BF16)
            osr = Ring(nc, st, "p3_os", 4, [128, 260], F32)
            for (vn, v) in vr.bufs:
                S.op('pool', lambda e: e.memset(v[:], 1.0), writes=[vn])
            box = {}

            def unit(g, d, r, blk):
                zv = zQKV.rearrange("(n d) c -> d n c", d=d)
                ov = oG[g].rearrange("(n d) c -> d n c", d=d)
                rows = slice(blk * 128, (blk + 1) * 128)
                qn, q = qr.next()
                kn, k_ = kr.next()
                vn, v = vr.next()
                S.dma('sp', q[:, :], zv[r, rows, 256 * g:256 * g + 256], reads=['zQKV'], writes=[qn])
                S.dma('sp', k_[:, :], zv[r, rows, 768 + 256 * g:768 + 256 * g + 256], reads=['zQKV'], writes=[kn])
                S.dma('sp', v[:, :, 0:64], zv[r, rows, 1536 + 256 * g:1536 + 256 * g + 256].rearrange("p (h e) -> p h e", e=64),
                      reads=['zQKV'], writes=[vn])
                ptn, pT = psum.next()
                for j in range(4):
                    S.op('pe', lambda e, j=j: e.matmul(pT[0:64, j * 128:(j + 1) * 128], lhsT=q[:, j * 64:(j + 1) * 64], rhs=idb[:], start=True, stop=True),
                         reads=[qn, 'p3_idb'], writes=[ptn], inc=(j == 3))
                ptn2, pT2 = psum.next()
                for j in range(4):
                    S.op('pe', lambda e, j=j: e.matmul(pT2[0:64, j * 128:(j + 1) * 128], lhsT=k_[:, j * 64:(j + 1) * 64], rhs=idb[:], start=True, stop=True),
                         reads=[kn, 'p3_idb'], writes=[ptn2], inc=(j == 3))
                qkn, qk = qkr.next()
                S.op('dve', lambda e: e.tensor_copy(qk[0:64, 0:4, :], pT[0:64, 0:512].rearrange("p (a b) -> p a b", a=4)), reads=[ptn], writes=[qkn])
                S.op('dve', lambda e: e.tensor_copy(qk[0:64, 4:8, :], pT2[0:64, 0:512].rearrange("p (a b) -> p a b", a=4)), reads=[ptn2], writes=[qkn])
                qTn, qT = qkn, qk[:, 0:4, :]
                kTn, kT = qkn, qk[:, 4:8, :]
                first = blk == 0
                if first:
                    kTp_n, kTp, vp_n, vp = kTn, kT, vn, v
                else:
                    kTp_n, kTp, vp_n, vp = box['prev']
                box['prev'] = (kTn, kT, vn, v)
                pts = []
                for pair in range(2):
                    pn, ps = psum.next()
                    for h2 in range(2):
                        hx = 2 * pair + h2
                        col = h2 * 256
                        if not first:
                            S.op('pe', lambda e, hx=hx, col=col, ps=ps: e.matmul(ps[:, col:col + 128], lhsT=kTp[0:64, hx, :], rhs=qT[0:64, hx, :], start=True, stop=True),
                                 reads=[kTp_n, qTn], writes=[pn], inc=False)
                        S.op('pe', lambda e, hx=hx, col=col, ps=ps: e.matmul(ps[:, col + 128:col + 256], lhsT=kT[0:64, hx, :], rhs=qT[0:64, hx, :], start=True, stop=True),
                             reads=[kTn, qTn], writes=[pn], inc=(h2 == 1))
                    scn, sc = scr.next()
                    hh0 = 4 * g + 2 * pair
                    mview = mk[:, 1 if first else 0, hh0:hh0 + 2, :].rearrange("p a b -> p (a b)")
                    if first:
                        S.op('pool', lambda e, sc=sc: e.memset(sc[:, :], 0.0), writes=[scn])
                        for h2 in range(2):
                            col = h2 * 256
                            S.op('dve', lambda e, col=col, sc=sc, ps=ps, mview=mview: e.scalar_tensor_tensor(sc[:, col + 128:col + 256], ps[:, col + 128:col + 256], 0.125, mview[:, col + 128:col + 256], ALU.mult, ALU.add),
                                 reads=[pn, 'p3_mk'], writes=[scn])
                            S.op('dve', lambda e, col=col, sc=sc, mview=mview: e.tensor_copy(sc[:, col:col + 128], mview[:, col:col + 128]), reads=['p3_mk'], writes=[scn])
                    else:
                        S.op('dve', lambda e, sc=sc, ps=ps, mview=mview: e.scalar_tensor_tensor(sc[:, :], ps[:, :], 0.125, mview, ALU.mult, ALU.add),
                             reads=[pn, 'p3_mk'], writes=[scn])
                    pn2, pt = ptr.next()
                    S.op('act', lambda e, pt=pt, sc=sc: e.activation(pt[:, :, :, :].rearrange("p a b c -> p (a b c)"), sc[:, :], AF.Exp), reads=[scn], writes=[pn2])
                    pts.append((pn2, pt))
                pn, ps = psum.next()
                for hh in range(4):
                    pn2, pt = pts[hh // 2]
                    S.op('pe', lambda e, hh=hh, pt=pt: e.matmul(ps[:, hh * 128:hh * 128 + 65], lhsT=pt[:, hh % 2, 0, :], rhs=vp[:, hh, 0:65], start=True, stop=False),
                         reads=[pn2, vp_n], writes=[pn], inc=False)
                    S.op('pe', lambda e, hh=hh, pt=pt: e.matmul(ps[:, hh * 128:hh * 128 + 65], lhsT=pt[:, hh % 2, 1, :], rhs=v[:, hh, 0:65], start=False, stop=True),
                         reads=[pn2, vn], writes=[pn], inc=(hh == 3))
                on, o = osr.next()
                S.op('act', lambda e: e.copy(o[:, :].rearrange("p (h e) -> p h e", e=65), ps[:, :].rearrange("p (h e) -> p h e", e=128)[:, :, 0:65]), reads=[pn], writes=[on])
                S.dma('sp', ov[r, rows, :], o[:, :], reads=[on], writes=['oG'])

            for g, d in enumerate(DILS):
                nb = L // (128 * d)
                for r in range(d):
                    if nb % 2 == 0:
                        for blk in range(0, nb, 2):
                            S.interleave([lambda blk=blk: unit(g, d, r, blk), lambda blk=blk: unit(g, d, r, blk + 1)])
                    else:
                        for blk in range(nb):
                            unit(g, d, r, blk)

    def sincos(self, st, th_name, th, n, out_s, out_c, key):
        S = self.S
        TWO_PI = 2.0 * math.pi
        ti = self.sb(st, "sc_i", [128, n], mybir.dt.int32)
        tf = self.sb(st, "sc_f", [128, n], F32)
        ph = self.sb(st, "sc_p", [128, n], F32)
        mm = self.sb(st, "sc_m", [128, n], F32)
        for (shift, outt) in ((0.0, out_s), (math.pi / 2, out_c)):
            S.op('dve', lambda e: e.tensor_scalar(ph[:, :], th, 1.0 / TWO_PI, shift / TWO_PI, ALU.mult, ALU.add), reads=[th_name], writes=[key + 'ph'])
            S.op('dve', lambda e: e.tensor_copy(ti[:, :], ph[:, :]), reads=[key + 'ph'], writes=[key + 'ti'])
            S.op('dve', lambda e: e.tensor_copy(tf[:, :], ti[:, :]), reads=[key + 'ti'], writes=[key + 'tf'])
            S.op('dve', lambda e: e.tensor_tensor(ph[:, :], ph[:, :], tf[:, :], ALU.subtract), reads=[key + 'ph', key + 'tf'], writes=[key + 'ph'])
            S.op('dve', lambda e: e.tensor_scalar(mm[:, :], ph[:, :], 0.5, None, ALU.is_gt), reads=[key + 'ph'], writes=[key + 'mm'])
            S.op('dve', lambda e: e.tensor_tensor(ph[:, :], ph[:, :], mm[:, :], ALU.subtract), reads=[key + 'ph', key + 'mm'], writes=[key + 'ph'])
            S.op('dve', lambda e: e.tensor_scalar(mm[:, :], ph[:, :], -0.5, None, ALU.is_lt), reads=[key + 'ph'], writes=[key + 'mm'])
            S.op('dve', lambda e: e.tensor_tensor(ph[:, :], ph[:, :], mm[:, :], ALU.add), reads=[key + 'ph', key + 'mm'], writes=[key + 'ph'])
            S.op('dve', lambda e: e.tensor_scalar(ph[:, :], ph[:, :], -0.4999, 0.4999, ALU.max, ALU.min), reads=[key + 'ph'], writes=[key + 'ph'])
            S.op('act', lambda e: e.activation(outt, ph[:, :], AF.Sin, scale=TWO_PI), reads=[key + 'ph'], writes=[key + 'out'])

    def phase_s5(self, l, zF, yC, s5A, s5R, s5B, s5C, ssmd):
        nc, S, L = self.nc, self.S, self.L
        with ExitStack() as st:
            rho = self.sb(st, "p4_rho", [128, 24], F32)
            BbR = self.sb(st, "p4_BbR", [128, 24, 128], BF16)
            BbI = self.sb(st, "p4_BbI", [128, 24, 128], BF16)
            CtR = self.sb(st, "p4_CtR", [128, 24, 128], BF16)
            CtI = self.sb(st, "p4_CtI", [128, 24, 128], BF16)
            CtRn = self.sb(st, "p4_CtRn", [128, 24, 128], BF16)
            dsk = self.sb(st, "p4_d", [128, 6], F32)
            xst = self.sb(st, "p4_xst", [128, 2, 24], F32)
            S.dma('sp', dsk[:], ssmd[:, l, :], writes=['p4_d'])
            S.op('dve', lambda e: e.memset(xst[:], 0.0), writes=[('p4_xst', p) for p in range(24)])
            S.dma('pool', CtR[:], s5C[l, 0], writes=['p4_CtR'])
            with ExitStack() as s2:
                N = 3072
                par = self.sb(s2, "p4s_par", [128, 3, N], F32)
                S.dma('sp', par[:], s5R[:, l, :, :], writes=['p4s_par'])
                ar, ai, ld = par[:, 0, :], par[:, 1, :], par[:, 2, :]
                dt = self.sb(s2, "p4s_dt", [128, N], F32)
                mag = self.sb(st if False else s2, "p4s_mag", [128, N], F32)
                th = self.sb(s2, "p4s_th", [128, N], F32)
                sn = self.sb(s2, "p4s_sn", [128, N], F32)
                cs = self.sb(s2, "p4s_cs", [128, N], F32)
                S.op('act', lambda e: e.activation(dt[:, :], ld, AF.Exp), reads=['p4s_par'], writes=['p4s_dt'])
                S.op('dve', lambda e: e.tensor_tensor(mag[:, :], ar, dt[:, :], ALU.mult), reads=['p4s_par', 'p4s_dt'], writes=['p4s_mag'])
                S.op('act', lambda e: e.activation(mag[:, :], mag[:, :], AF.Exp), reads=['p4s_mag'], writes=['p4s_mag'])
                S.op('dve', lambda e: e.tensor_tensor(th[:, :], ai, dt[:, :], ALU.mult), reads=['p4s_par', 'p4s_dt'], writes=['p4s_th'])
                for hf in range(2):
                    with ExitStack() as s3:
                        sl = slice(hf * 1536, (hf + 1) * 1536)
                        self.sincos(s3, 'p4s_th', th[:, sl], 1536, sn[:, sl], cs[:, sl], 'scB')
                        S.barrier()
                S.op('dve', lambda e: e.tensor_tensor(cs[:, :], cs[:, :], mag[:, :], ALU.mult), reads=['scBout', 'p4s_mag'], writes=['p4s_cs'])
                S.op('dve', lambda e: e.tensor_scalar(cs[:, :], cs[:, :], -1.0, None, ALU.add), reads=['p4s_cs'], writes=['p4s_cs'])
                S.op('dve', lambda e: e.tensor_tensor(sn[:, :], sn[:, :], mag[:, :], ALU.mult), reads=['scBout', 'p4s_mag'], writes=['p4s_sn'])
                S.op('dve', lambda e: e.tensor_tensor(dt[:, :], ar, ar, ALU.mult), reads=['p4s_par'], writes=['p4s_dt'])
                S.op('dve', lambda e: e.tensor_tensor(th[:, :], ai, ai, ALU.mult), reads=['p4s_par'], writes=['p4s_th'])
                S.op('dve', lambda e: e.tensor_tensor(dt[:, :], dt[:, :], th[:, :], ALU.add), reads=['p4s_dt', 'p4s_th'], writes=['p4s_dt'])
                S.op('dve', lambda e: e.reciprocal(dt[:, :], dt[:, :]), reads=['p4s_dt'], writes=['p4s_dt'])
                t1 = self.sb(s2, "p4s_t1", [128, N], F32)
                S.op('dve', lambda e: e.tensor_tensor(mag[:, :], cs[:, :], ar, ALU.mult), reads=['p4s_cs', 'p4s_par'], writes=['p4s_mag'])
                S.op('dve', lambda e: e.tensor_tensor(t1[:, :], sn[:, :], ai, ALU.mult), reads=['p4s_sn', 'p4s_par'], writes=['p4s_t1'])
                S.op('dve', lambda e: e.tensor_tensor(mag[:, :], mag[:, :], t1[:, :], ALU.add), reads=['p4s_mag', 'p4s_t1'], writes=['p4s_mag'])
                S.op('dve', lambda e: e.tensor_tensor(mag[:, :], mag[:, :], dt[:, :], ALU.mult), reads=['p4s_mag', 'p4s_dt'], writes=['p4s_mag'])
                S.op('dve', lambda e: e.tensor_tensor(th[:, :], sn[:, :], ar, ALU.mult), reads=['p4s_sn', 'p4s_par'], writes=['p4s_th'])
                S.op('dve', lambda e: e.tensor_tensor(t1[:, :], cs[:, :], ai, ALU.mult), reads=['p4s_cs', 'p4s_par'], writes=['p4s_t1'])
                S.op('dve', lambda e: e.tensor_tensor(th[:, :], th[:, :], t1[:, :], ALU.subtract), reads=['p4s_th', 'p4s_t1'], writes=['p4s_th'])
                S.op('dve', lambda e: e.tensor_tensor(th[:, :], th[:, :], dt[:, :], ALU.mult), reads=['p4s_th', 'p4s_dt'], writes=['p4s_th'])
                zr, zi = mag, th
                bre = self.sb(s2, "p4s_bre", [128, N], F32)
                bim = self.sb(s2, "p4s_bim", [128, N], F32)
                S.dma('sp', bre[:, :], s5B[l, 0].rearrange("p a b -> p (a b)"), writes=['p4s_bre'])
                S.dma('sp', bim[:, :], s5B[l, 1].rearrange("p a b -> p (a b)"), writes=['p4s_bim'])
                S.op('dve', lambda e: e.tensor_tensor(t1[:, :], zr[:, :], bre[:, :], ALU.mult), reads=['p4s_mag', 'p4s_bre'], writes=['p4s_t1'])
                S.op('dve', lambda e: e.tensor_tensor(cs[:, :], zi[:, :], bim[:, :], ALU.mult), reads=['p4s_th', 'p4s_bim'], writes=['p4s_cs'])
                S.op('dve', lambda e: e.tensor_tensor(BbR[:, :, :].rearrange("p a b -> p (a b)"), t1[:, :], cs[:, :], ALU.subtract), reads=['p4s_t1', 'p4s_cs'], writes=['p4_BbR'])
                S.op('dve', lambda e: e.tensor_tensor(t1[:, :], zr[:, :], bim[:, :], ALU.mult), reads=['p4s_mag', 'p4s_bim'], writes=['p4s_t1'])
                S.op('dve', lambda e: e.tensor_tensor(cs[:, :], zi[:, :], bre[:, :], ALU.mult), reads=['p4s_th', 'p4s_bre'], writes=['p4s_cs'])
                S.op('dve', lambda e: e.tensor_tensor(BbI[:, :, :].rearrange("p a b -> p (a b)"), t1[:, :], cs[:, :], ALU.add), reads=['p4s_t1', 'p4s_cs'], writes=['p4_BbI'])
                S.dma('sp', bre[:, :], s5C[l, 1].rearrange("p a b -> p (a b)"), reads=['p4s_bre'], writes=['p4s_bre'])
                S.op('act', lambda e: e.mul(CtI[:, :, :].rearrange("p a b -> p (a b)"), bre[:, :], -1.0), reads=['p4s_bre'], writes=['p4_CtI'])
                S.dma('sp', bim[:, :], s5C[l, 0].rearrange("p a b -> p (a b)"), reads=['p4s_bim'], writes=['p4s_bim'])
                S.op('act', lambda e: e.mul(CtRn[:, :, :].rearrange("p a b -> p (a b)"), bim[:, :], -1.0), reads=['p4s_bim'], writes=['p4_CtRn'])
                S.barrier()
            cosT = self.sb(st, "p4_cos", [128, 24, T], F32)
            sinT = self.sb(st, "p4_sin", [128, 24, T], F32)
            with ExitStack() as s2:
                pa = self.sb(s2, "p4a_par", [128, 3, 24], F32)
                S.dma('sp', pa[:], s5A[:, l, :, :], writes=['p4a_par'])
                dt = self.sb(s2, "p4a_dt", [128, 24], F32)
                th = self.sb(s2, "p4a_th", [128, 24], F32)
                sn = self.sb(s2, "p4a_sn", [128, 24], F32)
                cs = self.sb(s2, "p4a_cs", [128, 24], F32)
                S.op('act', lambda e: e.activation(dt[:, :], pa[:, 2, :], AF.Exp), reads=['p4a_par'], writes=['p4a_dt'])
                S.op('dve', lambda e: e.tensor_tensor(rho[:, :], pa[:, 0, :], dt[:, :], ALU.mult), reads=['p4a_par', 'p4a_dt'], writes=['p4_rho'])
                S.op('act', lambda e: e.activation(rho[:, :], rho[:, :], AF.Exp), reads=['p4_rho'], writes=['p4_rho'])
                S.op('dve', lambda e: e.tensor_tensor(th[:, :], pa[:, 1, :], dt[:, :], ALU.mult), reads=['p4a_par', 'p4a_dt'], writes=['p4a_th'])
                self.sincos(s2, 'p4a_th', th[:, :], 24, sn[:, :], cs[:, :], 'scA')
                tmp = self.sb(s2, "p4a_tmp", [128, T // 2], F32)
                for p in range(24):
                    S.op('act', lambda e: e.copy(cosT[:, p, 0:1], cs[:, p:p + 1]), reads=['scAout'], writes=[('p4_cos', p)])
                    S.op('act', lambda e: e.copy(sinT[:, p, 0:1], sn[:, p:p + 1]), reads=['scAout'], writes=[('p4_sin', p)])
                    w = 1
                    while w < T:
                        cw, sw = cosT[:, p, w - 1:w], sinT[:, p, w - 1:w]
                        S.op('dve', lambda e: e.tensor_scalar(tmp[:, 0:w], sinT[:, p, 0:w], sw, None, ALU.mult), reads=[('p4_sin', p)], writes=['p4a_tmp'])
                        S.op('dve', lambda e: e.scalar_tensor_tensor(cosT[:, p, w:2 * w], cosT[:, p, 0:w], cw, tmp[:, 0:w], ALU.mult, ALU.subtract),
                             reads=[('p4_cos', p), 'p4a_tmp'], writes=[('p4_cos', p)])
                        S.op('dve', lambda e: e.tensor_scalar(tmp[:, 0:w], cosT[:, p, 0:w], sw, None, ALU.mult), reads=[('p4_cos', p)], writes=['p4a_tmp'])
                        S.op('dve', lambda e: e.scalar_tensor_tensor(sinT[:, p, w:2 * w], sinT[:, p, 0:w], cw, tmp[:, 0:w], ALU.mult, ALU.add),
                             reads=[('p4_sin', p), ('p4_cos', p), 'p4a_tmp'], writes=[('p4_sin', p)])
                        w *= 2
                S.barrier()
            wg = self.sb(st, "p4_wg", [128, 6, 1536], BF16)
            S.dma('sp', wg[:], self.wB["ssm_glu"][l].rearrange("(k p) n -> p k n", p=128), reads=[('ssm_gluB', l)], writes=['p4_wg'])
            psy = Ring(nc, st, "p4_psy", 2, [128, 512], F32, psum=True)
            usr = Ring(nc, st, "p4_us", 1, [128, 6, T], BF16)
            tr = Ring(nc, st, "p4_t", 10, [128, T], F32)
            stmp = self.sb(st, "p4_stmp", [128, 24, 2], F32)
            xr = Ring(nc, st, "p4_x", 4, [128, T], F32)
            ubr = Ring(nc, st, "p4_ub", 12, [128, T], BF16)
            yfr = Ring(nc, st, "p4_yf", 2, [128, T], F32)
            gy = self.sb(st, "p4_gy", [128, 6, T], BF16)
            sgr = Ring(nc, st, "p4_sg", 2, [128, T], F32)
            ycr = Ring(nc, st, "p4_yc", 2, [128, T], BF16)
            zFv = zF.rearrange("(c p) l -> p c l", p=128)
            yCv = yC.rearrange("(c p) l -> p c l", p=128)
            for ti in range(L // T):
                t0 = ti * T
                un, us = usr.next()
                S.dma('sp', us[:], zFv[:, 12:18, t0:t0 + T], reads=['zF'], writes=[un])
                PP = {}

                def stageA0(p):
                    ch = p // 4
                    prn, pr = self.psum.next()
                    S.op('pe', lambda e: e.matmul(pr[:, :], lhsT=BbR[:, p, :], rhs=us[:, ch, :], start=True, stop=True), reads=['p4_BbR', un], writes=[prn])
                    pin, pi = self.psum.next()
                    S.op('pe', lambda e: e.matmul(pi[:, :], lhsT=BbI[:, p, :], rhs=us[:, ch, :], start=True, stop=True), reads=['p4_BbI', un], writes=[pin])
                    PP[p] = dict(pr=(prn, pr), pi=(pin, pi))

                def stageA(p):
                    c_, s_ = cosT[:, p, :], sinT[:, p, :]
                    prn, pr = PP[p]['pr']; pin, pi = PP[p]['pi']
                    t1n, t1 = tr.next(); t2n, t2 = tr.next(); t3n, t3 = tr.next(); t4n, t4 = tr.next()
                    S.op('dve', lambda e: e.tensor_tensor(t1[:, :], pr[:, :], c_, ALU.mult), reads=[prn, ('p4_cos', p)], writes=[t1n])
                    S.op('dve', lambda e: e.tensor_tensor(t2[:, :], pi[:, :], s_, ALU.mult), reads=[pin, ('p4_sin', p)], writes=[t2n])
                    S.op('dve', lambda e: e.tensor_tensor(t3[:, :], pi[:, :], c_, ALU.mult), reads=[pin, ('p4_cos', p)], writes=[t3n])
                    S.op('dve', lambda e: e.tensor_tensor(t4[:, :], pr[:, :], s_, ALU.mult), reads=[prn, ('p4_sin', p)], writes=[t4n])
                    S.op('pool', lambda e: e.tensor_tensor(t1[:, :], t1[:, :], t2[:, :], ALU.add), reads=[t1n, t2n], writes=[t1n])
                    S.op('pool', lambda e: e.tensor_tensor(t3[:, :], t3[:, :], t4[:, :], ALU.subtract), reads=[t3n, t4n], writes=[t3n])
                    PP[p].update(t1=(t1n, t1), t3=(t3n, t3))

                def stageC(p):
                    c_, s_ = cosT[:, p, :], sinT[:, p, :]
                    t1n, t1 = PP[p]['t1']; t3n, t3 = PP[p]['t3']
                    rb = rho[:, p:p + 1].to_broadcast([128, T])
                    xrn, xre = xr.next(); xin, xim = xr.next()
                    S.op('dve', lambda e: e.tensor_tensor_scan(xre[:, :], rb, t1[:, :], xst[:, 0, p:p + 1], ALU.mult, ALU.add),
                         reads=['p4_rho', t1n, ('p4_xst', p)], writes=[xrn])
                    S.op('dve', lambda e: e.tensor_tensor_scan(xim[:, :], rb, t3[:, :], xst[:, 1, p:p + 1], ALU.mult, ALU.add),
                         reads=['p4_rho', t3n, ('p4_xst', p)], writes=[xin])
                    cl, sl = cosT[:, p, T - 1:T], sinT[:, p, T - 1:T]
                    S.op('act', lambda e: e.activation(stmp[:, p, 0:1], xim[:, T - 1:T], AF.Copy, scale=sl), reads=[xin, ('p4_sin', p)], writes=[('p4_stmp', p)])
                    S.op('act', lambda e: e.activation(stmp[:, p, 1:2], xim[:, T - 1:T], AF.Copy, scale=cl), reads=[xin, ('p4_cos', p)], writes=[('p4_stmp', p)])
                    u1n, u1 = ubr.next(); u2n, u2 = ubr.next(); u3n, u3 = ubr.next(); u4n, u4 = ubr.next()
                    S.op('dve', lambda e: e.tensor_tensor(u1[:, :], xre[:, :], c_, ALU.mult), reads=[xrn, ('p4_cos', p)], writes=[u1n])
                    S.op('dve', lambda e: e.tensor_tensor(u2[:, :], xim[:, :], s_, ALU.mult), reads=[xin, ('p4_sin', p)], writes=[u2n])
                    S.op('dve', lambda e: e.tensor_tensor(u3[:, :], xre[:, :], s_, ALU.mult), reads=[xrn, ('p4_sin', p)], writes=[u3n])
                    S.op('dve', lambda e: e.tensor_tensor(u4[:, :], xim[:, :], c_, ALU.mult), reads=[xin, ('p4_cos', p)], writes=[u4n])
                    S.op('dve', lambda e: e.scalar_tensor_tensor(xst[:, 0, p:p + 1], xre[:, T - 1:T], cl, stmp[:, p, 0:1], ALU.mult, ALU.subtract),
                         reads=[xrn, ('p4_stmp', p), ('p4_cos', p)], writes=[('p4_xst', p)])
                    S.op('dve', lambda e: e.scalar_tensor_tensor(xst[:, 1, p:p + 1], xre[:, T - 1:T], sl, stmp[:, p, 1:2], ALU.mult, ALU.add),
                         reads=[xrn, ('p4_stmp', p), ('p4_sin', p)], writes=[('p4_xst', p)])
                    PP[p].update(u1=(u1n, u1), u2=(u2n, u2), u3=(u3n, u3), u4=(u4n, u4))

                def stageE(p):
                    u1n, u1 = PP[p]['u1']; u2n, u2 = PP[p]['u2']; u3n, u3 = PP[p]['u3']; u4n, u4 = PP[p]['u4']
                    if p % 4 == 0:
                        PP['py'] = psy.next()
                    pyn, py = PP['py']
                    S.op('pe', lambda e: e.matmul(py[:, :], lhsT=CtR[:, p, :], rhs=u1[:, :], start=(p % 4 == 0), stop=False), reads=['p4_CtR', u1n], writes=[pyn], inc=False)
                    S.op('pe', lambda e: e.matmul(py[:, :], lhsT=CtRn[:, p, :], rhs=u2[:, :], start=False, stop=False), reads=['p4_CtRn', u2n], writes=[pyn], inc=False)
                    S.op('pe', lambda e: e.matmul(py[:, :], lhsT=CtI[:, p, :], rhs=u3[:, :], start=False, stop=False), reads=['p4_CtI', u3n], writes=[pyn], inc=False)
                    S.op('pe', lambda e: e.matmul(py[:, :], lhsT=CtI[:, p, :], rhs=u4[:, :], start=False, stop=(p % 4 == 3)), reads=['p4_CtI', u4n], writes=[pyn])
                    if p % 4 == 3:
                        oc = p // 4
                        yfn, yf = yfr.next()
                        S.op('dve', lambda e: e.scalar_tensor_tensor(yf[:, :], us[:, oc, :], dsk[:, oc:oc + 1], py[:, :], ALU.mult, ALU.add),
                             reads=[un, 'p4_d', pyn], writes=[yfn])
                        S.op('act', lambda e: e.activation(gy[:, oc, :], yf[:, :], AF.Gelu_apprx_tanh), reads=[yfn], writes=[('p4_gy', oc)])
                    del PP[p]

                stageA0(0)
                for it in range(24 + 2):
                    if it + 1 < 24:
                        stageA0(it + 1)
                    lists = []
                    for (fn_, arg, ok) in ((stageA, it, it < 24), (stageC, it - 1, 1 <= it <= 24), (stageE, it - 2, it >= 2)):
                        if ok:
                            S.defer = []
                            fn_(arg)
                            lists.append(S.defer)
                            S.defer = None
                    while any(lists):
                        for lst in lists:
                            if lst:
                                S.run_deferred(lst.pop(0))
                for j in range(6):
                    pan, pa_ = self.psum.next()
                    for k in range(6):
                        S.op('pe', lambda e: e.matmul(pa_[:, :], lhsT=wg[:, k, j * 128:(j + 1) * 128], rhs=gy[:, k, :], start=(k == 0), stop=(k == 5)),
                             reads=['p4_wg', ('p4_gy', k)], writes=[pan], inc=(k == 5))
                    pbn, pb_ = self.psum.next()
                    for k in range(6):
                        S.op('pe', lambda e: e.matmul(pb_[:, :], lhsT=wg[:, k, 768 + j * 128:768 + (j + 1) * 128], rhs=gy[:, k, :], start=(k == 0), stop=(k == 5)),
                             reads=['p4_wg', ('p4_gy', k)], writes=[pbn], inc=(k == 5))
                    sgn, sg = sgr.next()
                    S.op('act', lambda e: e.activation(sg[:, :], pb_[:, :], AF.Sigmoid), reads=[pbn], writes=[sgn])
                    ycn, yc = ycr.next()
                    S.op('dve', lambda e: e.tensor_tensor(yc[:, :], pa_[:, :], sg[:, :], ALU.mult), reads=[pan, sgn], writes=[ycn])
                    S.dma('sp', yCv[:, j, t0:t0 + T], yc[:, :], reads=[ycn], writes=['yC'])

    def phase_xattn(self, l, mem, zF, yX):
        nc, S, L = self.nc, self.S, self.L
        with ExitStack() as st:
            kT = self.sb(st, "p5_kT", [128, 4, 2, 256], BF16)
            vm = self.sb(st, "p5_vm", [128, 2, 768], BF16)
            with ExitStack() as st2:
                memt = self.sb(st2, "p5_mem", [128, 2, D], F32)
                memT = self.sb(st2, "p5_memT", [128, 8, 256], F32)
                sq = self.sb(st2, "p5_sq", [128, 8, 256], BF16)
                rstd = self.sb(st2, "p5_rstd", [128, 256], F32)
                mn = self.sb(st2, "p5_mn", [128, 8, 256], BF16)
                wkv = self.sb(st2, "p5_wkv", [128, 8, 1536], BF16)
                S.dma('sp', memt[:], mem.rearrange("(b p) d -> p b d", p=128), writes=['p5_mem'])
                S.dma('sp', wkv[:], self.wB["mem_wkv"][l].rearrange("(k p) n -> p k n", p=128), reads=[('mem_wkvB', l)], writes=['p5_wkv'])
                for b in range(2):
                    for half in range(2):
                        pn, ps = self.psum.next()
                        for j in range(4):
                            c = half * 4 + j
                            S.op('pe', lambda e: e.transpose(ps[:, j * 128:(j + 1) * 128], memt[:, b, c * 128:(c + 1) * 128], self.ident_f[:]),
                                 reads=['p5_mem', 'ident_f'], writes=[pn], inc=(j == 3))
                        S.op('dve', lambda e: e.tensor_copy(memT[:, half * 4:(half + 1) * 4, b * 128:(b + 1) * 128], ps[:, :].rearrange("p (a b) -> p a b", a=4)),
                             reads=[pn], writes=['p5_memT'])
                for c in range(8):
                    S.op('act', lambda e: e.activation(sq[:, c, :], memT[:, c, :], AF.Square), reads=['p5_memT'], writes=['p5_sq'])
                self.rms_stats('p5_sq', sq, 'p5_rstd', rstd, 256)
                for c in range(8):
                    S.op('dve', lambda e: e.scalar_tensor_tensor(mn[:, c, :], memT[:, c, :], self.gains_s[:, l, 2, c:c + 1], rstd[:, :], ALU.mult, ALU.mult),
                         reads=['p5_memT', 'p5_rstd', 'gains_s'], writes=['p5_mn'])
                for h in range(4):
                    for part, (off, M) in enumerate(((0, 128), (128, 64))):
                        col = 192 * h + off
                        pn, ps = self.psum.next()
                        for k in range(8):
                            S.op('pe', lambda e: e.matmul(ps[:M, :256], lhsT=wkv[:, k, col:col + M], rhs=mn[:, k, :], start=(k == 0), stop=(k == 7)),
                                 reads=['p5_wkv', 'p5_mn'], writes=[pn], inc=(k == 7))
                        S.op('dve', lambda e: e.tensor_copy(kT[:M, h, part, :], ps[:M, :256]), reads=[pn], writes=['p5_kT'])
                for mc in range(2):
                    for (n0, nn) in ((0, 512), (512, 256)):
                        pn, ps = self.psum.next()
                        for k in range(8):
                            S.op('pe', lambda e: e.matmul(ps[:, :nn], lhsT=mn[:, k, mc * 128:(mc + 1) * 128], rhs=wkv[:, k, 768 + n0:768 + n0 + nn], start=(k == 0), stop=(k == 7)),
                                 reads=['p5_wkv', 'p5_mn'], writes=[pn], inc=(k == 7))
                        S.op('dve', lambda e: e.tensor_copy(vm[:, mc, n0:n0 + nn], ps[:, :nn]), reads=[pn], writes=['p5_vm'])
                S.barrier()
            xq0r = Ring(nc, st, "p5_xq0", 2, [128, T], BF16)
            xq1r = Ring(nc, st, "p5_xq1", 2, [128, T], BF16)
            ptr = Ring(nc, st, "p5_pt", 4, [128, T], BF16)
            rcr = Ring(nc, st, "p5_rc", 2, [128, T], F32)
            y0r = Ring(nc, st, "p5_y0", 2, [128, T], BF16)
            y1r = Ring(nc, st, "p5_y1", 2, [128, T], BF16)
            XQ0 = 18 * 128
            sc = 192.0 ** -0.5
            for ti in range(L // T):
                t0 = ti * T
                for h in range(4):
                    q0n, q0 = xq0r.next()
                    q1n, q1 = xq1r.next()
                    r0 = XQ0 + 192 * h
                    S.dma('sp', q0[:, :], zF[r0:r0 + 128, t0:t0 + T], reads=['zF'], writes=[q0n])
                    S.dma('sp', q1[0:64, :], zF[r0 + 128:r0 + 192, t0:t0 + T], reads=['zF'], writes=[q1n])
                    pts = []
                    for mc in range(2):
                        pn, ps = self.psum.next()
                        S.op('pe', lambda e: e.matmul(ps[:, :], lhsT=kT[:, h, 0, mc * 128:(mc + 1) * 128], rhs=q0[:, :], start=True, stop=False),
                             reads=['p5_kT', q0n], writes=[pn], inc=False)
                        S.op('pe', lambda e: e.matmul(ps[:, :], lhsT=kT[0:64, h, 1, mc * 128:(mc + 1) * 128], rhs=q1[0:64, :], start=False, stop=True),
                             reads=['p5_kT', q1n], writes=[pn])
                        ptn, pt = ptr.next()
                        S.op('act', lambda e: e.activation(pt[:, :], ps[:, :], AF.Exp, scale=sc), reads=[pn], writes=[ptn])
                        pts.append((ptn, pt))
                    pn, ps = self.psum.next()
                    for mc in range(2):
                        S.op('pe', lambda e: e.matmul(ps[:, :], lhsT=self.ones_b[:], rhs=pts[mc][1][:, :], start=(mc == 0), stop=(mc == 1)),
                             reads=['ones_b', pts[mc][0]], writes=[pn], inc=(mc == 1))
                    rcn, rc = rcr.next()
                    S.op('dve', lambda e: e.reciprocal(rc[:, :], ps[:, :]), reads=[pn], writes=[rcn])
                    for part, (off, M, yr_) in enumerate(((0, 128, y0r), (128, 64, y1r))):
                        pn, ps = self.psum.next()
                        for mc in range(2):
                            S.op('pe', lambda e: e.matmul(ps[:M, :], lhsT=vm[:, mc, 192 * h + off:192 * h + off + M], rhs=pts[mc][1][:, :], start=(mc == 0), stop=(mc == 1)),
                                 reads=['p5_vm', pts[mc][0]], writes=[pn], inc=(mc == 1))
                        yn, y = yr_.next()
                        S.op('dve', lambda e: e.tensor_tensor(y[:M, :], ps[:M, :], rc[:M, :], ALU.mult), reads=[pn, rcn], writes=[yn])
                        S.dma('sp', yX[192 * h + off:192 * h + off + M, t0:t0 + T], y[:M, :], reads=[yn], writes=['yX'])

    def phase_merge(self, l, xT, xmT, zF, yA, yC, yX, oG):
        nc, S, L = self.nc, self.S, self.L
        ph = self.phases
        branches = []
        if 'p2' in ph:
            branches.append((0, 'proj_a', 6))
        if 'p3' in ph:
            branches.append((1, 'proj_b', 2))
        if 'p4' in ph:
            branches.append((2, 'proj_c', 6))
        if 'p5' in ph:
            branches.append((3, 'proj_x', 6))
        with ExitStack() as st:
            wp = {}
            for (b, nm, nk) in branches:
                wp[b] = self.sb(st, "p6_" + nm, [128, nk, D], BF16)
                S.dma('sp', wp[b][:], self.wB[nm][l].rearrange("(k p) n -> p k n", p=128), reads=[(nm + 'B', l)], writes=['p6_w%d' % b])
            wo = self.sb(st, "p6_wo", [128, 8, D], BF16)
            S.dma('sp', wo[:], self.wB["w_out"][l].rearrange("(k p) n -> p k n", p=128), reads=[('w_outB', l)], writes=['p6_wo'])
            yr = {b: Ring(nc, st, "p6_y%d" % b, 1, [128, nk, T], BF16) for (b, nm, nk) in branches}
            sgr = Ring(nc, st, "p6_sg", 3, [128, 4, T], BF16)
            xr = Ring(nc, st, "p6_x", 1, [128, 8, T], F32)
            xor_ = Ring(nc, st, "p6_xo", 1, [128, 8, T], F32)
            macc = Ring(nc, st, "p6_macc", 2, [128, T], F32)
            mtmp = Ring(nc, st, "p6_mtmp", 4, [128, T], F32)
            mb = self.sb(st, "p6_mb", [128, 8, T], BF16)
            sq = self.sb(st, "p6_sq", [128, 8, T], BF16)
            rstd = self.sb(st, "p6_rstd", [128, T], F32)
            y = self.sb(st, "p6_yy", [128, 8, T], F32)
            og = Ring(nc, st, "p6_og", 2, [128, 3, 260], F32)
            ybt = Ring(nc, st, "p6_ybt", 2, [128, 256], F32)
            rl = Ring(nc, st, "p6_rl", 2, [128, 4], F32)
            srcs = {0: yA, 2: yC, 3: yX}
            for ti in range(L // T):
                t0 = ti * T
                ys = {}
                for (b, nm, nk) in branches:
                    yn, yt = yr[b].next()
                    ys[b] = (yn, yt)
                    if b != 1:
                        S.dma('sp', yt[:], srcs[b].rearrange("(c p) l -> p c l", p=128)[:, :, t0:t0 + T], reads=[srcs[b].tensor.name], writes=[yn])
                    else:
                        for tb in range(4):
                            on, o = og.next()
                            S.dma('sp', o[:], oG[:, t0 + tb * 128:t0 + (tb + 1) * 128, :].rearrange("g p n -> p g n"), reads=['oG'], writes=[on])
                            S.op('dve', lambda e: e.tensor_tensor(o[:, 0, :], o[:, 0, :], o[:, 1, :], ALU.add), reads=[on], writes=[on])
                            S.op('dve', lambda e: e.tensor_tensor(o[:, 0, :], o[:, 0, :], o[:, 2, :], ALU.add), reads=[on], writes=[on])
                            rn, r = rl.next()
                            ov = o[:, 0, :].rearrange("p (h e) -> p h e", e=65)
                            S.op('dve', lambda e: e.reciprocal(r[:, :], ov[:, :, 64]), reads=[on], writes=[rn])
                            bn, bt = ybt.next()
                            for hh in range(4):
                                S.op('dve', lambda e: e.tensor_scalar(bt[:, hh * 64:(hh + 1) * 64], ov[:, hh, 0:64], r[:, hh:hh + 1], None, ALU.mult),
                                     reads=[on, rn], writes=[bn])
                            pn, ps = self.psum.next()
                            for half in range(2):
                                S.op('pe', lambda e: e.transpose(ps[:, half * 128:(half + 1) * 128], bt[:, half * 128:(half + 1) * 128], self.ident_f[:]),
                                     reads=[bn, 'ident_f'], writes=[pn], inc=(half == 1))
                            S.op('act', lambda e: e.copy(yt[:, :, tb * 128:(tb + 1) * 128], ps[:, 0:256].rearrange("p (a b) -> p a b", a=2)),
                                 reads=[pn], writes=[yn])
                xn, xt = xr.next()
                S.dma('sp', xt[:], xT.rearrange("(c p) l -> p c l", p=128)[:, :, t0:t0 + T], reads=[('xT', ti)], writes=[xn])
                def mchunk(c):
                    mn_, m = macc.next()
                    sn, sg = sgr.next()
                    S.dma('sp', sg[:], zF.rearrange("(b c p) l -> p b c l", p=128, c=8)[:, 3:7, c, t0:t0 + T], reads=['zF'], writes=[sn])
                    for bi, (b, nm, nk) in enumerate(branches):
                        yn, yt = ys[b]
                        pn, ps = self.psum.next()
                        for k in range(nk):
                            S.op('pe', lambda e, b=b, k=k, ps=ps, yt=yt, nk=nk: e.matmul(ps[:, :], lhsT=wp[b][:, k, c * 128:(c + 1) * 128], rhs=yt[:, k, :], start=(k == 0), stop=(k == nk - 1)),
                                 reads=['p6_w%d' % b, yn], writes=[pn], inc=(k == nk - 1))
                        if bi == 0:
                            S.op('dve', lambda e, b=b, ps=ps: e.tensor_tensor(m[:, :], ps[:, :], sg[:, b, :], ALU.mult), reads=[pn, sn], writes=[mn_])
                        else:
                            tn, tm = mtmp.next()
                            S.op('dve', lambda e, b=b, ps=ps, tm=tm: e.tensor_tensor(tm[:, :], ps[:, :], sg[:, b, :], ALU.mult), reads=[pn, sn], writes=[tn])
                            S.op('dve', lambda e, tm=tm: e.tensor_tensor(m[:, :], m[:, :], tm[:, :], ALU.add), reads=[tn, mn_], writes=[mn_])
                    S.op('act', lambda e: e.copy(mb[:, c, :], m[:, :]), reads=[mn_], writes=[('p6_mb', c)])
                for c in range(0, 8, 2):
                    S.interleave([lambda c=c: mchunk(c), lambda c=c: mchunk(c + 1)])
                for c in range(8):
                    pn, ps = self.psum.next()
                    for k in range(8):
                        S.op('pe', lambda e: e.matmul(ps[:, :], lhsT=wo[:, k, c * 128:(c + 1) * 128], rhs=mb[:, k, :], start=(k == 0), stop=(k == 7)),
                             reads=['p6_wo', ('p6_mb', k)], writes=[pn], inc=(k == 7))
                    S.op('act', lambda e: e.activation(sq[:, c, :], ps[:, :], AF.Square), reads=[pn], writes=['p6_sq'])
                    S.op('act', lambda e: e.copy(y[:, c, :], ps[:, :]), reads=[pn], writes=['p6_yy'])
                on_, xo_t = xor_.next()
                self.postnorm_residual(l, 1, 'p6_yy', y, 'p6_sq', sq, 'p6_rstd', rstd, xn, xt, on_, xo_t)
                S.dma('sp', xmT.rearrange("(c p) l -> p c l", p=128)[:, :, t0:t0 + T], xo_t[:], reads=[on_], writes=[('xmT', ti)])

    def postnorm_residual(self, l, which, y_name, y, sq_name, sq, rstd_name, rstd, xres_name, xres, xo_name, xo):
        S = self.S
        self.rms_stats(sq_name, sq, rstd_name, rstd, T)
        for c in range(8):
            S.op('dve', lambda e: e.scalar_tensor_tensor(y[:, c, :], y[:, c, :], self.gains_s[:, l, which, c:c + 1], rstd[:, :],
                                                         ALU.mult, ALU.mult),
                 reads=[y_name, rstd_name, 'gains_s'], writes=[y_name])
            S.op('dve', lambda e: e.tensor_tensor(xo[:, c, :], y[:, c, :], xres[:, c, :], ALU.add),
                 reads=[y_name, xres_name], writes=[xo_name])

    def phase_ffn(self, l, xsrc, xdst, wUpB, wDnB, ffnp):
        nc, S, L = self.nc, self.S, self.L
        NT = 2
        with ExitStack() as st:
            xr = Ring(nc, st, "p7_x", 2, [128, 8, T], F32)
            sq = self.sb(st, "p7_sq", [128, 8, T], BF16)
            rstd = self.sb(st, "p7_rstd", [128, T], F32)
            hr = Ring(nc, st, "p7_h", 2, [128, 8, T], BF16)
            wr = Ring(nc, st, "p7_w", 2, [128, 8, 1024], BF16)
            actr = Ring(nc, st, "p7_act", 2, [128, 24, T], BF16)
            gsb = Ring(nc, st, "p7_g", 3, [128, T + 2], F32)
            gtmp = Ring(nc, st, "p7_gt", 3, [128, T], F32)
            tails = self.sb(st, "p7_tail", [128, 24, 2], F32)
            fp = self.sb(st, "p7_fp", [128, 4, 24], F32)
            self.ys = [self.sb(st, "p7_y%d" % i, [128, 8, T], F32) for i in range(2)]
            self.sqs = [self.sb(st, "p7_sqo%d" % i, [128, 8, T], BF16) for i in range(2)]
            S.dma('sp', fp[:], ffnp[:, l, :, :], writes=['p7_fp'])
            S.op('dve', lambda e: e.memset(tails[:], 0.0), writes=[('p7_tail', c) for c in range(24)])
            wup = wUpB[l].rearrange("(k p) n -> p k n", p=128)
            wdn = wDnB[l].rearrange("(k p) n -> p k n", p=128)
            wq = 0
            for tp in range(L // (NT * T)):
                tiles = []
                for ti in range(tp * NT, (tp + 1) * NT):
                    t0 = ti * T
                    xn, xt = xr.next()
                    S.dma('sp', xt[:], xsrc.rearrange("(c p) l -> p c l", p=128)[:, :, t0:t0 + T], reads=[(xsrc.tensor.name, ti)], writes=[xn])
                    hn, h = hr.next()
                    self.prenorm(l, 3, xn, xt, hn, h, "p7_sq", sq, "p7_rstd", rstd)
                    an, act = actr.next()
                    tiles.append((ti, t0, xn, xt, hn, h, an, act))
                for blk in range(6):
                    wn, w = wr.next()
                    wq += 1
                    q_ = 'sp' if wq % 2 == 0 else 'pool'
                    S.dma(q_, w[:, :, 0:512], wup[:, :, blk * 512:(blk + 1) * 512], reads=[('wUpB', l)], writes=[wn])
                    S.dma(q_, w[:, :, 512:1024], wup[:, :, DFF + blk * 512:DFF + (blk + 1) * 512], reads=[('wUpB', l)], writes=[wn])
                    for (ti, t0, xn, xt, hn, h, an, act) in tiles:
                        def fchunk(j, hn=hn, h=h, an=an, act=act, wn=wn, w=w):
                            ch = blk * 4 + j
                            pgn, pg = self.psum.next()
                            for k in range(8):
                                S.op('pe', lambda e, k=k: e.matmul(pg[:, :], lhsT=w[:, k, 512 + j * 128:512 + (j + 1) * 128], rhs=h[:, k, :], start=(k == 0), stop=(k == 7)),
                                     reads=[wn, hn], writes=[pgn], inc=(k == 7))
                            pvn, pv = self.psum.next()
                            for k in range(8):
                                S.op('pe', lambda e, k=k: e.matmul(pv[:, :], lhsT=w[:, k, j * 128:(j + 1) * 128], rhs=h[:, k, :], start=(k == 0), stop=(k == 7)),
                                     reads=[wn, hn], writes=[pvn], inc=(k == 7))
                            gn, g = gsb.next()
                            S.op('act', lambda e: e.copy(g[:, 2:T + 2], pg[:, :]), reads=[pgn], writes=[gn])
                            S.op('act', lambda e: e.copy(g[:, 0:2], tails[:, ch, :]), reads=[('p7_tail', ch)], writes=[gn])
                            S.op('act', lambda e: e.copy(tails[:, ch, :], g[:, T:T + 2]), reads=[gn], writes=[('p7_tail', ch)])
                            tn, tm = gtmp.next()
                            S.op('dve', lambda e: e.tensor_scalar(tm[:, :], g[:, 0:T], fp[:, 0, ch:ch + 1], fp[:, 3, ch:ch + 1], ALU.mult, ALU.add),
                                 reads=[gn, 'p7_fp'], writes=[tn])
                            S.op('dve', lambda e: e.scalar_tensor_tensor(tm[:, :], g[:, 1:T + 1], fp[:, 1, ch:ch + 1], tm[:, :], ALU.mult, ALU.add),
                                 reads=[gn, tn, 'p7_fp'], writes=[tn])
                            S.op('dve', lambda e: e.scalar_tensor_tensor(tm[:, :], g[:, 2:T + 2], fp[:, 2, ch:ch + 1], tm[:, :], ALU.mult, ALU.add),
                                 reads=[gn, tn, 'p7_fp'], writes=[tn])
                            S.op('act', lambda e: e.activation(tm[:, :], tm[:, :], AF.Gelu_apprx_tanh), reads=[tn], writes=[tn])
                            S.op('dve', lambda e: e.tensor_tensor(act[:, ch, :], pv[:, :], tm[:, :], ALU.mult), reads=[pvn, tn], writes=[(an, ch)])
                        for j in range(0, 4, 2):
                            S.interleave([lambda j=j: fchunk(j), lambda j=j: fchunk(j + 1)])
                for half in range(2):
                    banks = {}
                    for cpair in range(2):
                        for (ti, t0, xn, xt, hn, h, an, act) in tiles:
                            banks[ti] = [self.psum.next() for _ in range(2)]
                        for kb in range(3):
                            wn, w = wr.next()
                            wq += 1
                            q_ = 'sp' if wq % 2 == 0 else 'pool'
                            c00 = half * 512 + cpair * 256
                            S.dma(q_, w[:, :, 0:256], wdn[:, kb * 8:(kb + 1) * 8, c00:c00 + 256], reads=[('wDnB', l)], writes=[wn])
                            for (ti, t0, xn, xt, hn, h, an, act) in tiles:
                                for cc in range(2):
                                    pn, ps = banks[ti][cc]
                                    for k in range(8):
                                        kk = kb * 8 + k
                                        S.op('pe', lambda e: e.matmul(ps[:, :], lhsT=w[:, k, cc * 128:(cc + 1) * 128], rhs=act[:, kk, :], start=(kk == 0), stop=(kk == 23)),
                                             reads=[wn, (an, kk)], writes=[pn], inc=(k == 7))
                        for (ti, t0, xn, xt, hn, h, an, act) in tiles:
                            for cc in range(2):
                                c = half * 4 + cpair * 2 + cc
                                pn, ps = banks[ti][cc]
                                S.op('act', lambda e: e.activation(self.sqs[ti % 2][:, c, :], ps[:, :], AF.Square), reads=[pn], writes=[('p7_sq2', ti % 2)])
                                S.op('act', lambda e: e.copy(self.ys[ti % 2][:, c, :], ps[:, :]), reads=[pn], writes=[('p7_y2', ti % 2)])
                for (ti, t0, xn, xt, hn, h, an, act) in tiles:
                    on, xo_t = ('p7_y2', ti % 2), self.ys[ti % 2]
                    self.postnorm_residual(l, 4, ('p7_y2', ti % 2), self.ys[ti % 2], ('p7_sq2', ti % 2), self.sqs[ti % 2], 'p7_rstd', rstd, xn, xt, on, xo_t)
                    S.dma('sp', xdst.rearrange("(c p) l -> p c l", p=128)[:, :, t0:t0 + T], xo_t[:], reads=[on], writes=[(xdst.tensor.name, ti)])


def host_inputs(inputs, b, L):
    f = np.float32
    d = {}
    d["x"] = np.ascontiguousarray(inputs["x"][b, :L])
    d["mem"] = np.ascontiguousarray(inputs["mem"][b])
    d["ident"] = np.eye(128, dtype=f)
    gs = np.stack([inputs[k] for k in ("g_mix_pre", "g_mix_post", "g_mem", "g_mlp_pre", "g_mlp_post")], axis=1)
    d["gains"] = np.ascontiguousarray(gs.reshape(NL, 5, 8, 128).transpose(3, 0, 1, 2)).astype(f)
    d["w_in"] = inputs["w_in"]
    d["ffn_w_up"] = inputs["ffn_w_up"]
    d["ffn_w_down"] = inputs["ffn_w_down"]
    fp = np.concatenate([inputs["ffn_conv_w"], inputs["ffn_conv_b"][:, None, :]], axis=1)
    lp = np.concatenate([inputs["lru_conv_w"], inputs["lru_conv_b"][:, None], inputs["lru_ba"][:, None],
                         inputs["lru_bx"][:, None], inputs["lru_lambda"][:, None]], axis=1)
    d["lrup"] = np.ascontiguousarray(lp.reshape(NL, 8, 6, 128).transpose(3, 0, 1, 2)).astype(f)
    bd = np.zeros((NL, 2, 128, 6, 128), f)
    for wi, nm in enumerate(("lru_wa", "lru_wx")):
        w = inputs[nm]
        for c in range(6):
            bd[:, wi, 0:64, c, 0:64] = w[:, 2 * c]
            bd[:, wi, 64:128, c, 64:128] = w[:, 2 * c + 1]
    d["lru_bd"] = bd
    d["ssmd"] = np.ascontiguousarray(inputs["ssm_d"].reshape(NL, 6, 128).transpose(2, 0, 1)).astype(f)
    for nm in ("ssm_glu", "mem_wkv", "proj_a", "proj_b", "proj_c", "proj_x", "w_out"):
        d[nm] = inputs[nm]
    am = np.full((128, 2, 12, 256), -30000.0, f)
    kk = np.arange(128)[:, None]
    qq = np.arange(128)[None, :]
    for hd in range(12):
        dd = DILS[hd // 4]
        dp = (qq + 128 - kk).astype(f)
        dc = (qq - kk).astype(f)
        mp = np.where(kk >= qq, -ALIBI[hd] * dd * dp, -30000.0)
        mc = np.where(kk <= qq, -ALIBI[hd] * dd * dc, -30000.0)
        am[:, 0, hd, 0:128] = mp
        am[:, 0, hd, 128:256] = mc
        am[:, 1, hd, 128:256] = mc
    d["amask"] = am
    def lay_a(a):
        return a.reshape(NL, 24, 2, 64).transpose(2, 3, 0, 1).reshape(128, NL, 24)
    ld_full = np.repeat(inputs["ssm_log_dt"][:, :, None], 64, axis=2)
    d["s5A"] = np.ascontiguousarray(np.stack([lay_a(inputs["ssm_a_re"]), lay_a(inputs["ssm_a_im"]), lay_a(ld_full)], axis=2)).astype(f)
    def lay_r(a):
        return a.reshape(NL, 3072)
    rr = np.stack([lay_r(inputs["ssm_a_re"]), lay_r(inputs["ssm_a_im"]), lay_r(ld_full)], axis=1)
    d["s5R"] = np.ascontiguousarray(np.broadcast_to(rr[None], (128, NL, 3, 3072))).astype(f)
    sB = np.zeros((NL, 2, 128, 24, 128), f)
    sC = np.zeros((NL, 2, 128, 24, 128), f)
    for ri, (bn, cn) in enumerate((("ssm_b_re", "ssm_c_re"), ("ssm_b_im", "ssm_c_im"))):
        Bm = inputs[bn]
        Cm = inputs[cn]
        for p in range(24):
            for gl in range(2):
                r0 = 32 * (p % 4) + gl * 16
                sB[:, ri, r0:r0 + 16, p, gl * 64:(gl + 1) * 64] = Bm[:, 2 * p + gl].transpose(0, 2, 1)
                sC[:, ri, gl * 64:(gl + 1) * 64, p, r0:r0 + 16] = Cm[:, 2 * p + gl].transpose(0, 2, 1)
    d["s5B"] = sB
    d["s5C"] = sC
    d["ffnp"] = np.ascontiguousarray(fp.reshape(NL, 4, 24, 128).transpose(3, 0, 1, 2)).astype(f)
    return d


ALL_PHASES = ('prepass', 'prologue', 'p1', 'p2', 'p3', 'p4', 'p5', 'p6', 'p7', 'epilogue')


def kernel(**inputs):
    inputs = {k: np.asarray(v) for k, v in inputs.items()}
    L = inputs["x"].shape[1]
    kb = K(L, NL)
    nc = kb.build(ALL_PHASES)
    in_maps = []
    for b in range(2):
        hi = host_inputs(inputs, b, L)
        in_maps.append({k: hi[k] for k in kb.ins})
    res = run_bass_kernel_spmd(nc, in_maps, core_ids=[0, 1])
    return np.stack([res.results[b]["out"] for b in range(2)], axis=0)
```

```python
import math
from contextlib import ExitStack
import numpy as np
import concourse.bass as bass
import concourse.mybir as mybir
from concourse.bass_utils import run_bass_kernel_spmd

AF = mybir.ActivationFunctionType
ALU = mybir.AluOpType
F32 = mybir.dt.float32
BF16 = mybir.dt.bfloat16

D = 1024
NL = 2
SEQ = 16384
MEM = 256
IN_W = 9472
DFF = 3072
T = 512
EPS = 1e-6
ALIBI = [2.0 ** (-8.0 * (h + 1) / 12) for h in range(12)]
DILS = (1, 4, 16)
import os
P3STOP = int(os.environ.get("P3STOP", "0"))


class Sched:
    NSLOT = 8

    def __init__(self, nc, stack):
        self.nc = nc
        self.engs = {'pe': nc.tensor, 'act': nc.scalar, 'dve': nc.vector,
                     'pool': nc.gpsimd, 'sp': nc.sync}
        self.sem = {}
        self.cnt = {}
        for n in ['pe', 'act', 'dve', 'pool']:
            self.sem[n] = stack.enter_context(nc.semaphore("s_" + n))
            self.cnt[n] = 0
        self.dq = {}
        for q in ['sp', 'pool', 'act']:
            for i in range(self.NSLOT):
                self.sem[('dma', q, i)] = stack.enter_context(nc.semaphore("d_%s%d" % (q, i)))
            self.dq[q] = 0
        self.seen = {e: {} for e in self.engs}
        self.lastw = {}
        self.readers = {}

    def _deps(self, reads, writes):
        deps = []
        for r in reads:
            t = self.lastw.get(r)
            if t is not None:
                deps.append(t)
        for w in writes:
            t = self.lastw.get(w)
            if t is not None:
                deps.append(t)
            deps.extend(self.readers.get(w, ()))
        return deps

    def _wait(self, ename, deps):
        best = {}
        for (src, val) in deps:
            if best.get(src, 0) < val:
                best[src] = val
        seen = self.seen[ename]
        eng = self.engs[ename]
        for src, val in best.items():
            if src == 'pe' and ename == 'pe':
                continue
            if seen.get(src, 0) >= val:
                continue
            eng.wait_ge(self.sem[src], val)
            seen[src] = val

    def _record(self, ticket, reads, writes):
        for r in reads:
            self.readers.setdefault(r, []).append(ticket)
        for w in writes:
            self.lastw[w] = ticket
            self.readers[w] = []

    defer = None

    def op(self, ename, fn, reads=(), writes=(), inc=True):
        if self.defer is not None:
            self.defer.append(('__op__', (ename, fn, reads, writes, inc)))
            return None
        self._wait(ename, self._deps(reads, writes))
        ins = fn(self.engs[ename])
        if inc:
            self.cnt[ename] += 1
            ins.then_inc(self.sem[ename], 1)
            ticket = (ename, self.cnt[ename])
        else:
            ticket = (ename, self.cnt[ename] + 1)
        self._record(ticket, reads, writes)
        return ticket

    def dma(self, q, out, in_, reads=(), writes=(), **kw):
        if self.defer is not None:
            self.defer.append(('__dma__', (q, out, in_, reads, writes, kw)))
            return None
        i = self.dq[q]
        slot = i % self.NSLOT
        rnd = i // self.NSLOT
        src = ('dma', q, slot)
        deps = self._deps(reads, writes)
        if rnd > 0:
            deps.append((src, 16 * rnd))
        self._wait(q, deps)
        ins = self.engs[q].dma_start(out=out, in_=in_, **kw)
        ins.then_inc(self.sem[src], 16)
        self.dq[q] = i + 1
        ticket = (src, 16 * (rnd + 1))
        self._record(ticket, reads, writes)
        return ticket

    def coll(self, kind, src, dst, groups, reads=(), writes=()):
        q = 'pool'
        i = self.dq[q]
        slot = i % self.NSLOT
        rnd = i // self.NSLOT
        srck = ('dma', q, slot)
        deps = self._deps(reads, writes)
        if rnd > 0:
            deps.append((srck, 16 * rnd))
        self._wait(q, deps)
        ins = self.engs[q].collective_compute(kind, ALU.bypass, replica_groups=groups, ins=[src], outs=[dst])
        ins.then_inc(self.sem[srck], 16)
        self.dq[q] = i + 1
        ticket = (srck, 16 * (rnd + 1))
        self._record(ticket, reads, writes)
        return ticket

    def run_deferred(self, item):
        kind, a = item
        if kind == '__op__':
            self.op(*a)
        else:
            q, out, in_, reads, writes, kw = a
            self.dma(q, out, in_, reads=reads, writes=writes, **kw)

    def interleave(self, fns):
        lists = []
        for f in fns:
            self.defer = []
            f()
            lists.append(self.defer)
            self.defer = None
        while any(lists):
            for lst in lists:
                if lst:
                    self.run_deferred(lst.pop(0))

    def finish(self, ename='sp'):
        deps = list(self.lastw.values())
        for l in self.readers.values():
            deps.extend(l)
        self._wait(ename, deps)

    def barrier(self):
        for e in self.engs:
            self.finish(e)
        self.lastw = {}
        self.readers = {}


_UID = [0]


class Ring:
    def __init__(self, nc, stack, name, n, shape, dtype, psum=False):
        self.bufs = []
        _UID[0] += 1
        for i in range(n):
            nm = "%s_%d_%d" % (name, _UID[0], i)
            if psum:
                t = stack.enter_context(nc.psum_tensor(nm, shape, dtype))
            else:
                t = stack.enter_context(nc.sbuf_tensor(nm, shape, dtype))
            self.bufs.append((nm, t))
        self.i = 0

    def next(self):
        b = self.bufs[self.i % len(self.bufs)]
        self.i += 1
        return b


class K:
    def __init__(self, L, nl, dbg=False):
        self.L = L
        self.nl = nl
        self.dbg = dbg
        self.nc = bass.Bass("TRN2", target_bir_lowering=False)
        self.ins = {}
        self.scr = {}

    def inp(self, name, shape, dt=F32):
        t = self.nc.dram_tensor(name, list(shape), dt, kind="ExternalInput").ap()
        self.ins[name] = t
        return t

    def scratch(self, name, shape, dt):
        kind = "ExternalOutput" if self.dbg else "Internal"
        t = self.nc.dram_tensor(name, list(shape), dt, kind=kind).ap()
        self.scr[name] = t
        return t

    def sb(self, st, name, shape, dt):
        _UID[0] += 1
        return st.enter_context(self.nc.sbuf_tensor("%s_%d" % (name, _UID[0]), list(shape), dt))

    def build(self, phases):
        nc = self.nc
        L, nl = self.L, self.nl
        x = self.inp("x", [L, D])
        mem = self.inp("mem", [MEM, D])
        ident = self.inp("ident", [128, 128])
        gains = self.inp("gains", [128, NL, 5, 8])
        w_in = self.inp("w_in", [NL, D, IN_W])
        ffn_w_up = self.inp("ffn_w_up", [NL, D, 2 * DFF])
        ffn_w_down = self.inp("ffn_w_down", [NL, DFF, D])
        ffnp = self.inp("ffnp", [128, NL, 4, 24])
        lrup = self.inp("lrup", [128, NL, 8, 6])
        lru_bd = self.inp("lru_bd", [NL, 2, 128, 6, 128])
        ssmd = self.inp("ssmd", [128, NL, 6])
        wsrc = {}
        for nm, shp in (("ssm_glu", [NL, 768, 1536]), ("mem_wkv", [NL, D, 1536]), ("proj_a", [NL, 768, D]),
                        ("proj_b", [NL, 256, D]), ("proj_c", [NL, 768, D]), ("proj_x", [NL, 768, D]), ("w_out", [NL, D, D])):
            wsrc[nm] = (self.inp(nm, shp), self.scratch(nm + "B", shp, BF16), shp[1])
        self.wB = {nm: v[1] for nm, v in wsrc.items()}
        yA = self.scratch("yA", [768, L], BF16)
        yC = self.scratch("yC", [768, L], BF16)
        yX = self.scratch("yX", [768, L], BF16)
        oG = self.scratch("oG", [3, L, 260], F32)
        self.phases = phases
        amask = self.inp("amask", [128, 2, 12, 256])
        s5A = self.inp("s5A", [128, NL, 3, 24])
        s5R = self.inp("s5R", [128, NL, 3, 3072])
        s5B = self.inp("s5B", [NL, 2, 128, 24, 128])
        s5C = self.inp("s5C", [NL, 2, 128, 24, 128])
        out = self.nc.dram_tensor("out", [L, D], F32, kind="ExternalOutput").ap()
        self.out = out

        xT = self.scratch("xT", [D, L], F32)
        xmT = self.scratch("xmT", [D, L], F32)
        zF = self.scratch("zF", [56 * 128, L], BF16)
        zQKV = self.scratch("zQKV", [L, 2304], BF16)
        wInB = self.scratch("wInB", [NL, D, IN_W], BF16)
        wUpB = self.scratch("wUpB", [NL, D, 2 * DFF], BF16)
        wDnB = self.scratch("wDnB", [NL, DFF, D], BF16)

        with ExitStack() as st0:
            S = Sched(nc, st0)
            self.S = S
            ident_f = self.sb(st0, "ident_f", [128, 128], F32)
            ones_b = self.sb(st0, "ones_b", [128, 128], BF16)
            gains_s = self.sb(st0, "gains_s", [128, NL, 5, 8], F32)
            eps_c = self.sb(st0, "eps_c", [128, 1], F32)
            self.ident_f, self.ones_b, self.gains_s, self.eps_c = ident_f, ones_b, gains_s, eps_c
            one_c = self.sb(st0, "one_c", [128, 1], F32)
            self.one_c = one_c
            S.op('dve', lambda e: e.memset(one_c[:], 1.0), writes=['one_c'])
            S.dma('sp', ident_f[:], ident, writes=['ident_f'])
            S.dma('sp', gains_s[:], gains, writes=['gains_s'])
            S.op('dve', lambda e: e.memset(ones_b[:], 1.0), writes=['ones_b'])
            S.op('dve', lambda e: e.memset(eps_c[:], EPS), writes=['eps_c'])
            self.psum = Ring(nc, st0, "ps", 6, [128, 512], F32, psum=True)

            if 'prepass' in phases:
                for l in range(nl):
                    for (src, dst, rows) in [(w_in, wInB, D), (ffn_w_up, wUpB, D), (ffn_w_down, wDnB, DFF)] + list(wsrc.values()):
                        for r in range(0, rows, 128):
                            S.dma('pool', dst[l, r:r + 128, :], src[l, r:r + 128, :],
                                  writes=[(dst.tensor.name, l)])
            if 'prologue' in phases:
                self.transpose_in(x, xT, L)
                S.barrier()
            for l in range(nl):
                if 'p1' in phases:
                    self.phase_inproj(l, xT, wInB, zF, zQKV)
                    S.barrier()
                if 'p2' in phases:
                    self.phase_lru(l, zF, yA, lrup, lru_bd)
                    S.barrier()
                if 'p3' in phases:
                    self.phase_attn(l, zQKV, oG, amask)
                    S.barrier()
                if 'p4' in phases:
                    self.phase_s5(l, zF, yC, s5A, s5R, s5B, s5C, ssmd)
                    S.barrier()
                if 'p5' in phases:
                    self.phase_xattn(l, mem, zF, yX)
                    S.barrier()
                if 'p6' in phases:
                    self.phase_merge(l, xT, xmT, zF, yA, yC, yX, oG)
                    S.barrier()
                if 'p7' in phases:
                    self.phase_ffn(l, xmT if 'p6' in phases else xT, xT, wUpB, wDnB, ffnp)
                    S.barrier()
            if 'epilogue' in phases:
                self.transpose_out(xT, out, L)
            S.barrier()
        return nc

    def transpose_in(self, x, xT, L):
        nc, S = self.nc, self.S
        with ExitStack() as st:
            xin = Ring(nc, st, "ti_x", 2, [128, D], F32)
            xo = Ring(nc, st, "ti_o", 2, [128, 8, 128], F32)
            for b in range(L // 128):
                nm, xt = xin.next()
                S.dma('sp', xt[:], x[b * 128:(b + 1) * 128, :], writes=[nm])
                no, ot = xo.next()
                for half in range(2):
                    pn, ps = self.psum.next()
                    for j in range(4):
                        c = half * 4 + j
                        S.op('pe', lambda e: e.transpose(ps[:, j * 128:(j + 1) * 128], xt[:, c * 128:(c + 1) * 128], self.ident_f[:]),
                             reads=[nm, 'ident_f'], writes=[pn], inc=(j == 3))
                    eng = 'act' if half == 0 else 'dve'
                    if eng == 'act':
                        S.op('act', lambda e: e.copy(ot[:, half * 4:(half + 1) * 4, :], ps[:, :].rearrange("p (a b) -> p a b", a=4)),
                             reads=[pn], writes=[no])
                    else:
                        S.op('dve', lambda e: e.tensor_copy(ot[:, half * 4:(half + 1) * 4, :], ps[:, :].rearrange("p (a b) -> p a b", a=4)),
                             reads=[pn], writes=[no])
                S.dma('sp', xT.rearrange("(c p) l -> p c l", p=128)[:, :, b * 128:(b + 1) * 128], ot[:],
                      reads=[no], writes=[('xT', b // 4)])

    def transpose_out(self, xT, out, L):
        nc, S = self.nc, self.S
        with ExitStack() as st:
            xin = Ring(nc, st, "to_x", 2, [128, 8, 128], F32)
            xo = Ring(nc, st, "to_o", 2, [128, D], F32)
            for b in range(L // 128):
                nm, xt = xin.next()
                S.dma('sp', xt[:], xT.rearrange("(c p) l -> p c l", p=128)[:, :, b * 128:(b + 1) * 128],
                      reads=[('xT', b // 4)], writes=[nm])
                no, ot = xo.next()
                for half in range(2):
                    pn, ps = self.psum.next()
                    for j in range(4):
                        c = half * 4 + j
                        S.op('pe', lambda e: e.transpose(ps[:, j * 128:(j + 1) * 128], xt[:, c, :], self.ident_f[:]),
                             reads=[nm, 'ident_f'], writes=[pn], inc=(j == 3))
                    if half == 0:
                        S.op('act', lambda e: e.copy(ot[:, 0:512], ps[:, :]), reads=[pn], writes=[no])
                    else:
                        S.op('dve', lambda e: e.tensor_copy(ot[:, 512:1024], ps[:, :]), reads=[pn], writes=[no])
                S.dma('sp', out[b * 128:(b + 1) * 128, :], ot[:], reads=[no], writes=['out'])

    def rms_stats(self, sqname, sq, rstd_name, rstd, Tn):
        S = self.S
        pn, ps = self.psum.next()
        for c in range(8):
            S.op('pe', lambda e: e.matmul(ps[:, :Tn], lhsT=self.ones_b[:], rhs=sq[:, c, :], start=(c == 0), stop=(c == 7)),
                 reads=[sqname, 'ones_b'], writes=[pn], inc=(c == 7))
        S.op('act', lambda e: e.activation(rstd[:, :Tn], ps[:, :Tn], AF.Sqrt, bias=self.eps_c[:], scale=1.0 / D),
             reads=[pn, 'eps_c'], writes=[rstd_name])
        S.op('dve', lambda e: e.reciprocal(rstd[:, :Tn], rstd[:, :Tn]), reads=[rstd_name], writes=[rstd_name])

    def prenorm(self, l, which, xt_name, xt, h_name, h, sq_name, sq, rstd_name, rstd):
        S = self.S
        for c in range(8):
            S.op('act', lambda e: e.activation(sq[:, c, :], xt[:, c, :], AF.Square), reads=[xt_name], writes=[sq_name])
        self.rms_stats(sq_name, sq, rstd_name, rstd, T)
        for c in range(8):
            S.op('dve', lambda e: e.scalar_tensor_tensor(h[:, c, :], xt[:, c, :], self.gains_s[:, l, which, c:c + 1], rstd[:, :],
                                                         ALU.mult, ALU.mult),
                 reads=[xt_name, rstd_name, 'gains_s'], writes=[h_name])

    def phase_inproj(self, l, xT, wInB, zF, zQKV):
        nc, S, L = self.nc, self.S, self.L
        FM_COLS = list(range(0, 1536, 128)) + list(range(3840, IN_W, 128))
        NT = 4
        with ExitStack() as st:
            xr = Ring(nc, st, "p1_x", 2, [128, 8, T], F32)
            sq = self.sb(st, "p1_sq", [128, 8, T], BF16)
            rstd = self.sb(st, "p1_rstd", [128, T], F32)
            hr = Ring(nc, st, "p1_h", 5, [128, 8, T], BF16)
            wr = Ring(nc, st, "p1_w", 2, [128, 8, 1024], BF16)
            zo = Ring(nc, st, "p1_zo", 2, [128, 8, T], BF16)
            qo = Ring(nc, st, "p1_qo", 3, [128, 4, 1024], BF16)
            wv = wInB[l].rearrange("(k p) n -> p k n", p=128)
            zFv = zF.rearrange("(c p) l -> p c l", p=128)
            ev = 0
            wq = 0
            for tp in range(L // (NT * T)):
                hs = []
                for ti in range(tp * NT, (tp + 1) * NT):
                    t0 = ti * T
                    xn, xt = xr.next()
                    S.dma('sp', xt[:], xT.rearrange("(c p) l -> p c l", p=128)[:, :, t0:t0 + T], reads=[('xT', ti)], writes=[xn])
                    hn, h = hr.next()
                    self.prenorm(l, 0, xn, xt, hn, h, "p1_sq", sq, "p1_rstd", rstd)
                    hs.append((t0, hn, h))
                for blk in range(7):
                    wn, w = wr.next()
                    wq += 1
                    for j in range(8):
                        c0 = FM_COLS[blk * 8 + j]
                        if j == 0 or FM_COLS[blk * 8 + j - 1] + 128 != c0:
                            j2 = j
                            while j2 + 1 < 8 and FM_COLS[blk * 8 + j2 + 1] == FM_COLS[blk * 8 + j2] + 128:
                                j2 += 1
                            S.dma('sp' if wq % 2 == 0 else 'pool', w[:, :, j * 128:(j2 + 1) * 128], wv[:, :, c0:c0 + (j2 - j + 1) * 128],
                                  reads=[('wInB', l)], writes=[wn])
                    for (t0, hn, h) in hs:
                        zn, z = zo.next()
                        for j in range(8):
                            ch = blk * 8 + j
                            pn, ps = self.psum.next()
                            for k in range(8):
                                S.op('pe', lambda e: e.matmul(ps[:, :], lhsT=w[:, k, j * 128:(j + 1) * 128], rhs=h[:, k, :], start=(k == 0), stop=(k == 7)),
                                     reads=[wn, hn], writes=[pn], inc=(k == 7))
                            if 6 <= ch < 12:
                                S.op('act', lambda e: e.activation(z[:, j, :], ps[:, :], AF.Gelu_apprx_tanh), reads=[pn], writes=[zn])
                            elif ch >= 24:
                                S.op('act', lambda e: e.activation(z[:, j, :], ps[:, :], AF.Sigmoid), reads=[pn], writes=[zn])
                            else:
                                S.op('dve', lambda e: e.tensor_copy(z[:, j, :], ps[:, :]), reads=[pn], writes=[zn])
                        S.dma('pool', zFv[:, blk * 8:(blk + 1) * 8, t0:t0 + T], z[:], reads=[zn], writes=['zF'])
                for blk in range(3):
                    c0 = 1536 + blk * 1024
                    ncol = min(1024, 3840 - c0)
                    wn, w = wr.next()
                    wq += 1
                    S.dma('sp' if wq % 2 == 0 else 'pool', w[:, :, :ncol], wv[:, :, c0:c0 + ncol], reads=[('wInB', l)], writes=[wn])
                    for (t0, hn, h) in hs:
                        qn, q = qo.next()
                        for tb in range(4):
                            for n0 in range(0, ncol, 512):
                                nn = min(512, ncol - n0)
                                pn, ps = self.psum.next()
                                for k in range(8):
                                    S.op('pe', lambda e: e.matmul(ps[:, :nn], lhsT=h[:, k, tb * 128:(tb + 1) * 128], rhs=w[:, k, n0:n0 + nn], start=(k == 0), stop=(k == 7)),
                                         reads=[wn, hn], writes=[pn], inc=(k == 7))
                                dst = q[:, tb, n0:n0 + nn]
                                ev += 1
                                if ev % 2:
                                    S.op('dve', lambda e: e.tensor_copy(dst, ps[:, :nn]), reads=[pn], writes=[qn])
                                else:
                                    S.op('act', lambda e: e.copy(dst, ps[:, :nn]), reads=[pn], writes=[qn])
                        S.dma('pool', zQKV[t0:t0 + T, blk * 1024:blk * 1024 + ncol].rearrange("(tb p) n -> p tb n", p=128), q[:, :, :ncol], reads=[qn], writes=['zQKV'])

    def phase_lru(self, l, zF, yA, lrup, lru_bd):
        nc, S, L = self.nc, self.S, self.L
        with ExitStack() as st:
            lp = self.sb(st, "p2_lp", [128, 8, 6], F32)
            kap = self.sb(st, "p2_kap", [128, 2, 6], F32)
            bdA = self.sb(st, "p2_bdA", [128, 6, 128], BF16)
            bdX = self.sb(st, "p2_bdX", [128, 6, 128], BF16)
            state = self.sb(st, "p2_state", [128, 6], F32)
            xar = Ring(nc, st, "p2_xa", 3, [128, T + 3], BF16)
            ggr = Ring(nc, st, "p2_gg", 3, [128, T], BF16)
            xcr = Ring(nc, st, "p2_xc", 3, [128, T], F32)
            xcbr = Ring(nc, st, "p2_xcb", 3, [128, T], BF16)
            rr = Ring(nc, st, "p2_r", 3, [128, T], F32)
            ir = Ring(nc, st, "p2_i", 3, [128, T], F32)
            ar = Ring(nc, st, "p2_a", 3, [128, T], F32)
            a2r = Ring(nc, st, "p2_a2", 3, [128, T], F32)
            hr = Ring(nc, st, "p2_h", 3, [128, T], F32)
            yr = Ring(nc, st, "p2_y", 3, [128, T], BF16)
            S.dma('sp', lp[:], lrup[:, l, :, :], writes=['p2_lp'])
            S.dma('pool', bdA[:], lru_bd[l, 0], writes=['p2_bdA'])
            S.dma('pool', bdX[:], lru_bd[l, 1], writes=['p2_bdX'])
            S.op('dve', lambda e: e.memset(state[:], 0.0), writes=[('p2_state', c) for c in range(6)])
            S.op('act', lambda e: e.activation(kap[:, 0, :], lp[:, 7, :], AF.Exp, scale=-1.0), reads=['p2_lp'], writes=['p2_kap'])
            S.op('act', lambda e: e.activation(kap[:, 0, :], kap[:, 0, :], AF.Ln, bias=self.one_c[:]), reads=['p2_kap', 'one_c'], writes=['p2_kap'])
            S.op('dve', lambda e: e.tensor_scalar(kap[:, 1, :], kap[:, 0, :], -16.0, None, ALU.mult), reads=['p2_kap'], writes=['p2_kap'])
            S.op('dve', lambda e: e.tensor_scalar(kap[:, 0, :], kap[:, 0, :], -8.0, None, ALU.mult), reads=['p2_kap'], writes=['p2_kap'])
            zFv = zF.rearrange("(c p) l -> p c l", p=128)
            yAv = yA.rearrange("(c p) l -> p c l", p=128)
            for ti in range(L // T):
                t0 = ti * T
                def chunk(c):
                    xn, xa = xar.next()
                    if t0 == 0:
                        S.op('pool', lambda e: e.memset(xa[:, 0:3], 0.0), writes=[xn])
                        S.dma('sp', xa[:, 3:T + 3], zFv[:, c, 0:T], reads=['zF'], writes=[xn])
                    else:
                        S.dma('sp', xa[:, :], zFv[:, c, t0 - 3:t0 + T], reads=['zF'], writes=[xn])
                    gn, gg = ggr.next()
                    S.dma('sp', gg[:, :], zFv[:, 6 + c, t0:t0 + T], reads=['zF'], writes=[gn])
                    xcn, xc = xcr.next()
                    S.op('dve', lambda e: e.tensor_scalar(xc[:, :], xa[:, 0:T], lp[:, 0, c:c + 1], lp[:, 4, c:c + 1], ALU.mult, ALU.add),
                         reads=[xn, 'p2_lp'], writes=[xcn])
                    for j in range(1, 4):
                        S.op('dve', lambda e, j=j: e.scalar_tensor_tensor(xc[:, :], xa[:, j:j + T], lp[:, j, c:c + 1], xc[:, :], ALU.mult, ALU.add),
                             reads=[xn, xcn, 'p2_lp'], writes=[xcn])
                    xbn, xcb = xcbr.next()
                    S.op('dve', lambda e: e.tensor_copy(xcb[:, :], xc[:, :]), reads=[xcn], writes=[xbn])
                    prn, pr = self.psum.next()
                    S.op('pe', lambda e: e.matmul(pr[:, :], lhsT=bdA[:, c, :], rhs=xcb[:, :], start=True, stop=True), reads=['p2_bdA', xbn], writes=[prn])
                    pin, pi = self.psum.next()
                    S.op('pe', lambda e: e.matmul(pi[:, :], lhsT=bdX[:, c, :], rhs=xcb[:, :], start=True, stop=True), reads=['p2_bdX', xbn], writes=[pin])
                    rn, r = rr.next()
                    S.op('act', lambda e: e.activation(r[:, :], pr[:, :], AF.Sigmoid, bias=lp[:, 5, c:c + 1]), reads=[prn, 'p2_lp'], writes=[rn])
                    inn, iv = ir.next()
                    S.op('act', lambda e: e.activation(iv[:, :], pi[:, :], AF.Sigmoid, bias=lp[:, 6, c:c + 1]), reads=[pin, 'p2_lp'], writes=[inn])
                    an, a = ar.next()
                    S.op('act', lambda e: e.activation(a[:, :], r[:, :], AF.Exp, scale=kap[:, 0, c:c + 1]), reads=[rn, 'p2_kap'], writes=[an])
                    a2n, a2 = a2r.next()
                    S.op('act', lambda e: e.activation(a2[:, :], r[:, :], AF.Exp, scale=kap[:, 1, c:c + 1]), reads=[rn, 'p2_kap'], writes=[a2n])
                    S.op('dve', lambda e: e.tensor_scalar(a2[:, :], a2[:, :], -1.0, 1.0, ALU.mult, ALU.add), reads=[a2n], writes=[a2n])
                    S.op('act', lambda e: e.activation(a2[:, :], a2[:, :], AF.Sqrt), reads=[a2n], writes=[a2n])
                    S.op('dve', lambda e: e.tensor_tensor(iv[:, :], iv[:, :], a2[:, :], ALU.mult), reads=[inn, a2n], writes=[inn])
                    S.op('dve', lambda e: e.tensor_tensor(iv[:, :], iv[:, :], xc[:, :], ALU.mult), reads=[inn, xcn], writes=[inn])
                    hn, h = hr.next()
                    S.op('dve', lambda e: e.tensor_tensor_scan(h[:, :], a[:, :], iv[:, :], state[:, c:c + 1], ALU.mult, ALU.add),
                         reads=[an, inn, ('p2_state', c)], writes=[hn])
                    S.op('act', lambda e: e.copy(state[:, c:c + 1], h[:, T - 1:T]), reads=[hn], writes=[('p2_state', c)])
                    yn, y = yr.next()
                    S.op('dve', lambda e: e.tensor_tensor(y[:, :], h[:, :], gg[:, :], ALU.mult), reads=[hn, gn], writes=[yn])
                    S.dma('sp', yAv[:, c, t0:t0 + T], y[:, :], reads=[yn], writes=['yA'])
                for c in range(0, 6, 3):
                    S.interleave([lambda c=c: chunk(c), lambda c=c: chunk(c + 1), lambda c=c: chunk(c + 2)])

    def phase_attn(self, l, zQKV, oG, amask):
        nc, S, L = self.nc, self.S, self.L
        with ExitStack() as st:
            mk = self.sb(st, "p3_mk", [128, 2, 12, 256], F32)
            idb = self.sb(st, "p3_idb", [128, 128], BF16)
            S.dma('sp', mk[:], amask, writes=['p3_mk'])
            S.op('dve', lambda e: e.tensor_copy(idb[:], self.ident_f[:]), reads=['ident_f'], writes=['p3_idb'])
            extra = Ring(nc, st, "p3_psx", 2, [128, 512], F32, psum=True)
            psum = Ring.__new__(Ring)
            psum.bufs = list(self.psum.bufs) + list(extra.bufs)
            psum.i = 0
            qr = Ring(nc, st, "p3_q", 4, [128, 256], BF16)
            kr = Ring(nc, st, "p3_k", 4, [128, 256], BF16)
            vr = Ring(nc, st, "p3_v", 5, [128, 4, 128], BF16)
            qkr = Ring(nc, st, "p3_qk", 4, [128, 8, 128], BF16)
            scr = Ring(nc, st, "p3_sc", 4, [128, 512], F32)
            ptr = Ring(nc, st, "p3_pt", 6, [128, 2, 2, 128], BF16)
            osr = Ring(nc, st, "p3_os", 4, [128, 260], F32)
            for (vn, v) in vr.bufs:
                S.op('pool', lambda e: e.memset(v[:], 1.0), writes=[vn])
            box = {}

            def unit(g, d, r, blk):
                zv = zQKV.rearrange("(n d) c -> d n c", d=d)
                ov = oG[g].rearrange("(n d) c -> d n c", d=d)
                rows = slice(blk * 128, (blk + 1) * 128)
                qn, q = qr.next()
                kn, k_ = kr.next()
                vn, v = vr.next()
                S.dma('sp', q[:, :], zv[r, rows, 256 * g:256 * g + 256], reads=['zQKV'], writes=[qn])
                S.dma('sp', k_[:, :], zv[r, rows, 768 + 256 * g:768 + 256 * g + 256], reads=['zQKV'], writes=[kn])
                S.dma('sp', v[:, :, 0:64], zv[r, rows, 1536 + 256 * g:1536 + 256 * g + 256].rearrange("p (h e) -> p h e", e=64),
                      reads=['zQKV'], writes=[vn])
                ptn, pT = psum.next()
                for j in range(4):
                    S.op('pe', lambda e, j=j: e.matmul(pT[0:64, j * 128:(j + 1) * 128], lhsT=q[:, j * 64:(j + 1) * 64], rhs=idb[:], start=True, stop=True),
                         reads=[qn, 'p3_idb'], writes=[ptn], inc=(j == 3))
                ptn2, pT2 = psum.next()
                for j in range(4):
                    S.op('pe', lambda e, j=j: e.matmul(pT2[0:64, j * 128:(j + 1) * 128], lhsT=k_[:, j * 64:(j + 1) * 64], rhs=idb[:], start=True, stop=True),
                         reads=[kn, 'p3_idb'], writes=[ptn2], inc=(j == 3))
                qkn, qk = qkr.next()
                S.op('dve', lambda e: e.tensor_copy(qk[0:64, 0:4, :], pT[0:64, 0:512].rearrange("p (a b) -> p a b", a=4)), reads=[ptn], writes=[qkn])
                S.op('dve', lambda e: e.tensor_copy(qk[0:64, 4:8, :], pT2[0:64, 0:512].rearrange("p (a b) -> p a b", a=4)), reads=[ptn2], writes=[qkn])
                qTn, qT = qkn, qk[:, 0:4, :]
                kTn, kT = qkn, qk[:, 4:8, :]
                first = blk == 0
                if first:
                    kTp_n, kTp, vp_n, vp = kTn, kT, vn, v
                else:
                    kTp_n, kTp, vp_n, vp = box['prev']
                box['prev'] = (kTn, kT, vn, v)
                pts = []
                for pair in range(2):
                    pn, ps = psum.next()
                    for h2 in range(2):
                        hx = 2 * pair + h2
                        col = h2 * 256
                        if not first:
                            S.op('pe', lambda e, hx=hx, col=col, ps=ps: e.matmul(ps[:, col:col + 128], lhsT=kTp[0:64, hx, :], rhs=qT[0:64, hx, :], start=True, stop=True),
                                 reads=[kTp_n, qTn], writes=[pn], inc=False)
                        S.op('pe', lambda e, hx=hx, col=col, ps=ps: e.matmul(ps[:, col + 128:col + 256], lhsT=kT[0:64, hx, :], rhs=qT[0:64, hx, :], start=True, stop=True),
                             reads=[kTn, qTn], writes=[pn], inc=(h2 == 1))
                    scn, sc = scr.next()
                    hh0 = 4 * g + 2 * pair
                    mview = mk[:, 1 if first else 0, hh0:hh0 + 2, :].rearrange("p a b -> p (a b)")
                    if first:
                        S.op('pool', lambda e, sc=sc: e.memset(sc[:, :], 0.0), writes=[scn])
                        for h2 in range(2):
                            col = h2 * 256
                            S.op('dve', lambda e, col=col, sc=sc, ps=ps, mview=mview: e.scalar_tensor_tensor(sc[:, col + 128:col + 256], ps[:, col + 128:col + 256], 0.125, mview[:, col + 128:col + 256], ALU.mult, ALU.add),
                                 reads=[pn, 'p3_mk'], writes=[scn])
                            S.op('dve', lambda e, col=col, sc=sc, mview=mview: e.tensor_copy(sc[:, col:col + 128], mview[:, col:col + 128]), reads=['p3_mk'], writes=[scn])
                    else:
                        S.op('dve', lambda e, sc=sc, ps=ps, mview=mview: e.scalar_tensor_tensor(sc[:, :], ps[:, :], 0.125, mview, ALU.mult, ALU.add),
                             reads=[pn, 'p3_mk'], writes=[scn])
                    pn2, pt = ptr.next()
                    S.op('act', lambda e, pt=pt, sc=sc: e.activation(pt[:, :, :, :].rearrange("p a b c -> p (a b c)"), sc[:, :], AF.Exp), reads=[scn], writes=[pn2])
                    pts.append((pn2, pt))
                pn, ps = psum.next()
                for hh in range(4):
                    pn2, pt = pts[hh // 2]
                    S.op('pe', lambda e, hh=hh, pt=pt: e.matmul(ps[:, hh * 128:hh * 128 + 65], lhsT=pt[:, hh % 2, 0, :], rhs=vp[:, hh, 0:65], start=True, stop=False),
                         reads=[pn2, vp_n], writes=[pn], inc=False)
                    S.op('pe', lambda e, hh=hh, pt=pt: e.matmul(ps[:, hh * 128:hh * 128 + 65], lhsT=pt[:, hh % 2, 1, :], rhs=v[:, hh, 0:65], start=False, stop=True),
                         reads=[pn2, vn], writes=[pn], inc=(hh == 3))
                on, o = osr.next()
                S.op('act', lambda e: e.copy(o[:, :].rearrange("p (h e) -> p h e", e=65), ps[:, :].rearrange("p (h e) -> p h e", e=128)[:, :, 0:65]), reads=[pn], writes=[on])
                S.dma('sp', ov[r, rows, :], o[:, :], reads=[on], writes=['oG'])

            for g, d in enumerate(DILS):
                nb = L // (128 * d)
                for r in range(d):
                    if nb % 2 == 0:
                        for blk in range(0, nb, 2):
                            S.interleave([lambda blk=blk: unit(g, d, r, blk), lambda blk=blk: unit(g, d, r, blk + 1)])
                    else:
                        for blk in range(nb):
                            unit(g, d, r, blk)

    def sincos(self, st, th_name, th, n, out_s, out_c, key):
        S = self.S
        TWO_PI = 2.0 * math.pi
        ti = self.sb(st, "sc_i", [128, n], mybir.dt.int32)
        tf = self.sb(st, "sc_f", [128, n], F32)
        ph = self.sb(st, "sc_p", [128, n], F32)
        mm = self.sb(st, "sc_m", [128, n], F32)
        for (shift, outt) in ((0.0, out_s), (math.pi / 2, out_c)):
            S.op('dve', lambda e: e.tensor_scalar(ph[:, :], th, 1.0 / TWO_PI, shift / TWO_PI, ALU.mult, ALU.add), reads=[th_name], writes=[key + 'ph'])
            S.op('dve', lambda e: e.tensor_copy(ti[:, :], ph[:, :]), reads=[key + 'ph'], writes=[key + 'ti'])
            S.op('dve', lambda e: e.tensor_copy(tf[:, :], ti[:, :]), reads=[key + 'ti'], writes=[key + 'tf'])
            S.op('dve', lambda e: e.tensor_tensor(ph[:, :], ph[:, :], tf[:, :], ALU.subtract), reads=[key + 'ph', key + 'tf'], writes=[key + 'ph'])
            S.op('dve', lambda e: e.tensor_scalar(mm[:, :], ph[:, :], 0.5, None, ALU.is_gt), reads=[key + 'ph'], writes=[key + 'mm'])
            S.op('dve', lambda e: e.tensor_tensor(ph[:, :], ph[:, :], mm[:, :], ALU.subtract), reads=[key + 'ph', key + 'mm'], writes=[key + 'ph'])
            S.op('dve', lambda e: e.tensor_scalar(mm[:, :], ph[:, :], -0.5, None, ALU.is_lt), reads=[key + 'ph'], writes=[key + 'mm'])
            S.op('dve', lambda e: e.tensor_tensor(ph[:, :], ph[:, :], mm[:, :], ALU.add), reads=[key + 'ph', key + 'mm'], writes=[key + 'ph'])
            S.op('dve', lambda e: e.tensor_scalar(ph[:, :], ph[:, :], -0.4999, 0.4999, ALU.max, ALU.min), reads=[key + 'ph'], writes=[key + 'ph'])
            S.op('act', lambda e: e.activation(outt, ph[:, :], AF.Sin, scale=TWO_PI), reads=[key + 'ph'], writes=[key + 'out'])

    def phase_s5(self, l, zF, yC, s5A, s5R, s5B, s5C, ssmd):
        nc, S, L = self.nc, self.S, self.L
        with ExitStack() as st:
            rho = self.sb(st, "p4_rho", [128, 24], F32)
            BbR = self.sb(st, "p4_BbR", [128, 24, 128], BF16)
            BbI = self.sb(st, "p4_BbI", [128, 24, 128], BF16)
            CtR = self.sb(st, "p4_CtR", [128, 24, 128], BF16)
            CtI = self.sb(st, "p4_CtI", [128, 24, 128], BF16)
            CtRn = self.sb(st, "p4_CtRn", [128, 24, 128], BF16)
            dsk = self.sb(st, "p4_d", [128, 6], F32)
            xst = self.sb(st, "p4_xst", [128, 2, 24], F32)
            S.dma('sp', dsk[:], ssmd[:, l, :], writes=['p4_d'])
            S.op('dve', lambda e: e.memset(xst[:], 0.0), writes=[('p4_xst', p) for p in range(24)])
            S.dma('pool', CtR[:], s5C[l, 0], writes=['p4_CtR'])
            with ExitStack() as s2:
                N = 3072
                par = self.sb(s2, "p4s_par", [128, 3, N], F32)
                S.dma('sp', par[:], s5R[:, l, :, :], writes=['p4s_par'])
                ar, ai, ld = par[:, 0, :], par[:, 1, :], par[:, 2, :]
                dt = self.sb(s2, "p4s_dt", [128, N], F32)
                mag = self.sb(st if False else s2, "p4s_mag", [128, N], F32)
                th = self.sb(s2, "p4s_th", [128, N], F32)
                sn = self.sb(s2, "p4s_sn", [128, N], F32)
                cs = self.sb(s2, "p4s_cs", [128, N], F32)
                S.op('act', lambda e: e.activation(dt[:, :], ld, AF.Exp), reads=['p4s_par'], writes=['p4s_dt'])
                S.op('dve', lambda e: e.tensor_tensor(mag[:, :], ar, dt[:, :], ALU.mult), reads=['p4s_par', 'p4s_dt'], writes=['p4s_mag'])
                S.op('act', lambda e: e.activation(mag[:, :], mag[:, :], AF.Exp), reads=['p4s_mag'], writes=['p4s_mag'])
                S.op('dve', lambda e: e.tensor_tensor(th[:, :], ai, dt[:, :], ALU.mult), reads=['p4s_par', 'p4s_dt'], writes=['p4s_th'])
                for hf in range(2):
                    with ExitStack() as s3:
                        sl = slice(hf * 1536, (hf + 1) * 1536)
                        self.sincos(s3, 'p4s_th', th[:, sl], 1536, sn[:, sl], cs[:, sl], 'scB')
                        S.barrier()
                S.op('dve', lambda e: e.tensor_tensor(cs[:, :], cs[:, :], mag[:, :], ALU.mult), reads=['scBout', 'p4s_mag'], writes=['p4s_cs'])
                S.op('dve', lambda e: e.tensor_scalar(cs[:, :], cs[:, :], -1.0, None, ALU.add), reads=['p4s_cs'], writes=['p4s_cs'])
                S.op('dve', lambda e: e.tensor_tensor(sn[:, :], sn[:, :], mag[:, :], ALU.mult), reads=['scBout', 'p4s_mag'], writes=['p4s_sn'])
                S.op('dve', lambda e: e.tensor_tensor(dt[:, :], ar, ar, ALU.mult), reads=['p4s_par'], writes=['p4s_dt'])
                S.op('dve', lambda e: e.tensor_tensor(th[:, :], ai, ai, ALU.mult), reads=['p4s_par'], writes=['p4s_th'])
                S.op('dve', lambda e: e.tensor_tensor(dt[:, :], dt[:, :], th[:, :], ALU.add), reads=['p4s_dt', 'p4s_th'], writes=['p4s_dt'])
                S.op('dve', lambda e: e.reciprocal(dt[:, :], dt[:, :]), reads=['p4s_dt'], writes=['p4s_dt'])
                t1 = self.sb(s2, "p4s_t1", [128, N], F32)
                S.op('dve', lambda e: e.tensor_tensor(mag[:, :], cs[:, :], ar, ALU.mult), reads=['p4s_cs', 'p4s_par'], writes=['p4s_mag'])
                S.op('dve', lambda e: e.tensor_tensor(t1[:, :], sn[:, :], ai, ALU.mult), reads=['p4s_sn', 'p4s_par'], writes=['p4s_t1'])
                S.op('dve', lambda e: e.tensor_tensor(mag[:, :], mag[:, :], t1[:, :], ALU.add), reads=['p4s_mag', 'p4s_t1'], writes=['p4s_mag'])
                S.op('dve', lambda e: e.tensor_tensor(mag[:, :], mag[:, :], dt[:, :], ALU.mult), reads=['p4s_mag', 'p4s_dt'], writes=['p4s_mag'])
                S.op('dve', lambda e: e.tensor_tensor(th[:, :], sn[:, :], ar, ALU.mult), reads=['p4s_sn', 'p4s_par'], writes=['p4s_th'])
                S.op('dve', lambda e: e.tensor_tensor(t1[:, :], cs[:, :], ai, ALU.mult), reads=['p4s_cs', 'p4s_par'], writes=['p4s_t1'])
                S.op('dve', lambda e: e.tensor_tensor(th[:, :], th[:, :], t1[:, :], ALU.subtract), reads=['p4s_th', 'p4s_t1'], writes=['p4s_th'])
                S.op('dve', lambda e: e.tensor_tensor(th[:, :], th[:, :], dt[:, :], ALU.mult), reads=['p4s_th', 'p4s_dt'], writes=['p4s_th'])
                zr, zi = mag, th
                bre = self.sb(s2, "p4s_bre", [128, N], F32)
                bim = self.sb(s2, "p4s_bim", [128, N], F32)
                S.dma('sp', bre[:, :], s5B[l, 0].rearrange("p a b -> p (a b)"), writes=['p4s_bre'])
                S.dma('sp', bim[:, :], s5B[l, 1].rearrange("p a b -> p (a b)"), writes=['p4s_bim'])
                S.op('dve', lambda e: e.tensor_tensor(t1[:, :], zr[:, :], bre[:, :], ALU.mult), reads=['p4s_mag', 'p4s_bre'], writes=['p4s_t1'])
                S.op('dve', lambda e: e.tensor_tensor(cs[:, :], zi[:, :], bim[:, :], ALU.mult), reads=['p4s_th', 'p4s_bim'], writes=['p4s_cs'])
                S.op('dve', lambda e: e.tensor_tensor(BbR[:, :, :].rearrange("p a b -> p (a b)"), t1[:, :], cs[:, :], ALU.subtract), reads=['p4s_t1', 'p4s_cs'], writes=['p4_BbR'])
                S.op('dve', lambda e: e.tensor_tensor(t1[:, :], zr[:, :], bim[:, :], ALU.mult), reads=['p4s_mag', 'p4s_bim'], writes=['p4s_t1'])
                S.op('dve', lambda e: e.tensor_tensor(cs[:, :], zi[:, :], bre[:, :], ALU.mult), reads=['p4s_th', 'p4s_bre'], writes=['p4s_cs'])
                S.op('dve', lambda e: e.tensor_tensor(BbI[:, :, :].rearrange("p a b -> p (a b)"), t1[:, :], cs[:, :], ALU.add), reads=['p4s_t1', 'p4s_cs'], writes=['p4_BbI'])
                S.dma('sp', bre[:, :], s5C[l, 1].rearrange("p a b -> p (a b)"), reads=['p4s_bre'], writes=['p4s_bre'])
                S.op('act', lambda e: e.mul(CtI[:, :, :].rearrange("p a b -> p (a b)"), bre[:, :], -1.0), reads=['p4s_bre'], writes=['p4_CtI'])
                S.dma('sp', bim[:, :], s5C[l, 0].rearrange("p a b -> p (a b)"), reads=['p4s_bim'], writes=['p4s_bim'])
                S.op('act', lambda e: e.mul(CtRn[:, :, :].rearrange("p a b -> p (a b)"), bim[:, :], -1.0), reads=['p4s_bim'], writes=['p4_CtRn'])
                S.barrier()
            cosT = self.sb(st, "p4_cos", [128, 24, T], F32)
            sinT = self.sb(st, "p4_sin", [128, 24, T], F32)
            with ExitStack() as s2:
                pa = self.sb(s2, "p4a_par", [128, 3, 24], F32)
                S.dma('sp', pa[:], s5A[:, l, :, :], writes=['p4a_par'])
                dt = self.sb(s2, "p4a_dt", [128, 24], F32)
                th = self.sb(s2, "p4a_th", [128, 24], F32)
                sn = self.sb(s2, "p4a_sn", [128, 24], F32)
                cs = self.sb(s2, "p4a_cs", [128, 24], F32)
                S.op('act', lambda e: e.activation(dt[:, :], pa[:, 2, :], AF.Exp), reads=['p4a_par'], writes=['p4a_dt'])
                S.op('dve', lambda e: e.tensor_tensor(rho[:, :], pa[:, 0, :], dt[:, :], ALU.mult), reads=['p4a_par', 'p4a_dt'], writes=['p4_rho'])
                S.op('act', lambda e: e.activation(rho[:, :], rho[:, :], AF.Exp), reads=['p4_rho'], writes=['p4_rho'])
                S.op('dve', lambda e: e.tensor_tensor(th[:, :], pa[:, 1, :], dt[:, :], ALU.mult), reads=['p4a_par', 'p4a_dt'], writes=['p4a_th'])
                self.sincos(s2, 'p4a_th', th[:, :], 24, sn[:, :], cs[:, :], 'scA')
                tmp = self.sb(s2, "p4a_tmp", [128, T // 2], F32)
                for p in range(24):
                    S.op('act', lambda e: e.copy(cosT[:, p, 0:1], cs[:, p:p + 1]), reads=['scAout'], writes=[('p4_cos', p)])
                    S.op('act', lambda e: e.copy(sinT[:, p, 0:1], sn[:, p:p + 1]), reads=['scAout'], writes=[('p4_sin', p)])
                    w = 1
                    while w < T:
                        cw, sw = cosT[:, p, w - 1:w], sinT[:, p, w - 1:w]
                        S.op('dve', lambda e: e.tensor_scalar(tmp[:, 0:w], sinT[:, p, 0:w], sw, None, ALU.mult), reads=[('p4_sin', p)], writes=['p4a_tmp'])
                        S.op('dve', lambda e: e.scalar_tensor_tensor(cosT[:, p, w:2 * w], cosT[:, p, 0:w], cw, tmp[:, 0:w], ALU.mult, ALU.subtract),
                             reads=[('p4_cos', p), 'p4a_tmp'], writes=[('p4_cos', p)])
                        S.op('dve', lambda e: e.tensor_scalar(tmp[:, 0:w], cosT[:, p, 0:w], sw, None, ALU.mult), reads=[('p4_cos', p)], writes=['p4a_tmp'])
                        S.op('dve', lambda e: e.scalar_tensor_tensor(sinT[:, p, w:2 * w], sinT[:, p, 0:w], cw, tmp[:, 0:w], ALU.mult, ALU.add),
                             reads=[('p4_sin', p), ('p4_cos', p), 'p4a_tmp'], writes=[('p4_sin', p)])
                        w *= 2
                S.barrier()
            wg = self.sb(st, "p4_wg", [128, 6, 1536], BF16)
            S.dma('sp', wg[:], self.wB["ssm_glu"][l].rearrange("(k p) n -> p k n", p=128), reads=[('ssm_gluB', l)], writes=['p4_wg'])
            psy = Ring(nc, st, "p4_psy", 2, [128, 512], F32, psum=True)
            usr = Ring(nc, st, "p4_us", 1, [128, 6, T], BF16)
            tr = Ring(nc, st, "p4_t", 10, [128, T], F32)
            stmp = self.sb(st, "p4_stmp", [128, 24, 2], F32)
            xr = Ring(nc, st, "p4_x", 4, [128, T], F32)
            ubr = Ring(nc, st, "p4_ub", 12, [128, T], BF16)
            yfr = Ring(nc, st, "p4_yf", 2, [128, T], F32)
            gy = self.sb(st, "p4_gy", [128, 6, T], BF16)
            sgr = Ring(nc, st, "p4_sg", 2, [128, T], F32)
            ycr = Ring(nc, st, "p4_yc", 2, [128, T], BF16)
            zFv = zF.rearrange("(c p) l -> p c l", p=128)
            yCv = yC.rearrange("(c p) l -> p c l", p=128)
            for ti in range(L // T):
                t0 = ti * T
                un, us = usr.next()
                S.dma('sp', us[:], zFv[:, 12:18, t0:t0 + T], reads=['zF'], writes=[un])
                PP = {}

                def stageA0(p):
                    ch = p // 4
                    prn, pr = self.psum.next()
                    S.op('pe', lambda e: e.matmul(pr[:, :], lhsT=BbR[:, p, :], rhs=us[:, ch, :], start=True, stop=True), reads=['p4_BbR', un], writes=[prn])
                    pin, pi = self.psum.next()
                    S.op('pe', lambda e: e.matmul(pi[:, :], lhsT=BbI[:, p, :], rhs=us[:, ch, :], start=True, stop=True), reads=['p4_BbI', un], writes=[pin])
                    PP[p] = dict(pr=(prn, pr), pi=(pin, pi))

                def stageA(p):
                    c_, s_ = cosT[:, p, :], sinT[:, p, :]
                    prn, pr = PP[p]['pr']; pin, pi = PP[p]['pi']
                    t1n, t1 = tr.next(); t2n, t2 = tr.next(); t3n, t3 = tr.next(); t4n, t4 = tr.next()
                    S.op('dve', lambda e: e.tensor_tensor(t1[:, :], pr[:, :], c_, ALU.mult), reads=[prn, ('p4_cos', p)], writes=[t1n])
                    S.op('dve', lambda e: e.tensor_tensor(t2[:, :], pi[:, :], s_, ALU.mult), reads=[pin, ('p4_sin', p)], writes=[t2n])
                    S.op('dve', lambda e: e.tensor_tensor(t3[:, :], pi[:, :], c_, ALU.mult), reads=[pin, ('p4_cos', p)], writes=[t3n])
                    S.op('dve', lambda e: e.tensor_tensor(t4[:, :], pr[:, :], s_, ALU.mult), reads=[prn, ('p4_sin', p)], writes=[t4n])
                    S.op('pool', lambda e: e.tensor_tensor(t1[:, :], t1[:, :], t2[:, :], ALU.add), reads=[t1n, t2n], writes=[t1n])
                    S.op('pool', lambda e: e.tensor_tensor(t3[:, :], t3[:, :], t4[:, :], ALU.subtract), reads=[t3n, t4n], writes=[t3n])
                    PP[p].update(t1=(t1n, t1), t3=(t3n, t3))

                def stageC(p):
                    c_, s_ = cosT[:, p, :], sinT[:, p, :]
                    t1n, t1 = PP[p]['t1']; t3n, t3 = PP[p]['t3']
                    rb = rho[:, p:p + 1].to_broadcast([128, T])
                    xrn, xre = xr.next(); xin, xim = xr.next()
                    S.op('dve', lambda e: e.tensor_tensor_scan(xre[:, :], rb, t1[:, :], xst[:, 0, p:p + 1], ALU.mult, ALU.add),
                         reads=['p4_rho', t1n, ('p4_xst', p)], writes=[xrn])
                    S.op('dve', lambda e: e.tensor_tensor_scan(xim[:, :], rb, t3[:, :], xst[:, 1, p:p + 1], ALU.mult, ALU.add),
                         reads=['p4_rho', t3n, ('p4_xst', p)], writes=[xin])
                    cl, sl = cosT[:, p, T - 1:T], sinT[:, p, T - 1:T]
                    S.op('act', lambda e: e.activation(stmp[:, p, 0:1], xim[:, T - 1:T], AF.Copy, scale=sl), reads=[xin, ('p4_sin', p)], writes=[('p4_stmp', p)])
                    S.op('act', lambda e: e.activation(stmp[:, p, 1:2], xim[:, T - 1:T], AF.Copy, scale=cl), reads=[xin, ('p4_cos', p)], writes=[('p4_stmp', p)])
                    u1n, u1 = ubr.next(); u2n, u2 = ubr.next(); u3n, u3 = ubr.next(); u4n, u4 = ubr.next()
                    S.op('dve', lambda e: e.tensor_tensor(u1[:, :], xre[:, :], c_, ALU.mult), reads=[xrn, ('p4_cos', p)], writes=[u1n])
                    S.op('dve', lambda e: e.tensor_tensor(u2[:, :], xim[:, :], s_, ALU.mult), reads=[xin, ('p4_sin', p)], writes=[u2n])
                    S.op('dve', lambda e: e.tensor_tensor(u3[:, :], xre[:, :], s_, ALU.mult), reads=[xrn, ('p4_sin', p)], writes=[u3n])
                    S.op('dve', lambda e: e.tensor_tensor(u4[:, :], xim[:, :], c_, ALU.mult), reads=[xin, ('p4_cos', p)], writes=[u4n])
                    S.op('dve', lambda e: e.scalar_tensor_tensor(xst[:, 0, p:p + 1], xre[:, T - 1:T], cl, stmp[:, p, 0:1], ALU.mult, ALU.subtract),
                         reads=[xrn, ('p4_stmp', p), ('p4_cos', p)], writes=[('p4_xst', p)])
                    S.op('dve', lambda e: e.scalar_tensor_tensor(xst[:, 1, p:p + 1], xre[:, T - 1:T], sl, stmp[:, p, 1:2], ALU.mult, ALU.add),
                         reads=[xrn, ('p4_stmp', p), ('p4_sin', p)], writes=[('p4_xst', p)])
                    PP[p].update(u1=(u1n, u1), u2=(u2n, u2), u3=(u3n, u3), u4=(u4n, u4))

                def stageE(p):
                    u1n, u1 = PP[p]['u1']; u2n, u2 = PP[p]['u2']; u3n, u3 = PP[p]['u3']; u4n, u4 = PP[p]['u4']
                    if p % 4 == 0:
                        PP['py'] = psy.next()
                    pyn, py = PP['py']
                    S.op('pe', lambda e: e.matmul(py[:, :], lhsT=CtR[:, p, :], rhs=u1[:, :], start=(p % 4 == 0), stop=False), reads=['p4_CtR', u1n], writes=[pyn], inc=False)
                    S.op('pe', lambda e: e.matmul(py[:, :], lhsT=CtRn[:, p, :], rhs=u2[:, :], start=False, stop=False), reads=['p4_CtRn', u2n], writes=[pyn], inc=False)
                    S.op('pe', lambda e: e.matmul(py[:, :], lhsT=CtI[:, p, :], rhs=u3[:, :], start=False, stop=False), reads=['p4_CtI', u3n], writes=[pyn], inc=False)
                    S.op('pe', lambda e: e.matmul(py[:, :], lhsT=CtI[:, p, :], rhs=u4[:, :], start=False, stop=(p % 4 == 3)), reads=['p4_CtI', u4n], writes=[pyn])
                    if p % 4 == 3:
                        oc = p // 4
                        yfn, yf = yfr.next()
                        S.op('dve', lambda e: e.scalar_tensor_tensor(yf[:, :], us[:, oc, :], dsk[:, oc:oc + 1], py[:, :], ALU.mult, ALU.add),
                             reads=[un, 'p4_d', pyn], writes=[yfn])
                        S.op('act', lambda e: e.activation(gy[:, oc, :], yf[:, :], AF.Gelu_apprx_tanh), reads=[yfn], writes=[('p4_gy', oc)])
                    del PP[p]

                stageA0(0)
                for it in range(24 + 2):
                    if it + 1 < 24:
                        stageA0(it + 1)
                    lists = []
                    for (fn_, arg, ok) in ((stageA, it, it < 24), (stageC, it - 1, 1 <= it <= 24), (stageE, it - 2, it >= 2)):
                        if ok:
                            S.defer = []
                            fn_(arg)
                            lists.append(S.defer)
                            S.defer = None
                    while any(lists):
                        for lst in lists:
                            if lst:
                                S.run_deferred(lst.pop(0))
                for j in range(6):
                    pan, pa_ = self.psum.next()
                    for k in range(6):
                        S.op('pe', lambda e: e.matmul(pa_[:, :], lhsT=wg[:, k, j * 128:(j + 1) * 128], rhs=gy[:, k, :], start=(k == 0), stop=(k == 5)),
                             reads=['p4_wg', ('p4_gy', k)], writes=[pan], inc=(k == 5))
                    pbn, pb_ = self.psum.next()
                    for k in range(6):
                        S.op('pe', lambda e: e.matmul(pb_[:, :], lhsT=wg[:, k, 768 + j * 128:768 + (j + 1) * 128], rhs=gy[:, k, :], start=(k == 0), stop=(k == 5)),
                             reads=['p4_wg', ('p4_gy', k)], writes=[pbn], inc=(k == 5))
                    sgn, sg = sgr.next()
                    S.op('act', lambda e: e.activation(sg[:, :], pb_[:, :], AF.Sigmoid), reads=[pbn], writes=[sgn])
                    ycn, yc = ycr.next()
                    S.op('dve', lambda e: e.tensor_tensor(yc[:, :], pa_[:, :], sg[:, :], ALU.mult), reads=[pan, sgn], writes=[ycn])
                    S.dma('sp', yCv[:, j, t0:t0 + T], yc[:, :], reads=[ycn], writes=['yC'])

    def phase_xattn(self, l, mem, zF, yX):
        nc, S, L = self.nc, self.S, self.L
        with ExitStack() as st:
            kT = self.sb(st, "p5_kT", [128, 4, 2, 256], BF16)
            vm = self.sb(st, "p5_vm", [128, 2, 768], BF16)
            with ExitStack() as st2:
                memt = self.sb(st2, "p5_mem", [128, 2, D], F32)
                memT = self.sb(st2, "p5_memT", [128, 8, 256], F32)
                sq = self.sb(st2, "p5_sq", [128, 8, 256], BF16)
                rstd = self.sb(st2, "p5_rstd", [128, 256], F32)
                mn = self.sb(st2, "p5_mn", [128, 8, 256], BF16)
                wkv = self.sb(st2, "p5_wkv", [128, 8, 1536], BF16)
                S.dma('sp', memt[:], mem.rearrange("(b p) d -> p b d", p=128), writes=['p5_mem'])
                S.dma('sp', wkv[:], self.wB["mem_wkv"][l].rearrange("(k p) n -> p k n", p=128), reads=[('mem_wkvB', l)], writes=['p5_wkv'])
                for b in range(2):
                    for half in range(2):
                        pn, ps = self.psum.next()
                        for j in range(4):
                            c = half * 4 + j
                            S.op('pe', lambda e: e.transpose(ps[:, j * 128:(j + 1) * 128], memt[:, b, c * 128:(c + 1) * 128], self.ident_f[:]),
                                 reads=['p5_mem', 'ident_f'], writes=[pn], inc=(j == 3))
                        S.op('dve', lambda e: e.tensor_copy(memT[:, half * 4:(half + 1) * 4, b * 128:(b + 1) * 128], ps[:, :].rearrange("p (a b) -> p a b", a=4)),
                             reads=[pn], writes=['p5_memT'])
                for c in range(8):
                    S.op('act', lambda e: e.activation(sq[:, c, :], memT[:, c, :], AF.Square), reads=['p5_memT'], writes=['p5_sq'])
                self.rms_stats('p5_sq', sq, 'p5_rstd', rstd, 256)
                for c in range(8):
                    S.op('dve', lambda e: e.scalar_tensor_tensor(mn[:, c, :], memT[:, c, :], self.gains_s[:, l, 2, c:c + 1], rstd[:, :], ALU.mult, ALU.mult),
                         reads=['p5_memT', 'p5_rstd', 'gains_s'], writes=['p5_mn'])
                for h in range(4):
                    for part, (off, M) in enumerate(((0, 128), (128, 64))):
                        col = 192 * h + off
                        pn, ps = self.psum.next()
                        for k in range(8):
                            S.op('pe', lambda e: e.matmul(ps[:M, :256], lhsT=wkv[:, k, col:col + M], rhs=mn[:, k, :], start=(k == 0), stop=(k == 7)),
                                 reads=['p5_wkv', 'p5_mn'], writes=[pn], inc=(k == 7))
                        S.op('dve', lambda e: e.tensor_copy(kT[:M, h, part, :], ps[:M, :256]), reads=[pn], writes=['p5_kT'])
                for mc in range(2):
                    for (n0, nn) in ((0, 512), (512, 256)):
                        pn, ps = self.psum.next()
                        for k in range(8):
                            S.op('pe', lambda e: e.matmul(ps[:, :nn], lhsT=mn[:, k, mc * 128:(mc + 1) * 128], rhs=wkv[:, k, 768 + n0:768 + n0 + nn], start=(k == 0), stop=(k == 7)),
                                 reads=['p5_wkv', 'p5_mn'], writes=[pn], inc=(k == 7))
                        S.op('dve', lambda e: e.tensor_copy(vm[:, mc, n0:n0 + nn], ps[:, :nn]), reads=[pn], writes=['p5_vm'])
                S.barrier()
            xq0r = Ring(nc, st, "p5_xq0", 2, [128, T], BF16)
            xq1r = Ring(nc, st, "p5_xq1", 2, [128, T], BF16)
            ptr = Ring(nc, st, "p5_pt", 4, [128, T], BF16)
            rcr = Ring(nc, st, "p5_rc", 2, [128, T], F32)
            y0r = Ring(nc, st, "p5_y0", 2, [128, T], BF16)
            y1r = Ring(nc, st, "p5_y1", 2, [128, T], BF16)
            XQ0 = 18 * 128
            sc = 192.0 ** -0.5
            for ti in range(L // T):
                t0 = ti * T
                for h in range(4):
                    q0n, q0 = xq0r.next()
                    q1n, q1 = xq1r.next()
                    r0 = XQ0 + 192 * h
                    S.dma('sp', q0[:, :], zF[r0:r0 + 128, t0:t0 + T], reads=['zF'], writes=[q0n])
                    S.dma('sp', q1[0:64, :], zF[r0 + 128:r0 + 192, t0:t0 + T], reads=['zF'], writes=[q1n])
                    pts = []
                    for mc in range(2):
                        pn, ps = self.psum.next()
                        S.op('pe', lambda e: e.matmul(ps[:, :], lhsT=kT[:, h, 0, mc * 128:(mc + 1) * 128], rhs=q0[:, :], start=True, stop=False),
                             reads=['p5_kT', q0n], writes=[pn], inc=False)
                        S.op('pe', lambda e: e.matmul(ps[:, :], lhsT=kT[0:64, h, 1, mc * 128:(mc + 1) * 128], rhs=q1[0:64, :], start=False, stop=True),
                             reads=['p5_kT', q1n], writes=[pn])
                        ptn, pt = ptr.next()
                        S.op('act', lambda e: e.activation(pt[:, :], ps[:, :], AF.Exp, scale=sc), reads=[pn], writes=[ptn])
                        pts.append((ptn, pt))
                    pn, ps = self.psum.next()
                    for mc in range(2):
                        S.op('pe', lambda e: e.matmul(ps[:, :], lhsT=self.ones_b[:], rhs=pts[mc][1][:, :], start=(mc == 0), stop=(mc == 1)),
                             reads=['ones_b', pts[mc][0]], writes=[pn], inc=(mc == 1))
                    rcn, rc = rcr.next()
                    S.op('dve', lambda e: e.reciprocal(rc[:, :], ps[:, :]), reads=[pn], writes=[rcn])
                    for part, (off, M, yr_) in enumerate(((0, 128, y0r), (128, 64, y1r))):
                        pn, ps = self.psum.next()
                        for mc in range(2):
                            S.op('pe', lambda e: e.matmul(ps[:M, :], lhsT=vm[:, mc, 192 * h + off:192 * h + off + M], rhs=pts[mc][1][:, :], start=(mc == 0), stop=(mc == 1)),
                                 reads=['p5_vm', pts[mc][0]], writes=[pn], inc=(mc == 1))
                        yn, y = yr_.next()
                        S.op('dve', lambda e: e.tensor_tensor(y[:M, :], ps[:M, :], rc[:M, :], ALU.mult), reads=[pn, rcn], writes=[yn])
                        S.dma('sp', yX[192 * h + off:192 * h + off + M, t0:t0 + T], y[:M, :], reads=[yn], writes=['yX'])

    def phase_merge(self, l, xT, xmT, zF, yA, yC, yX, oG):
        nc, S, L = self.nc, self.S, self.L
        ph = self.phases
        branches = []
        if 'p2' in ph:
            branches.append((0, 'proj_a', 6))
        if 'p3' in ph:
            branches.append((1, 'proj_b', 2))
        if 'p4' in ph:
            branches.append((2, 'proj_c', 6))
        if 'p5' in ph:
            branches.append((3, 'proj_x', 6))
        with ExitStack() as st:
            wp = {}
            for (b, nm, nk) in branches:
                wp[b] = self.sb(st, "p6_" + nm, [128, nk, D], BF16)
                S.dma('sp', wp[b][:], self.wB[nm][l].rearrange("(k p) n -> p k n", p=128), reads=[(nm + 'B', l)], writes=['p6_w%d' % b])
            wo = self.sb(st, "p6_wo", [128, 8, D], BF16)
            S.dma('sp', wo[:], self.wB["w_out"][l].rearrange("(k p) n -> p k n", p=128), reads=[('w_outB', l)], writes=['p6_wo'])
            yr = {b: Ring(nc, st, "p6_y%d" % b, 1, [128, nk, T], BF16) for (b, nm, nk) in branches}
            sgr = Ring(nc, st, "p6_sg", 3, [128, 4, T], BF16)
            xr = Ring(nc, st, "p6_x", 1, [128, 8, T], F32)
            xor_ = Ring(nc, st, "p6_xo", 1, [128, 8, T], F32)
            macc = Ring(nc, st, "p6_macc", 2, [128, T], F32)
            mtmp = Ring(nc, st, "p6_mtmp", 4, [128, T], F32)
            mb = self.sb(st, "p6_mb", [128, 8, T], BF16)
            sq = self.sb(st, "p6_sq", [128, 8, T], BF16)
            rstd = self.sb(st, "p6_rstd", [128, T], F32)
            y = self.sb(st, "p6_yy", [128, 8, T], F32)
            og = Ring(nc, st, "p6_og", 2, [128, 3, 260], F32)
            ybt = Ring(nc, st, "p6_ybt", 2, [128, 256], F32)
            rl = Ring(nc, st, "p6_rl", 2, [128, 4], F32)
            srcs = {0: yA, 2: yC, 3: yX}
            for ti in range(L // T):
                t0 = ti * T
                ys = {}
                for (b, nm, nk) in branches:
                    yn, yt = yr[b].next()
                    ys[b] = (yn, yt)
                    if b != 1:
                        S.dma('sp', yt[:], srcs[b].rearrange("(c p) l -> p c l", p=128)[:, :, t0:t0 + T], reads=[srcs[b].tensor.name], writes=[yn])
                    else:
                        for tb in range(4):
                            on, o = og.next()
                            S.dma('sp', o[:], oG[:, t0 + tb * 128:t0 + (tb + 1) * 128, :].rearrange("g p n -> p g n"), reads=['oG'], writes=[on])
                            S.op('dve', lambda e: e.tensor_tensor(o[:, 0, :], o[:, 0, :], o[:, 1, :], ALU.add), reads=[on], writes=[on])
                            S.op('dve', lambda e: e.tensor_tensor(o[:, 0, :], o[:, 0, :], o[:, 2, :], ALU.add), reads=[on], writes=[on])
                            rn, r = rl.next()
                            ov = o[:, 0, :].rearrange("p (h e) -> p h e", e=65)
                            S.op('dve', lambda e: e.reciprocal(r[:, :], ov[:, :, 64]), reads=[on], writes=[rn])
                            bn, bt = ybt.next()
                            for hh in range(4):
                                S.op('dve', lambda e: e.tensor_scalar(bt[:, hh * 64:(hh + 1) * 64], ov[:, hh, 0:64], r[:, hh:hh + 1], None, ALU.mult),
                                     reads=[on, rn], writes=[bn])
                            pn, ps = self.psum.next()
                            for half in range(2):
                                S.op('pe', lambda e: e.transpose(ps[:, half * 128:(half + 1) * 128], bt[:, half * 128:(half + 1) * 128], self.ident_f[:]),
                                     reads=[bn, 'ident_f'], writes=[pn], inc=(half == 1))
                            S.op('act', lambda e: e.copy(yt[:, :, tb * 128:(tb + 1) * 128], ps[:, 0:256].rearrange("p (a b) -> p a b", a=2)),
                                 reads=[pn], writes=[yn])
                xn, xt = xr.next()
                S.dma('sp', xt[:], xT.rearrange("(c p) l -> p c l", p=128)[:, :, t0:t0 + T], reads=[('xT', ti)], writes=[xn])
                def mchunk(c):
                    mn_, m = macc.next()
                    sn, sg = sgr.next()
                    S.dma('sp', sg[:], zF.rearrange("(b c p) l -> p b c l", p=128, c=8)[:, 3:7, c, t0:t0 + T], reads=['zF'], writes=[sn])
                    for bi, (b, nm, nk) in enumerate(branches):
                        yn, yt = ys[b]
                        pn, ps = self.psum.next()
                        for k in range(nk):
                            S.op('pe', lambda e, b=b, k=k, ps=ps, yt=yt, nk=nk: e.matmul(ps[:, :], lhsT=wp[b][:, k, c * 128:(c + 1) * 128], rhs=yt[:, k, :], start=(k == 0), stop=(k == nk - 1)),
                                 reads=['p6_w%d' % b, yn], writes=[pn], inc=(k == nk - 1))
                        if bi == 0:
                            S.op('dve', lambda e, b=b, ps=ps: e.tensor_tensor(m[:, :], ps[:, :], sg[:, b, :], ALU.mult), reads=[pn, sn], writes=[mn_])
                        else:
                            tn, tm = mtmp.next()
                            S.op('dve', lambda e, b=b, ps=ps, tm=tm: e.tensor_tensor(tm[:, :], ps[:, :], sg[:, b, :], ALU.mult), reads=[pn, sn], writes=[tn])
                            S.op('dve', lambda e, tm=tm: e.tensor_tensor(m[:, :], m[:, :], tm[:, :], ALU.add), reads=[tn, mn_], writes=[mn_])
                    S.op('act', lambda e: e.copy(mb[:, c, :], m[:, :]), reads=[mn_], writes=[('p6_mb', c)])
                for c in range(0, 8, 2):
                    S.interleave([lambda c=c: mchunk(c), lambda c=c: mchunk(c + 1)])
                for c in range(8):
                    pn, ps = self.psum.next()
                    for k in range(8):
                        S.op('pe', lambda e: e.matmul(ps[:, :], lhsT=wo[:, k, c * 128:(c + 1) * 128], rhs=mb[:, k, :], start=(k == 0), stop=(k == 7)),
                             reads=['p6_wo', ('p6_mb', k)], writes=[pn], inc=(k == 7))
                    S.op('act', lambda e: e.activation(sq[:, c, :], ps[:, :], AF.Square), reads=[pn], writes=['p6_sq'])
                    S.op('act', lambda e: e.copy(y[:, c, :], ps[:, :]), reads=[pn], writes=['p6_yy'])
                on_, xo_t = xor_.next()
                self.postnorm_residual(l, 1, 'p6_yy', y, 'p6_sq', sq, 'p6_rstd', rstd, xn, xt, on_, xo_t)
                S.dma('pool', xmT.rearrange("(c p) l -> p c l", p=128)[:, :, t0:t0 + T], xo_t[:], reads=[on_], writes=[('xmT', ti)])

    def postnorm_residual(self, l, which, y_name, y, sq_name, sq, rstd_name, rstd, xres_name, xres, xo_name, xo):
        S = self.S
        self.rms_stats(sq_name, sq, rstd_name, rstd, T)
        for c in range(8):
            S.op('dve', lambda e: e.scalar_tensor_tensor(y[:, c, :], y[:, c, :], self.gains_s[:, l, which, c:c + 1], rstd[:, :],
                                                         ALU.mult, ALU.mult),
                 reads=[y_name, rstd_name, 'gains_s'], writes=[y_name])
            S.op('dve', lambda e: e.tensor_tensor(xo[:, c, :], y[:, c, :], xres[:, c, :], ALU.add),
                 reads=[y_name, xres_name], writes=[xo_name])

    def phase_ffn(self, l, xsrc, xdst, wUpB, wDnB, ffnp):
        nc, S, L = self.nc, self.S, self.L
        NT = 2
        with ExitStack() as st:
            xr = Ring(nc, st, "p7_x", 2, [128, 8, T], F32)
            sq = self.sb(st, "p7_sq", [128, 8, T], BF16)
            rstd = self.sb(st, "p7_rstd", [128, T], F32)
            hr = Ring(nc, st, "p7_h", 2, [128, 8, T], BF16)
            wr = Ring(nc, st, "p7_w", 2, [128, 8, 1024], BF16)
            actr = Ring(nc, st, "p7_act", 2, [128, 24, T], BF16)
            gsb = Ring(nc, st, "p7_g", 3, [128, T + 2], F32)
            gtmp = Ring(nc, st, "p7_gt", 3, [128, T], F32)
            tails = self.sb(st, "p7_tail", [128, 24, 2], F32)
            fp = self.sb(st, "p7_fp", [128, 4, 24], F32)
            self.ys = [self.sb(st, "p7_y%d" % i, [128, 8, T], F32) for i in range(2)]
            self.sqs = [self.sb(st, "p7_sqo%d" % i, [128, 8, T], BF16) for i in range(2)]
            S.dma('sp', fp[:], ffnp[:, l, :, :], writes=['p7_fp'])
            S.op('dve', lambda e: e.memset(tails[:], 0.0), writes=[('p7_tail', c) for c in range(24)])
            wup = wUpB[l].rearrange("(k p) n -> p k n", p=128)
            wdn = wDnB[l].rearrange("(k p) n -> p k n", p=128)
            wq = 0
            for tp in range(L // (NT * T)):
                tiles = []
                for ti in range(tp * NT, (tp + 1) * NT):
                    t0 = ti * T
                    xn, xt = xr.next()
                    S.dma('sp', xt[:], xsrc.rearrange("(c p) l -> p c l", p=128)[:, :, t0:t0 + T], reads=[(xsrc.tensor.name, ti)], writes=[xn])
                    hn, h = hr.next()
                    self.prenorm(l, 3, xn, xt, hn, h, "p7_sq", sq, "p7_rstd", rstd)
                    an, act = actr.next()
                    tiles.append((ti, t0, xn, xt, hn, h, an, act))
                for blk in range(6):
                    wn, w = wr.next()
                    wq += 1
                    q_ = 'sp' if wq % 2 == 0 else 'pool'
                    S.dma(q_, w[:, :, 0:512], wup[:, :, blk * 512:(blk + 1) * 512], reads=[('wUpB', l)], writes=[wn])
                    S.dma(q_, w[:, :, 512:1024], wup[:, :, DFF + blk * 512:DFF + (blk + 1) * 512], reads=[('wUpB', l)], writes=[wn])
                    for (ti, t0, xn, xt, hn, h, an, act) in tiles:
                        def fchunk(j, hn=hn, h=h, an=an, act=act, wn=wn, w=w):
                            ch = blk * 4 + j
                            pgn, pg = self.psum.next()
                            for k in range(8):
                                S.op('pe', lambda e, k=k: e.matmul(pg[:, :], lhsT=w[:, k, 512 + j * 128:512 + (j + 1) * 128], rhs=h[:, k, :], start=(k == 0), stop=(k == 7)),
                                     reads=[wn, hn], writes=[pgn], inc=(k == 7))
                            pvn, pv = self.psum.next()
                            for k in range(8):
                                S.op('pe', lambda e, k=k: e.matmul(pv[:, :], lhsT=w[:, k, j * 128:(j + 1) * 128], rhs=h[:, k, :], start=(k == 0), stop=(k == 7)),
                                     reads=[wn, hn], writes=[pvn], inc=(k == 7))
                            gn, g = gsb.next()
                            S.op('act', lambda e: e.copy(g[:, 2:T + 2], pg[:, :]), reads=[pgn], writes=[gn])
                            S.op('act', lambda e: e.copy(g[:, 0:2], tails[:, ch, :]), reads=[('p7_tail', ch)], writes=[gn])
                            S.op('act', lambda e: e.copy(tails[:, ch, :], g[:, T:T + 2]), reads=[gn], writes=[('p7_tail', ch)])
                            tn, tm = gtmp.next()
                            S.op('dve', lambda e: e.tensor_scalar(tm[:, :], g[:, 0:T], fp[:, 0, ch:ch + 1], fp[:, 3, ch:ch + 1], ALU.mult, ALU.add),
                                 reads=[gn, 'p7_fp'], writes=[tn])
                            S.op('dve', lambda e: e.scalar_tensor_tensor(tm[:, :], g[:, 1:T + 1], fp[:, 1, ch:ch + 1], tm[:, :], ALU.mult, ALU.add),
                                 reads=[gn, tn, 'p7_fp'], writes=[tn])
                            S.op('dve', lambda e: e.scalar_tensor_tensor(tm[:, :], g[:, 2:T + 2], fp[:, 2, ch:ch + 1], tm[:, :], ALU.mult, ALU.add),
                                 reads=[gn, tn, 'p7_fp'], writes=[tn])
                            S.op('act', lambda e: e.activation(tm[:, :], tm[:, :], AF.Gelu_apprx_tanh), reads=[tn], writes=[tn])
                            S.op('dve', lambda e: e.tensor_tensor(act[:, ch, :], pv[:, :], tm[:, :], ALU.mult), reads=[pvn, tn], writes=[(an, ch)])
                        for j in range(0, 4, 2):
                            S.interleave([lambda j=j: fchunk(j), lambda j=j: fchunk(j + 1)])
                for half in range(2):
                    banks = {}
                    for cpair in range(2):
                        for (ti, t0, xn, xt, hn, h, an, act) in tiles:
                            banks[ti] = [self.psum.next() for _ in range(2)]
                        for kb in range(3):
                            wn, w = wr.next()
                            wq += 1
                            q_ = 'sp' if wq % 2 == 0 else 'pool'
                            c00 = half * 512 + cpair * 256
                            S.dma(q_, w[:, :, 0:256], wdn[:, kb * 8:(kb + 1) * 8, c00:c00 + 256], reads=[('wDnB', l)], writes=[wn])
                            for (ti, t0, xn, xt, hn, h, an, act) in tiles:
                                for cc in range(2):
                                    pn, ps = banks[ti][cc]
                                    for k in range(8):
                                        kk = kb * 8 + k
                                        S.op('pe', lambda e: e.matmul(ps[:, :], lhsT=w[:, k, cc * 128:(cc + 1) * 128], rhs=act[:, kk, :], start=(kk == 0), stop=(kk == 23)),
                                             reads=[wn, (an, kk)], writes=[pn], inc=(k == 7))
                        for (ti, t0, xn, xt, hn, h, an, act) in tiles:
                            for cc in range(2):
                                c = half * 4 + cpair * 2 + cc
                                pn, ps = banks[ti][cc]
                                S.op('act', lambda e: e.activation(self.sqs[ti % 2][:, c, :], ps[:, :], AF.Square), reads=[pn], writes=[('p7_sq2', ti % 2)])
                                S.op('act', lambda e: e.copy(self.ys[ti % 2][:, c, :], ps[:, :]), reads=[pn], writes=[('p7_y2', ti % 2)])
                for (ti, t0, xn, xt, hn, h, an, act) in tiles:
                    on, xo_t = ('p7_y2', ti % 2), self.ys[ti % 2]
                    self.postnorm_residual(l, 4, ('p7_y2', ti % 2), self.ys[ti % 2], ('p7_sq2', ti % 2), self.sqs[ti % 2], 'p7_rstd', rstd, xn, xt, on, xo_t)
                    S.dma('pool', xdst.rearrange("(c p) l -> p c l", p=128)[:, :, t0:t0 + T], xo_t[:], reads=[on], writes=[(xdst.tensor.name, ti)])


def host_inputs(inputs, b, L):
    f = np.float32
    d = {}
    d["x"] = np.ascontiguousarray(inputs["x"][b, :L])
    d["mem"] = np.ascontiguousarray(inputs["mem"][b])
    d["ident"] = np.eye(128, dtype=f)
    gs = np.stack([inputs[k] for k in ("g_mix_pre", "g_mix_post", "g_mem", "g_mlp_pre", "g_mlp_post")], axis=1)
    d["gains"] = np.ascontiguousarray(gs.reshape(NL, 5, 8, 128).transpose(3, 0, 1, 2)).astype(f)
    d["w_in"] = inputs["w_in"]
    d["ffn_w_up"] = inputs["ffn_w_up"]
    d["ffn_w_down"] = inputs["ffn_w_down"]
    fp = np.concatenate([inputs["ffn_conv_w"], inputs["ffn_conv_b"][:, None, :]], axis=1)
    lp = np.concatenate([inputs["lru_conv_w"], inputs["lru_conv_b"][:, None], inputs["lru_ba"][:, None],
                         inputs["lru_bx"][:, None], inputs["lru_lambda"][:, None]], axis=1)
    d["lrup"] = np.ascontiguousarray(lp.reshape(NL, 8, 6, 128).transpose(3, 0, 1, 2)).astype(f)
    bd = np.zeros((NL, 2, 128, 6, 128), f)
    for wi, nm in enumerate(("lru_wa", "lru_wx")):
        w = inputs[nm]
        for c in range(6):
            bd[:, wi, 0:64, c, 0:64] = w[:, 2 * c]
            bd[:, wi, 64:128, c, 64:128] = w[:, 2 * c + 1]
    d["lru_bd"] = bd
    d["ssmd"] = np.ascontiguousarray(inputs["ssm_d"].reshape(NL, 6, 128).transpose(2, 0, 1)).astype(f)
    for nm in ("ssm_glu", "mem_wkv", "proj_a", "proj_b", "proj_c", "proj_x", "w_out"):
        d[nm] = inputs[nm]
    am = np.full((128, 2, 12, 256), -30000.0, f)
    kk = np.arange(128)[:, None]
    qq = np.arange(128)[None, :]
    for hd in range(12):
        dd = DILS[hd // 4]
        dp = (qq + 128 - kk).astype(f)
        dc = (qq - kk).astype(f)
        mp = np.where(kk >= qq, -ALIBI[hd] * dd * dp, -30000.0)
        mc = np.where(kk <= qq, -ALIBI[hd] * dd * dc, -30000.0)
        am[:, 0, hd, 0:128] = mp
        am[:, 0, hd, 128:256] = mc
        am[:, 1, hd, 128:256] = mc
    d["amask"] = am
    def lay_a(a):
        return a.reshape(NL, 24, 2, 64).transpose(2, 3, 0, 1).reshape(128, NL, 24)
    ld_full = np.repeat(inputs["ssm_log_dt"][:, :, None], 64, axis=2)
    d["s5A"] = np.ascontiguousarray(np.stack([lay_a(inputs["ssm_a_re"]), lay_a(inputs["ssm_a_im"]), lay_a(ld_full)], axis=2)).astype(f)
    def lay_r(a):
        return a.reshape(NL, 3072)
    rr = np.stack([lay_r(inputs["ssm_a_re"]), lay_r(inputs["ssm_a_im"]), lay_r(ld_full)], axis=1)
    d["s5R"] = np.ascontiguousarray(np.broadcast_to(rr[None], (128, NL, 3, 3072))).astype(f)
    sB = np.zeros((NL, 2, 128, 24, 128), f)
    sC = np.zeros((NL, 2, 128, 24, 128), f)
    for ri, (bn, cn) in enumerate((("ssm_b_re", "ssm_c_re"), ("ssm_b_im", "ssm_c_im"))):
        Bm = inputs[bn]
        Cm = inputs[cn]
        for p in range(24):
            for gl in range(2):
                r0 = 32 * (p % 4) + gl * 16
                sB[:, ri, r0:r0 + 16, p, gl * 64:(gl + 1) * 64] = Bm[:, 2 * p + gl].transpose(0, 2, 1)
                sC[:, ri, gl * 64:(gl + 1) * 64, p, r0:r0 + 16] = Cm[:, 2 * p + gl].transpose(0, 2, 1)
    d["s5B"] = sB
    d["s5C"] = sC
    d["ffnp"] = np.ascontiguousarray(fp.reshape(NL, 4, 24, 128).transpose(3, 0, 1, 2)).astype(f)
    return d


ALL_PHASES = ('prepass', 'prologue', 'p1', 'p2', 'p3', 'p4', 'p5', 'p6', 'p7', 'epilogue')


def kernel(**inputs):
    inputs = {k: np.asarray(v) for k, v in inputs.items()}
    L = inputs["x"].shape[1]
    kb = K(L, NL)
    nc = kb.build(ALL_PHASES)
    in_maps = []
    for b in range(2):
        hi = host_inputs(inputs, b, L)
        in_maps.append({k: hi[k] for k in kb.ins})
    res = run_bass_kernel_spmd(nc, in_maps, core_ids=[0, 1])
    return np.stack([res.results[b]["out"] for b in range(2)], axis=0)
```

```python
import math
from contextlib import ExitStack
import numpy as np
import concourse.bass as bass
import concourse.mybir as mybir
from concourse.bass_utils import run_bass_kernel_spmd

AF = mybir.ActivationFunctionType
ALU = mybir.AluOpType
F32 = mybir.dt.float32
BF16 = mybir.dt.bfloat16

D = 1024
NL = 2
SEQ = 16384
MEM = 256
IN_W = 9472
DFF = 3072
T = 512
EPS = 1e-6
ALIBI = [2.0 ** (-8.0 * (h + 1) / 12) for h in range(12)]
DILS = (1, 4, 16)
import os
P3STOP = int(os.environ.get("P3STOP", "0"))


class Sched:
    NSLOT = 8

    def __init__(self, nc, stack):
        self.nc = nc
        self.engs = {'pe': nc.tensor, 'act': nc.scalar, 'dve': nc.vector,
                     'pool': nc.gpsimd, 'sp': nc.sync}
        self.sem = {}
        self.cnt = {}
        for n in ['pe', 'act', 'dve', 'pool']:
            self.sem[n] = stack.enter_context(nc.semaphore("s_" + n))
            self.cnt[n] = 0
        self.dq = {}
        for q in ['sp', 'pool', 'act']:
            for i in range(self.NSLOT):
                self.sem[('dma', q, i)] = stack.enter_context(nc.semaphore("d_%s%d" % (q, i)))
            self.dq[q] = 0
        self.seen = {e: {} for e in self.engs}
        self.lastw = {}
        self.readers = {}

    def _deps(self, reads, writes):
        deps = []
        for r in reads:
            t = self.lastw.get(r)
            if t is not None:
                deps.append(t)
        for w in writes:
            t = self.lastw.get(w)
            if t is not None:
                deps.append(t)
            deps.extend(self.readers.get(w, ()))
        return deps

    def _wait(self, ename, deps):
        best = {}
        for (src, val) in deps:
            if best.get(src, 0) < val:
                best[src] = val
        seen = self.seen[ename]
        eng = self.engs[ename]
        for src, val in best.items():
            if src == 'pe' and ename == 'pe':
                continue
            if seen.get(src, 0) >= val:
                continue
            eng.wait_ge(self.sem[src], val)
            seen[src] = val

    def _record(self, ticket, reads, writes):
        for r in reads:
            self.readers.setdefault(r, []).append(ticket)
        for w in writes:
            self.lastw[w] = ticket
            self.readers[w] = []

    defer = None

    def op(self, ename, fn, reads=(), writes=(), inc=True):
        if self.defer is not None:
            self.defer.append(('__op__', (ename, fn, reads, writes, inc)))
            return None
        self._wait(ename, self._deps(reads, writes))
        ins = fn(self.engs[ename])
        if inc:
            self.cnt[ename] += 1
            ins.then_inc(self.sem[ename], 1)
            ticket = (ename, self.cnt[ename])
        else:
            ticket = (ename, self.cnt[ename] + 1)
        self._record(ticket, reads, writes)
        return ticket

    def dma(self, q, out, in_, reads=(), writes=(), **kw):
        if self.defer is not None:
            self.defer.append(('__dma__', (q, out, in_, reads, writes, kw)))
            return None
        i = self.dq[q]
        slot = i % self.NSLOT
        rnd = i // self.NSLOT
        src = ('dma', q, slot)
        deps = self._deps(reads, writes)
        if rnd > 0:
            deps.append((src, 16 * rnd))
        self._wait(q, deps)
        ins = self.engs[q].dma_start(out=out, in_=in_, **kw)
        ins.then_inc(self.sem[src], 16)
        self.dq[q] = i + 1
        ticket = (src, 16 * (rnd + 1))
        self._record(ticket, reads, writes)
        return ticket

    def coll(self, kind, src, dst, groups, reads=(), writes=()):
        q = 'pool'
        i = self.dq[q]
        slot = i % self.NSLOT
        rnd = i // self.NSLOT
        srck = ('dma', q, slot)
        deps = self._deps(reads, writes)
        if rnd > 0:
            deps.append((srck, 16 * rnd))
        self._wait(q, deps)
        ins = self.engs[q].collective_compute(kind, ALU.bypass, replica_groups=groups, ins=[src], outs=[dst])
        ins.then_inc(self.sem[srck], 16)
        self.dq[q] = i + 1
        ticket = (srck, 16 * (rnd + 1))
        self._record(ticket, reads, writes)
        return ticket

    def run_deferred(self, item):
        kind, a = item
        if kind == '__op__':
            self.op(*a)
        else:
            q, out, in_, reads, writes, kw = a
            self.dma(q, out, in_, reads=reads, writes=writes, **kw)

    def interleave(self, fns):
        lists = []
        for f in fns:
            self.defer = []
            f()
            lists.append(self.defer)
            self.defer = None
        while any(lists):
            for lst in lists:
                if lst:
                    self.run_deferred(lst.pop(0))

    def finish(self, ename='sp'):
        deps = list(self.lastw.values())
        for l in self.readers.values():
            deps.extend(l)
        self._wait(ename, deps)

    def barrier(self):
        for e in self.engs:
            self.finish(e)
        self.lastw = {}
        self.readers = {}


_UID = [0]


class Ring:
    def __init__(self, nc, stack, name, n, shape, dtype, psum=False):
        self.bufs = []
        _UID[0] += 1
        for i in range(n):
            nm = "%s_%d_%d" % (name, _UID[0], i)
            if psum:
                t = stack.enter_context(nc.psum_tensor(nm, shape, dtype))
            else:
                t = stack.enter_context(nc.sbuf_tensor(nm, shape, dtype))
            self.bufs.append((nm, t))
        self.i = 0

    def next(self):
        b = self.bufs[self.i % len(self.bufs)]
        self.i += 1
        return b


class K:
    def __init__(self, L, nl, dbg=False):
        self.L = L
        self.nl = nl
        self.dbg = dbg
        self.nc = bass.Bass("TRN2", target_bir_lowering=False)
        self.ins = {}
        self.scr = {}

    def inp(self, name, shape, dt=F32):
        t = self.nc.dram_tensor(name, list(shape), dt, kind="ExternalInput").ap()
        self.ins[name] = t
        return t

    def scratch(self, name, shape, dt):
        kind = "ExternalOutput" if self.dbg else "Internal"
        t = self.nc.dram_tensor(name, list(shape), dt, kind=kind).ap()
        self.scr[name] = t
        return t

    def sb(self, st, name, shape, dt):
        _UID[0] += 1
        return st.enter_context(self.nc.sbuf_tensor("%s_%d" % (name, _UID[0]), list(shape), dt))

    def build(self, phases):
        nc = self.nc
        L, nl = self.L, self.nl
        x = self.inp("x", [L, D])
        mem = self.inp("mem", [MEM, D])
        ident = self.inp("ident", [128, 128])
        gains = self.inp("gains", [128, NL, 5, 8])
        w_in = self.inp("w_in", [NL, D, IN_W])
        ffn_w_up = self.inp("ffn_w_up", [NL, D, 2 * DFF])
        ffn_w_down = self.inp("ffn_w_down", [NL, DFF, D])
        ffnp = self.inp("ffnp", [128, NL, 4, 24])
        lrup = self.inp("lrup", [128, NL, 8, 6])
        lru_bd = self.inp("lru_bd", [NL, 2, 128, 6, 128])
        ssmd = self.inp("ssmd", [128, NL, 6])
        wsrc = {}
        for nm, shp in (("ssm_glu", [NL, 768, 1536]), ("mem_wkv", [NL, D, 1536]), ("proj_a", [NL, 768, D]),
                        ("proj_b", [NL, 256, D]), ("proj_c", [NL, 768, D]), ("proj_x", [NL, 768, D]), ("w_out", [NL, D, D])):
            wsrc[nm] = (self.inp(nm, shp), self.scratch(nm + "B", shp, BF16), shp[1])
        self.wB = {nm: v[1] for nm, v in wsrc.items()}
        yA = self.scratch("yA", [768, L], BF16)
        yC = self.scratch("yC", [768, L], BF16)
        yX = self.scratch("yX", [768, L], BF16)
        oG = self.scratch("oG", [3, L, 260], F32)
        self.phases = phases
        amask = self.inp("amask", [128, 2, 12, 256])
        s5A = self.inp("s5A", [128, NL, 3, 24])
        s5R = self.inp("s5R", [128, NL, 3, 3072])
        s5B = self.inp("s5B", [NL, 2, 128, 24, 128])
        s5C = self.inp("s5C", [NL, 2, 128, 24, 128])
        out = self.nc.dram_tensor("out", [L, D], F32, kind="ExternalOutput").ap()
        self.out = out

        xT = self.scratch("xT", [D, L], F32)
        xmT = self.scratch("xmT", [D, L], F32)
        zF = self.scratch("zF", [56 * 128, L], BF16)
        zQKV = self.scratch("zQKV", [L, 2304], BF16)
        wInB = self.scratch("wInB", [NL, D, IN_W], BF16)
        wUpB = self.scratch("wUpB", [NL, D, 2 * DFF], BF16)
        wDnB = self.scratch("wDnB", [NL, DFF, D], BF16)

        with ExitStack() as st0:
            S = Sched(nc, st0)
            self.S = S
            ident_f = self.sb(st0, "ident_f", [128, 128], F32)
            ones_b = self.sb(st0, "ones_b", [128, 128], BF16)
            gains_s = self.sb(st0, "gains_s", [128, NL, 5, 8], F32)
            eps_c = self.sb(st0, "eps_c", [128, 1], F32)
            self.ident_f, self.ones_b, self.gains_s, self.eps_c = ident_f, ones_b, gains_s, eps_c
            one_c = self.sb(st0, "one_c", [128, 1], F32)
            self.one_c = one_c
            S.op('dve', lambda e: e.memset(one_c[:], 1.0), writes=['one_c'])
            S.dma('sp', ident_f[:], ident, writes=['ident_f'])
            S.dma('sp', gains_s[:], gains, writes=['gains_s'])
            S.op('dve', lambda e: e.memset(ones_b[:], 1.0), writes=['ones_b'])
            S.op('dve', lambda e: e.memset(eps_c[:], EPS), writes=['eps_c'])
            self.psum = Ring(nc, st0, "ps", 6, [128, 512], F32, psum=True)

            if 'prepass' in phases:
                for l in range(nl):
                    for (src, dst, rows) in [(w_in, wInB, D), (ffn_w_up, wUpB, D), (ffn_w_down, wDnB, DFF)] + list(wsrc.values()):
                        for r in range(0, rows, 128):
                            S.dma('pool', dst[l, r:r + 128, :], src[l, r:r + 128, :],
                                  writes=[(dst.tensor.name, l)])
            if 'prologue' in phases:
                self.transpose_in(x, xT, L)
                S.barrier()
            for l in range(nl):
                if 'p1' in phases:
                    self.phase_inproj(l, xT, wInB, zF, zQKV)
                    S.barrier()
                if 'p2' in phases:
                    self.phase_lru(l, zF, yA, lrup, lru_bd)
                    S.barrier()
                if 'p3' in phases:
                    self.phase_attn(l, zQKV, oG, amask)
                    S.barrier()
                if 'p4' in phases:
                    self.phase_s5(l, zF, yC, s5A, s5R, s5B, s5C, ssmd)
                    S.barrier()
                if 'p5' in phases:
                    self.phase_xattn(l, mem, zF, yX)
                    S.barrier()
                if 'p6' in phases:
                    self.phase_merge(l, xT, xmT, zF, yA, yC, yX, oG)
                    S.barrier()
                if 'p7' in phases:
                    self.phase_ffn(l, xmT if 'p6' in phases else xT, xT, wUpB, wDnB, ffnp)
                    S.barrier()
            if 'epilogue' in phases:
                self.transpose_out(xT, out, L)
            S.barrier()
        return nc

    def transpose_in(self, x, xT, L):
        nc, S = self.nc, self.S
        with ExitStack() as st:
            xin = Ring(nc, st, "ti_x", 2, [128, D], F32)
            xo = Ring(nc, st, "ti_o", 2, [128, 8, 128], F32)
            for b in range(L // 128):
                nm, xt = xin.next()
                S.dma('sp', xt[:], x[b * 128:(b + 1) * 128, :], writes=[nm])
                no, ot = xo.next()
                for half in range(2):
                    pn, ps = self.psum.next()
                    for j in range(4):
                        c = half * 4 + j
                        S.op('pe', lambda e: e.transpose(ps[:, j * 128:(j + 1) * 128], xt[:, c * 128:(c + 1) * 128], self.ident_f[:]),
                             reads=[nm, 'ident_f'], writes=[pn], inc=(j == 3))
                    eng = 'act' if half == 0 else 'dve'
                    if eng == 'act':
                        S.op('act', lambda e: e.copy(ot[:, half * 4:(half + 1) * 4, :], ps[:, :].rearrange("p (a b) -> p a b", a=4)),
                             reads=[pn], writes=[no])
                    else:
                        S.op('dve', lambda e: e.tensor_copy(ot[:, half * 4:(half + 1) * 4, :], ps[:, :].rearrange("p (a b) -> p a b", a=4)),
                             reads=[pn], writes=[no])
                S.dma('pool', xT.rearrange("(c p) l -> p c l", p=128)[:, :, b * 128:(b + 1) * 128], ot[:],
                      reads=[no], writes=[('xT', b // 4)])

    def transpose_out(self, xT, out, L):
        nc, S = self.nc, self.S
        with ExitStack() as st:
            xin = Ring(nc, st, "to_x", 2, [128, 8, 128], F32)
            xo = Ring(nc, st, "to_o", 2, [128, D], F32)
            for b in range(L // 128):
                nm, xt = xin.next()
                S.dma('sp', xt[:], xT.rearrange("(c p) l -> p c l", p=128)[:, :, b * 128:(b + 1) * 128],
                      reads=[('xT', b // 4)], writes=[nm])
                no, ot = xo.next()
                for half in range(2):
                    pn, ps = self.psum.next()
                    for j in range(4):
                        c = half * 4 + j
                        S.op('pe', lambda e: e.transpose(ps[:, j * 128:(j + 1) * 128], xt[:, c, :], self.ident_f[:]),
                             reads=[nm, 'ident_f'], writes=[pn], inc=(j == 3))
                    if half == 0:
                        S.op('act', lambda e: e.copy(ot[:, 0:512], ps[:, :]), reads=[pn], writes=[no])
                    else:
                        S.op('dve', lambda e: e.tensor_copy(ot[:, 512:1024], ps[:, :]), reads=[pn], writes=[no])
                S.dma('pool', out[b * 128:(b + 1) * 128, :], ot[:], reads=[no], writes=['out'])

    def rms_stats(self, sqname, sq, rstd_name, rstd, Tn):
        S = self.S
        pn, ps = self.psum.next()
        for c in range(8):
            S.op('pe', lambda e: e.matmul(ps[:, :Tn], lhsT=self.ones_b[:], rhs=sq[:, c, :], start=(c == 0), stop=(c == 7)),
                 reads=[sqname, 'ones_b'], writes=[pn], inc=(c == 7))
        S.op('act', lambda e: e.activation(rstd[:, :Tn], ps[:, :Tn], AF.Sqrt, bias=self.eps_c[:], scale=1.0 / D),
             reads=[pn, 'eps_c'], writes=[rstd_name])
        S.op('dve', lambda e: e.reciprocal(rstd[:, :Tn], rstd[:, :Tn]), reads=[rstd_name], writes=[rstd_name])

    def prenorm(self, l, which, xt_name, xt, h_name, h, sq_name, sq, rstd_name, rstd):
        S = self.S
        for c in range(8):
            S.op('act', lambda e: e.activation(sq[:, c, :], xt[:, c, :], AF.Square), reads=[xt_name], writes=[sq_name])
        self.rms_stats(sq_name, sq, rstd_name, rstd, T)
        for c in range(8):
            S.op('dve', lambda e: e.scalar_tensor_tensor(h[:, c, :], xt[:, c, :], self.gains_s[:, l, which, c:c + 1], rstd[:, :],
                                                         ALU.mult, ALU.mult),
                 reads=[xt_name, rstd_name, 'gains_s'], writes=[h_name])

    def phase_inproj(self, l, xT, wInB, zF, zQKV):
        nc, S, L = self.nc, self.S, self.L
        FM_COLS = list(range(0, 1536, 128)) + list(range(3840, IN_W, 128))
        NT = 4
        with ExitStack() as st:
            xr = Ring(nc, st, "p1_x", 2, [128, 8, T], F32)
            sq = self.sb(st, "p1_sq", [128, 8, T], BF16)
            rstd = self.sb(st, "p1_rstd", [128, T], F32)
            hr = Ring(nc, st, "p1_h", 5, [128, 8, T], BF16)
            wr = Ring(nc, st, "p1_w", 2, [128, 8, 1024], BF16)
            zo = Ring(nc, st, "p1_zo", 2, [128, 8, T], BF16)
            qo = Ring(nc, st, "p1_qo", 3, [128, 4, 1024], BF16)
            wv = wInB[l].rearrange("(k p) n -> p k n", p=128)
            zFv = zF.rearrange("(c p) l -> p c l", p=128)
            ev = 0
            wq = 0
            for tp in range(L // (NT * T)):
                hs = []
                for ti in range(tp * NT, (tp + 1) * NT):
                    t0 = ti * T
                    xn, xt = xr.next()
                    S.dma('sp', xt[:], xT.rearrange("(c p) l -> p c l", p=128)[:, :, t0:t0 + T], reads=[('xT', ti)], writes=[xn])
                    hn, h = hr.next()
                    self.prenorm(l, 0, xn, xt, hn, h, "p1_sq", sq, "p1_rstd", rstd)
                    hs.append((t0, hn, h))
                for blk in range(7):
                    wn, w = wr.next()
                    wq += 1
                    for j in range(8):
                        c0 = FM_COLS[blk * 8 + j]
                        if j == 0 or FM_COLS[blk * 8 + j - 1] + 128 != c0:
                            j2 = j
                            while j2 + 1 < 8 and FM_COLS[blk * 8 + j2 + 1] == FM_COLS[blk * 8 + j2] + 128:
                                j2 += 1
                            S.dma('sp' if wq % 2 == 0 else 'pool', w[:, :, j * 128:(j2 + 1) * 128], wv[:, :, c0:c0 + (j2 - j + 1) * 128],
                                  reads=[('wInB', l)], writes=[wn])
                    for (t0, hn, h) in hs:
                        zn, z = zo.next()
                        for j in range(8):
                            ch = blk * 8 + j
                            pn, ps = self.psum.next()
                            for k in range(8):
                                S.op('pe', lambda e: e.matmul(ps[:, :], lhsT=w[:, k, j * 128:(j + 1) * 128], rhs=h[:, k, :], start=(k == 0), stop=(k == 7)),
                                     reads=[wn, hn], writes=[pn], inc=(k == 7))
                            if 6 <= ch < 12:
                                S.op('act', lambda e: e.activation(z[:, j, :], ps[:, :], AF.Gelu_apprx_tanh), reads=[pn], writes=[zn])
                            elif ch >= 24:
                                S.op('act', lambda e: e.activation(z[:, j, :], ps[:, :], AF.Sigmoid), reads=[pn], writes=[zn])
                            else:
                                S.op('dve', lambda e: e.tensor_copy(z[:, j, :], ps[:, :]), reads=[pn], writes=[zn])
                        S.dma('pool', zFv[:, blk * 8:(blk + 1) * 8, t0:t0 + T], z[:], reads=[zn], writes=['zF'])
                for blk in range(3):
                    c0 = 1536 + blk * 1024
                    ncol = min(1024, 3840 - c0)
                    wn, w = wr.next()
                    wq += 1
                    S.dma('sp' if wq % 2 == 0 else 'pool', w[:, :, :ncol], wv[:, :, c0:c0 + ncol], reads=[('wInB', l)], writes=[wn])
                    for (t0, hn, h) in hs:
                        qn, q = qo.next()
                        for tb in range(4):
                            for n0 in range(0, ncol, 512):
                                nn = min(512, ncol - n0)
                                pn, ps = self.psum.next()
                                for k in range(8):
                                    S.op('pe', lambda e: e.matmul(ps[:, :nn], lhsT=h[:, k, tb * 128:(tb + 1) * 128], rhs=w[:, k, n0:n0 + nn], start=(k == 0), stop=(k == 7)),
                                         reads=[wn, hn], writes=[pn], inc=(k == 7))
                                dst = q[:, tb, n0:n0 + nn]
                                ev += 1
                                if ev % 2:
                                    S.op('dve', lambda e: e.tensor_copy(dst, ps[:, :nn]), reads=[pn], writes=[qn])
                                else:
                                    S.op('act', lambda e: e.copy(dst, ps[:, :nn]), reads=[pn], writes=[qn])
                        S.dma('pool', zQKV[t0:t0 + T, blk * 1024:blk * 1024 + ncol].rearrange("(tb p) n -> p tb n", p=128), q[:, :, :ncol], reads=[qn], writes=['zQKV'])

    def phase_lru(self, l, zF, yA, lrup, lru_bd):
        nc, S, L = self.nc, self.S, self.L
        with ExitStack() as st:
            lp = self.sb(st, "p2_lp", [128, 8, 6], F32)
            kap = self.sb(st, "p2_kap", [128, 2, 6], F32)
            bdA = self.sb(st, "p2_bdA", [128, 6, 128], BF16)
            bdX = self.sb(st, "p2_bdX", [128, 6, 128], BF16)
            state = self.sb(st, "p2_state", [128, 6], F32)
            xar = Ring(nc, st, "p2_xa", 3, [128, T + 3], BF16)
            ggr = Ring(nc, st, "p2_gg", 3, [128, T], BF16)
            xcr = Ring(nc, st, "p2_xc", 3, [128, T], F32)
            xcbr = Ring(nc, st, "p2_xcb", 3, [128, T], BF16)
            rr = Ring(nc, st, "p2_r", 3, [128, T], F32)
            ir = Ring(nc, st, "p2_i", 3, [128, T], F32)
            ar = Ring(nc, st, "p2_a", 3, [128, T], F32)
            a2r = Ring(nc, st, "p2_a2", 3, [128, T], F32)
            hr = Ring(nc, st, "p2_h", 3, [128, T], F32)
            yr = Ring(nc, st, "p2_y", 3, [128, T], BF16)
            S.dma('sp', lp[:], lrup[:, l, :, :], writes=['p2_lp'])
            S.dma('pool', bdA[:], lru_bd[l, 0], writes=['p2_bdA'])
            S.dma('pool', bdX[:], lru_bd[l, 1], writes=['p2_bdX'])
            S.op('dve', lambda e: e.memset(state[:], 0.0), writes=[('p2_state', c) for c in range(6)])
            S.op('act', lambda e: e.activation(kap[:, 0, :], lp[:, 7, :], AF.Exp, scale=-1.0), reads=['p2_lp'], writes=['p2_kap'])
            S.op('act', lambda e: e.activation(kap[:, 0, :], kap[:, 0, :], AF.Ln, bias=self.one_c[:]), reads=['p2_kap', 'one_c'], writes=['p2_kap'])
            S.op('dve', lambda e: e.tensor_scalar(kap[:, 1, :], kap[:, 0, :], -16.0, None, ALU.mult), reads=['p2_kap'], writes=['p2_kap'])
            S.op('dve', lambda e: e.tensor_scalar(kap[:, 0, :], kap[:, 0, :], -8.0, None, ALU.mult), reads=['p2_kap'], writes=['p2_kap'])
            zFv = zF.rearrange("(c p) l -> p c l", p=128)
            yAv = yA.rearrange("(c p) l -> p c l", p=128)
            for ti in range(L // T):
                t0 = ti * T
                def chunk(c):
                    xn, xa = xar.next()
                    if t0 == 0:
                        S.op('pool', lambda e: e.memset(xa[:, 0:3], 0.0), writes=[xn])
                        S.dma('sp', xa[:, 3:T + 3], zFv[:, c, 0:T], reads=['zF'], writes=[xn])
                    else:
                        S.dma('sp', xa[:, :], zFv[:, c, t0 - 3:t0 + T], reads=['zF'], writes=[xn])
                    gn, gg = ggr.next()
                    S.dma('sp', gg[:, :], zFv[:, 6 + c, t0:t0 + T], reads=['zF'], writes=[gn])
                    xcn, xc = xcr.next()
                    S.op('dve', lambda e: e.tensor_scalar(xc[:, :], xa[:, 0:T], lp[:, 0, c:c + 1], lp[:, 4, c:c + 1], ALU.mult, ALU.add),
                         reads=[xn, 'p2_lp'], writes=[xcn])
                    for j in range(1, 4):
                        S.op('dve', lambda e, j=j: e.scalar_tensor_tensor(xc[:, :], xa[:, j:j + T], lp[:, j, c:c + 1], xc[:, :], ALU.mult, ALU.add),
                             reads=[xn, xcn, 'p2_lp'], writes=[xcn])
                    xbn, xcb = xcbr.next()
                    S.op('dve', lambda e: e.tensor_copy(xcb[:, :], xc[:, :]), reads=[xcn], writes=[xbn])
                    prn, pr = self.psum.next()
                    S.op('pe', lambda e: e.matmul(pr[:, :], lhsT=bdA[:, c, :], rhs=xcb[:, :], start=True, stop=True), reads=['p2_bdA', xbn], writes=[prn])
                    pin, pi = self.psum.next()
                    S.op('pe', lambda e: e.matmul(pi[:, :], lhsT=bdX[:, c, :], rhs=xcb[:, :], start=True, stop=True), reads=['p2_bdX', xbn], writes=[pin])
                    rn, r = rr.next()
                    S.op('act', lambda e: e.activation(r[:, :], pr[:, :], AF.Sigmoid, bias=lp[:, 5, c:c + 1]), reads=[prn, 'p2_lp'], writes=[rn])
                    inn, iv = ir.next()
                    S.op('act', lambda e: e.activation(iv[:, :], pi[:, :], AF.Sigmoid, bias=lp[:, 6, c:c + 1]), reads=[pin, 'p2_lp'], writes=[inn])
                    an, a = ar.next()
                    S.op('act', lambda e: e.activation(a[:, :], r[:, :], AF.Exp, scale=kap[:, 0, c:c + 1]), reads=[rn, 'p2_kap'], writes=[an])
                    a2n, a2 = a2r.next()
                    S.op('act', lambda e: e.activation(a2[:, :], r[:, :], AF.Exp, scale=kap[:, 1, c:c + 1]), reads=[rn, 'p2_kap'], writes=[a2n])
                    S.op('dve', lambda e: e.tensor_scalar(a2[:, :], a2[:, :], -1.0, 1.0, ALU.mult, ALU.add), reads=[a2n], writes=[a2n])
                    S.op('act', lambda e: e.activation(a2[:, :], a2[:, :], AF.Sqrt), reads=[a2n], writes=[a2n])
                    S.op('dve', lambda e: e.tensor_tensor(iv[:, :], iv[:, :], a2[:, :], ALU.mult), reads=[inn, a2n], writes=[inn])
                    S.op('dve', lambda e: e.tensor_tensor(iv[:, :], iv[:, :], xc[:, :], ALU.mult), reads=[inn, xcn], writes=[inn])
                    hn, h = hr.next()
                    S.op('dve', lambda e: e.tensor_tensor_scan(h[:, :], a[:, :], iv[:, :], state[:, c:c + 1], ALU.mult, ALU.add),
                         reads=[an, inn, ('p2_state', c)], writes=[hn])
                    S.op('act', lambda e: e.copy(state[:, c:c + 1], h[:, T - 1:T]), reads=[hn], writes=[('p2_state', c)])
                    yn, y = yr.next()
                    S.op('dve', lambda e: e.tensor_tensor(y[:, :], h[:, :], gg[:, :], ALU.mult), reads=[hn, gn], writes=[yn])
                    S.dma('pool', yAv[:, c, t0:t0 + T], y[:, :], reads=[yn], writes=['yA'])
                for c in range(0, 6, 3):
                    S.interleave([lambda c=c: chunk(c), lambda c=c: chunk(c + 1), lambda c=c: chunk(c + 2)])

    def phase_attn(self, l, zQKV, oG, amask):
        nc, S, L = self.nc, self.S, self.L
        with ExitStack() as st:
            mk = self.sb(st, "p3_mk", [128, 2, 12, 256], F32)
            idb = self.sb(st, "p3_idb", [128, 128], BF16)
            S.dma('sp', mk[:], amask, writes=['p3_mk'])
            S.op('dve', lambda e: e.tensor_copy(idb[:], self.ident_f[:]), reads=['ident_f'], writes=['p3_idb'])
            extra = Ring(nc, st, "p3_psx", 2, [128, 512], F32, psum=True)
            psum = Ring.__new__(Ring)
            psum.bufs = list(self.psum.bufs) + list(extra.bufs)
            psum.i = 0
            qr = Ring(nc, st, "p3_q", 4, [128, 256], BF16)
            kr = Ring(nc, st, "p3_k", 4, [128, 256], BF16)
            vr = Ring(nc, st, "p3_v", 5, [128, 4, 128], BF16)
            qkr = Ring(nc, st, "p3_qk", 4, [128, 8, 128], BF16)
            scr = Ring(nc, st, "p3_sc", 4, [128, 512], F32)
            ptr = Ring(nc, st, "p3_pt", 6, [128, 2, 2, 128], BF16)
            osr = Ring(nc, st, "p3_os", 4, [128, 260], F32)
            for (vn, v) in vr.bufs:
                S.op('pool', lambda e: e.memset(v[:], 1.0), writes=[vn])
            box = {}

            def unit(g, d, r, blk):
                zv = zQKV.rearrange("(n d) c -> d n c", d=d)
                ov = oG[g].rearrange("(n d) c -> d n c", d=d)
                rows = slice(blk * 128, (blk + 1) * 128)
                qn, q = qr.next()
                kn, k_ = kr.next()
                vn, v = vr.next()
                S.dma('sp', q[:, :], zv[r, rows, 256 * g:256 * g + 256], reads=['zQKV'], writes=[qn])
                S.dma('sp', k_[:, :], zv[r, rows, 768 + 256 * g:768 + 256 * g + 256], reads=['zQKV'], writes=[kn])
                S.dma('sp', v[:, :, 0:64], zv[r, rows, 1536 + 256 * g:1536 + 256 * g + 256].rearrange("p (h e) -> p h e", e=64),
                      reads=['zQKV'], writes=[vn])
                ptn, pT = psum.next()
                for j in range(4):
                    S.op('pe', lambda e, j=j: e.matmul(pT[0:64, j * 128:(j + 1) * 128], lhsT=q[:, j * 64:(j + 1) * 64], rhs=idb[:], start=True, stop=True),
                         reads=[qn, 'p3_idb'], writes=[ptn], inc=(j == 3))
                ptn2, pT2 = psum.next()
                for j in range(4):
                    S.op('pe', lambda e, j=j: e.matmul(pT2[0:64, j * 128:(j + 1) * 128], lhsT=k_[:, j * 64:(j + 1) * 64], rhs=idb[:], start=True, stop=True),
                         reads=[kn, 'p3_idb'], writes=[ptn2], inc=(j == 3))
                qkn, qk = qkr.next()
                S.op('dve', lambda e: e.tensor_copy(qk[0:64, 0:4, :], pT[0:64, 0:512].rearrange("p (a b) -> p a b", a=4)), reads=[ptn], writes=[qkn])
                S.op('dve', lambda e: e.tensor_copy(qk[0:64, 4:8, :], pT2[0:64, 0:512].rearrange("p (a b) -> p a b", a=4)), reads=[ptn2], writes=[qkn])
                qTn, qT = qkn, qk[:, 0:4, :]
                kTn, kT = qkn, qk[:, 4:8, :]
                first = blk == 0
                if first:
                    kTp_n, kTp, vp_n, vp = kTn, kT, vn, v
                else:
                    kTp_n, kTp, vp_n, vp = box['prev']
                box['prev'] = (kTn, kT, vn, v)
                pts = []
                for pair in range(2):
                    pn, ps = psum.next()
                    for h2 in range(2):
                        hx = 2 * pair + h2
                        col = h2 * 256
                        if not first:
                            S.op('pe', lambda e, hx=hx, col=col, ps=ps: e.matmul(ps[:, col:col + 128], lhsT=kTp[0:64, hx, :], rhs=qT[0:64, hx, :], start=True, stop=True),
                                 reads=[kTp_n, qTn], writes=[pn], inc=False)
                        S.op('pe', lambda e, hx=hx, col=col, ps=ps: e.matmul(ps[:, col + 128:col + 256], lhsT=kT[0:64, hx, :], rhs=qT[0:64, hx, :], start=True, stop=True),
                             reads=[kTn, qTn], writes=[pn], inc=(h2 == 1))
                    scn, sc = scr.next()
                    hh0 = 4 * g + 2 * pair
                    mview = mk[:, 1 if first else 0, hh0:hh0 + 2, :].rearrange("p a b -> p (a b)")
                    if first:
                        S.op('pool', lambda e, sc=sc: e.memset(sc[:, :], 0.0), writes=[scn])
                        for h2 in range(2):
                            col = h2 * 256
                            S.op('dve', lambda e, col=col, sc=sc, ps=ps, mview=mview: e.scalar_tensor_tensor(sc[:, col + 128:col + 256], ps[:, col + 128:col + 256], 0.125, mview[:, col + 128:col + 256], ALU.mult, ALU.add),
                                 reads=[pn, 'p3_mk'], writes=[scn])
                            S.op('dve', lambda e, col=col, sc=sc, mview=mview: e.tensor_copy(sc[:, col:col + 128], mview[:, col:col + 128]), reads=['p3_mk'], writes=[scn])
                    else:
                        S.op('dve', lambda e, sc=sc, ps=ps, mview=mview: e.scalar_tensor_tensor(sc[:, :], ps[:, :], 0.125, mview, ALU.mult, ALU.add),
                             reads=[pn, 'p3_mk'], writes=[scn])
                    pn2, pt = ptr.next()
                    S.op('act', lambda e, pt=pt, sc=sc: e.activation(pt[:, :, :, :].rearrange("p a b c -> p (a b c)"), sc[:, :], AF.Exp), reads=[scn], writes=[pn2])
                    pts.append((pn2, pt))
                pn, ps = psum.next()
                for hh in range(4):
                    pn2, pt = pts[hh // 2]
                    S.op('pe', lambda e, hh=hh, pt=pt: e.matmul(ps[:, hh * 128:hh * 128 + 65], lhsT=pt[:, hh % 2, 0, :], rhs=vp[:, hh, 0:65], start=True, stop=False),
                         reads=[pn2, vp_n], writes=[pn], inc=False)
                    S.op('pe', lambda e, hh=hh, pt=pt: e.matmul(ps[:, hh * 128:hh * 128 + 65], lhsT=pt[:, hh % 2, 1, :], rhs=v[:, hh, 0:65], start=False, stop=True),
                         reads=[pn2, vn], writes=[pn], inc=(hh == 3))
                on, o = osr.next()
                S.op('act', lambda e: e.copy(o[:, :].rearrange("p (h e) -> p h e", e=65), ps[:, :].rearrange("p (h e) -> p h e", e=128)[:, :, 0:65]), reads=[pn], writes=[on])
                S.dma('sp', ov[r, rows, :], o[:, :], reads=[on], writes=['oG'])

            for g, d in enumerate(DILS):
                nb = L // (128 * d)
                for r in range(d):
                    if nb % 2 == 0:
                        for blk in range(0, nb, 2):
                            S.interleave([lambda blk=blk: unit(g, d, r, blk), lambda blk=blk: unit(g, d, r, blk + 1)])
                    else:
                        for blk in range(nb):
                            unit(g, d, r, blk)

    def sincos(self, st, th_name, th, n, out_s, out_c, key):
        S = self.S
        TWO_PI = 2.0 * math.pi
        ti = self.sb(st, "sc_i", [128, n], mybir.dt.int32)
        tf = self.sb(st, "sc_f", [128, n], F32)
        ph = self.sb(st, "sc_p", [128, n], F32)
        mm = self.sb(st, "sc_m", [128, n], F32)
        for (shift, outt) in ((0.0, out_s), (math.pi / 2, out_c)):
            S.op('dve', lambda e: e.tensor_scalar(ph[:, :], th, 1.0 / TWO_PI, shift / TWO_PI, ALU.mult, ALU.add), reads=[th_name], writes=[key + 'ph'])
            S.op('dve', lambda e: e.tensor_copy(ti[:, :], ph[:, :]), reads=[key + 'ph'], writes=[key + 'ti'])
            S.op('dve', lambda e: e.tensor_copy(tf[:, :], ti[:, :]), reads=[key + 'ti'], writes=[key + 'tf'])
            S.op('dve', lambda e: e.tensor_tensor(ph[:, :], ph[:, :], tf[:, :], ALU.subtract), reads=[key + 'ph', key + 'tf'], writes=[key + 'ph'])
            S.op('dve', lambda e: e.tensor_scalar(mm[:, :], ph[:, :], 0.5, None, ALU.is_gt), reads=[key + 'ph'], writes=[key + 'mm'])
            S.op('dve', lambda e: e.tensor_tensor(ph[:, :], ph[:, :], mm[:, :], ALU.subtract), reads=[key + 'ph', key + 'mm'], writes=[key + 'ph'])
            S.op('dve', lambda e: e.tensor_scalar(mm[:, :], ph[:, :], -0.5, None, ALU.is_lt), reads=[key + 'ph'], writes=[key + 'mm'])
            S.op('dve', lambda e: e.tensor_tensor(ph[:, :], ph[:, :], mm[:, :], ALU.add), reads=[key + 'ph', key + 'mm'], writes=[key + 'ph'])
            S.op('dve', lambda e: e.tensor_scalar(ph[:, :], ph[:, :], -0.4999, 0.4999, ALU.max, ALU.min), reads=[key + 'ph'], writes=[key + 'ph'])
            S.op('act', lambda e: e.activation(outt, ph[:, :], AF.Sin, scale=TWO_PI), reads=[key + 'ph'], writes=[key + 'out'])

    def phase_s5(self, l, zF, yC, s5A, s5R, s5B, s5C, ssmd):
        nc, S, L = self.nc, self.S, self.L
        with ExitStack() as st:
            rho = self.sb(st, "p4_rho", [128, 24], F32)
            BbR = self.sb(st, "p4_BbR", [128, 24, 128], BF16)
            BbI = self.sb(st, "p4_BbI", [128, 24, 128], BF16)
            CtR = self.sb(st, "p4_CtR", [128, 24, 128], BF16)
            CtI = self.sb(st, "p4_CtI", [128, 24, 128], BF16)
            CtRn = self.sb(st, "p4_CtRn", [128, 24, 128], BF16)
            dsk = self.sb(st, "p4_d", [128, 6], F32)
            xst = self.sb(st, "p4_xst", [128, 2, 24], F32)
            S.dma('sp', dsk[:], ssmd[:, l, :], writes=['p4_d'])
            S.op('dve', lambda e: e.memset(xst[:], 0.0), writes=[('p4_xst', p) for p in range(24)])
            S.dma('pool', CtR[:], s5C[l, 0], writes=['p4_CtR'])
            with ExitStack() as s2:
                N = 3072
                par = self.sb(s2, "p4s_par", [128, 3, N], F32)
                S.dma('sp', par[:], s5R[:, l, :, :], writes=['p4s_par'])
                ar, ai, ld = par[:, 0, :], par[:, 1, :], par[:, 2, :]
                dt = self.sb(s2, "p4s_dt", [128, N], F32)
                mag = self.sb(st if False else s2, "p4s_mag", [128, N], F32)
                th = self.sb(s2, "p4s_th", [128, N], F32)
                sn = self.sb(s2, "p4s_sn", [128, N], F32)
                cs = self.sb(s2, "p4s_cs", [128, N], F32)
                S.op('act', lambda e: e.activation(dt[:, :], ld, AF.Exp), reads=['p4s_par'], writes=['p4s_dt'])
                S.op('dve', lambda e: e.tensor_tensor(mag[:, :], ar, dt[:, :], ALU.mult), reads=['p4s_par', 'p4s_dt'], writes=['p4s_mag'])
                S.op('act', lambda e: e.activation(mag[:, :], mag[:, :], AF.Exp), reads=['p4s_mag'], writes=['p4s_mag'])
                S.op('dve', lambda e: e.tensor_tensor(th[:, :], ai, dt[:, :], ALU.mult), reads=['p4s_par', 'p4s_dt'], writes=['p4s_th'])
                for hf in range(2):
                    with ExitStack() as s3:
                        sl = slice(hf * 1536, (hf + 1) * 1536)
                        self.sincos(s3, 'p4s_th', th[:, sl], 1536, sn[:, sl], cs[:, sl], 'scB')
                        S.barrier()
                S.op('dve', lambda e: e.tensor_tensor(cs[:, :], cs[:, :], mag[:, :], ALU.mult), reads=['scBout', 'p4s_mag'], writes=['p4s_cs'])
                S.op('dve', lambda e: e.tensor_scalar(cs[:, :], cs[:, :], -1.0, None, ALU.add), reads=['p4s_cs'], writes=['p4s_cs'])
                S.op('dve', lambda e: e.tensor_tensor(sn[:, :], sn[:, :], mag[:, :], ALU.mult), reads=['scBout', 'p4s_mag'], writes=['p4s_sn'])
                S.op('dve', lambda e: e.tensor_tensor(dt[:, :], ar, ar, ALU.mult), reads=['p4s_par'], writes=['p4s_dt'])
                S.op('dve', lambda e: e.tensor_tensor(th[:, :], ai, ai, ALU.mult), reads=['p4s_par'], writes=['p4s_th'])
                S.op('dve', lambda e: e.tensor_tensor(dt[:, :], dt[:, :], th[:, :], ALU.add), reads=['p4s_dt', 'p4s_th'], writes=['p4s_dt'])
                S.op('dve', lambda e: e.reciprocal(dt[:, :], dt[:, :]), reads=['p4s_dt'], writes=['p4s_dt'])
                t1 = self.sb(s2, "p4s_t1", [128, N], F32)
                S.op('dve', lambda e: e.tensor_tensor(mag[:, :], cs[:, :], ar, ALU.mult), reads=['p4s_cs', 'p4s_par'], writes=['p4s_mag'])
                S.op('dve', lambda e: e.tensor_tensor(t1[:, :], sn[:, :], ai, ALU.mult), reads=['p4s_sn', 'p4s_par'], writes=['p4s_t1'])
                S.op('dve', lambda e: e.tensor_tensor(mag[:, :], mag[:, :], t1[:, :], ALU.add), reads=['p4s_mag', 'p4s_t1'], writes=['p4s_mag'])
                S.op('dve', lambda e: e.tensor_tensor(mag[:, :], mag[:, :], dt[:, :], ALU.mult), reads=['p4s_mag', 'p4s_dt'], writes=['p4s_mag'])
                S.op('dve', lambda e: e.tensor_tensor(th[:, :], sn[:, :], ar, ALU.mult), reads=['p4s_sn', 'p4s_par'], writes=['p4s_th'])
                S.op('dve', lambda e: e.tensor_tensor(t1[:, :], cs[:, :], ai, ALU.mult), reads=['p4s_cs', 'p4s_par'], writes=['p4s_t1'])
                S.op('dve', lambda e: e.tensor_tensor(th[:, :], th[:, :], t1[:, :], ALU.subtract), reads=['p4s_th', 'p4s_t1'], writes=['p4s_th'])
                S.op('dve', lambda e: e.tensor_tensor(th[:, :], th[:, :], dt[:, :], ALU.mult), reads=['p4s_th', 'p4s_dt'], writes=['p4s_th'])
                zr, zi = mag, th
                bre = self.sb(s2, "p4s_bre", [128, N], F32)
                bim = self.sb(s2, "p4s_bim", [128, N], F32)
                S.dma('sp', bre[:, :], s5B[l, 0].rearrange("p a b -> p (a b)"), writes=['p4s_bre'])
                S.dma('sp', bim[:, :], s5B[l, 1].rearrange("p a b -> p (a b)"), writes=['p4s_bim'])
                S.op('dve', lambda e: e.tensor_tensor(t1[:, :], zr[:, :], bre[:, :], ALU.mult), reads=['p4s_mag', 'p4s_bre'], writes=['p4s_t1'])
                S.op('dve', lambda e: e.tensor_tensor(cs[:, :], zi[:, :], bim[:, :], ALU.mult), reads=['p4s_th', 'p4s_bim'], writes=['p4s_cs'])
                S.op('dve', lambda e: e.tensor_tensor(BbR[:, :, :].rearrange("p a b -> p (a b)"), t1[:, :], cs[:, :], ALU.subtract), reads=['p4s_t1', 'p4s_cs'], writes=['p4_BbR'])
                S.op('dve', lambda e: e.tensor_tensor(t1[:, :], zr[:, :], bim[:, :], ALU.mult), reads=['p4s_mag', 'p4s_bim'], writes=['p4s_t1'])
                S.op('dve', lambda e: e.tensor_tensor(cs[:, :], zi[:, :], bre[:, :], ALU.mult), reads=['p4s_th', 'p4s_bre'], writes=['p4s_cs'])
                S.op('dve', lambda e: e.tensor_tensor(BbI[:, :, :].rearrange("p a b -> p (a b)"), t1[:, :], cs[:, :], ALU.add), reads=['p4s_t1', 'p4s_cs'], writes=['p4_BbI'])
                S.dma('sp', bre[:, :], s5C[l, 1].rearrange("p a b -> p (a b)"), reads=['p4s_bre'], writes=['p4s_bre'])
                S.op('act', lambda e: e.mul(CtI[:, :, :].rearrange("p a b -> p (a b)"), bre[:, :], -1.0), reads=['p4s_bre'], writes=['p4_CtI'])
                S.dma('sp', bim[:, :], s5C[l, 0].rearrange("p a b -> p (a b)"), reads=['p4s_bim'], writes=['p4s_bim'])
                S.op('act', lambda e: e.mul(CtRn[:, :, :].rearrange("p a b -> p (a b)"), bim[:, :], -1.0), reads=['p4s_bim'], writes=['p4_CtRn'])
                S.barrier()
            cosT = self.sb(st, "p4_cos", [128, 24, T], F32)
            sinT = self.sb(st, "p4_sin", [128, 24, T], F32)
            with ExitStack() as s2:
                pa = self.sb(s2, "p4a_par", [128, 3, 24], F32)
                S.dma('sp', pa[:], s5A[:, l, :, :], writes=['p4a_par'])
                dt = self.sb(s2, "p4a_dt", [128, 24], F32)
                th = self.sb(s2, "p4a_th", [128, 24], F32)
                sn = self.sb(s2, "p4a_sn", [128, 24], F32)
                cs = self.sb(s2, "p4a_cs", [128, 24], F32)
                S.op('act', lambda e: e.activation(dt[:, :], pa[:, 2, :], AF.Exp), reads=['p4a_par'], writes=['p4a_dt'])
                S.op('dve', lambda e: e.tensor_tensor(rho[:, :], pa[:, 0, :], dt[:, :], ALU.mult), reads=['p4a_par', 'p4a_dt'], writes=['p4_rho'])
                S.op('act', lambda e: e.activation(rho[:, :], rho[:, :], AF.Exp), reads=['p4_rho'], writes=['p4_rho'])
                S.op('dve', lambda e: e.tensor_tensor(th[:, :], pa[:, 1, :], dt[:, :], ALU.mult), reads=['p4a_par', 'p4a_dt'], writes=['p4a_th'])
                self.sincos(s2, 'p4a_th', th[:, :], 24, sn[:, :], cs[:, :], 'scA')
                tmp = self.sb(s2, "p4a_tmp", [128, T // 2], F32)
                for p in range(24):
                    S.op('act', lambda e: e.copy(cosT[:, p, 0:1], cs[:, p:p + 1]), reads=['scAout'], writes=[('p4_cos', p)])
                    S.op('act', lambda e: e.copy(sinT[:, p, 0:1], sn[:, p:p + 1]), reads=['scAout'], writes=[('p4_sin', p)])
                    w = 1
                    while w < T:
                        cw, sw = cosT[:, p, w - 1:w], sinT[:, p, w - 1:w]
                        S.op('dve', lambda e: e.tensor_scalar(tmp[:, 0:w], sinT[:, p, 0:w], sw, None, ALU.mult), reads=[('p4_sin', p)], writes=['p4a_tmp'])
                        S.op('dve', lambda e: e.scalar_tensor_tensor(cosT[:, p, w:2 * w], cosT[:, p, 0:w], cw, tmp[:, 0:w], ALU.mult, ALU.subtract),
                             reads=[('p4_cos', p), 'p4a_tmp'], writes=[('p4_cos', p)])
                        S.op('dve', lambda e: e.tensor_scalar(tmp[:, 0:w], cosT[:, p, 0:w], sw, None, ALU.mult), reads=[('p4_cos', p)], writes=['p4a_tmp'])
                        S.op('dve', lambda e: e.scalar_tensor_tensor(sinT[:, p, w:2 * w], sinT[:, p, 0:w], cw, tmp[:, 0:w], ALU.mult, ALU.add),
                             reads=[('p4_sin', p), ('p4_cos', p), 'p4a_tmp'], writes=[('p4_sin', p)])
                        w *= 2
                S.barrier()
            wg = self.sb(st, "p4_wg", [128, 6, 1536], BF16)
            S.dma('sp', wg[:], self.wB["ssm_glu"][l].rearrange("(k p) n -> p k n", p=128), reads=[('ssm_gluB', l)], writes=['p4_wg'])
            psy = Ring(nc, st, "p4_psy", 2, [128, 512], F32, psum=True)
            usr = Ring(nc, st, "p4_us", 1, [128, 6, T], BF16)
            tr = Ring(nc, st, "p4_t", 10, [128, T], F32)
            stmp = self.sb(st, "p4_stmp", [128, 24, 2], F32)
            xr = Ring(nc, st, "p4_x", 4, [128, T], F32)
            ubr = Ring(nc, st, "p4_ub", 12, [128, T], BF16)
            yfr = Ring(nc, st, "p4_yf", 2, [128, T], F32)
            gy = self.sb(st, "p4_gy", [128, 6, T], BF16)
            sgr = Ring(nc, st, "p4_sg", 2, [128, T], F32)
            ycr = Ring(nc, st, "p4_yc", 2, [128, T], BF16)
            zFv = zF.rearrange("(c p) l -> p c l", p=128)
            yCv = yC.rearrange("(c p) l -> p c l", p=128)
            for ti in range(L // T):
                t0 = ti * T
                un, us = usr.next()
                S.dma('sp', us[:], zFv[:, 12:18, t0:t0 + T], reads=['zF'], writes=[un])
                PP = {}

                def stageA0(p):
                    ch = p // 4
                    prn, pr = self.psum.next()
                    S.op('pe', lambda e: e.matmul(pr[:, :], lhsT=BbR[:, p, :], rhs=us[:, ch, :], start=True, stop=True), reads=['p4_BbR', un], writes=[prn])
                    pin, pi = self.psum.next()
                    S.op('pe', lambda e: e.matmul(pi[:, :], lhsT=BbI[:, p, :], rhs=us[:, ch, :], start=True, stop=True), reads=['p4_BbI', un], writes=[pin])
                    PP[p] = dict(pr=(prn, pr), pi=(pin, pi))

                def stageA(p):
                    c_, s_ = cosT[:, p, :], sinT[:, p, :]
                    prn, pr = PP[p]['pr']; pin, pi = PP[p]['pi']
                    t1n, t1 = tr.next(); t2n, t2 = tr.next(); t3n, t3 = tr.next(); t4n, t4 = tr.next()
                    S.op('dve', lambda e: e.tensor_tensor(t1[:, :], pr[:, :], c_, ALU.mult), reads=[prn, ('p4_cos', p)], writes=[t1n])
                    S.op('dve', lambda e: e.tensor_tensor(t2[:, :], pi[:, :], s_, ALU.mult), reads=[pin, ('p4_sin', p)], writes=[t2n])
                    S.op('dve', lambda e: e.tensor_tensor(t3[:, :], pi[:, :], c_, ALU.mult), reads=[pin, ('p4_cos', p)], writes=[t3n])
                    S.op('dve', lambda e: e.tensor_tensor(t4[:, :], pr[:, :], s_, ALU.mult), reads=[prn, ('p4_sin', p)], writes=[t4n])
                    S.op('pool', lambda e: e.tensor_tensor(t1[:, :], t1[:, :], t2[:, :], ALU.add), reads=[t1n, t2n], writes=[t1n])
                    S.op('pool', lambda e: e.tensor_tensor(t3[:, :], t3[:, :], t4[:, :], ALU.subtract), reads=[t3n, t4n], writes=[t3n])
                    PP[p].update(t1=(t1n, t1), t3=(t3n, t3))

                def stageC(p):
                    c_, s_ = cosT[:, p, :], sinT[:, p, :]
                    t1n, t1 = PP[p]['t1']; t3n, t3 = PP[p]['t3']
                    rb = rho[:, p:p + 1].to_broadcast([128, T])
                    xrn, xre = xr.next(); xin, xim = xr.next()
                    S.op('dve', lambda e: e.tensor_tensor_scan(xre[:, :], rb, t1[:, :], xst[:, 0, p:p + 1], ALU.mult, ALU.add),
                         reads=['p4_rho', t1n, ('p4_xst', p)], writes=[xrn])
                    S.op('dve', lambda e: e.tensor_tensor_scan(xim[:, :], rb, t3[:, :], xst[:, 1, p:p + 1], ALU.mult, ALU.add),
                         reads=['p4_rho', t3n, ('p4_xst', p)], writes=[xin])
                    cl, sl = cosT[:, p, T - 1:T], sinT[:, p, T - 1:T]
                    S.op('act', lambda e: e.activation(stmp[:, p, 0:1], xim[:, T - 1:T], AF.Copy, scale=sl), reads=[xin, ('p4_sin', p)], writes=[('p4_stmp', p)])
                    S.op('act', lambda e: e.activation(stmp[:, p, 1:2], xim[:, T - 1:T], AF.Copy, scale=cl), reads=[xin, ('p4_cos', p)], writes=[('p4_stmp', p)])
                    u1n, u1 = ubr.next(); u2n, u2 = ubr.next(); u3n, u3 = ubr.next(); u4n, u4 = ubr.next()
                    S.op('dve', lambda e: e.tensor_tensor(u1[:, :], xre[:, :], c_, ALU.mult), reads=[xrn, ('p4_cos', p)], writes=[u1n])
                    S.op('dve', lambda e: e.tensor_tensor(u2[:, :], xim[:, :], s_, ALU.mult), reads=[xin, ('p4_sin', p)], writes=[u2n])
                    S.op('dve', lambda e: e.tensor_tensor(u3[:, :], xre[:, :], s_, ALU.mult), reads=[xrn, ('p4_sin', p)], writes=[u3n])
                    S.op('dve', lambda e: e.tensor_tensor(u4[:, :], xim[:, :], c_, ALU.mult), reads=[xin, ('p4_cos', p)], writes=[u4n])
                    S.op('dve', lambda e: e.scalar_tensor_tensor(xst[:, 0, p:p + 1], xre[:, T - 1:T], cl, stmp[:, p, 0:1], ALU.mult, ALU.subtract),
                         reads=[xrn, ('p4_stmp', p), ('p4_cos', p)], writes=[('p4_xst', p)])
                    S.op('dve', lambda e: e.scalar_tensor_tensor(xst[:, 1, p:p + 1], xre[:, T - 1:T], sl, stmp[:, p, 1:2], ALU.mult, ALU.add),
                         reads=[xrn, ('p4_stmp', p), ('p4_sin', p)], writes=[('p4_xst', p)])
                    PP[p].update(u1=(u1n, u1), u2=(u2n, u2), u3=(u3n, u3), u4=(u4n, u4))

                def stageE(p):
                    u1n, u1 = PP[p]['u1']; u2n, u2 = PP[p]['u2']; u3n, u3 = PP[p]['u3']; u4n, u4 = PP[p]['u4']
                    if p % 4 == 0:
                        PP['py'] = psy.next()
                    pyn, py = PP['py']
                    S.op('pe', lambda e: e.matmul(py[:, :], lhsT=CtR[:, p, :], rhs=u1[:, :], start=(p % 4 == 0), stop=False), reads=['p4_CtR', u1n], writes=[pyn], inc=False)
                    S.op('pe', lambda e: e.matmul(py[:, :], lhsT=CtRn[:, p, :], rhs=u2[:, :], start=False, stop=False), reads=['p4_CtRn', u2n], writes=[pyn], inc=False)
                    S.op('pe', lambda e: e.matmul(py[:, :], lhsT=CtI[:, p, :], rhs=u3[:, :], start=False, stop=False), reads=['p4_CtI', u3n], writes=[pyn], inc=False)
                    S.op('pe', lambda e: e.matmul(py[:, :], lhsT=CtI[:, p, :], rhs=u4[:, :], start=False, stop=(p % 4 == 3)), reads=['p4_CtI', u4n], writes=[pyn])
                    if p % 4 == 3:
                        oc = p // 4
                        yfn, yf = yfr.next()
                        S.op('dve', lambda e: e.scalar_tensor_tensor(yf[:, :], us[:, oc, :], dsk[:, oc:oc + 1], py[:, :], ALU.mult, ALU.add),
                             reads=[un, 'p4_d', pyn], writes=[yfn])
                        S.op('act', lambda e: e.activation(gy[:, oc, :], yf[:, :], AF.Gelu_apprx_tanh), reads=[yfn], writes=[('p4_gy', oc)])
                    del PP[p]

                stageA0(0)
                for it in range(24 + 2):
                    if it + 1 < 24:
                        stageA0(it + 1)
                    lists = []
                    for (fn_, arg, ok) in ((stageA, it, it < 24), (stageC, it - 1, 1 <= it <= 24), (stageE, it - 2, it >= 2)):
                        if ok:
                            S.defer = []
                            fn_(arg)
                            lists.append(S.defer)
                            S.defer = None
                    while any(lists):
                        for lst in lists:
                            if lst:
                                S.run_deferred(lst.pop(0))
                for j in range(6):
                    pan, pa_ = self.psum.next()
                    for k in range(6):
                        S.op('pe', lambda e: e.matmul(pa_[:, :], lhsT=wg[:, k, j * 128:(j + 1) * 128], rhs=gy[:, k, :], start=(k == 0), stop=(k == 5)),
                             reads=['p4_wg', ('p4_gy', k)], writes=[pan], inc=(k == 5))
                    pbn, pb_ = self.psum.next()
                    for k in range(6):
                        S.op('pe', lambda e: e.matmul(pb_[:, :], lhsT=wg[:, k, 768 + j * 128:768 + (j + 1) * 128], rhs=gy[:, k, :], start=(k == 0), stop=(k == 5)),
                             reads=['p4_wg', ('p4_gy', k)], writes=[pbn], inc=(k == 5))
                    sgn, sg = sgr.next()
                    S.op('act', lambda e: e.activation(sg[:, :], pb_[:, :], AF.Sigmoid), reads=[pbn], writes=[sgn])
                    ycn, yc = ycr.next()
                    S.op('dve', lambda e: e.tensor_tensor(yc[:, :], pa_[:, :], sg[:, :], ALU.mult), reads=[pan, sgn], writes=[ycn])
                    S.dma('sp', yCv[:, j, t0:t0 + T], yc[:, :], reads=[ycn], writes=['yC'])

    def phase_xattn(self, l, mem, zF, yX):
        nc, S, L = self.nc, self.S, self.L
        with ExitStack() as st:
            kT = self.sb(st, "p5_kT", [128, 4, 2, 256], BF16)
            vm = self.sb(st, "p5_vm", [128, 2, 768], BF16)
            with ExitStack() as st2:
                memt = self.sb(st2, "p5_mem", [128, 2, D], F32)
                memT = self.sb(st2, "p5_memT", [128, 8, 256], F32)
                sq = self.sb(st2, "p5_sq", [128, 8, 256], BF16)
                rstd = self.sb(st2, "p5_rstd", [128, 256], F32)
                mn = self.sb(st2, "p5_mn", [128, 8, 256], BF16)
                wkv = self.sb(st2, "p5_wkv", [128, 8, 1536], BF16)
                S.dma('sp', memt[:], mem.rearrange("(b p) d -> p b d", p=128), writes=['p5_mem'])
                S.dma('sp', wkv[:], self.wB["mem_wkv"][l].rearrange("(k p) n -> p k n", p=128), reads=[('mem_wkvB', l)], writes=['p5_wkv'])
                for b in range(2):
                    for half in range(2):
                        pn, ps = self.psum.next()
                        for j in range(4):
                            c = half * 4 + j
                            S.op('pe', lambda e: e.transpose(ps[:, j * 128:(j + 1) * 128], memt[:, b, c * 128:(c + 1) * 128], self.ident_f[:]),
                                 reads=['p5_mem', 'ident_f'], writes=[pn], inc=(j == 3))
                        S.op('dve', lambda e: e.tensor_copy(memT[:, half * 4:(half + 1) * 4, b * 128:(b + 1) * 128], ps[:, :].rearrange("p (a b) -> p a b", a=4)),
                             reads=[pn], writes=['p5_memT'])
                for c in range(8):
                    S.op('act', lambda e: e.activation(sq[:, c, :], memT[:, c, :], AF.Square), reads=['p5_memT'], writes=['p5_sq'])
                self.rms_stats('p5_sq', sq, 'p5_rstd', rstd, 256)
                for c in range(8):
                    S.op('dve', lambda e: e.scalar_tensor_tensor(mn[:, c, :], memT[:, c, :], self.gains_s[:, l, 2, c:c + 1], rstd[:, :], ALU.mult, ALU.mult),
                         reads=['p5_memT', 'p5_rstd', 'gains_s'], writes=['p5_mn'])
                for h in range(4):
                    for part, (off, M) in enumerate(((0, 128), (128, 64))):
                        col = 192 * h + off
                        pn, ps = self.psum.next()
                        for k in range(8):
                            S.op('pe', lambda e: e.matmul(ps[:M, :256], lhsT=wkv[:, k, col:col + M], rhs=mn[:, k, :], start=(k == 0), stop=(k == 7)),
                                 reads=['p5_wkv', 'p5_mn'], writes=[pn], inc=(k == 7))
                        S.op('dve', lambda e: e.tensor_copy(kT[:M, h, part, :], ps[:M, :256]), reads=[pn], writes=['p5_kT'])
                for mc in range(2):
                    for (n0, nn) in ((0, 512), (512, 256)):
                        pn, ps = self.psum.next()
                        for k in range(8):
                            S.op('pe', lambda e: e.matmul(ps[:, :nn], lhsT=mn[:, k, mc * 128:(mc + 1) * 128], rhs=wkv[:, k, 768 + n0:768 + n0 + nn], start=(k == 0), stop=(k == 7)),
                                 reads=['p5_wkv', 'p5_mn'], writes=[pn], inc=(k == 7))
                        S.op('dve', lambda e: e.tensor_copy(vm[:, mc, n0:n0 + nn], ps[:, :nn]), reads=[pn], writes=['p5_vm'])
                S.barrier()
            xq0r = Ring(nc, st, "p5_xq0", 2, [128, T], BF16)
            xq1r = Ring(nc, st, "p5_xq1", 2, [128, T], BF16)
            ptr = Ring(nc, st, "p5_pt", 4, [128, T], BF16)
            rcr = Ring(nc, st, "p5_rc", 2, [128, T], F32)
            y0r = Ring(nc, st, "p5_y0", 2, [128, T], BF16)
            y1r = Ring(nc, st, "p5_y1", 2, [128, T], BF16)
            XQ0 = 18 * 128
            sc = 192.0 ** -0.5
            for ti in range(L // T):
                t0 = ti * T
                for h in range(4):
                    q0n, q0 = xq0r.next()
                    q1n, q1 = xq1r.next()
                    r0 = XQ0 + 192 * h
                    S.dma('sp', q0[:, :], zF[r0:r0 + 128, t0:t0 + T], reads=['zF'], writes=[q0n])
                    S.dma('sp', q1[0:64, :], zF[r0 + 128:r0 + 192, t0:t0 + T], reads=['zF'], writes=[q1n])
                    pts = []
                    for mc in range(2):
                        pn, ps = self.psum.next()
                        S.op('pe', lambda e: e.matmul(ps[:, :], lhsT=kT[:, h, 0, mc * 128:(mc + 1) * 128], rhs=q0[:, :], start=True, stop=False),
                             reads=['p5_kT', q0n], writes=[pn], inc=False)
                        S.op('pe', lambda e: e.matmul(ps[:, :], lhsT=kT[0:64, h, 1, mc * 128:(mc + 1) * 128], rhs=q1[0:64, :], start=False, stop=True),
                             reads=['p5_kT', q1n], writes=[pn])
                        ptn, pt = ptr.next()
                        S.op('act', lambda e: e.activation(pt[:, :], ps[:, :], AF.Exp, scale=sc), reads=[pn], writes=[ptn])
                        pts.append((ptn, pt))
                    pn, ps = self.psum.next()
                    for mc in range(2):
                        S.op('pe', lambda e: e.matmul(ps[:, :], lhsT=self.ones_b[:], rhs=pts[mc][1][:, :], start=(mc == 0), stop=(mc == 1)),
                             reads=['ones_b', pts[mc][0]], writes=[pn], inc=(mc == 1))
                    rcn, rc = rcr.next()
                    S.op('dve', lambda e: e.reciprocal(rc[:, :], ps[:, :]), reads=[pn], writes=[rcn])
                    for part, (off, M, yr_) in enumerate(((0, 128, y0r), (128, 64, y1r))):
                        pn, ps = self.psum.next()
                        for mc in range(2):
                            S.op('pe', lambda e: e.matmul(ps[:M, :], lhsT=vm[:, mc, 192 * h + off:192 * h + off + M], rhs=pts[mc][1][:, :], start=(mc == 0), stop=(mc == 1)),
                                 reads=['p5_vm', pts[mc][0]], writes=[pn], inc=(mc == 1))
                        yn, y = yr_.next()
                        S.op('dve', lambda e: e.tensor_tensor(y[:M, :], ps[:M, :], rc[:M, :], ALU.mult), reads=[pn, rcn], writes=[yn])
                        S.dma('pool', yX[192 * h + off:192 * h + off + M, t0:t0 + T], y[:M, :], reads=[yn], writes=['yX'])

    def phase_merge(self, l, xT, xmT, zF, yA, yC, yX, oG):
        nc, S, L = self.nc, self.S, self.L
        ph = self.phases
        branches = []
        if 'p2' in ph:
            branches.append((0, 'proj_a', 6))
        if 'p3' in ph:
            branches.append((1, 'proj_b', 2))
        if 'p4' in ph:
            branches.append((2, 'proj_c', 6))
        if 'p5' in ph:
            branches.append((3, 'proj_x', 6))
        with ExitStack() as st:
            wp = {}
            for (b, nm, nk) in branches:
                wp[b] = self.sb(st, "p6_" + nm, [128, nk, D], BF16)
                S.dma('sp', wp[b][:], self.wB[nm][l].rearrange("(k p) n -> p k n", p=128), reads=[(nm + 'B', l)], writes=['p6_w%d' % b])
            wo = self.sb(st, "p6_wo", [128, 8, D], BF16)
            S.dma('sp', wo[:], self.wB["w_out"][l].rearrange("(k p) n -> p k n", p=128), reads=[('w_outB', l)], writes=['p6_wo'])
            yr = {b: Ring(nc, st, "p6_y%d" % b, 1, [128, nk, T], BF16) for (b, nm, nk) in branches}
            sgr = Ring(nc, st, "p6_sg", 3, [128, 4, T], BF16)
            xr = Ring(nc, st, "p6_x", 1, [128, 8, T], F32)
            xor_ = Ring(nc, st, "p6_xo", 1, [128, 8, T], F32)
            macc = Ring(nc, st, "p6_macc", 2, [128, T], F32)
            mtmp = Ring(nc, st, "p6_mtmp", 4, [128, T], F32)
            mb = self.sb(st, "p6_mb", [128, 8, T], BF16)
            sq = self.sb(st, "p6_sq", [128, 8, T], BF16)
            rstd = self.sb(st, "p6_rstd", [128, T], F32)
            y = self.sb(st, "p6_yy", [128, 8, T], F32)
            og = Ring(nc, st, "p6_og", 2, [128, 3, 260], F32)
            ybt = Ring(nc, st, "p6_ybt", 2, [128, 256], F32)
            rl = Ring(nc, st, "p6_rl", 2, [128, 4], F32)
            srcs = {0: yA, 2: yC, 3: yX}
            for ti in range(L // T):
                t0 = ti * T
                ys = {}
                for (b, nm, nk) in branches:
                    yn, yt = yr[b].next()
                    ys[b] = (yn, yt)
                    if b != 1:
                        S.dma('sp', yt[:], srcs[b].rearrange("(c p) l -> p c l", p=128)[:, :, t0:t0 + T], reads=[srcs[b].tensor.name], writes=[yn])
                    else:
                        for tb in range(4):
                            on, o = og.next()
                            S.dma('sp', o[:], oG[:, t0 + tb * 128:t0 + (tb + 1) * 128, :].rearrange("g p n -> p g n"), reads=['oG'], writes=[on])
                            S.op('dve', lambda e: e.tensor_tensor(o[:, 0, :], o[:, 0, :], o[:, 1, :], ALU.add), reads=[on], writes=[on])
                            S.op('dve', lambda e: e.tensor_tensor(o[:, 0, :], o[:, 0, :], o[:, 2, :], ALU.add), reads=[on], writes=[on])
                            rn, r = rl.next()
                            ov = o[:, 0, :].rearrange("p (h e) -> p h e", e=65)
                            S.op('dve', lambda e: e.reciprocal(r[:, :], ov[:, :, 64]), reads=[on], writes=[rn])
                            bn, bt = ybt.next()
                            for hh in range(4):
                                S.op('dve', lambda e: e.tensor_scalar(bt[:, hh * 64:(hh + 1) * 64], ov[:, hh, 0:64], r[:, hh:hh + 1], None, ALU.mult),
                                     reads=[on, rn], writes=[bn])
                            pn, ps = self.psum.next()
                            for half in range(2):
                                S.op('pe', lambda e: e.transpose(ps[:, half * 128:(half + 1) * 128], bt[:, half * 128:(half + 1) * 128], self.ident_f[:]),
                                     reads=[bn, 'ident_f'], writes=[pn], inc=(half == 1))
                            S.op('act', lambda e: e.copy(yt[:, :, tb * 128:(tb + 1) * 128], ps[:, 0:256].rearrange("p (a b) -> p a b", a=2)),
                                 reads=[pn], writes=[yn])
                xn, xt = xr.next()
                S.dma('sp', xt[:], xT.rearrange("(c p) l -> p c l", p=128)[:, :, t0:t0 + T], reads=[('xT', ti)], writes=[xn])
                def mchunk(c):
                    mn_, m = macc.next()
                    sn, sg = sgr.next()
                    S.dma('sp', sg[:], zF.rearrange("(b c p) l -> p b c l", p=128, c=8)[:, 3:7, c, t0:t0 + T], reads=['zF'], writes=[sn])
                    for bi, (b, nm, nk) in enumerate(branches):
                        yn, yt = ys[b]
                        pn, ps = self.psum.next()
                        for k in range(nk):
                            S.op('pe', lambda e, b=b, k=k, ps=ps, yt=yt, nk=nk: e.matmul(ps[:, :], lhsT=wp[b][:, k, c * 128:(c + 1) * 128], rhs=yt[:, k, :], start=(k == 0), stop=(k == nk - 1)),
                                 reads=['p6_w%d' % b, yn], writes=[pn], inc=(k == nk - 1))
                        if bi == 0:
                            S.op('dve', lambda e, b=b, ps=ps: e.tensor_tensor(m[:, :], ps[:, :], sg[:, b, :], ALU.mult), reads=[pn, sn], writes=[mn_])
                        else:
                            tn, tm = mtmp.next()
                            S.op('dve', lambda e, b=b, ps=ps, tm=tm: e.tensor_tensor(tm[:, :], ps[:, :], sg[:, b, :], ALU.mult), reads=[pn, sn], writes=[tn])
                            S.op('dve', lambda e, tm=tm: e.tensor_tensor(m[:, :], m[:, :], tm[:, :], ALU.add), reads=[tn, mn_], writes=[mn_])
                    S.op('act', lambda e: e.copy(mb[:, c, :], m[:, :]), reads=[mn_], writes=[('p6_mb', c)])
                for c in range(0, 8, 2):
                    S.interleave([lambda c=c: mchunk(c), lambda c=c: mchunk(c + 1)])
                for c in range(8):
                    pn, ps = self.psum.next()
                    for k in range(8):
                        S.op('pe', lambda e: e.matmul(ps[:, :], lhsT=wo[:, k, c * 128:(c + 1) * 128], rhs=mb[:, k, :], start=(k == 0), stop=(k == 7)),
                             reads=['p6_wo', ('p6_mb', k)], writes=[pn], inc=(k == 7))
                    S.op('act', lambda e: e.activation(sq[:, c, :], ps[:, :], AF.Square), reads=[pn], writes=['p6_sq'])
                    S.op('act', lambda e: e.copy(y[:, c, :], ps[:, :]), reads=[pn], writes=['p6_yy'])
                on_, xo_t = xor_.next()
                self.postnorm_residual(l, 1, 'p6_yy', y, 'p6_sq', sq, 'p6_rstd', rstd, xn, xt, on_, xo_t)
                S.dma('pool', xmT.rearrange("(c p) l -> p c l", p=128)[:, :, t0:t0 + T], xo_t[:], reads=[on_], writes=[('xmT', ti)])

    def postnorm_residual(self, l, which, y_name, y, sq_name, sq, rstd_name, rstd, xres_name, xres, xo_name, xo):
        S = self.S
        self.rms_stats(sq_name, sq, rstd_name, rstd, T)
        for c in range(8):
            S.op('dve', lambda e: e.scalar_tensor_tensor(y[:, c, :], y[:, c, :], self.gains_s[:, l, which, c:c + 1], rstd[:, :],
                                                         ALU.mult, ALU.mult),
                 reads=[y_name, rstd_name, 'gains_s'], writes=[y_name])
            S.op('dve', lambda e: e.tensor_tensor(xo[:, c, :], y[:, c, :], xres[:, c, :], ALU.add),
                 reads=[y_name, xres_name], writes=[xo_name])

    def phase_ffn(self, l, xsrc, xdst, wUpB, wDnB, ffnp):
        nc, S, L = self.nc, self.S, self.L
        NT = 2
        with ExitStack() as st:
            xr = Ring(nc, st, "p7_x", 2, [128, 8, T], F32)
            sq = self.sb(st, "p7_sq", [128, 8, T], BF16)
            rstd = self.sb(st, "p7_rstd", [128, T], F32)
            hr = Ring(nc, st, "p7_h", 2, [128, 8, T], BF16)
            wr = Ring(nc, st, "p7_w", 2, [128, 8, 1024], BF16)
            actr = Ring(nc, st, "p7_act", 2, [128, 24, T], BF16)
            gsb = Ring(nc, st, "p7_g", 3, [128, T + 2], F32)
            gtmp = Ring(nc, st, "p7_gt", 3, [128, T], F32)
            tails = self.sb(st, "p7_tail", [128, 24, 2], F32)
            fp = self.sb(st, "p7_fp", [128, 4, 24], F32)
            self.ys = [self.sb(st, "p7_y%d" % i, [128, 8, T], F32) for i in range(2)]
            self.sqs = [self.sb(st, "p7_sqo%d" % i, [128, 8, T], BF16) for i in range(2)]
            S.dma('sp', fp[:], ffnp[:, l, :, :], writes=['p7_fp'])
            S.op('dve', lambda e: e.memset(tails[:], 0.0), writes=[('p7_tail', c) for c in range(24)])
            wup = wUpB[l].rearrange("(k p) n -> p k n", p=128)
            wdn = wDnB[l].rearrange("(k p) n -> p k n", p=128)
            wq = 0
            for tp in range(L // (NT * T)):
                tiles = []
                for ti in range(tp * NT, (tp + 1) * NT):
                    t0 = ti * T
                    xn, xt = xr.next()
                    S.dma('sp', xt[:], xsrc.rearrange("(c p) l -> p c l", p=128)[:, :, t0:t0 + T], reads=[(xsrc.tensor.name, ti)], writes=[xn])
                    hn, h = hr.next()
                    self.prenorm(l, 3, xn, xt, hn, h, "p7_sq", sq, "p7_rstd", rstd)
                    an, act = actr.next()
                    tiles.append((ti, t0, xn, xt, hn, h, an, act))
                for blk in range(6):
                    wn, w = wr.next()
                    wq += 1
                    q_ = 'sp' if wq % 2 == 0 else 'pool'
                    S.dma(q_, w[:, :, 0:512], wup[:, :, blk * 512:(blk + 1) * 512], reads=[('wUpB', l)], writes=[wn])
                    S.dma(q_, w[:, :, 512:1024], wup[:, :, DFF + blk * 512:DFF + (blk + 1) * 512], reads=[('wUpB', l)], writes=[wn])
                    for (ti, t0, xn, xt, hn, h, an, act) in tiles:
                        def fchunk(j, hn=hn, h=h, an=an, act=act, wn=wn, w=w):
                            ch = blk * 4 + j
                            pgn, pg = self.psum.next()
                            for k in range(8):
                                S.op('pe', lambda e, k=k: e.matmul(pg[:, :], lhsT=w[:, k, 512 + j * 128:512 + (j + 1) * 128], rhs=h[:, k, :], start=(k == 0), stop=(k == 7)),
                                     reads=[wn, hn], writes=[pgn], inc=(k == 7))
                            pvn, pv = self.psum.next()
                            for k in range(8):
                                S.op('pe', lambda e, k=k: e.matmul(pv[:, :], lhsT=w[:, k, j * 128:(j + 1) * 128], rhs=h[:, k, :], start=(k == 0), stop=(k == 7)),
                                     reads=[wn, hn], writes=[pvn], inc=(k == 7))
                            gn, g = gsb.next()
                            S.op('act', lambda e: e.copy(g[:, 2:T + 2], pg[:, :]), reads=[pgn], writes=[gn])
                            S.op('act', lambda e: e.copy(g[:, 0:2], tails[:, ch, :]), reads=[('p7_tail', ch)], writes=[gn])
                            S.op('act', lambda e: e.copy(tails[:, ch, :], g[:, T:T + 2]), reads=[gn], writes=[('p7_tail', ch)])
                            tn, tm = gtmp.next()
                            S.op('dve', lambda e: e.tensor_scalar(tm[:, :], g[:, 0:T], fp[:, 0, ch:ch + 1], fp[:, 3, ch:ch + 1], ALU.mult, ALU.add),
                                 reads=[gn, 'p7_fp'], writes=[tn])
                            S.op('dve', lambda e: e.scalar_tensor_tensor(tm[:, :], g[:, 1:T + 1], fp[:, 1, ch:ch + 1], tm[:, :], ALU.mult, ALU.add),
                                 reads=[gn, tn, 'p7_fp'], writes=[tn])
                            S.op('dve', lambda e: e.scalar_tensor_tensor(tm[:, :], g[:, 2:T + 2], fp[:, 2, ch:ch + 1], tm[:, :], ALU.mult, ALU.add),
                                 reads=[gn, tn, 'p7_fp'], writes=[tn])
                            S.op('act', lambda e: e.activation(tm[:, :], tm[:, :], AF.Gelu_apprx_tanh), reads=[tn], writes=[tn])
                            S.op('dve', lambda e: e.tensor_tensor(act[:, ch, :], pv[:, :], tm[:, :], ALU.mult), reads=[pvn, tn], writes=[(an, ch)])
                        for j in range(0, 4, 2):
                            S.interleave([lambda j=j: fchunk(j), lambda j=j: fchunk(j + 1)])
                for half in range(2):
                    banks = {}
                    for cpair in range(2):
                        for (ti, t0, xn, xt, hn, h, an, act) in tiles:
                            banks[ti] = [self.psum.next() for _ in range(2)]
                        for kb in range(3):
                            wn, w = wr.next()
                            wq += 1
                            q_ = 'sp' if wq % 2 == 0 else 'pool'
                            c00 = half * 512 + cpair * 256
                            S.dma(q_, w[:, :, 0:256], wdn[:, kb * 8:(kb + 1) * 8, c00:c00 + 256], reads=[('wDnB', l)], writes=[wn])
                            for (ti, t0, xn, xt, hn, h, an, act) in tiles:
                                for cc in range(2):
                                    pn, ps = banks[ti][cc]
                                    for k in range(8):
                                        kk = kb * 8 + k
                                        S.op('pe', lambda e: e.matmul(ps[:, :], lhsT=w[:, k, cc * 128:(cc + 1) * 128], rhs=act[:, kk, :], start=(kk == 0), stop=(kk == 23)),
                                             reads=[wn, (an, kk)], writes=[pn], inc=(k == 7))
                        for (ti, t0, xn, xt, hn, h, an, act) in tiles:
                            for cc in range(2):
                                c = half * 4 + cpair * 2 + cc
                                pn, ps = banks[ti][cc]
                                S.op('act', lambda e: e.activation(self.sqs[ti % 2][:, c, :], ps[:, :], AF.Square), reads=[pn], writes=[('p7_sq2', ti % 2)])
                                S.op('act', lambda e: e.copy(self.ys[ti % 2][:, c, :], ps[:, :]), reads=[pn], writes=[('p7_y2', ti % 2)])
                for (ti, t0, xn, xt, hn, h, an, act) in tiles:
                    on, xo_t = ('p7_y2', ti % 2), self.ys[ti % 2]
                    self.postnorm_residual(l, 4, ('p7_y2', ti % 2), self.ys[ti % 2], ('p7_sq2', ti % 2), self.sqs[ti % 2], 'p7_rstd', rstd, xn, xt, on, xo_t)
                    S.dma('pool', xdst.rearrange("(c p) l -> p c l", p=128)[:, :, t0:t0 + T], xo_t[:], reads=[on], writes=[(xdst.tensor.name, ti)])


def host_inputs(inputs, b, L):
    f = np.float32
    d = {}
    d["x"] = np.ascontiguousarray(inputs["x"][b, :L])
    d["mem"] = np.ascontiguousarray(inputs["mem"][b])
    d["ident"] = np.eye(128, dtype=f)
    gs = np.stack([inputs[k] for k in ("g_mix_pre", "g_mix_post", "g_mem", "g_mlp_pre", "g_mlp_post")], axis=1)
    d["gains"] = np.ascontiguousarray(gs.reshape(NL, 5, 8, 128).transpose(3, 0, 1, 2)).astype(f)
    d["w_in"] = inputs["w_in"]
    d["ffn_w_up"] = inputs["ffn_w_up"]
    d["ffn_w_down"] = inputs["ffn_w_down"]
    fp = np.concatenate([inputs["ffn_conv_w"], inputs["ffn_conv_b"][:, None, :]], axis=1)
    lp = np.concatenate([inputs["lru_conv_w"], inputs["lru_conv_b"][:, None], inputs["lru_ba"][:, None],
                         inputs["lru_bx"][:, None], inputs["lru_lambda"][:, None]], axis=1)
    d["lrup"] = np.ascontiguousarray(lp.reshape(NL, 8, 6, 128).transpose(3, 0, 1, 2)).astype(f)
    bd = np.zeros((NL, 2, 128, 6, 128), f)
    for wi, nm in enumerate(("lru_wa", "lru_wx")):
        w = inputs[nm]
        for c in range(6):
            bd[:, wi, 0:64, c, 0:64] = w[:, 2 * c]
            bd[:, wi, 64:128, c, 64:128] = w[:, 2 * c + 1]
    d["lru_bd"] = bd
    d["ssmd"] = np.ascontiguousarray(inputs["ssm_d"].reshape(NL, 6, 128).transpose(2, 0, 1)).astype(f)
    for nm in ("ssm_glu", "mem_wkv", "proj_a", "proj_b", "proj_c", "proj_x", "w_out"):
        d[nm] = inputs[nm]
    am = np.full((128, 2, 12, 256), -30000.0, f)
    kk = np.arange(128)[:, None]
    qq = np.arange(128)[None, :]
    for hd in range(12):
        dd = DILS[hd // 4]
        dp = (qq + 128 - kk).astype(f)
        dc = (qq - kk).astype(f)
        mp = np.where(kk >= qq, -ALIBI[hd] * dd * dp, -30000.0)
        mc = np.where(kk <= qq, -ALIBI[hd] * dd * dc, -30000.0)
        am[:, 0, hd, 0:128] = mp
        am[:, 0, hd, 128:256] = mc
        am[:, 1, hd, 128:256] = mc
    d["amask"] = am
    def lay_a(a):
        return a.reshape(NL, 24, 2, 64).transpose(2, 3, 0, 1).reshape(128, NL, 24)
    ld_full = np.repeat(inputs["ssm_log_dt"][:, :, None], 64, axis=2)
    d["s5A"] = np.ascontiguousarray(np.stack([lay_a(inputs["ssm_a_re"]), lay_a(inputs["ssm_a_im"]), lay_a(ld_full)], axis=2)).astype(f)
    def lay_r(a):
        return a.reshape(NL, 3072)
    rr = np.stack([lay_r(inputs["ssm_a_re"]), lay_r(inputs["ssm_a_im"]), lay_r(ld_full)], axis=1)
    d["s5R"] = np.ascontiguousarray(np.broadcast_to(rr[None], (128, NL, 3, 3072))).astype(f)
    sB = np.zeros((NL, 2, 128, 24, 128), f)
    sC = np.zeros((NL, 2, 128, 24, 128), f)
    for ri, (bn, cn) in enumerate((("ssm_b_re", "ssm_c_re"), ("ssm_b_im", "ssm_c_im"))):
        Bm = inputs[bn]
        Cm = inputs[cn]
        for p in range(24):
            for gl in range(2):
                r0 = 32 * (p % 4) + gl * 16
                sB[:, ri, r0:r0 + 16, p, gl * 64:(gl + 1) * 64] = Bm[:, 2 * p + gl].transpose(0, 2, 1)
                sC[:, ri, gl * 64:(gl + 1) * 64, p, r0:r0 + 16] = Cm[:, 2 * p + gl].transpose(0, 2, 1)
    d["s5B"] = sB
    d["s5C"] = sC
    d["ffnp"] = np.ascontiguousarray(fp.reshape(NL, 4, 24, 128).transpose(3, 0, 1, 2)).astype(f)
    return d


ALL_PHASES = ('prepass', 'prologue', 'p1', 'p2', 'p3', 'p4', 'p5', 'p6', 'p7', 'epilogue')


def kernel(**inputs):
    inputs = {k: np.asarray(v) for k, v in inputs.items()}
    L = inputs["x"].shape[1]
    kb = K(L, NL)
    nc = kb.build(ALL_PHASES)
    in_maps = []
    for b in range(2):
        hi = host_inputs(inputs, b, L)
        in_maps.append({k: hi[k] for k in kb.ins})
    res = run_bass_kernel_spmd(nc, in_maps, core_ids=[0, 1])
    return np.stack([res.results[b]["out"] for b in range(2)], axis=0)
```
